# Optimizing a Trainium2 kernel written in Bass

```python
import math
import jax, jax.numpy as jnp
from jax import lax
import numpy as np

D_MODEL = 2048
BATCH = 4
SEQ = 2048
DEPTH = 4
DEC_BATCH = 128
DEC_SEQ = 8
PAST_LEN = 16384
PAGE_SIZE = 128

PLE_DIM = 256
SSM_WIDTH = D_MODEL // 2
SSM_GROUP = 16
SSM_GROUPS = SSM_WIDTH // SSM_GROUP
SSM_STATE = 64
GLA_HEADS = 4
GLA_KEY_WIDTH = D_MODEL // 4
GLA_VAL_WIDTH = D_MODEL // 2
GLA_DK = GLA_KEY_WIDTH // GLA_HEADS
GLA_DV = GLA_VAL_WIDTH // GLA_HEADS
GLA_GATE_RANK = 16
GLA_TAU = 16.0
GLA_CHUNK = 64
CONV_WIDTH = D_MODEL // 2
CONV_K = 31
N_BRANCH = 3
DN_ALPHA = (2 * DEPTH) ** 0.25
DN_BETA = (8 * DEPTH) ** -0.25
LN_EPS = 1e-5

kernel_name = 'hybrid_s5_gla_conformer_gated_step'


def _in_sizes():
    return [SSM_WIDTH, SSM_WIDTH,
            GLA_KEY_WIDTH, GLA_KEY_WIDTH, GLA_VAL_WIDTH,
            GLA_GATE_RANK, GLA_VAL_WIDTH,
            CONV_WIDTH, CONV_WIDTH, CONV_WIDTH,
            N_BRANCH * D_MODEL]


def _split_points():
    return [int(v) for v in np.cumsum(_in_sizes())[:-1]]


def layer_norm(x, g, b):
    xf = x.astype(jnp.float32)
    mu = jnp.mean(xf, axis=-1, keepdims=True)
    var = jnp.mean(jnp.square(xf - mu), axis=-1, keepdims=True)
    return ((xf - mu) * lax.rsqrt(var + LN_EPS) * g.astype(jnp.float32) + b.astype(jnp.float32)).astype(x.dtype)


def s5_mixer(u, h0_re, h0_im, a_re, a_im, log_dt, b_re, b_im, c_re, c_im, d):
    f32 = jnp.float32
    bsz, t, _ = u.shape
    lam = lax.complex(a_re.astype(f32), a_im.astype(f32))
    dt = jnp.exp(log_dt.astype(f32))[:, None]
    a_bar = jnp.exp(lam * dt)
    b_bar = ((a_bar - 1.0) / lam)[..., None] * lax.complex(b_re.astype(f32), b_im.astype(f32))
    ug = u.astype(f32).reshape(bsz, t, SSM_GROUPS, SSM_GROUP)
    bu = jnp.einsum('btgc,gpc->btgp', ug.astype(jnp.complex64), b_bar)
    h0 = lax.complex(h0_re.astype(f32), h0_im.astype(f32))
    bu = bu.at[:, 0].add(a_bar * h0)
    a_seq = jnp.broadcast_to(a_bar, bu.shape)

    def combine(left, right):
        a_l, b_l = left
        a_r, b_r = right
        return a_l * a_r, a_r * b_l + b_r

    _, h = lax.associative_scan(combine, (a_seq, bu), axis=1)
    c = lax.complex(c_re.astype(f32), c_im.astype(f32))
    y = jnp.einsum('btgp,gcp->btgc', h, c).real + d.astype(f32).reshape(SSM_GROUPS, SSM_GROUP) * ug
    h_last = h[:, -1]
    return (y.reshape(bsz, t, SSM_WIDTH).astype(u.dtype),
            h_last.real.astype(h0_re.dtype), h_last.imag.astype(h0_im.dtype))


def gla_mixer(q, k, v, a_low, s0, w_a2, b_a, norm_g):
    f32 = jnp.float32
    bsz, t, _ = q.shape
    out_dtype = v.dtype
    q = q.astype(f32).reshape(bsz, t, GLA_HEADS, GLA_DK) * (GLA_DK ** -0.5)
    k = k.astype(f32).reshape(bsz, t, GLA_HEADS, GLA_DK)
    v = v.astype(f32).reshape(bsz, t, GLA_HEADS, GLA_DV)
    logit = (a_low @ w_a2 + b_a).astype(f32)
    log_a = (jax.nn.log_sigmoid(logit) / GLA_TAU).reshape(bsz, t, GLA_HEADS, GLA_DK)
    c = min(GLA_CHUNK, t)
    tp = -(-t // c) * c
    pad = tp - t
    if pad:
        padf = lambda z: jnp.pad(z, ((0, 0), (0, pad), (0, 0), (0, 0)))
        q, k, v, log_a = padf(q), padf(k), padf(v), padf(log_a)
    n = tp // c
    chunk = lambda z: z.reshape(bsz, n, c, GLA_HEADS, z.shape[-1])
    qc, kc, vc, la = chunk(q), chunk(k), chunk(v), chunk(log_a)
    cum = jnp.cumsum(la, axis=2)
    qd = qc * jnp.exp(cum)
    kd = kc * jnp.exp(-cum)
    mask = jnp.tril(jnp.ones((c, c), dtype=bool))
    att = jnp.where(mask, jnp.einsum('bnchd,bnjhd->bnhcj', qd, kd), 0.0)
    o_intra = jnp.einsum('bnhcj,bnjhv->bnchv', att, vc)
    last = cum[:, :, -1]
    kv = jnp.einsum('bnchd,bnchv->bnhdv', kc * jnp.exp(last[:, :, None] - cum), vc)

    def step(s, xs):
        qd_n, last_n, kv_n = xs
        o = jnp.einsum('bchd,bhdv->bchv', qd_n, s)
        return jnp.exp(last_n)[..., None] * s + kv_n, o

    s_t, o_inter = lax.scan(step, s0.astype(f32),
                            (jnp.moveaxis(qd, 1, 0), jnp.moveaxis(last, 1, 0), jnp.moveaxis(kv, 1, 0)))
    o = (o_intra + jnp.moveaxis(o_inter, 0, 1)).reshape(bsz, tp, GLA_HEADS, GLA_DV)[:, :t]
    mu = jnp.mean(o, axis=-1, keepdims=True)
    var = jnp.mean(jnp.square(o - mu), axis=-1, keepdims=True)
    o = (o - mu) * lax.rsqrt(var + LN_EPS) * norm_g.astype(f32).reshape(GLA_HEADS, GLA_DV)
    return o.reshape(bsz, t, GLA_VAL_WIDTH).astype(out_dtype), s_t.astype(s0.dtype)


def conv_mixer(ga, gb, buf, w_dw, b_dw, ln_g, ln_b):
    g = ga * jax.nn.sigmoid(gb)
    xp = jnp.concatenate([buf.astype(g.dtype), g], axis=1)
    y = lax.conv_general_dilated(xp, w_dw[:, None, :].astype(g.dtype), window_strides=(1,), padding='VALID',
                                 dimension_numbers=('NWC', 'WIO', 'NWC'),
                                 feature_group_count=CONV_WIDTH) + b_dw
    y = jax.nn.silu(layer_norm(y, ln_g, ln_b))
    return y, xp[:, -(CONV_K - 1):].astype(buf.dtype)


def trunk_layer(x, p, s5_re, s5_im, gla_s, conv_buf,
                w_in, b_in, s5_a_re, s5_a_im, s5_log_dt, s5_b_re, s5_b_im, s5_c_re, s5_c_im, s5_d,
                w_glu, b_glu, gla_w_a2, gla_b_a, gla_norm_g, conv_w, conv_b, conv_ln_g, conv_ln_b,
                p_s5, p_gla, p_conv, w_o, w_pg, w_pe, ln_g, ln_b):
    proj = x @ w_in + b_in
    (s5_u, s5_z, q, k, v, a_low, gla_z, ca, cb, cz, gates) = jnp.split(proj, _split_points(), axis=-1)
    y_s5, s5_re_new, s5_im_new = s5_mixer(s5_u, s5_re, s5_im, s5_a_re, s5_a_im, s5_log_dt,
                                          s5_b_re, s5_b_im, s5_c_re, s5_c_im, s5_d)
    y_s5 = jax.nn.gelu(y_s5)
    y_s5 = y_s5 * jax.nn.sigmoid(y_s5 @ w_glu + b_glu)
    br_s5 = (y_s5 * jax.nn.silu(s5_z)) @ p_s5
    o_gla, gla_new = gla_mixer(q, k, v, a_low, gla_s, gla_w_a2, gla_b_a, gla_norm_g)
    br_gla = (o_gla * jax.nn.silu(gla_z)) @ p_gla
    y_c, conv_new = conv_mixer(ca, cb, conv_buf, conv_w, conv_b, conv_ln_g, conv_ln_b)
    br_conv = (y_c * jax.nn.silu(cz)) @ p_conv
    g_s5, g_gla, g_conv = jnp.split(jax.nn.sigmoid(gates), N_BRANCH, axis=-1)
    out = (g_s5 * br_s5 + g_gla * br_gla + g_conv * br_conv) @ w_o
    h = DN_ALPHA * x + out
    h = h + jax.nn.sigmoid(h @ w_pg) * (p @ w_pe)
    return layer_norm(h, ln_g, ln_b), s5_re_new, s5_im_new, gla_new, conv_new


def setup_inputs(seed: int = 0) -> dict:
    key = jax.random.key(seed)
    ks = iter(jax.random.split(key, 40))
    f32 = jnp.float32
    nrm = lambda shape, scale: jax.random.normal(next(ks), shape, f32) * scale
    n_in = sum(_in_sizes())
    G, P, CG = SSM_GROUPS, SSM_STATE, SSM_GROUP
    a_im0 = jnp.pi * jnp.arange(P, dtype=f32)
    return {
        'x_prompt': nrm((BATCH, SEQ, D_MODEL), 1.0),
        'x_sample': nrm((DEC_BATCH, DEC_SEQ, D_MODEL), 1.0),
        'p_prompt': nrm((DEPTH, BATCH, SEQ, PLE_DIM), 1.0),
        'p_sample': nrm((DEPTH, DEC_BATCH, DEC_SEQ, PLE_DIM), 1.0),
        'state_s5_re': nrm((DEPTH, DEC_BATCH, G, P), 0.1),
        'state_s5_im': nrm((DEPTH, DEC_BATCH, G, P), 0.1),
        'state_gla': nrm((DEPTH, DEC_BATCH, GLA_HEADS, GLA_DK, GLA_DV), 0.1),
        'cache_conv': nrm((DEPTH, DEC_BATCH, CONV_K - 1, CONV_WIDTH), 0.5),
        'w_in': nrm((DEPTH, D_MODEL, n_in), D_MODEL ** -0.5),
        'b_in': nrm((DEPTH, n_in), 0.01),
        's5_a_re': -0.5 + nrm((DEPTH, G, P), 0.01),
        's5_a_im': a_im0 + nrm((DEPTH, G, P), 0.01),
        's5_log_dt': jax.random.uniform(next(ks), (DEPTH, G), f32, math.log(1e-3), math.log(1e-1)),
        's5_b_re': nrm((DEPTH, G, P, CG), (2 * CG) ** -0.5),
        's5_b_im': nrm((DEPTH, G, P, CG), (2 * CG) ** -0.5),
        's5_c_re': nrm((DEPTH, G, CG, P), P ** -0.5),
        's5_c_im': nrm((DEPTH, G, CG, P), P ** -0.5),
        's5_d': nrm((DEPTH, SSM_WIDTH), 1.0),
        'w_glu': nrm((DEPTH, SSM_WIDTH, SSM_WIDTH), SSM_WIDTH ** -0.5),
        'b_glu': nrm((DEPTH, SSM_WIDTH), 0.01),
        'gla_w_a2': nrm((DEPTH, GLA_GATE_RANK, GLA_KEY_WIDTH), GLA_GATE_RANK ** -0.5),
        'gla_b_a': nrm((DEPTH, GLA_KEY_WIDTH), 0.01),
        'gla_norm_g': 1.0 + nrm((DEPTH, GLA_VAL_WIDTH), 0.02),
        'conv_w': nrm((DEPTH, CONV_K, CONV_WIDTH), CONV_K ** -0.5),
        'conv_b': nrm((DEPTH, CONV_WIDTH), 0.01),
        'conv_ln_g': 1.0 + nrm((DEPTH, CONV_WIDTH), 0.02),
        'conv_ln_b': nrm((DEPTH, CONV_WIDTH), 0.01),
        'p_s5': nrm((DEPTH, SSM_WIDTH, D_MODEL), DN_BETA * SSM_WIDTH ** -0.5),
        'p_gla': nrm((DEPTH, GLA_VAL_WIDTH, D_MODEL), DN_BETA * GLA_VAL_WIDTH ** -0.5),
        'p_conv': nrm((DEPTH, CONV_WIDTH, D_MODEL), DN_BETA * CONV_WIDTH ** -0.5),
        'w_o': nrm((DEPTH, D_MODEL, D_MODEL), DN_BETA * D_MODEL ** -0.5),
        'w_pg': nrm((DEPTH, D_MODEL, D_MODEL), D_MODEL ** -0.5),
        'w_pe': nrm((DEPTH, PLE_DIM, D_MODEL), DN_BETA * PLE_DIM ** -0.5),
        'ln_g': 1.0 + nrm((DEPTH, D_MODEL), 0.02),
        'ln_b': nrm((DEPTH, D_MODEL), 0.01),
    }


def reference(x_prompt, x_sample, p_prompt, p_sample, state_s5_re, state_s5_im, state_gla, cache_conv,
              w_in, b_in, s5_a_re, s5_a_im, s5_log_dt, s5_b_re, s5_b_im, s5_c_re, s5_c_im, s5_d,
              w_glu, b_glu, gla_w_a2, gla_b_a, gla_norm_g, conv_w, conv_b, conv_ln_g, conv_ln_b,
              p_s5, p_gla, p_conv, w_o, w_pg, w_pe, ln_g, ln_b):
    bp = x_prompt.shape[0]
    sd = state_s5_re.dtype
    zero_s5 = jnp.zeros((bp, SSM_GROUPS, SSM_STATE), sd)
    zero_gla = jnp.zeros((bp, GLA_HEADS, GLA_DK, GLA_DV), state_gla.dtype)
    zero_conv = jnp.zeros((bp, CONV_K - 1, CONV_WIDTH), cache_conv.dtype)
    hp, hs = x_prompt, x_sample
    pr_re, pr_im, pr_gla, pr_conv = [], [], [], []
    sm_re, sm_im, sm_gla, sm_conv = [], [], [], []
    for i in range(DEPTH):
        wts = (w_in[i], b_in[i], s5_a_re[i], s5_a_im[i], s5_log_dt[i], s5_b_re[i], s5_b_im[i],
               s5_c_re[i], s5_c_im[i], s5_d[i], w_glu[i], b_glu[i], gla_w_a2[i], gla_b_a[i],
               gla_norm_g[i], conv_w[i], conv_b[i], conv_ln_g[i], conv_ln_b[i],
               p_s5[i], p_gla[i], p_conv[i], w_o[i], w_pg[i], w_pe[i], ln_g[i], ln_b[i])
        hp, a, b, c, d = trunk_layer(hp, p_prompt[i], zero_s5, zero_s5, zero_gla, zero_conv, *wts)
        pr_re.append(a); pr_im.append(b); pr_gla.append(c); pr_conv.append(d)
        hs, a, b, c, d = trunk_layer(hs, p_sample[i], state_s5_re[i], state_s5_im[i], state_gla[i],
                                     cache_conv[i], *wts)
        sm_re.append(a); sm_im.append(b); sm_gla.append(c); sm_conv.append(d)
    return (hp, hs,
            jnp.stack(pr_re), jnp.stack(pr_im), jnp.stack(pr_gla), jnp.stack(pr_conv),
            jnp.stack(sm_re), jnp.stack(sm_im), jnp.stack(sm_gla), jnp.stack(sm_conv))
```

```python
import math
from contextlib import ExitStack

import numpy as np
import concourse.bass as bass
import concourse.mybir as mybir
from concourse.bass_utils import run_bass_kernel_spmd

F32 = mybir.dt.float32
BF16 = mybir.dt.bfloat16
AF = mybir.ActivationFunctionType
ALU = mybir.AluOpType

ENGS = ["pe", "act", "dve", "pool", "sp"]
SAME_ENGINE_SYNC = True


class Buf:
    __slots__ = ("writers", "readers")

    def __init__(self):
        self.writers = {}
        self.readers = {}


class Op:
    __slots__ = ("id", "eng", "fn", "deps", "dma", "inc", "count", "semi", "target", "nd")

    def __init__(self, id, eng, fn, deps, dma, nd):
        self.id = id; self.eng = eng; self.fn = fn; self.deps = deps; self.dma = dma
        self.inc = False; self.count = 0; self.semi = -1; self.target = 0; self.nd = nd


class Prog:
    def __init__(self, n_dma_sems=40):
        self.ops = []
        self.n_dma_sems = n_dma_sems

    def add(self, eng, fn, r=(), w=(), dma=False, nd=1, extra=()):
        deps = set(extra)
        for b in r:
            deps.update(b.writers.values())
        for b in w:
            deps.update(b.writers.values())
            deps.update(b.readers.values())
        oid = len(self.ops)
        self.ops.append(Op(oid, eng, fn, deps, dma, nd))
        key = ("d", oid) if dma else eng
        for b in r:
            b.readers[key] = oid
        for b in w:
            if b.readers:
                b.writers = {key: oid}
                b.readers = {}
            else:
                b.writers[key] = oid
        return oid

    def emit(self, block, sems, dma_sems):
        ops = self.ops

        def skip(dop, op):
            return (not dop.dma) and (not op.dma) and dop.eng == op.eng and (op.eng == "pe" or not SAME_ENGINE_SYNC)

        for op in ops:
            for d in op.deps:
                dop = ops[d]
                if dop.dma or skip(dop, op):
                    continue
                dop.inc = True
        cnt = {e: 0 for e in ENGS}
        dtarget = [0] * self.n_dma_sems
        dlast = [None] * self.n_dma_sems
        half = self.n_dma_sems // 2
        rrs = {"pool": 0, "sp": 0}
        for op in ops:
            if op.dma:
                base = 0 if op.eng == "pool" else half
                rr = base + rrs[op.eng]
                rrs[op.eng] = (rrs[op.eng] + 1) % half
                op.semi = rr
                if dlast[rr] is not None:
                    op.deps.add(dlast[rr])
                dtarget[rr] += 16 * op.nd
                op.target = dtarget[rr]
                dlast[rr] = op.id
            elif op.inc:
                cnt[op.eng] += 1
                op.count = cnt[op.eng]
        per_eng = {e: [] for e in ENGS}
        for op in ops:
            per_eng[op.eng].append(op)
        self.counts = cnt

        def run(ename, eh):
            waited = {}
            for op in per_eng[ename]:
                for d in sorted(op.deps):
                    dop = ops[d]
                    if dop.dma:
                        s = dma_sems[dop.semi]; v = dop.target; k = ("d", dop.semi)
                    else:
                        if skip(dop, op):
                            continue
                        s = sems[dop.eng]; v = dop.count; k = dop.eng
                    if waited.get(k, 0) >= v:
                        continue
                    waited[k] = v
                    eh.wait_ge(s, v)
                ins = op.fn(eh)
                if ins is None:
                    continue
                if op.dma:
                    for i in ins:
                        i.then_inc(dma_sems[op.semi], 16)
                elif op.inc:
                    ins.then_inc(sems[ename], 1)

        @block.tensor
        def _(e):
            run("pe", e)

        @block.scalar
        def _(e):
            run("act", e)

        @block.vector
        def _(e):
            run("dve", e)

        @block.gpsimd
        def _(e):
            run("pool", e)

        @block.sync
        def _(e):
            run("sp", e)


class T:
    __slots__ = ("ap", "bufs", "slabs")

    def __init__(self, ap, bufs, slabs=None):
        self.ap = ap; self.bufs = bufs; self.slabs = slabs


def B(*ts):
    out = []
    for t in ts:
        if t is None:
            continue
        out.extend(t.bufs)
    return out


D = 2048
NIN = 14352
OFF = dict(u=0, z=1024, q=2048, k=2560, v=3072, al=4096, gz=4112, ca=5136, cb=6160, cz=7184, g=8208)
DN_ALPHA = (2 * 4) ** 0.25
LN_EPS = 1e-5
TWO_PI = 2.0 * math.pi
MAGIC = 12582912.0
GELU_C = 1.5957691216057308
SLAB = 2048
NSLAB = 46
WSLOT_ELEMS = 4096
NWSLOT = 4
SCAN_ENG = "pool"

W_NAMES = ["w_in", "b_in", "s5_a_re", "s5_a_im", "s5_log_dt", "s5_b_re", "s5_b_im", "s5_c_re", "s5_c_im", "s5_d",
           "w_glu", "b_glu", "gla_w_a2", "gla_b_a", "gla_norm_g", "conv_w", "conv_b", "conv_ln_g", "conv_ln_b",
           "p_s5", "p_gla", "p_conv", "w_o", "w_pg", "w_pe", "ln_g", "ln_b"]


class TileCfg:
    def __init__(self, kind, idx, TT, tok0, first, last):
        self.kind = kind; self.idx = idx; self.TT = TT; self.tok0 = tok0; self.first = first; self.last = last
        self.NB = TT // 4
        if kind == "p":
            self.NSEG = 1; self.SEGB = self.NB; self.NCH = TT // 64; self.NE = self.NCH; self.EL = 64
        else:
            self.NSEG = 16; self.SEGB = 2; self.NCH = 2; self.NE = 16; self.EL = 8


class Builder:
    def __init__(self, NL, NPT, debug=None):
        self.NL = NL; self.NPT = NPT; self.TP = NPT * 512
        self.debug = debug or {}
        self.nc = bass.Bass("TRN2", target_bir_lowering=False)
        self.P = Prog()
        self.st = ExitStack()
        self.store_ops = []
        self.bank_i = 0
        self.bank_alloc = [None] * 8
        self.wslot_i = 0
        self.uid = 0

    def sb(self, shape, dt=F32, name=None):
        self.uid += 1
        h = self.st.enter_context(self.nc.sbuf_tensor(name or ("t%d" % self.uid), shape, dt))
        return h

    def pt(self, shape, dt=F32):
        h = self.sb(shape, dt)
        return T(h, [Buf()])

    def dram_in(self, name, shape, dt=F32):
        return self.nc.dram_tensor(name, list(shape), dt, kind="ExternalInput").ap()

    def dram_out(self, name, shape, dt=F32):
        return self.nc.dram_tensor(name, list(shape), dt, kind="ExternalOutput").ap()

    def salloc(self, nelem, dt):
        esz = 4 if dt == F32 else 2
        nb = nelem * esz
        ns = (nb + SLAB - 1) // SLAB
        free = self.slab_free
        for s0 in range(0, NSLAB - ns + 1):
            if all(free[s0:s0 + ns]):
                for i in range(s0, s0 + ns):
                    free[i] = False
                if dt == F32:
                    ap = self.scrF[:, s0 * (SLAB // 4): s0 * (SLAB // 4) + nelem]
                else:
                    ap = self.scrB[:, s0 * (SLAB // 2): s0 * (SLAB // 2) + nelem]
                return T(ap, self.slab_bufs[s0:s0 + ns], (s0, ns))
        raise RuntimeError("scratch exhausted: need %d slabs, free map %s" % (ns, "".join("1" if f else "0" for f in free)))

    def sfree(self, *ts):
        for t in ts:
            s0, ns = t.slabs
            for i in range(s0, s0 + ns):
                assert not self.slab_free[i]
                self.slab_free[i] = True

    def bank(self):
        for k in range(8):
            i = (self.bank_i + k) % 8
            b = self.banks[i].bufs[0]
            a = self.bank_alloc[i]
            free = a is None or (b.writers and max(b.writers.values()) >= a and len(b.readers) > 0)
            if free:
                self.bank_alloc[i] = len(self.P.ops)
                self.bank_i = (i + 1) % 8
                return self.banks[i]
        raise RuntimeError("no free PSUM bank")

    def mm(self, out, lhsT, rhs, start, stop, r, w):
        self.P.add("pe", lambda e: e.matmul(out, lhsT=lhsT, rhs=rhs, start=start, stop=stop), r=r, w=w)

    def tr(self, out, in_, ident, r, w):
        self.P.add("pe", lambda e: e.transpose(out, in_, ident), r=r, w=w)

    def act(self, out, in_, func, r, w, bias=None, scale=None):
        kw = {}
        if bias is not None:
            kw["bias"] = bias
        if scale is not None:
            kw["scale"] = scale
        self.P.add("act", lambda e: e.activation(out=out, in_=in_, func=func, **kw), r=r, w=w)

    def tt(self, out, in0, in1, op, r, w, eng="dve"):
        self.P.add(eng, lambda e: e.tensor_tensor(out=out, in0=in0, in1=in1, op=op), r=r, w=w)

    def ts(self, out, in0, s1, s2, op0, op1, r, w, eng="dve"):
        if s2 is None:
            self.P.add(eng, lambda e: e.tensor_scalar(out=out, in0=in0, scalar1=s1, scalar2=None, op0=op0), r=r, w=w)
        else:
            self.P.add(eng, lambda e: e.tensor_scalar(out=out, in0=in0, scalar1=s1, scalar2=s2, op0=op0, op1=op1), r=r, w=w)

    def stt(self, out, in0, scalar, in1, op0, op1, r, w, eng="dve"):
        self.P.add(eng, lambda e: e.scalar_tensor_tensor(out=out, in0=in0, scalar=scalar, in1=in1, op0=op0, op1=op1), r=r, w=w)

    def cp(self, out, in_, r, w, eng="dve"):
        if eng == "act":
            self.act(out, in_, AF.Identity, r, w)
        else:
            self.P.add(eng, lambda e: e.tensor_copy(out=out, in_=in_), r=r, w=w)

    def memset(self, ap, val, w, eng="dve"):
        self.P.add(eng, lambda e: e.memset(ap, val), w=w)

    def recip(self, out, in_, r, w):
        self.P.add("dve", lambda e: e.reciprocal(out=out, in_=in_), r=r, w=w)

    def scan(self, out, d0, d1, r, w):
        self.P.add("dve", lambda e: e.tensor_tensor_scan(out=out, data0=d0, data1=d1, initial=0.0, op0=ALU.mult, op1=ALU.add), r=r, w=w)

    def asel(self, ap, pattern, op, base, cm, w):
        self.P.add("pool", lambda e: e.affine_select(out=ap, in_=ap, pattern=pattern, compare_op=op, fill=0.0, base=base, channel_multiplier=cm), r=w, w=w)

    def dma(self, eng, out, in_, r, w, nc_ok=False, store=False):
        nc = self.nc
        if nc_ok:
            def fn(e):
                with nc.allow_non_contiguous_dma(reason="small strided parameter / layout load"):
                    return [e.dma_start(out=out, in_=in_)]
        else:
            def fn(e):
                return [e.dma_start(out=out, in_=in_)]
        if store and eng == "sp":
            eng = "pool"
        oid = self.P.add(eng, fn, r=r, w=w, dma=True)
        if store:
            self.store_ops.append(oid)
        return oid

    def wload(self, src, KT, NC, rb=(), key=None):
        slot = self.wslots[self.wslot_i]
        self.wslot_i = (self.wslot_i + 1) % NWSLOT
        view = slot.ap[:, 0:KT * NC].rearrange("p (k n) -> p k n", n=NC)
        if key is None:
            self.dma("sp", view, src, r=list(rb), w=B(slot))
            return T(view, slot.bufs)
        Lk = key[1]
        if key not in self.wblk:
            bi = self.wblk_n[Lk]
            self.wblk_n[Lk] += 1
            assert bi < self.NBLK, "too many weight blocks"
            bb = Buf()
            self.wblk[key] = (bi, bb)
            dstv = self.wscr[Lk][bi, :, 0:KT * NC].rearrange("p (k n) -> p k n", n=NC)
            self.dma("pool", dstv, src, r=[], w=[bb])
        bi, bb = self.wblk[key]
        self.dma("sp", slot.ap[:, 0:KT * NC], self.wscr[Lk][bi, :, 0:KT * NC], r=[bb], w=B(slot))
        return T(view, slot.bufs)

    def build(self):
        nc = self.nc; NL = self.NL; TP = self.TP
        I = {}
        I["xp"] = self.dram_in("xp", [TP, D]); I["xs"] = self.dram_in("xs", [128, D])
        I["pp"] = self.dram_in("pp", [NL, TP, 256]); I["pps"] = self.dram_in("pps", [NL, 128, 256])
        I["sre"] = self.dram_in("sre", [NL, 512, 128]); I["sim"] = self.dram_in("sim", [NL, 512, 128])
        I["sgla"] = self.dram_in("sgla", [NL, 16, 4, 128, 256]); I["cconv"] = self.dram_in("cconv", [NL, 480, 1024])
        shapes = dict(w_in=[NL, D, NIN], b_in=[NL, NIN], s5_a_re=[NL, 64, 64], s5_a_im=[NL, 64, 64], s5_log_dt=[NL, 64],
                      s5_b_re=[NL, 64, 64, 16], s5_b_im=[NL, 64, 64, 16], s5_c_re=[NL, 64, 16, 64], s5_c_im=[NL, 64, 16, 64],
                      s5_d=[NL, 1024], w_glu=[NL, 1024, 1024], b_glu=[NL, 1024], gla_w_a2=[NL, 16, 512], gla_b_a=[NL, 512],
                      gla_norm_g=[NL, 1024], conv_w=[NL, 31, 1024], conv_b=[NL, 1024], conv_ln_g=[NL, 1024], conv_ln_b=[NL, 1024],
                      p_s5=[NL, 1024, D], p_gla=[NL, 1024, D], p_conv=[NL, 1024, D], w_o=[NL, D, D], w_pg=[NL, D, D],
                      w_pe=[NL, 256, D], ln_g=[NL, D], ln_b=[NL, D])
        for n in W_NAMES:
            I[n] = self.dram_in(n, shapes[n])
        O = {}
        O["yp"] = self.dram_out("yp", [TP, D]); O["ys"] = self.dram_out("ys", [128, D])
        O["sre_p"] = self.dram_out("sre_p", [NL, 32, 128]); O["sim_p"] = self.dram_out("sim_p", [NL, 32, 128])
        O["gla_p"] = self.dram_out("gla_p", [NL, 4, 128, 256]); O["conv_p"] = self.dram_out("conv_p", [NL, 30, 1024])
        O["sre_s"] = self.dram_out("sre_s", [NL, 512, 128]); O["sim_s"] = self.dram_out("sim_s", [NL, 512, 128])
        O["gla_s"] = self.dram_out("gla_s", [NL, 16, 4, 128, 256]); O["conv_s"] = self.dram_out("conv_s", [NL, 480, 1024])
        self.I = I; self.O = O
        ntiles = self.NPT + 1
        self.s5w = nc.dram_tensor("s5w_scr", [NL, 5, 128, 4096], BF16, kind="Internal").ap()
        self.spill = nc.dram_tensor("x_spill", [2, ntiles, 128, 16 * 512], BF16, kind="Internal").ap()
        self.s5w_buf = [[Buf() for _ in range(5)] for _ in range(NL)]
        self.NBLK = 90
        self.wscr = [nc.dram_tensor("w_bf16_scr%d" % l, [self.NBLK, 128, WSLOT_ELEMS], BF16, kind="Internal").ap() for l in range(NL)]
        self.wblk = {}
        self.wblk_n = [0] * NL
        self.spill_buf = [[Buf() for _ in range(ntiles)] for _ in range(2)]

        st = self.st
        with st:
            self.sems = {e: st.enter_context(nc.semaphore("s_" + e)) for e in ENGS}
            self.dsems = [st.enter_context(nc.semaphore("d%d" % i)) for i in range(self.P.n_dma_sems)]
            self.banks = []
            for i in range(8):
                h = st.enter_context(nc.psum_tensor("bank%d" % i, [128, 512], F32))
                t = T(h, [Buf()])
                self.banks.append(t)
            scr = self.sb([128, NSLAB * SLAB // 2], BF16, "scratch")
            self.scrB = scr
            self.scrF = scr.bitcast(F32)
            self.slab_bufs = [Buf() for _ in range(NSLAB)]
            self.slab_free = [True] * NSLAB
            self.wslots = [self.pt([128, WSLOT_ELEMS], BF16) for _ in range(NWSLOT)]
            self.xT = self.pt([128, 16, 512], BF16)
            self.pT = self.pt([128, 2, 512], BF16)
            self.ys5g = self.pt([128, 8, 512], BF16)
            self.ogg = self.pt([128, 8, 512], BF16)
            self.ycg = self.pt([128, 8, 512], BF16)
            self.Sst = self.pt([128, 4, 256], F32)
            self.Sbf = self.pt([128, 4, 256], BF16)
            self.halo = self.pt([128, 8, 30], F32)
            self.carry = self.pt([128, 2, 32], F32)
            self.T4 = self.pt([128, NL, 2, 2, 32], F32)
            self.consts()
            self.params_alloc()
            for L in range(NL):
                self.s5_prep(L)
            tiles = [TileCfg("p", i, 512, 512 * i, i == 0, i == self.NPT - 1) for i in range(self.NPT)]
            tiles.append(TileCfg("s", self.NPT, 128, 0, True, True))
            for L in range(NL):
                self.params_load(L)
                for tc in tiles:
                    self.tile_layer(L, tc)
            self.P.add("sp", lambda e: None, extra=list(self.store_ops))
            assert all(self.slab_free), "scratch leak"
            with nc.Block() as block:
                self.P.emit(block, self.sems, self.dsems)
        return nc

    def consts(self):
        self.identF = self.pt([128, 128], F32)
        self.identB = self.pt([128, 128], BF16)
        for t in (self.identF, self.identB):
            self.memset(t.ap[:], 1.0, B(t), eng="pool")
            self.asel(t.ap[:], [[-1, 128]], ALU.is_equal, 0, 1, B(t))
        self.permI = self.pt([128, 128], F32)
        self.cp(self.permI.ap[:].rearrange("p (t g c) -> p t g c", t=4, g=2, c=16),
                self.identF.ap[:].rearrange("p (g t c) -> p t g c", t=4, g=2, c=16), B(self.identF), B(self.permI))
        self.ones = {}
        for n in (256, 1024, 2048):
            t = self.pt([128, 128], F32)
            self.memset(t.ap[:], 1.0 / n, B(t), eng="pool")
            self.ones[n] = t
        self.maskP = self.pt([64, 64], F32)
        self.memset(self.maskP.ap[:], 1.0, B(self.maskP), eng="pool")
        self.asel(self.maskP.ap[:], [[1, 64]], ALU.is_ge, 0, -1, B(self.maskP))
        self.maskS = self.pt([64, 8, 8], F32)
        self.memset(self.maskS.ap[:], 1.0, B(self.maskS), eng="pool")
        self.asel(self.maskS.ap[:], [[8, 8], [1, 8]], ALU.is_ge, 0, -1, B(self.maskS))
        self.asel(self.maskS.ap[:], [[-8, 8], [0, 8]], ALU.is_ge, 0, 1, B(self.maskS))
        self.rowm = self.pt([64, 8], F32)
        self.memset(self.rowm.ap[:], 1.0, B(self.rowm), eng="pool")
        self.asel(self.rowm.ap[:], [[-8, 8]], ALU.is_ge, 0, 1, B(self.rowm))
        self.asel(self.rowm.ap[:], [[8, 8]], ALU.is_ge, 7, -1, B(self.rowm))
        self.TMa = self.pt([128, 4, 16], F32)
        self.TMb = self.pt([128, 4, 16], F32)
        self.memset(self.TMa.ap[:], 1.0, B(self.TMa), eng="pool")
        self.asel(self.TMa.ap[:], [[16, 4], [0, 16]], ALU.is_ge, 15, -1, B(self.TMa))
        self.memset(self.TMb.ap[:], 1.0, B(self.TMb), eng="pool")
        self.asel(self.TMb.ap[:], [[16, 4], [0, 16]], ALU.is_ge, 79, -1, B(self.TMb))
        self.coefP = self.pt([128, 512], F32)
        self.memset(self.coefP.ap[:], 1.0, B(self.coefP), eng="pool")
        self.memset(self.coefP.ap[:, 0:512:64], 0.0, B(self.coefP), eng="pool")
        self.coefS = self.pt([128, 128], F32)
        self.memset(self.coefS.ap[:], 1.0, B(self.coefS), eng="pool")
        self.memset(self.coefS.ap[:, 0:128:8], 0.0, B(self.coefS), eng="pool")

    def params_alloc(self):
        self.bcol = self.pt([128, 96], F32)
        self.bal = self.pt([16, 1], F32)
        self.wa2 = self.pt([16, 512], BF16)
        self.bubc = self.pt([128, 1024], F32)
        self.bvbc = self.pt([128, 1024], F32)
        self.bglu = self.pt([128, 8], F32)
        self.nba = self.pt([128, 4], F32)
        self.normg = self.pt([128, 8], F32)
        self.convw = self.pt([128, 8, 31], F32)
        self.convb = self.pt([128, 8], F32)
        self.clng = self.pt([128, 8], F32)
        self.clnb = self.pt([128, 8], F32)
        self.lng = self.pt([128, 16], F32)
        self.lnb = self.pt([128, 16], F32)
        self.BC = dict(z=0, q=8, k=12, gz=16, ca=24, cb=32, cz=40, g=48)

    def params_load(self, L):
        I = self.I
        segs = [("z", 8), ("q", 4), ("k", 4), ("gz", 8), ("ca", 8), ("cb", 8), ("cz", 8), ("g", 48)]
        for nm, n in segs:
            c0 = self.BC[nm]
            src = I["b_in"][L, OFF[nm]:OFF[nm] + 128 * n].rearrange("(n p) -> p n", p=128)
            self.dma("sp", self.bcol.ap[:, c0:c0 + n], src, [], B(self.bcol), nc_ok=True)
        self.dma("sp", self.bal.ap[:, :], I["b_in"][L, OFF["al"]:OFF["al"] + 16].rearrange("(p o) -> p o", o=1), [], B(self.bal), nc_ok=True)
        self.dma("pool", self.wa2.ap[:, :], I["gla_w_a2"][L, :, :], [], B(self.wa2))
        self.dma("sp", self.bubc.ap[:, :], I["b_in"][L:L + 1, 0:1024].to_broadcast([128, 1024]), [], B(self.bubc))
        self.dma("sp", self.bvbc.ap[:, :], I["b_in"][L:L + 1, OFF["v"]:OFF["v"] + 1024].to_broadcast([128, 1024]), [], B(self.bvbc))

        def col(dst, src1d, n):
            self.dma("sp", dst.ap[:, 0:n], src1d.rearrange("(n p) -> p n", p=128), [], B(dst), nc_ok=True)
        col(self.bglu, I["b_glu"][L, :], 8)
        col(self.nba, I["gla_b_a"][L, :], 4)
        self.ts(self.nba.ap[:, :], self.nba.ap[:, :], -1.0, None, ALU.mult, None, B(self.nba), B(self.nba))
        col(self.normg, I["gla_norm_g"][L, :], 8)
        col(self.convb, I["conv_b"][L, :], 8)
        col(self.clng, I["conv_ln_g"][L, :], 8)
        col(self.clnb, I["conv_ln_b"][L, :], 8)
        col(self.lng, I["ln_g"][L, :], 16)
        col(self.lnb, I["ln_b"][L, :], 16)
        for ct in range(8):
            self.dma("sp", self.convw.ap[:, ct, :], I["conv_w"][L, :, ct * 128:(ct + 1) * 128].rearrange("k p -> p k"), [], B(self.convw), nc_ok=True)

    def s5_prep(self, L):
        I = self.I
        f = lambda n: self.salloc(n, F32)
        are = f(32); aim = f(32); ldt = f(32); dK = f(32)
        Bre = f(512); Bim = f(512); Cre = f(512); Cim = f(512)
        for g in range(2):
            ps_ = slice(64 * g, 64 * g + 64)
            self.dma("sp", are.ap[ps_, :], I["s5_a_re"][L].rearrange("(j two) p -> two p j", two=2)[g], [], B(are), nc_ok=True)
            self.dma("sp", aim.ap[ps_, :], I["s5_a_im"][L].rearrange("(j two) p -> two p j", two=2)[g], [], B(aim), nc_ok=True)
            self.dma("sp", ldt.ap[ps_, :], I["s5_log_dt"][L].rearrange("(j two) -> two j", two=2)[g:g + 1, :].to_broadcast([64, 32]), [], B(ldt), nc_ok=True)
            self.dma("sp", Bre.ap[ps_, :].rearrange("p (j c) -> p j c", c=16), I["s5_b_re"][L].rearrange("(j two) p c -> two p j c", two=2)[g], [], B(Bre), nc_ok=True)
            self.dma("sp", Bim.ap[ps_, :].rearrange("p (j c) -> p j c", c=16), I["s5_b_im"][L].rearrange("(j two) p c -> two p j c", two=2)[g], [], B(Bim), nc_ok=True)
            for c in range(16):
                self.dma("sp", Cre.ap[ps_, c:512:16], I["s5_c_re"][L].rearrange("(j two) c p -> two c p j", two=2)[g, c], [], B(Cre), nc_ok=True)
                self.dma("sp", Cim.ap[ps_, c:512:16], I["s5_c_im"][L].rearrange("(j two) c p -> two c p j", two=2)[g, c], [], B(Cim), nc_ok=True)
            for s in range(4):
                p0 = 64 * g + 16 * s
                self.dma("sp", dK.ap[p0:p0 + 16, :], I["s5_d"][L].rearrange("(j g c) -> g c j", g=2, c=16)[g], [], B(dK), nc_ok=True)
        dt = f(32); ardt = f(32); aidt = f(32)
        self.act(dt.ap, ldt.ap, AF.Exp, B(ldt), B(dt))
        self.tt(ardt.ap, are.ap, dt.ap, ALU.mult, B(are, dt), B(ardt))
        self.tt(aidt.ap, aim.ap, dt.ap, ALU.mult, B(aim, dt), B(aidt))
        MAG = f(256); TSC = f(512); R1 = f(512); R2 = f(512); SC = f(512)
        for k in range(8):
            m = k - 3
            self.act(MAG.ap[:, k * 32:(k + 1) * 32], ardt.ap, AF.Exp, B(ardt), B(MAG), scale=float(m))
            self.ts(TSC.ap[:, k * 32:(k + 1) * 32], aidt.ap, float(m) / TWO_PI, None, ALU.mult, None, B(aidt), B(TSC))
        self.ts(TSC.ap[:, 256:512], TSC.ap[:, 0:256], 0.25, None, ALU.add, None, B(TSC), B(TSC))
        self.ts(R1.ap, TSC.ap, MAGIC, None, ALU.add, None, B(TSC), B(R1))
        self.ts(R2.ap, R1.ap, -MAGIC, None, ALU.add, None, B(R1), B(R2))
        self.tt(R1.ap, TSC.ap, R2.ap, ALU.subtract, B(TSC, R2), B(R1))
        self.act(SC.ap, R1.ap, AF.Sin, B(R1), B(SC), scale=TWO_PI)
        PWr = f(256); PWi = f(256)
        self.tt(PWr.ap, MAG.ap, SC.ap[:, 256:512], ALU.mult, B(MAG, SC), B(PWr))
        self.tt(PWi.ap, MAG.ap, SC.ap[:, 0:256], ALU.mult, B(MAG, SC), B(PWi))
        self.sfree(MAG, TSC, R1, R2, SC, dt, ardt, aidt, ldt)
        pw = lambda Tt, k: Tt.ap[:, k * 32:(k + 1) * 32]
        T4 = self.T4
        self.cp(T4.ap[:, L, 0, 0, :], pw(PWr, 7), B(PWr), B(T4))
        self.cp(T4.ap[:, L, 0, 1, :], pw(PWr, 7), B(PWr), B(T4))
        self.cp(T4.ap[:, L, 1, 0, :], pw(PWi, 7), B(PWi), B(T4))
        self.ts(T4.ap[:, L, 1, 1, :], pw(PWi, 7), -1.0, None, ALU.mult, None, B(PWi), B(T4))
        nr = f(32); t1 = f(32); t2 = f(32); den = f(32); Ere = f(32); Eim = f(32)
        self.ts(nr.ap, pw(PWr, 4), -1.0, None, ALU.add, None, B(PWr), B(nr))
        ni = pw(PWi, 4)
        self.tt(den.ap, are.ap, are.ap, ALU.mult, B(are), B(den))
        self.tt(t1.ap, aim.ap, aim.ap, ALU.mult, B(aim), B(t1))
        self.tt(den.ap, den.ap, t1.ap, ALU.add, B(den, t1), B(den))
        self.recip(den.ap, den.ap, B(den), B(den))
        self.tt(t1.ap, nr.ap, are.ap, ALU.mult, B(nr, are), B(t1))
        self.tt(t2.ap, ni, aim.ap, ALU.mult, B(PWi, aim), B(t2))
        self.tt(t1.ap, t1.ap, t2.ap, ALU.add, B(t1, t2), B(t1))
        self.tt(Ere.ap, t1.ap, den.ap, ALU.mult, B(t1, den), B(Ere))
        self.tt(t1.ap, ni, are.ap, ALU.mult, B(PWi, are), B(t1))
        self.tt(t2.ap, nr.ap, aim.ap, ALU.mult, B(nr, aim), B(t2))
        self.tt(t1.ap, t1.ap, t2.ap, ALU.subtract, B(t1, t2), B(t1))
        self.tt(Eim.ap, t1.ap, den.ap, ALU.mult, B(t1, den), B(Eim))
        self.sfree(nr, t2, den, are, aim)

        def v3(Tt):
            return Tt.ap.rearrange("p (j c) -> p j c", c=16)

        def bc(ap32):
            return ap32.rearrange("p (j o) -> p j o", o=1).to_broadcast([128, 32, 16])

        def cmul(outr, outi, ar, ai, br, bi, rb, tmpT, neg_im=False):
            tv = v3(tmpT)
            if outr is not None:
                self.tt(outr, bc(ar), br, ALU.mult, rb, rb)
                self.tt(tv, bc(ai), bi, ALU.mult, rb + B(tmpT), B(tmpT))
                self.tt(outr, outr, tv, ALU.subtract, rb + B(tmpT), rb)
            if outi is not None:
                self.tt(outi, bc(ar), bi, ALU.mult, rb, rb)
                self.tt(tv, bc(ai), br, ALU.mult, rb + B(tmpT), B(tmpT))
                self.tt(outi, outi, tv, ALU.add, rb + B(tmpT), rb)
                if neg_im:
                    self.ts(outi, outi, -1.0, None, ALU.mult, None, rb, rb)

        tmp = f(512)
        bbr = f(512); bbi = f(512)
        allb = B(PWr, PWi, Ere, Eim, Bre, Bim, Cre, Cim, bbr, bbi)
        cmul(v3(bbr), v3(bbi), Ere.ap, Eim.ap, v3(Bre), v3(Bim), allb, tmp)
        self.sfree(Ere, Eim, Bre, Bim, t1)
        Xr = f(2048); XiN = f(2048); Zr = f(2048); Zi = f(2048)
        v4 = lambda Tt: Tt.ap.rearrange("p (j s c) -> p j s c", s=4, c=16)
        rb = allb + B(Xr, XiN, Zr, Zi)
        for s in range(4):
            cmul(v4(Xr)[:, :, s, :], v4(XiN)[:, :, s, :], pw(PWr, 3 - s), pw(PWi, 3 - s), v3(bbr), v3(bbi), rb, tmp, neg_im=True)
            cmul(v4(Zr)[:, :, s, :], v4(Zi)[:, :, s, :], pw(PWr, 3 + s), pw(PWi, 3 + s), v3(Cre), v3(Cim), rb, tmp)
        Mf = f(4096)
        self.memset(Mf.ap, 0.0, B(Mf))
        Mf3 = Mf.ap.rearrange("p (j n) -> p j n", n=128)
        for j4 in range(8):
            bk = self.bank()
            for jj in range(4):
                j = j4 * 4 + jj
                for g in range(2):
                    pr = slice(64 * g, 64 * g + 64)
                    o = bk.ap[pr, jj * 128 + 64 * g: jj * 128 + 64 * g + 64]
                    self.mm(o, Xr.ap[pr, j * 64:(j + 1) * 64], Zr.ap[pr, j * 64:(j + 1) * 64], True, False, B(Xr, Zr), B(bk))
                    self.mm(o, XiN.ap[pr, j * 64:(j + 1) * 64], Zi.ap[pr, j * 64:(j + 1) * 64], False, True, B(XiN, Zi), B(bk))
            for g in range(2):
                pr = slice(64 * g, 64 * g + 64)
                TM = self.TMa if g == 0 else self.TMb
                outv = Mf3[pr, j4 * 4:(j4 + 1) * 4, :].rearrange("p j (t g c) -> p j t g c", t=4, g=2, c=16)[:, :, :, g, :]
                inv = bk.ap[pr, :].rearrange("p (j g t c) -> p j g t c", j=4, g=2, t=4, c=16)[:, :, g, :, :]
                mk = TM.ap[pr, :, :].rearrange("p (o t) c -> p o t c", o=1).to_broadcast([64, 4, 4, 16])
                self.tt(outv, inv, mk, ALU.mult, B(bk, TM), B(Mf))
        for j in range(32):
            self.stt(Mf3[:, j, :], self.permI.ap[:, :], dK.ap[:, j:j + 1], Mf3[:, j, :], ALU.mult, ALU.add, B(self.permI, dK, Mf), B(Mf))
        Mb = self.salloc(4096, BF16)
        self.cp(Mb.ap, Mf.ap, B(Mf), B(Mb), eng="act")
        self.dma("sp", self.s5w[L, 0], Mb.ap, B(Mb), [self.s5w_buf[L][0]])
        self.sfree(Xr, XiN, Zr, Zi, Mf, Mb, dK)
        Wn_r = f(2048); Wn_i = f(2048)
        rb = allb + B(Wn_r, Wn_i)
        for s in range(4):
            cmul(v4(Wn_r)[:, :, s, :], v4(Wn_i)[:, :, s, :], pw(PWr, 6 - s), pw(PWi, 6 - s), v3(bbr), v3(bbi), rb, tmp)
        for plane, Wn in enumerate((Wn_r, Wn_i)):
            VS = f(4096)
            self.memset(VS.ap, 0.0, B(VS))
            VS3 = VS.ap.rearrange("p (j n) -> p j n", n=128)
            Wn3 = Wn.ap.rearrange("p (j n) -> p j n", n=64)
            for g in range(2):
                pr = slice(64 * g, 64 * g + 64)
                self.cp(VS3[pr, :, 64 * g:64 * g + 64], Wn3[pr, :, :], B(Wn), B(VS))
            WSt = self.salloc(4096, BF16)
            for j4 in range(8):
                bk = self.bank()
                for jj in range(4):
                    j = j4 * 4 + jj
                    self.tr(bk.ap[:, jj * 128:(jj + 1) * 128], VS3[:, j, :], self.identF.ap[:, :], B(VS, self.identF), B(bk))
                self.cp(WSt.ap[:, j4 * 512:(j4 + 1) * 512], bk.ap[:, :], B(bk), B(WSt), eng="act")
            self.dma("sp", self.s5w[L, 1 + plane], WSt.ap, B(WSt), [self.s5w_buf[L][1 + plane]])
            self.sfree(VS, WSt)
        rb = allb + B(Wn_r, Wn_i)
        for t in range(4):
            cmul(v4(Wn_r)[:, :, t, :], v4(Wn_i)[:, :, t, :], pw(PWr, 4 + t), pw(PWi, 4 + t), v3(Cre), v3(Cim), rb, tmp, neg_im=True)
        for plane, Wn in enumerate((Wn_r, Wn_i)):
            WYb = self.salloc(4096, BF16)
            self.memset(WYb.ap, 0.0, B(WYb))
            for g in range(2):
                pr = slice(64 * g, 64 * g + 64)
                outv = WYb.ap[pr, :].rearrange("p (j t g c) -> p j t g c", t=4, g=2, c=16)[:, :, :, g, :]
                inv = Wn.ap[pr, :].rearrange("p (j t c) -> p j t c", t=4, c=16)
                self.cp(outv, inv, B(Wn), B(WYb))
            self.dma("sp", self.s5w[L, 3 + plane], WYb.ap, B(WYb), [self.s5w_buf[L][3 + plane]])
            self.sfree(WYb)
        self.sfree(Wn_r, Wn_i, tmp, bbr, bbi, Cre, Cim, PWr, PWi)

    def win(self, L, col0, ncols, KT=16):
        return self.I["w_in"][L, :, col0:col0 + ncols].rearrange("(k p) n -> p k n", p=128)

    def wmat(self, name, L, col0, ncols):
        return self.I[name][L, :, col0:col0 + ncols].rearrange("(k p) n -> p k n", p=128)

    def wl_in(self, L, col0, ncols):
        return self.wload(self.win(L, col0, ncols), 16, ncols, key=("w_in", L, col0, ncols))

    def wl_mat(self, name, L, col0, ncols, KT):
        return self.wload(self.wmat(name, L, col0, ncols), KT, ncols, key=(name, L, col0, ncols))

    def proj_fm(self, ws, jj, KT, rhs_fn, rhsT, TT):
        bk = self.bank()
        for kt in range(KT):
            self.mm(bk.ap[:, 0:TT], ws.ap[:, kt, jj * 128:(jj + 1) * 128], rhs_fn(kt), kt == 0, kt == KT - 1, B(ws, rhsT), B(bk))
        return bk

    def tile_layer(self, L, tc):
        self.load_x(L, tc)
        self.s5_phase(L, tc)
        self.conv_phase(L, tc)
        self.dbg_dump("dbg_conv", self.ycg, L, tc)
        self.s5_back(L, tc)
        self.dbg_dump("dbg_s5", self.ys5g, L, tc)
        self.gla_phase(L, tc)
        self.dbg_dump("dbg_gla", self.ogg, L, tc)
        self.merge_phase(L, tc)

    def dbg_dump(self, name, t, L, tc):
        if not self.debug or L != 0:
            return
        key = name + "_" + tc.kind + str(tc.idx)
        o = self.dram_out(key, [128, 8, 512], BF16)
        self.dma("sp", o[:, :, :], t.ap[:, :, :], B(t), [], store=True)

    def xrhs(self, tc):
        xT = self.xT
        return lambda kt: xT.ap[:, kt, 0:tc.TT]

    def load_x(self, L, tc):
        I = self.I; TT = tc.TT
        ngr = TT // 128
        if L == 0:
            src = I["xp"] if tc.kind == "p" else I["xs"]
            for tg in range(ngr):
                stg = self.salloc(2048, F32)
                r0 = tc.tok0 + tg * 128
                self.dma("pool", stg.ap, src[r0:r0 + 128, :], [], B(stg))
                for k4 in range(4):
                    bk = self.bank()
                    for kk in range(4):
                        kt = k4 * 4 + kk
                        self.tr(bk.ap[:, kk * 128:(kk + 1) * 128], stg.ap[:, kt * 128:(kt + 1) * 128], self.identF.ap[:, :], B(stg, self.identF), B(bk))
                    outv = self.xT.ap[:, k4 * 4:(k4 + 1) * 4, tg * 128:(tg + 1) * 128]
                    self.cp(outv, bk.ap[:, :].rearrange("p (k t) -> p k t", t=128), B(bk), B(self.xT), eng=("act" if k4 % 2 else "dve"))
                self.sfree(stg)
        else:
            par = (L - 1) % 2
            src = self.spill[par, tc.idx].rearrange("p (k t) -> p k t", t=512)[:, :, 0:TT]
            self.dma("pool", self.xT.ap[:, :, 0:TT], src, [self.spill_buf[par][tc.idx]], B(self.xT))
        psrc = I["pp"][L] if tc.kind == "p" else I["pps"][L]
        for tg in range(ngr):
            stg = self.salloc(256, F32)
            r0 = tc.tok0 + tg * 128
            self.dma("pool", stg.ap, psrc[r0:r0 + 128, :], [], B(stg))
            bk = self.bank()
            for kk in range(2):
                self.tr(bk.ap[:, kk * 128:(kk + 1) * 128], stg.ap[:, kk * 128:(kk + 1) * 128], self.identF.ap[:, :], B(stg, self.identF), B(bk))
            self.cp(self.pT.ap[:, :, tg * 128:(tg + 1) * 128], bk.ap[:, 0:256].rearrange("p (k t) -> p k t", t=128), B(bk), B(self.pT), eng="act")
            self.sfree(stg)

    def s5_phase(self, L, tc):
        I = self.I; O = self.O
        TT, NB, NSEG, SEGB = tc.TT, tc.NB, tc.NSEG, tc.SEGB
        xT = self.xT
        Dt = self.salloc(4096, BF16)
        D5 = Dt.ap.rearrange("p (j g s c) -> p j g s c", j=32, g=2, s=4, c=16)
        for cs in range(4):
            ws = self.wl_in(L, OFF["u"] + 256 * cs, 256)
            for s in range(4):
                bk = self.bank()
                for kt in range(16):
                    self.mm(bk.ap[0:NB, 0:256], xT.ap[:, kt, s:TT:4], ws.ap[:, kt, :], kt == 0, kt == 15, B(xT, ws), B(bk))
                outv = D5[0:NB, 8 * cs:8 * cs + 8, :, s, :]
                inv = bk.ap[0:NB, 0:256].rearrange("p (j g c) -> p j g c", j=8, g=2, c=16)
                bv = self.bubc.ap[0:NB, 256 * cs:256 * cs + 256].rearrange("p (j g c) -> p j g c", j=8, g=2, c=16)
                self.tt(outv, inv, bv, ALU.add, B(bk, self.bubc), B(Dt))
        U2 = self.salloc(32 * NB, BF16)
        U23 = U2.ap.rearrange("p (j n) -> p j n", n=NB)
        per = 1024 // NB
        j = 0
        while j < 32:
            bk = self.bank()
            bkb = bk.ap.bitcast(BF16)
            nj = min(per, 32 - j)
            for jj in range(nj):
                self.tr(bkb[:, jj * NB:(jj + 1) * NB], Dt.ap[0:NB, (j + jj) * 128:(j + jj + 1) * 128], self.identB.ap[0:NB, 0:NB], B(Dt, self.identB), B(bk))
            self.cp(U2.ap[:, j * NB:(j + nj) * NB], bkb[:, 0:nj * NB], B(bk), B(U2), eng="act")
            j += nj
        self.sfree(Dt)
        wSr = self.wload_s5(L, 1)
        wSi = self.wload_s5(L, 2)
        HW = NSEG * (SEGB + 1)
        H = self.salloc(2 * 32 * HW, F32)
        H5 = H.ap.rearrange("p (a j q b) -> p a j q b", a=2, j=32, q=NSEG, b=SEGB + 1)
        if tc.kind == "p":
            if tc.first:
                self.memset(self.carry.ap[:], 0.0, B(self.carry))
            self.cp(H5[:, :, :, 0, 0], self.carry.ap[:, :, :], B(self.carry), B(H))
        else:
            for plane, nm in enumerate(("sre", "sim")):
                for r in range(4):
                    stg = self.salloc(128, F32)
                    self.dma("pool", stg.ap, I[nm][L, r * 128:(r + 1) * 128, :], [], B(stg))
                    bk = self.bank()
                    self.tr(bk.ap[:, 0:128], stg.ap, self.identF.ap[:, :], B(stg, self.identF), B(bk))
                    outv = H5[:, plane, :, 4 * r:4 * r + 4, 0]
                    inv = bk.ap[:, 0:128].rearrange("p (q j) -> p j q", q=4, j=32)
                    self.cp(outv, inv, B(bk), B(H))
                    self.sfree(stg)
        pairs_per_bank = 512 // NB
        for plane, wS in enumerate((wSr, wSi)):
            j = 0
            while j < 32:
                bk = self.bank()
                nj = min(pairs_per_bank, 32 - j)
                for jj in range(nj):
                    self.mm(bk.ap[:, jj * NB:(jj + 1) * NB], wS.ap[:, j + jj, :], U23[:, j + jj, :], True, True, B(wS, U2), B(bk))
                outv = H5[:, plane, j:j + nj, :, 1:SEGB + 1]
                inv = bk.ap[:, 0:nj * NB].rearrange("p (j q b) -> p j q b", j=nj, q=NSEG, b=SEGB)
                self.cp(outv, inv, B(bk), B(H), eng="act")
                j += nj
        u = self.salloc(2 * 2 * 32 * NSEG, F32)
        u5 = u.ap.rearrange("p (r a j q) -> p r a j q", r=2, a=2, j=32, q=NSEG)
        T4 = self.T4
        for b in range(SEGB):
            for rpt in range(2):
                tb = T4.ap[:, L, rpt, :, :].rearrange("p a (j o) -> p a j o", o=1).to_broadcast([128, 2, 32, NSEG])
                self.tt(u5[:, rpt], H5[:, :, :, :, b], tb, ALU.mult, B(H, T4), B(u), eng=SCAN_ENG)
            self.tt(H5[:, :, :, :, b + 1], H5[:, :, :, :, b + 1], u5[:, 0], ALU.add, B(H, u), B(H), eng=SCAN_ENG)
            self.tt(H5[:, :, :, :, b + 1], H5[:, :, :, :, b + 1], u5[:, 1, ::-1], ALU.add, B(H, u), B(H), eng=SCAN_ENG)
        self.sfree(u)
        self.s5_ctx = (U2, U23, H, H5)

    def s5_back(self, L, tc):
        I = self.I; O = self.O
        TT, NB, NSEG, SEGB = tc.TT, tc.NB, tc.NSEG, tc.SEGB
        xT = self.xT
        U2, U23, H, H5 = self.s5_ctx
        Hbf = self.salloc(2 * 32 * NB, BF16)
        Hb4 = Hbf.ap.rearrange("p (a j n) -> p a j n", a=2, j=32, n=NB)
        for plane in range(2):
            outv = Hbf.ap[:, plane * 32 * NB:(plane + 1) * 32 * NB].rearrange("p (j q b) -> p j q b", j=32, q=NSEG, b=SEGB)
            self.cp(outv, H5[:, plane, :, :, 0:SEGB], B(H), B(Hbf), eng=("act" if plane else "dve"))
        if tc.kind == "p":
            self.cp(self.carry.ap[:, :, :], H5[:, :, :, 0, SEGB], B(H), B(self.carry))
            if tc.last:
                for plane, nm in enumerate(("sre_p", "sim_p")):
                    bk = self.bank()
                    self.tr(bk.ap[0:32, 0:128], self.carry.ap[:, plane, :], self.identF.ap[:, :], B(self.carry, self.identF), B(bk))
                    stg = self.salloc(128, F32)
                    self.cp(stg.ap[0:32, :], bk.ap[0:32, 0:128], B(bk), B(stg))
                    self.dma("sp", O[nm][L, :, :], stg.ap[0:32, :], B(stg), [], store=True)
                    self.sfree(stg)
        else:
            for plane, nm in enumerate(("sre_s", "sim_s")):
                fin = self.salloc(512, F32)
                self.cp(fin.ap.rearrange("p (q j) -> p j q", q=16, j=32), H5[:, plane, :, :, SEGB], B(H), B(fin))
                for r in range(4):
                    bk = self.bank()
                    self.tr(bk.ap[:, 0:128], fin.ap[:, r * 128:(r + 1) * 128], self.identF.ap[:, :], B(fin, self.identF), B(bk))
                    stg = self.salloc(128, F32)
                    self.cp(stg.ap, bk.ap[:, 0:128], B(bk), B(stg), eng="act")
                    self.dma("sp", O[nm][L, r * 128:(r + 1) * 128, :], stg.ap, B(stg), [], store=True)
                    self.sfree(stg)
                self.sfree(fin)
        self.sfree(H)
        wM = self.wload_s5(L, 0)
        wYr = self.wload_s5(L, 3)
        wYi = self.wload_s5(L, 4)
        ys = self.salloc(4096, BF16)
        ys3 = ys.ap.rearrange("p (t c) -> p t c", t=4)
        for j4 in range(8):
            bk = self.bank()
            for jj in range(4):
                j = j4 * 4 + jj
                o = bk.ap[0:NB, jj * 128:(jj + 1) * 128]
                self.mm(o, U23[:, j, :], wM.ap[:, j, :], True, False, B(U2, wM), B(bk))
                self.mm(o, Hb4[:, 0, j, :], wYr.ap[:, j, :], False, False, B(Hbf, wYr), B(bk))
                self.mm(o, Hb4[:, 1, j, :], wYi.ap[:, j, :], False, True, B(Hbf, wYi), B(bk))
            yf = self.salloc(512, F32); tq = self.salloc(512, F32)
            self.cp(yf.ap[0:NB, :], bk.ap[0:NB, :], B(bk), B(yf), eng="act")
            self.act(tq.ap[0:NB, :], bk.ap[0:NB, :], AF.Square, B(bk), B(tq))
            self.ts(tq.ap[0:NB, :], tq.ap[0:NB, :], 0.044715, 1.0, ALU.mult, ALU.add, B(tq), B(tq))
            self.tt(tq.ap[0:NB, :], tq.ap[0:NB, :], yf.ap[0:NB, :], ALU.mult, B(tq, yf), B(tq))
            self.act(tq.ap[0:NB, :], tq.ap[0:NB, :], AF.Sigmoid, B(tq), B(tq), scale=GELU_C)
            outv = ys3[0:NB, :, j4 * 128:(j4 + 1) * 128].rearrange("p t (j c) -> p j t c", j=4, c=32)
            self.tt(outv, yf.ap[0:NB, :].rearrange("p (j t c) -> p j t c", j=4, t=4, c=32),
                    tq.ap[0:NB, :].rearrange("p (j t c) -> p j t c", j=4, t=4, c=32), ALU.mult, B(yf, tq), B(ys))
            self.sfree(yf, tq)
        self.sfree(U2, Hbf)
        ysT = self.salloc(8 * TT, BF16)
        ysT3 = ysT.ap.rearrange("p (k t) -> p k t", t=TT)
        per = 1024 // NB
        items = [(t, ct) for ct in range(8) for t in range(4)]
        i = 0
        while i < len(items):
            bk = self.bank(); bkb = bk.ap.bitcast(BF16)
            grp = items[i:i + per]
            for gi, (t, ct) in enumerate(grp):
                self.tr(bkb[:, gi * NB:(gi + 1) * NB], ys3[0:NB, t, ct * 128:(ct + 1) * 128], self.identB.ap[0:NB, 0:NB], B(ys, self.identB), B(bk))
            for gi, (t, ct) in enumerate(grp):
                self.cp(ysT3[:, ct, t:TT:4], bkb[:, gi * NB:(gi + 1) * NB], B(bk), B(ysT), eng=("act" if gi % 2 else "dve"))
            i += per
        self.sfree(ys)
        zsT = self.salloc(8 * TT, BF16)
        zs3 = zsT.ap.rearrange("p (k t) -> p k t", t=TT)
        for cs in range(4):
            ws = self.wl_in(L, OFF["z"] + 256 * cs, 256)
            for jj in range(2):
                ct = cs * 2 + jj
                bk = self.proj_fm(ws, jj, 16, self.xrhs(tc), xT, TT)
                self.act(zs3[:, ct, :], bk.ap[:, 0:TT], AF.Silu, B(bk, self.bcol), B(zsT), bias=self.bcol.ap[:, self.BC["z"] + ct:self.BC["z"] + ct + 1])
        for hs in range(2):
            ws = self.wl_mat("w_glu", L, 512 * hs, 512, 8)
            for jj in range(4):
                o = hs * 4 + jj
                bk = self.proj_fm(ws, jj, 8, lambda kt: ysT3[:, kt, :], ysT, TT)
                sg = self.salloc(TT, F32)
                self.act(sg.ap, bk.ap[:, 0:TT], AF.Sigmoid, B(bk, self.bglu), B(sg), bias=self.bglu.ap[:, o:o + 1])
                self.tt(sg.ap, sg.ap, ysT3[:, o, :], ALU.mult, B(sg, ysT), B(sg))
                self.tt(self.ys5g.ap[:, o, 0:TT], sg.ap, zs3[:, o, :], ALU.mult, B(sg, zsT), B(self.ys5g))
                self.sfree(sg)
        self.sfree(ysT, zsT)

    def wload_s5(self, L, idx):
        return self.wload(self.s5w[L, idx].rearrange("p (k n) -> p k n", n=128), 32, 128, rb=[self.s5w_buf[L][idx]])

    def gla_phase(self, L, tc):
        I = self.I; O = self.O
        TT, NCH, NE, EL = tc.TT, tc.NCH, tc.NE, tc.EL
        xT = self.xT; xr = self.xrhs(tc)
        BCq, BCk, BCgz = self.BC["q"], self.BC["k"], self.BC["gz"]
        ws = self.wl_in(L, OFF["al"], 16)
        bk = self.bank()
        for kt in range(16):
            self.mm(bk.ap[0:16, 0:TT], ws.ap[:, kt, :], xr(kt), kt == 0, kt == 15, B(ws, xT), B(bk))
        alT = self.salloc(TT, BF16)
        self.act(alT.ap[0:16, :], bk.ap[0:16, 0:TT], AF.Identity, B(bk, self.bal), B(alT), bias=self.bal.ap[:, 0:1])
        la = self.salloc(4 * TT, F32); cs = self.salloc(4 * TT, F32)
        la3 = la.ap.rearrange("p (h t) -> p h t", t=TT); cs3 = cs.ap.rearrange("p (h t) -> p h t", t=TT)
        coef = self.coefP if tc.kind == "p" else self.coefS
        for h in range(4):
            bk = self.bank()
            self.mm(bk.ap[:, 0:TT], self.wa2.ap[0:16, h * 128:(h + 1) * 128], alT.ap[0:16, :], True, True, B(self.wa2, alT), B(bk))
            self.act(la3[:, h, :], bk.ap[:, 0:TT], AF.Exp, B(bk, self.nba), B(la), bias=self.nba.ap[:, h:h + 1], scale=-1.0)
            self.act(la3[:, h, :], la3[:, h, :], AF.Ln, B(la), B(la), bias=1.0)
            self.scan(cs3[:, h, :], coef.ap[:, 0:TT], la3[:, h, :], B(coef, la), B(cs))
        self.sfree(alT, la)
        ecs = self.salloc(4 * TT, F32); encs = self.salloc(4 * TT, F32); el = self.salloc(4 * NE, F32)
        ecs3 = ecs.ap.rearrange("p (h t) -> p h t", t=TT); encs3 = encs.ap.rearrange("p (h t) -> p h t", t=TT)
        el3 = el.ap.rearrange("p (h e) -> p h e", e=NE)
        self.act(ecs.ap, cs.ap, AF.Exp, B(cs), B(ecs), scale=-1.0 / 16.0, bias=float(math.log(128.0 ** -0.5)))
        self.act(encs.ap, cs.ap, AF.Exp, B(cs), B(encs), scale=1.0 / 16.0)
        self.act(el3, cs3[:, :, EL - 1:TT:EL], AF.Exp, B(cs), B(el), scale=-1.0 / 16.0)
        self.sfree(cs)
        qd = self.salloc(4 * TT, BF16); kd = self.salloc(4 * TT, BF16)
        qd3 = qd.ap.rearrange("p (h t) -> p h t", t=TT); kd3 = kd.ap.rearrange("p (h t) -> p h t", t=TT)
        for nm, dst3, dstT, sc3, scT, bc0 in (("q", qd3, qd, ecs3, ecs, BCq), ("k", kd3, kd, encs3, encs, BCk)):
            for cs_ in range(2):
                ws = self.wl_in(L, OFF[nm] + 256 * cs_, 256)
                for jj in range(2):
                    h = cs_ * 2 + jj
                    bk = self.proj_fm(ws, jj, 16, xr, xT, TT)
                    self.stt(dst3[:, h, :], bk.ap[:, 0:TT], self.bcol.ap[:, bc0 + h:bc0 + h + 1], sc3[:, h, :], ALU.add, ALU.mult, B(bk, self.bcol, scT), B(dstT))
        self.sfree(ecs, encs)
        kk = self.salloc(4 * TT, BF16)
        kk3 = kk.ap.rearrange("p (h t) -> p h t", t=TT)
        for h in range(4):
            outv = kk3[:, h, :].rearrange("p (e t) -> p e t", t=EL)
            inv = kd3[:, h, :].rearrange("p (e t) -> p e t", t=EL)
            ev = el3[:, h, :].rearrange("p (e o) -> p e o", o=1).to_broadcast([128, NE, EL])
            self.tt(outv, inv, ev, ALU.mult, B(kd, el), B(kk))
        kkT = self.salloc(NCH * 512, BF16)
        kkT4 = kkT.ap.rearrange("p (c h d) -> p c h d", h=4, d=128)
        for ch2 in range(0, NCH, 2):
            bk = self.bank(); bkb = bk.ap.bitcast(BF16)
            for cc in range(2):
                for h in range(4):
                    c = ch2 + cc
                    self.tr(bkb[0:64, (cc * 4 + h) * 128:(cc * 4 + h + 1) * 128], kk3[:, h, c * 64:(c + 1) * 64], self.identB.ap[:, :], B(kk, self.identB), B(bk))
            self.cp(kkT.ap[0:64, ch2 * 512:(ch2 + 2) * 512], bkb[0:64, 0:1024], B(bk), B(kkT), eng="act")
        self.sfree(kk)
        vt = self.salloc(NCH * 1024, BF16)
        vt3 = vt.ap.rearrange("p (c v) -> p c v", v=1024)
        for cs_ in range(4):
            ws = self.wl_in(L, OFF["v"] + 256 * cs_, 256)
            for c in range(NCH):
                bk = self.bank()
                for kt in range(16):
                    self.mm(bk.ap[0:64, 0:256], xT.ap[:, kt, c * 64:(c + 1) * 64], ws.ap[:, kt, :], kt == 0, kt == 15, B(xT, ws), B(bk))
                self.tt(vt3[0:64, c, cs_ * 256:(cs_ + 1) * 256], bk.ap[0:64, 0:256], self.bvbc.ap[0:64, cs_ * 256:(cs_ + 1) * 256], ALU.add, B(bk, self.bvbc), B(vt))
        gz = self.salloc(8 * TT, BF16)
        gz3 = gz.ap.rearrange("p (k t) -> p k t", t=TT)
        for cs_ in range(4):
            ws = self.wl_in(L, OFF["gz"] + 256 * cs_, 256)
            for jj in range(2):
                ct = cs_ * 2 + jj
                bk = self.proj_fm(ws, jj, 16, xr, xT, TT)
                self.act(gz3[:, ct, :], bk.ap[:, 0:TT], AF.Silu, B(bk, self.bcol), B(gz), bias=self.bcol.ap[:, BCgz + ct:BCgz + ct + 1])
        o = self.salloc(8 * TT, F32)
        o3 = o.ap.rearrange("p (k t) -> p k t", t=TT)
        attT = self.salloc(NCH * 256, BF16)
        attT4 = attT.ap.rearrange("p (c h t) -> p c h t", h=4, t=64)
        Sst, Sbf = self.Sst, self.Sbf
        if tc.kind == "p" and tc.first:
            self.memset(Sst.ap[:], 0.0, B(Sst))
            self.memset(Sbf.ap[:], 0.0, B(Sbf))
        for c in range(NCH):
            csl = slice(c * 64, (c + 1) * 64)
            bkA = self.bank()
            for h in range(4):
                self.mm(bkA.ap[0:64, h * 64:(h + 1) * 64], kd3[:, h, csl], qd3[:, h, csl], True, True, B(kd, qd), B(bkA))
            if tc.kind == "p":
                mk = self.maskP.ap[:, :].rearrange("p (o t) -> p o t", o=1).to_broadcast([64, 4, 64]); mkT = self.maskP
            else:
                mk = self.maskS.ap[:, :, :].rearrange("p a b -> p (a b)").rearrange("p (o t) -> p o t", o=1).to_broadcast([64, 4, 64]); mkT = self.maskS
            self.tt(attT4[0:64, c, :, :], bkA.ap[0:64, 0:256].rearrange("p (h t) -> p h t", t=64), mk, ALU.mult, B(bkA, mkT), B(attT))
            bkO = self.bank()
            if tc.kind == "p":
                for h in range(4):
                    for vh in range(2):
                        oo = bkO.ap[:, (h * 2 + vh) * 64:(h * 2 + vh + 1) * 64]
                        self.mm(oo, vt3[0:64, c, h * 256 + vh * 128:h * 256 + (vh + 1) * 128], attT4[0:64, c, h, :], True, False, B(vt, attT), B(bkO))
                        self.mm(oo, Sbf.ap[:, h, vh * 128:(vh + 1) * 128], qd3[:, h, csl], False, True, B(Sbf, qd), B(bkO))
                self.cp(o3[:, :, csl], bkO.ap[:, :].rearrange("p (k t) -> p k t", t=64), B(bkO), B(o), eng="act")
                bkK = [self.bank(), self.bank()]
                for h in range(4):
                    self.mm(bkK[h // 2].ap[:, (h % 2) * 256:(h % 2 + 1) * 256], kkT4[0:64, c, h, :], vt3[0:64, c, h * 256:(h + 1) * 256], True, True, B(kkT, vt), B(bkK[h // 2]))
                for h in range(4):
                    self.stt(Sst.ap[:, h, :], Sst.ap[:, h, :], el3[:, h, c:c + 1], bkK[h // 2].ap[:, (h % 2) * 256:(h % 2 + 1) * 256], ALU.mult, ALU.add, B(Sst, el, bkK[h // 2]), B(Sst))
                self.cp(Sbf.ap[:], Sst.ap[:], B(Sst), B(Sbf), eng="act")
            else:
                S0f = self.salloc(8 * 1024, F32); S0b = self.salloc(8 * 1024, BF16)
                S0f4 = S0f.ap.rearrange("p (q h v) -> p q h v", q=8, h=4, v=256)
                S0b4 = S0b.ap.rearrange("p (q h v) -> p q h v", q=8, h=4, v=256)
                for q in range(8):
                    seq = c * 8 + q
                    self.dma("pool", S0f4[:, q, :, :], I["sgla"][L, seq].rearrange("h d v -> d h v"), [], B(S0f))
                self.cp(S0b.ap[:, 0:4096], S0f.ap[:, 0:4096], B(S0f), B(S0b), eng="act")
                self.cp(S0b.ap[:, 4096:8192], S0f.ap[:, 4096:8192], B(S0f), B(S0b), eng="dve")
                for h in range(4):
                    for vh in range(2):
                        oo = bkO.ap[:, (h * 2 + vh) * 64:(h * 2 + vh + 1) * 64]
                        self.mm(oo, vt3[0:64, c, h * 256 + vh * 128:h * 256 + (vh + 1) * 128], attT4[0:64, c, h, :], True, False, B(vt, attT), B(bkO))
                        for q in range(8):
                            tsl = slice(c * 64 + q * 8, c * 64 + q * 8 + 8)
                            self.mm(oo[:, q * 8:(q + 1) * 8], S0b4[:, q, h, vh * 128:(vh + 1) * 128], qd3[:, h, tsl], False, q == 7, B(S0b, qd), B(bkO))
                self.cp(o3[:, :, csl], bkO.ap[:, :].rearrange("p (k t) -> p k t", t=64), B(bkO), B(o), eng="act")
                for q in range(8):
                    seq = c * 8 + q
                    kkm = self.salloc(512, BF16)
                    self.ts(kkm.ap[0:64, :], kkT.ap[0:64, c * 512:(c + 1) * 512], self.rowm.ap[:, q:q + 1], None, ALU.mult, None, B(kkT, self.rowm), B(kkm))
                    bkK = [self.bank(), self.bank()]
                    for h in range(4):
                        self.mm(bkK[h // 2].ap[:, (h % 2) * 256:(h % 2 + 1) * 256], kkm.ap[0:64, h * 128:(h + 1) * 128], vt3[0:64, c, h * 256:(h + 1) * 256], True, True, B(kkm, vt), B(bkK[h // 2]))
                    Sn = self.salloc(1024, F32)
                    Sn3 = Sn.ap.rearrange("p (h v) -> p h v", v=256)
                    for h in range(4):
                        self.stt(Sn3[:, h, :], S0f4[:, q, h, :], el3[:, h, seq:seq + 1], bkK[h // 2].ap[:, (h % 2) * 256:(h % 2 + 1) * 256], ALU.mult, ALU.add, B(S0f, el, bkK[h // 2]), B(Sn))
                    self.dma("sp", O["gla_s"][L, seq].rearrange("h d v -> d h v"), Sn3, B(Sn), [], store=True)
                    self.sfree(kkm, Sn)
                self.sfree(S0f, S0b)
        if tc.kind == "p" and tc.last:
            self.dma("sp", O["gla_p"][L].rearrange("h d v -> d h v"), Sst.ap[:, :, :], B(Sst), [], store=True)
        self.sfree(attT, kkT, vt, qd, kd, el)
        for h in range(4):
            sq = self.salloc(2 * TT, F32)
            self.act(sq.ap, o.ap[:, 2 * h * TT:(2 * h + 2) * TT], AF.Square, B(o), B(sq))
            bkM = self.bank(); bkQ = self.bank()
            for vh in range(2):
                self.mm(bkM.ap[:, 0:TT], self.ones[256].ap[:, :], o3[:, 2 * h + vh, :], vh == 0, vh == 1, B(self.ones[256], o), B(bkM))
            for vh in range(2):
                self.mm(bkQ.ap[:, 0:TT], self.ones[256].ap[:, :], sq.ap[:, vh * TT:(vh + 1) * TT], vh == 0, vh == 1, B(self.ones[256], sq), B(bkQ))
            mean, rstd = self.ln_stats(bkM, bkQ, TT)
            self.sfree(sq)
            for vh in range(2):
                k = 2 * h + vh
                tmp = self.salloc(TT, F32)
                self.tt(tmp.ap, o3[:, k, :], mean.ap, ALU.subtract, B(o, mean), B(tmp))
                self.tt(tmp.ap, tmp.ap, rstd.ap, ALU.mult, B(tmp, rstd), B(tmp))
                self.stt(self.ogg.ap[:, k, 0:TT], tmp.ap, self.normg.ap[:, k:k + 1], gz3[:, k, :], ALU.mult, ALU.mult, B(tmp, self.normg, gz), B(self.ogg))
                self.sfree(tmp)
            self.sfree(mean, rstd)
        self.sfree(o, gz)

    def ln_stats(self, bkM, bkQ, TT):
        mean = self.salloc(TT, F32); rstd = self.salloc(TT, F32); m2 = self.salloc(TT, F32)
        self.cp(mean.ap, bkM.ap[:, 0:TT], B(bkM), B(mean), eng="act")
        self.act(m2.ap, bkM.ap[:, 0:TT], AF.Square, B(bkM), B(m2))
        self.tt(rstd.ap, bkQ.ap[:, 0:TT], m2.ap, ALU.subtract, B(bkQ, m2), B(rstd))
        self.ts(rstd.ap, rstd.ap, 0.0, None, ALU.max, None, B(rstd), B(rstd))
        self.act(rstd.ap, rstd.ap, AF.Sqrt, B(rstd), B(rstd), bias=LN_EPS)
        self.recip(rstd.ap, rstd.ap, B(rstd), B(rstd))
        self.sfree(m2)
        return mean, rstd

    def conv_phase(self, L, tc):
        I = self.I; O = self.O
        TT = tc.TT; xT = self.xT; xr = self.xrhs(tc)
        BCa, BCb, BCz = self.BC["ca"], self.BC["cb"], self.BC["cz"]
        if tc.kind == "p":
            GW = 30 + TT
            G = self.salloc(8 * GW, BF16)
            G3 = G.ap.rearrange("p (k t) -> p k t", t=GW)
            if tc.first:
                self.memset(self.halo.ap[:], 0.0, B(self.halo))
            self.cp(G3[:, :, 0:30], self.halo.ap[:, :, :], B(self.halo), B(G))
            gdst = lambda ct: G3[:, ct, 30:30 + TT]
        else:
            GW = 16 * 38
            G = self.salloc(8 * GW, F32)
            G4 = G.ap.rearrange("p (k q t) -> p k q t", q=16, t=38)
            for r in range(4):
                stg = self.salloc(1024, F32)
                self.dma("pool", stg.ap[0:120, :], I["cconv"][L, r * 120:(r + 1) * 120, :], [], B(stg))
                for c4 in range(2):
                    bk = self.bank()
                    for cc in range(4):
                        ct = c4 * 4 + cc
                        self.tr(bk.ap[:, cc * 120:(cc + 1) * 120], stg.ap[0:120, ct * 128:(ct + 1) * 128], self.identF.ap[0:120, 0:120], B(stg, self.identF), B(bk))
                    outv = G4[:, c4 * 4:(c4 + 1) * 4, 4 * r:4 * r + 4, 0:30]
                    inv = bk.ap[:, 0:480].rearrange("p (k q t) -> p k q t", k=4, q=4, t=30)
                    self.cp(outv, inv, B(bk), B(G))
                self.sfree(stg)
            gdst = lambda ct: G4[:, ct, :, 30:38]
        czs = self.salloc(8 * TT, BF16)
        czs3 = czs.ap.rearrange("p (k t) -> p k t", t=TT)
        for cs_ in range(4):
            wa = self.wl_in(L, OFF["ca"] + 256 * cs_, 256)
            wb = self.wl_in(L, OFF["cb"] + 256 * cs_, 256)
            for jj in range(2):
                ct = cs_ * 2 + jj
                bkb_ = self.proj_fm(wb, jj, 16, xr, xT, TT)
                sg = self.salloc(TT, F32)
                self.act(sg.ap, bkb_.ap[:, 0:TT], AF.Sigmoid, B(bkb_, self.bcol), B(sg), bias=self.bcol.ap[:, BCb + ct:BCb + ct + 1])
                bka = self.proj_fm(wa, jj, 16, xr, xT, TT)
                if tc.kind == "p":
                    self.stt(gdst(ct), bka.ap[:, 0:TT], self.bcol.ap[:, BCa + ct:BCa + ct + 1], sg.ap, ALU.add, ALU.mult, B(bka, self.bcol, sg), B(G))
                    self.stt(self.halo.ap[:, ct, :], bka.ap[:, TT - 30:TT], self.bcol.ap[:, BCa + ct:BCa + ct + 1], sg.ap[:, TT - 30:TT], ALU.add, ALU.mult, B(bka, self.bcol, sg), B(self.halo))
                else:
                    self.stt(gdst(ct), bka.ap[:, 0:TT].rearrange("p (q t) -> p q t", t=8), self.bcol.ap[:, BCa + ct:BCa + ct + 1],
                             sg.ap.rearrange("p (q t) -> p q t", t=8), ALU.add, ALU.mult, B(bka, self.bcol, sg), B(G))
                self.sfree(sg)
        for cs_ in range(4):
            ws = self.wl_in(L, OFF["cz"] + 256 * cs_, 256)
            for jj in range(2):
                ct = cs_ * 2 + jj
                bk = self.proj_fm(ws, jj, 16, xr, xT, TT)
                self.act(czs3[:, ct, :], bk.ap[:, 0:TT], AF.Silu, B(bk, self.bcol), B(czs), bias=self.bcol.ap[:, BCz + ct:BCz + ct + 1])
        acc = self.salloc(8 * TT, F32)
        acc3 = acc.ap.rearrange("p (k t) -> p k t", t=TT)
        cw = self.convw
        if tc.kind == "p":
            for ct in range(8):
                dg = self.salloc(31 * 128, BF16)
                dg3 = dg.ap.rearrange("p (k m) -> p k m", m=128)
                idb = self.identB.ap[:, :].rearrange("p (o m) -> p o m", o=1).to_broadcast([128, 31, 128])
                wv = cw.ap[:, ct, :].rearrange("p (k o) -> p k o", o=1).to_broadcast([128, 31, 128])
                self.tt(dg3, idb, wv, ALU.mult, B(self.identB, cw), B(dg))
                bk = self.bank()
                for k in range(31):
                    self.mm(bk.ap[:, 0:TT], dg3[:, k, :], G3[:, ct, k:k + TT], k == 0, k == 30, B(dg, G), B(bk))
                self.act(acc3[:, ct, :], bk.ap[:, 0:TT], AF.Identity, B(bk, self.convb), B(acc), bias=self.convb.ap[:, ct:ct + 1])
                self.sfree(dg)
        else:
            for ct in range(8):
                src = lambda k: G4[:, ct, :, k:k + 8]
                dst = acc3[:, ct, :].rearrange("p (q t) -> p q t", t=8)
                self.ts(dst, src(0), cw.ap[:, ct, 0:1], self.convb.ap[:, ct:ct + 1], ALU.mult, ALU.add, B(G, cw, self.convb), B(acc))
                for k in range(1, 31):
                    self.stt(dst, src(k), cw.ap[:, ct, k:k + 1], dst, ALU.mult, ALU.add, B(G, cw, acc), B(acc))
        if tc.kind == "p":
            if tc.last:
                stg = self.salloc(1024, F32)
                for c4 in range(2):
                    bk = self.bank()
                    for cc in range(4):
                        ct = c4 * 4 + cc
                        self.tr(bk.ap[0:30, cc * 128:(cc + 1) * 128], self.halo.ap[:, ct, :], self.identF.ap[:, :], B(self.halo, self.identF), B(bk))
                    self.cp(stg.ap[0:30, c4 * 512:(c4 + 1) * 512], bk.ap[0:30, :], B(bk), B(stg), eng="act")
                self.dma("sp", O["conv_p"][L, :, :], stg.ap[0:30, :], B(stg), [], store=True)
                self.sfree(stg)
        else:
            cn = self.salloc(8 * 480, F32)
            cn4 = cn.ap.rearrange("p (k q t) -> p k q t", q=16, t=30)
            self.cp(cn4, G4[:, :, :, 8:38], B(G), B(cn), eng="act")
            for r in range(4):
                stg = self.salloc(1024, F32)
                for c4 in range(2):
                    bk = self.bank()
                    for cc in range(4):
                        ct = c4 * 4 + cc
                        self.tr(bk.ap[0:120, cc * 128:(cc + 1) * 128], cn.ap[:, ct * 480 + r * 120:ct * 480 + (r + 1) * 120], self.identF.ap[:, :], B(cn, self.identF), B(bk))
                    self.cp(stg.ap[0:120, c4 * 512:(c4 + 1) * 512], bk.ap[0:120, :], B(bk), B(stg), eng="act")
                self.dma("sp", O["conv_s"][L, r * 120:(r + 1) * 120, :], stg.ap[0:120, :], B(stg), [], store=True)
                self.sfree(stg)
            self.sfree(cn)
        self.sfree(G)
        bkM = self.bank(); bkQ = self.bank()
        for ct in range(8):
            sq = self.salloc(TT, F32)
            self.act(sq.ap, acc3[:, ct, :], AF.Square, B(acc), B(sq))
            self.mm(bkM.ap[:, 0:TT], self.ones[1024].ap[:, :], acc3[:, ct, :], ct == 0, ct == 7, B(self.ones[1024], acc), B(bkM))
            self.mm(bkQ.ap[:, 0:TT], self.ones[1024].ap[:, :], sq.ap, ct == 0, ct == 7, B(self.ones[1024], sq), B(bkQ))
            self.sfree(sq)
        mean, rstd = self.ln_stats(bkM, bkQ, TT)
        for ct in range(8):
            tmp = self.salloc(TT, F32)
            self.tt(tmp.ap, acc3[:, ct, :], mean.ap, ALU.subtract, B(acc, mean), B(tmp))
            self.tt(tmp.ap, tmp.ap, rstd.ap, ALU.mult, B(tmp, rstd), B(tmp))
            self.act(tmp.ap, tmp.ap, AF.Silu, B(tmp, self.clng, self.clnb), B(tmp), bias=self.clnb.ap[:, ct:ct + 1], scale=self.clng.ap[:, ct:ct + 1])
            self.tt(self.ycg.ap[:, ct, 0:TT], tmp.ap, czs3[:, ct, :], ALU.mult, B(tmp, czs), B(self.ycg))
            self.sfree(tmp)
        self.sfree(mean, rstd, acc, czs)

    def merge_phase(self, L, tc):
        I = self.I; O = self.O
        TT = tc.TT; xT = self.xT; xr = self.xrhs(tc)
        NL = self.NL
        BCg = self.BC["g"]
        merged = self.salloc(16 * TT, BF16)
        mg3 = merged.ap.rearrange("p (k t) -> p k t", t=TT)
        branches = [("p_s5", self.ys5g, 0), ("p_gla", self.ogg, 1), ("p_conv", self.ycg, 2)]
        for jo4 in range(4):
            macc = self.salloc(4 * TT, F32)
            macc3 = macc.ap.rearrange("p (k t) -> p k t", t=TT)
            for (wn, br, bi) in branches:
                wp = self.wl_mat(wn, L, 512 * jo4, 512, 8)
                for half in range(2):
                    gcol = OFF["g"] + bi * 2048 + jo4 * 512 + half * 256
                    wg = self.wl_in(L, gcol, 256)
                    for jj in range(2):
                        jl = half * 2 + jj
                        jo = jo4 * 4 + jl
                        bkG = self.proj_fm(wg, jj, 16, xr, xT, TT)
                        sg = self.salloc(TT, F32)
                        bcix = BCg + bi * 16 + jo
                        self.act(sg.ap, bkG.ap[:, 0:TT], AF.Sigmoid, B(bkG, self.bcol), B(sg), bias=self.bcol.ap[:, bcix:bcix + 1])
                        bkA = self.proj_fm(wp, jl, 8, lambda kt, br=br: br.ap[:, kt, 0:TT], br, TT)
                        if bi == 0:
                            self.tt(macc3[:, jl, :], bkA.ap[:, 0:TT], sg.ap, ALU.mult, B(bkA, sg), B(macc))
                        else:
                            self.tt(sg.ap, bkA.ap[:, 0:TT], sg.ap, ALU.mult, B(bkA, sg), B(sg))
                            if bi == 1:
                                self.tt(macc3[:, jl, :], macc3[:, jl, :], sg.ap, ALU.add, B(macc, sg), B(macc))
                            else:
                                self.tt(mg3[:, jo, :], macc3[:, jl, :], sg.ap, ALU.add, B(macc, sg), B(merged))
                        self.sfree(sg)
            self.sfree(macc)
        hT = self.salloc(16 * TT, F32); hbf = self.salloc(16 * TT, BF16)
        h3 = hT.ap.rearrange("p (k t) -> p k t", t=TT); hb3 = hbf.ap.rearrange("p (k t) -> p k t", t=TT)
        for cs_ in range(8):
            ws = self.wl_mat("w_o", L, 256 * cs_, 256, 16)
            for jj in range(2):
                jo = cs_ * 2 + jj
                bk = self.proj_fm(ws, jj, 16, lambda kt: mg3[:, kt, :], merged, TT)
                self.stt(h3[:, jo, :], xT.ap[:, jo, 0:TT], float(DN_ALPHA), bk.ap[:, 0:TT], ALU.mult, ALU.add, B(xT, bk), B(hT))
                self.cp(hb3[:, jo, :], h3[:, jo, :], B(hT), B(hbf), eng="act")
        self.sfree(merged)
        wpe = self.wload(self.I["w_pe"][L].rearrange("(k p) n -> p k n", p=128), 2, 2048, key=("w_pe", L, 0, 2048))
        pe_all = self.salloc(16 * TT, BF16)
        pe3 = pe_all.ap.rearrange("p (k t) -> p k t", t=TT)
        for jo in range(16):
            bk = self.bank()
            for kt in range(2):
                self.mm(bk.ap[:, 0:TT], wpe.ap[:, kt, jo * 128:(jo + 1) * 128], self.pT.ap[:, kt, 0:TT], kt == 0, kt == 1, B(wpe, self.pT), B(bk))
            self.cp(pe3[:, jo, :], bk.ap[:, 0:TT], B(bk), B(pe_all), eng="act")
        for cs_ in range(8):
            ws = self.wl_mat("w_pg", L, 256 * cs_, 256, 16)
            for jj in range(2):
                jo = cs_ * 2 + jj
                bk = self.proj_fm(ws, jj, 16, lambda kt: hb3[:, kt, :], hbf, TT)
                sg = self.salloc(TT, F32)
                self.act(sg.ap, bk.ap[:, 0:TT], AF.Sigmoid, B(bk), B(sg))
                self.tt(sg.ap, sg.ap, pe3[:, jo, :], ALU.mult, B(sg, pe_all), B(sg))
                self.tt(h3[:, jo, :], h3[:, jo, :], sg.ap, ALU.add, B(hT, sg), B(hT))
                self.sfree(sg)
        self.sfree(hbf, pe_all)
        bkM = self.bank(); bkQ = self.bank()
        for jo in range(16):
            sq = self.salloc(TT, F32)
            self.act(sq.ap, h3[:, jo, :], AF.Square, B(hT), B(sq))
            self.mm(bkM.ap[:, 0:TT], self.ones[2048].ap[:, :], h3[:, jo, :], jo == 0, jo == 15, B(self.ones[2048], hT), B(bkM))
            self.mm(bkQ.ap[:, 0:TT], self.ones[2048].ap[:, :], sq.ap, jo == 0, jo == 15, B(self.ones[2048], sq), B(bkQ))
            self.sfree(sq)
        mean, rstd = self.ln_stats(bkM, bkQ, TT)
        last = (L == NL - 1)
        if not last:
            xo = self.salloc(16 * 512, BF16)
            xo3 = xo.ap.rearrange("p (k t) -> p k t", t=512)
        for jo in range(16):
            self.tt(h3[:, jo, :], h3[:, jo, :], mean.ap, ALU.subtract, B(hT, mean), B(hT))
            self.tt(h3[:, jo, :], h3[:, jo, :], rstd.ap, ALU.mult, B(hT, rstd), B(hT))
            if last:
                self.act(h3[:, jo, :], h3[:, jo, :], AF.Identity, B(hT, self.lng, self.lnb), B(hT), bias=self.lnb.ap[:, jo:jo + 1], scale=self.lng.ap[:, jo:jo + 1])
            else:
                self.act(xo3[:, jo, 0:TT], h3[:, jo, :], AF.Identity, B(hT, self.lng, self.lnb), B(xo), bias=self.lnb.ap[:, jo:jo + 1], scale=self.lng.ap[:, jo:jo + 1])
        self.sfree(mean, rstd)
        if not last:
            par = L % 2
            dst = self.spill[par, tc.idx].rearrange("p (k t) -> p k t", t=512)[:, :, 0:TT]
            self.dma("pool", dst, xo3[:, :, 0:TT], B(xo), [self.spill_buf[par][tc.idx]])
            self.sfree(xo)
        else:
            dsto = O["yp"] if tc.kind == "p" else O["ys"]
            for tg in range(TT // 128):
                stg = self.salloc(2048, F32)
                for k4 in range(4):
                    bk = self.bank()
                    for kk in range(4):
                        jo = k4 * 4 + kk
                        self.tr(bk.ap[:, kk * 128:(kk + 1) * 128], h3[:, jo, tg * 128:(tg + 1) * 128], self.identF.ap[:, :], B(hT, self.identF), B(bk))
                    self.cp(stg.ap[:, k4 * 512:(k4 + 1) * 512], bk.ap[:, :], B(bk), B(stg), eng=("act" if k4 % 2 else "dve"))
                r0 = tc.tok0 + tg * 128
                self.dma("sp", dsto[r0:r0 + 128, :], stg.ap, B(stg), [], store=True)
                self.sfree(stg)
        self.sfree(hT)


_CACHE = {}


def _get_prog(NL, NPT):
    key = (NL, NPT)
    if key not in _CACHE:
        _CACHE[key] = Builder(NL, NPT).build()
    return _CACHE[key]


def core_inputs(inputs, c, NL, NPT, prompt_b, seq0):
    f = lambda a: np.ascontiguousarray(np.asarray(a, dtype=np.float32))
    TP = NPT * 512
    m = {}
    m["xp"] = f(inputs["x_prompt"][prompt_b, :TP])
    m["xs"] = f(inputs["x_sample"][seq0:seq0 + 16]).reshape(128, D)
    m["pp"] = f(inputs["p_prompt"][:NL, prompt_b, :TP])
    m["pps"] = f(inputs["p_sample"][:NL, seq0:seq0 + 16]).reshape(NL, 128, 256)
    m["sre"] = f(inputs["state_s5_re"][:NL, seq0:seq0 + 16]).reshape(NL, 512, 128)
    m["sim"] = f(inputs["state_s5_im"][:NL, seq0:seq0 + 16]).reshape(NL, 512, 128)
    m["sgla"] = f(inputs["state_gla"][:NL, seq0:seq0 + 16])
    m["cconv"] = f(inputs["cache_conv"][:NL, seq0:seq0 + 16]).reshape(NL, 480, 1024)
    return m


def kernel(**inputs):
    NL, NPT = 4, 4
    nc = _get_prog(NL, NPT)
    wshared = {n: np.ascontiguousarray(np.asarray(inputs[n], dtype=np.float32)[:NL]) for n in W_NAMES}
    in_maps = []
    for c in range(8):
        m = core_inputs(inputs, c, NL, NPT, c // 2, 16 * c)
        m.update(wshared)
        in_maps.append(m)
    res = run_bass_kernel_spmd(nc, in_maps, core_ids=list(range(8)))
    R = res.results
    y_p = np.stack([R[2 * b]["yp"] for b in range(4)]).reshape(4, 2048, D)
    y_s = np.concatenate([R[c]["ys"].reshape(16, 8, D) for c in range(8)], axis=0)
    s5re_p = np.stack([R[2 * b]["sre_p"].reshape(NL, 64, 64) for b in range(4)], axis=1)
    s5im_p = np.stack([R[2 * b]["sim_p"].reshape(NL, 64, 64) for b in range(4)], axis=1)
    gla_p = np.stack([R[2 * b]["gla_p"] for b in range(4)], axis=1)
    conv_p = np.stack([R[2 * b]["conv_p"] for b in range(4)], axis=1)
    s5re_s = np.concatenate([R[c]["sre_s"].reshape(NL, 16, 64, 64) for c in range(8)], axis=1)
    s5im_s = np.concatenate([R[c]["sim_s"].reshape(NL, 16, 64, 64) for c in range(8)], axis=1)
    gla_s = np.concatenate([R[c]["gla_s"] for c in range(8)], axis=1)
    conv_s = np.concatenate([R[c]["conv_s"].reshape(NL, 16, 30, 1024) for c in range(8)], axis=1)
    outs = (y_p, y_s, s5re_p, s5im_p, gla_p, conv_p, s5re_s, s5im_s, gla_s, conv_s)
    return tuple(np.ascontiguousarray(o, dtype=np.float32) for o in outs)
```

```python
import math
from contextlib import ExitStack

import numpy as np
import concourse.bass as bass
import concourse.mybir as mybir
from concourse.bass_utils import run_bass_kernel_spmd

F32 = mybir.dt.float32
BF16 = mybir.dt.bfloat16
AF = mybir.ActivationFunctionType
ALU = mybir.AluOpType

ENGS = ["pe", "act", "dve", "pool", "sp"]
SAME_ENGINE_SYNC = True


class Buf:
    __slots__ = ("writers", "readers")

    def __init__(self):
        self.writers = {}
        self.readers = {}


class Op:
    __slots__ = ("id", "eng", "fn", "deps", "dma", "inc", "count", "semi", "target", "nd")

    def __init__(self, id, eng, fn, deps, dma, nd):
        self.id = id; self.eng = eng; self.fn = fn; self.deps = deps; self.dma = dma
        self.inc = False; self.count = 0; self.semi = -1; self.target = 0; self.nd = nd


class Prog:
    def __init__(self, n_dma_sems=40):
        self.ops = []
        self.n_dma_sems = n_dma_sems

    def add(self, eng, fn, r=(), w=(), dma=False, nd=1, extra=()):
        deps = set(extra)
        for b in r:
            deps.update(b.writers.values())
        for b in w:
            deps.update(b.writers.values())
            deps.update(b.readers.values())
        oid = len(self.ops)
        self.ops.append(Op(oid, eng, fn, deps, dma, nd))
        key = ("d", oid) if dma else eng
        for b in r:
            b.readers[key] = oid
        for b in w:
            if b.readers:
                b.writers = {key: oid}
                b.readers = {}
            else:
                b.writers[key] = oid
        return oid

    def emit(self, block, sems, dma_sems):
        ops = self.ops

        def skip(dop, op):
            return (not dop.dma) and (not op.dma) and dop.eng == op.eng and (op.eng == "pe" or not SAME_ENGINE_SYNC)

        for op in ops:
            for d in op.deps:
                dop = ops[d]
                if dop.dma or skip(dop, op):
                    continue
                dop.inc = True
        cnt = {e: 0 for e in ENGS}
        dtarget = [0] * self.n_dma_sems
        dlast = [None] * self.n_dma_sems
        half = self.n_dma_sems // 2
        rrs = {"pool": 0, "sp": 0}
        for op in ops:
            if op.dma:
                base = 0 if op.eng == "pool" else half
                rr = base + rrs[op.eng]
                rrs[op.eng] = (rrs[op.eng] + 1) % half
                op.semi = rr
                if dlast[rr] is not None:
                    op.deps.add(dlast[rr])
                dtarget[rr] += 16 * op.nd
                op.target = dtarget[rr]
                dlast[rr] = op.id
            elif op.inc:
                cnt[op.eng] += 1
                op.count = cnt[op.eng]
        per_eng = {e: [] for e in ENGS}
        for op in ops:
            per_eng[op.eng].append(op)
        self.counts = cnt

        def run(ename, eh):
            waited = {}
            for op in per_eng[ename]:
                for d in sorted(op.deps):
                    dop = ops[d]
                    if dop.dma:
                        s = dma_sems[dop.semi]; v = dop.target; k = ("d", dop.semi)
                    else:
                        if skip(dop, op):
                            continue
                        s = sems[dop.eng]; v = dop.count; k = dop.eng
                    if waited.get(k, 0) >= v:
                        continue
                    waited[k] = v
                    eh.wait_ge(s, v)
                ins = op.fn(eh)
                if ins is None:
                    continue
                if op.dma:
                    for i in ins:
                        i.then_inc(dma_sems[op.semi], 16)
                elif op.inc:
                    ins.then_inc(sems[ename], 1)

        @block.tensor
        def _(e):
            run("pe", e)

        @block.scalar
        def _(e):
            run("act", e)

        @block.vector
        def _(e):
            run("dve", e)

        @block.gpsimd
        def _(e):
            run("pool", e)

        @block.sync
        def _(e):
            run("sp", e)


class T:
    __slots__ = ("ap", "bufs", "slabs")

    def __init__(self, ap, bufs, slabs=None):
        self.ap = ap; self.bufs = bufs; self.slabs = slabs


def B(*ts):
    out = []
    for t in ts:
        if t is None:
            continue
        out.extend(t.bufs)
    return out


D = 2048
NIN = 14352
OFF = dict(u=0, z=1024, q=2048, k=2560, v=3072, al=4096, gz=4112, ca=5136, cb=6160, cz=7184, g=8208)
DN_ALPHA = (2 * 4) ** 0.25
LN_EPS = 1e-5
TWO_PI = 2.0 * math.pi
MAGIC = 12582912.0
GELU_C = 1.5957691216057308
SLAB = 2048
NSLAB = 46
WSLOT_ELEMS = 4096
NWSLOT = 4
SCAN_ENG = "pool"
PRECONVERT = False

W_NAMES = ["w_in", "b_in", "s5_a_re", "s5_a_im", "s5_log_dt", "s5_b_re", "s5_b_im", "s5_c_re", "s5_c_im", "s5_d",
           "w_glu", "b_glu", "gla_w_a2", "gla_b_a", "gla_norm_g", "conv_w", "conv_b", "conv_ln_g", "conv_ln_b",
           "p_s5", "p_gla", "p_conv", "w_o", "w_pg", "w_pe", "ln_g", "ln_b"]


class TileCfg:
    def __init__(self, kind, idx, TT, tok0, first, last):
        self.kind = kind; self.idx = idx; self.TT = TT; self.tok0 = tok0; self.first = first; self.last = last
        self.NB = TT // 4
        if kind == "p":
            self.NSEG = 1; self.SEGB = self.NB; self.NCH = TT // 64; self.NE = self.NCH; self.EL = 64
        else:
            self.NSEG = 16; self.SEGB = 2; self.NCH = 2; self.NE = 16; self.EL = 8


class Builder:
    def __init__(self, NL, NPT, debug=None):
        self.NL = NL; self.NPT = NPT; self.TP = NPT * 512
        self.debug = debug or {}
        self.nc = bass.Bass("TRN2", target_bir_lowering=False)
        self.P = Prog()
        self.st = ExitStack()
        self.store_ops = []
        self.bank_i = 0
        self.bank_alloc = [None] * 8
        self.wslot_i = 0
        self.uid = 0

    def sb(self, shape, dt=F32, name=None):
        self.uid += 1
        h = self.st.enter_context(self.nc.sbuf_tensor(name or ("t%d" % self.uid), shape, dt))
        return h

    def pt(self, shape, dt=F32):
        h = self.sb(shape, dt)
        return T(h, [Buf()])

    def dram_in(self, name, shape, dt=F32):
        return self.nc.dram_tensor(name, list(shape), dt, kind="ExternalInput").ap()

    def dram_out(self, name, shape, dt=F32):
        return self.nc.dram_tensor(name, list(shape), dt, kind="ExternalOutput").ap()

    def salloc(self, nelem, dt):
        esz = 4 if dt == F32 else 2
        nb = nelem * esz
        ns = (nb + SLAB - 1) // SLAB
        free = self.slab_free
        for s0 in range(0, NSLAB - ns + 1):
            if all(free[s0:s0 + ns]):
                for i in range(s0, s0 + ns):
                    free[i] = False
                if dt == F32:
                    ap = self.scrF[:, s0 * (SLAB // 4): s0 * (SLAB // 4) + nelem]
                else:
                    ap = self.scrB[:, s0 * (SLAB // 2): s0 * (SLAB // 2) + nelem]
                return T(ap, self.slab_bufs[s0:s0 + ns], (s0, ns))
        raise RuntimeError("scratch exhausted: need %d slabs, free map %s" % (ns, "".join("1" if f else "0" for f in free)))

    def sfree(self, *ts):
        for t in ts:
            s0, ns = t.slabs
            for i in range(s0, s0 + ns):
                assert not self.slab_free[i]
                self.slab_free[i] = True

    def bank(self):
        for k in range(8):
            i = (self.bank_i + k) % 8
            b = self.banks[i].bufs[0]
            a = self.bank_alloc[i]
            free = a is None or (b.writers and max(b.writers.values()) >= a and len(b.readers) > 0)
            if free:
                self.bank_alloc[i] = len(self.P.ops)
                self.bank_i = (i + 1) % 8
                return self.banks[i]
        raise RuntimeError("no free PSUM bank")

    def mm(self, out, lhsT, rhs, start, stop, r, w):
        self.P.add("pe", lambda e: e.matmul(out, lhsT=lhsT, rhs=rhs, start=start, stop=stop), r=r, w=w)

    def tr(self, out, in_, ident, r, w):
        self.P.add("pe", lambda e: e.transpose(out, in_, ident), r=r, w=w)

    def act(self, out, in_, func, r, w, bias=None, scale=None):
        kw = {}
        if bias is not None:
            kw["bias"] = bias
        if scale is not None:
            kw["scale"] = scale
        self.P.add("act", lambda e: e.activation(out=out, in_=in_, func=func, **kw), r=r, w=w)

    def tt(self, out, in0, in1, op, r, w, eng="dve"):
        self.P.add(eng, lambda e: e.tensor_tensor(out=out, in0=in0, in1=in1, op=op), r=r, w=w)

    def ts(self, out, in0, s1, s2, op0, op1, r, w, eng="dve"):
        if s2 is None:
            self.P.add(eng, lambda e: e.tensor_scalar(out=out, in0=in0, scalar1=s1, scalar2=None, op0=op0), r=r, w=w)
        else:
            self.P.add(eng, lambda e: e.tensor_scalar(out=out, in0=in0, scalar1=s1, scalar2=s2, op0=op0, op1=op1), r=r, w=w)

    def stt(self, out, in0, scalar, in1, op0, op1, r, w, eng="dve"):
        self.P.add(eng, lambda e: e.scalar_tensor_tensor(out=out, in0=in0, scalar=scalar, in1=in1, op0=op0, op1=op1), r=r, w=w)

    def cp(self, out, in_, r, w, eng="dve"):
        if eng == "act":
            self.act(out, in_, AF.Identity, r, w)
        else:
            self.P.add(eng, lambda e: e.tensor_copy(out=out, in_=in_), r=r, w=w)

    def memset(self, ap, val, w, eng="dve"):
        self.P.add(eng, lambda e: e.memset(ap, val), w=w)

    def recip(self, out, in_, r, w):
        self.P.add("dve", lambda e: e.reciprocal(out=out, in_=in_), r=r, w=w)

    def scan(self, out, d0, d1, r, w):
        self.P.add("dve", lambda e: e.tensor_tensor_scan(out=out, data0=d0, data1=d1, initial=0.0, op0=ALU.mult, op1=ALU.add), r=r, w=w)

    def asel(self, ap, pattern, op, base, cm, w):
        self.P.add("pool", lambda e: e.affine_select(out=ap, in_=ap, pattern=pattern, compare_op=op, fill=0.0, base=base, channel_multiplier=cm), r=w, w=w)

    def dma(self, eng, out, in_, r, w, nc_ok=False, store=False):
        nc = self.nc
        if nc_ok:
            def fn(e):
                with nc.allow_non_contiguous_dma(reason="small strided parameter / layout load"):
                    return [e.dma_start(out=out, in_=in_)]
        else:
            def fn(e):
                return [e.dma_start(out=out, in_=in_)]
        if store and eng == "sp":
            eng = "pool"
        oid = self.P.add(eng, fn, r=r, w=w, dma=True)
        if store:
            self.store_ops.append(oid)
        return oid

    def wload(self, src, KT, NC, rb=(), key=None):
        slot = self.wslots[self.wslot_i]
        self.wslot_i = (self.wslot_i + 1) % NWSLOT
        view = slot.ap[:, 0:KT * NC].rearrange("p (k n) -> p k n", n=NC)
        if key is None:
            self.dma("sp", view, src, r=list(rb), w=B(slot))
            return T(view, slot.bufs)
        Lk = key[1]
        if key not in self.wblk:
            bi = self.wblk_n[Lk]
            self.wblk_n[Lk] += 1
            assert bi < self.NBLK, "too many weight blocks"
            bb = Buf()
            self.wblk[key] = (bi, bb)
            dstv = self.wscr[Lk][bi, :, 0:KT * NC].rearrange("p (k n) -> p k n", n=NC)
            self.dma("pool", dstv, src, r=[], w=[bb])
        bi, bb = self.wblk[key]
        self.dma("sp", slot.ap[:, 0:KT * NC], self.wscr[Lk][bi, :, 0:KT * NC], r=[bb], w=B(slot))
        return T(view, slot.bufs)

    def build(self):
        nc = self.nc; NL = self.NL; TP = self.TP
        I = {}
        I["xp"] = self.dram_in("xp", [TP, D]); I["xs"] = self.dram_in("xs", [128, D])
        I["pp"] = self.dram_in("pp", [NL, TP, 256]); I["pps"] = self.dram_in("pps", [NL, 128, 256])
        I["sre"] = self.dram_in("sre", [NL, 512, 128]); I["sim"] = self.dram_in("sim", [NL, 512, 128])
        I["sgla"] = self.dram_in("sgla", [NL, 16, 4, 128, 256]); I["cconv"] = self.dram_in("cconv", [NL, 480, 1024])
        shapes = dict(w_in=[NL, D, NIN], b_in=[NL, NIN], s5_a_re=[NL, 64, 64], s5_a_im=[NL, 64, 64], s5_log_dt=[NL, 64],
                      s5_b_re=[NL, 64, 64, 16], s5_b_im=[NL, 64, 64, 16], s5_c_re=[NL, 64, 16, 64], s5_c_im=[NL, 64, 16, 64],
                      s5_d=[NL, 1024], w_glu=[NL, 1024, 1024], b_glu=[NL, 1024], gla_w_a2=[NL, 16, 512], gla_b_a=[NL, 512],
                      gla_norm_g=[NL, 1024], conv_w=[NL, 31, 1024], conv_b=[NL, 1024], conv_ln_g=[NL, 1024], conv_ln_b=[NL, 1024],
                      p_s5=[NL, 1024, D], p_gla=[NL, 1024, D], p_conv=[NL, 1024, D], w_o=[NL, D, D], w_pg=[NL, D, D],
                      w_pe=[NL, 256, D], ln_g=[NL, D], ln_b=[NL, D])
        for n in W_NAMES:
            I[n] = self.dram_in(n, shapes[n])
        O = {}
        O["yp"] = self.dram_out("yp", [TP, D]); O["ys"] = self.dram_out("ys", [128, D])
        O["sre_p"] = self.dram_out("sre_p", [NL, 32, 128]); O["sim_p"] = self.dram_out("sim_p", [NL, 32, 128])
        O["gla_p"] = self.dram_out("gla_p", [NL, 4, 128, 256]); O["conv_p"] = self.dram_out("conv_p", [NL, 30, 1024])
        O["sre_s"] = self.dram_out("sre_s", [NL, 512, 128]); O["sim_s"] = self.dram_out("sim_s", [NL, 512, 128])
        O["gla_s"] = self.dram_out("gla_s", [NL, 16, 4, 128, 256]); O["conv_s"] = self.dram_out("conv_s", [NL, 480, 1024])
        self.I = I; self.O = O
        ntiles = self.NPT + 1
        self.s5w = nc.dram_tensor("s5w_scr", [NL, 5, 128, 4096], BF16, kind="Internal").ap()
        self.spill = nc.dram_tensor("x_spill", [2, ntiles, 128, 16 * 512], BF16, kind="Internal").ap()
        self.s5w_buf = [[Buf() for _ in range(5)] for _ in range(NL)]
        self.NBLK = 90
        self.wscr = [nc.dram_tensor("w_bf16_scr%d" % l, [self.NBLK, 128, WSLOT_ELEMS], BF16, kind="Internal").ap() for l in range(NL)]
        self.wblk = {}
        self.wblk_n = [0] * NL
        self.spill_buf = [[Buf() for _ in range(ntiles)] for _ in range(2)]

        st = self.st
        with st:
            self.sems = {e: st.enter_context(nc.semaphore("s_" + e)) for e in ENGS}
            self.dsems = [st.enter_context(nc.semaphore("d%d" % i)) for i in range(self.P.n_dma_sems)]
            self.banks = []
            for i in range(8):
                h = st.enter_context(nc.psum_tensor("bank%d" % i, [128, 512], F32))
                t = T(h, [Buf()])
                self.banks.append(t)
            scr = self.sb([128, NSLAB * SLAB // 2], BF16, "scratch")
            self.scrB = scr
            self.scrF = scr.bitcast(F32)
            self.slab_bufs = [Buf() for _ in range(NSLAB)]
            self.slab_free = [True] * NSLAB
            self.wslots = [self.pt([128, WSLOT_ELEMS], BF16) for _ in range(NWSLOT)]
            self.xT = self.pt([128, 16, 512], BF16)
            self.pT = self.pt([128, 2, 512], BF16)
            self.ys5g = self.pt([128, 8, 512], BF16)
            self.ogg = self.pt([128, 8, 512], BF16)
            self.ycg = self.pt([128, 8, 512], BF16)
            self.Sst = self.pt([128, 4, 256], F32)
            self.Sbf = self.pt([128, 4, 256], BF16)
            self.halo = self.pt([128, 8, 30], F32)
            self.carry = self.pt([128, 2, 32], F32)
            self.T4 = self.pt([128, NL, 2, 2, 32], F32)
            self.consts()
            self.params_alloc()
            for L in range(NL):
                self.s5_prep(L)
            tiles = [TileCfg("p", i, 512, 512 * i, i == 0, i == self.NPT - 1) for i in range(self.NPT)]
            tiles.append(TileCfg("s", self.NPT, 128, 0, True, True))
            for L in range(NL):
                self.params_load(L)
                for tc in tiles:
                    self.tile_layer(L, tc)
            self.P.add("sp", lambda e: None, extra=list(self.store_ops))
            assert all(self.slab_free), "scratch leak"
            with nc.Block() as block:
                self.P.emit(block, self.sems, self.dsems)
        return nc

    def consts(self):
        self.identF = self.pt([128, 128], F32)
        self.identB = self.pt([128, 128], BF16)
        for t in (self.identF, self.identB):
            self.memset(t.ap[:], 1.0, B(t), eng="pool")
            self.asel(t.ap[:], [[-1, 128]], ALU.is_equal, 0, 1, B(t))
        self.permI = self.pt([128, 128], F32)
        self.cp(self.permI.ap[:].rearrange("p (t g c) -> p t g c", t=4, g=2, c=16),
                self.identF.ap[:].rearrange("p (g t c) -> p t g c", t=4, g=2, c=16), B(self.identF), B(self.permI))
        self.ones = {}
        for n in (256, 1024, 2048):
            t = self.pt([128, 128], F32)
            self.memset(t.ap[:], 1.0 / n, B(t), eng="pool")
            self.ones[n] = t
        self.maskP = self.pt([64, 64], F32)
        self.memset(self.maskP.ap[:], 1.0, B(self.maskP), eng="pool")
        self.asel(self.maskP.ap[:], [[1, 64]], ALU.is_ge, 0, -1, B(self.maskP))
        self.maskS = self.pt([64, 8, 8], F32)
        self.memset(self.maskS.ap[:], 1.0, B(self.maskS), eng="pool")
        self.asel(self.maskS.ap[:], [[8, 8], [1, 8]], ALU.is_ge, 0, -1, B(self.maskS))
        self.asel(self.maskS.ap[:], [[-8, 8], [0, 8]], ALU.is_ge, 0, 1, B(self.maskS))
        self.rowm = self.pt([64, 8], F32)
        self.memset(self.rowm.ap[:], 1.0, B(self.rowm), eng="pool")
        self.asel(self.rowm.ap[:], [[-8, 8]], ALU.is_ge, 0, 1, B(self.rowm))
        self.asel(self.rowm.ap[:], [[8, 8]], ALU.is_ge, 7, -1, B(self.rowm))
        self.TMa = self.pt([128, 4, 16], F32)
        self.TMb = self.pt([128, 4, 16], F32)
        self.memset(self.TMa.ap[:], 1.0, B(self.TMa), eng="pool")
        self.asel(self.TMa.ap[:], [[16, 4], [0, 16]], ALU.is_ge, 15, -1, B(self.TMa))
        self.memset(self.TMb.ap[:], 1.0, B(self.TMb), eng="pool")
        self.asel(self.TMb.ap[:], [[16, 4], [0, 16]], ALU.is_ge, 79, -1, B(self.TMb))
        self.coefP = self.pt([128, 512], F32)
        self.memset(self.coefP.ap[:], 1.0, B(self.coefP), eng="pool")
        self.memset(self.coefP.ap[:, 0:512:64], 0.0, B(self.coefP), eng="pool")
        self.coefS = self.pt([128, 128], F32)
        self.memset(self.coefS.ap[:], 1.0, B(self.coefS), eng="pool")
        self.memset(self.coefS.ap[:, 0:128:8], 0.0, B(self.coefS), eng="pool")

    def params_alloc(self):
        self.bcol = self.pt([128, 96], F32)
        self.bal = self.pt([16, 1], F32)
        self.wa2 = self.pt([16, 512], BF16)
        self.bubc = self.pt([128, 1024], F32)
        self.bvbc = self.pt([128, 1024], F32)
        self.bglu = self.pt([128, 8], F32)
        self.nba = self.pt([128, 4], F32)
        self.normg = self.pt([128, 8], F32)
        self.convw = self.pt([128, 8, 31], F32)
        self.convb = self.pt([128, 8], F32)
        self.clng = self.pt([128, 8], F32)
        self.clnb = self.pt([128, 8], F32)
        self.lng = self.pt([128, 16], F32)
        self.lnb = self.pt([128, 16], F32)
        self.BC = dict(z=0, q=8, k=12, gz=16, ca=24, cb=32, cz=40, g=48)

    def params_load(self, L):
        I = self.I
        segs = [("z", 8), ("q", 4), ("k", 4), ("gz", 8), ("ca", 8), ("cb", 8), ("cz", 8), ("g", 48)]
        for nm, n in segs:
            c0 = self.BC[nm]
            src = I["b_in"][L, OFF[nm]:OFF[nm] + 128 * n].rearrange("(n p) -> p n", p=128)
            self.dma("sp", self.bcol.ap[:, c0:c0 + n], src, [], B(self.bcol), nc_ok=True)
        self.dma("sp", self.bal.ap[:, :], I["b_in"][L, OFF["al"]:OFF["al"] + 16].rearrange("(p o) -> p o", o=1), [], B(self.bal), nc_ok=True)
        self.dma("pool", self.wa2.ap[:, :], I["gla_w_a2"][L, :, :], [], B(self.wa2))
        self.dma("sp", self.bubc.ap[:, :], I["b_in"][L:L + 1, 0:1024].to_broadcast([128, 1024]), [], B(self.bubc))
        self.dma("sp", self.bvbc.ap[:, :], I["b_in"][L:L + 1, OFF["v"]:OFF["v"] + 1024].to_broadcast([128, 1024]), [], B(self.bvbc))

        def col(dst, src1d, n):
            self.dma("sp", dst.ap[:, 0:n], src1d.rearrange("(n p) -> p n", p=128), [], B(dst), nc_ok=True)
        col(self.bglu, I["b_glu"][L, :], 8)
        col(self.nba, I["gla_b_a"][L, :], 4)
        self.ts(self.nba.ap[:, :], self.nba.ap[:, :], -1.0, None, ALU.mult, None, B(self.nba), B(self.nba))
        col(self.normg, I["gla_norm_g"][L, :], 8)
        col(self.convb, I["conv_b"][L, :], 8)
        col(self.clng, I["conv_ln_g"][L, :], 8)
        col(self.clnb, I["conv_ln_b"][L, :], 8)
        col(self.lng, I["ln_g"][L, :], 16)
        col(self.lnb, I["ln_b"][L, :], 16)
        for ct in range(8):
            self.dma("sp", self.convw.ap[:, ct, :], I["conv_w"][L, :, ct * 128:(ct + 1) * 128].rearrange("k p -> p k"), [], B(self.convw), nc_ok=True)

    def s5_prep(self, L):
        I = self.I
        f = lambda n: self.salloc(n, F32)
        are = f(32); aim = f(32); ldt = f(32); dK = f(32)
        Bre = f(512); Bim = f(512); Cre = f(512); Cim = f(512)
        for g in range(2):
            ps_ = slice(64 * g, 64 * g + 64)
            self.dma("sp", are.ap[ps_, :], I["s5_a_re"][L].rearrange("(j two) p -> two p j", two=2)[g], [], B(are), nc_ok=True)
            self.dma("sp", aim.ap[ps_, :], I["s5_a_im"][L].rearrange("(j two) p -> two p j", two=2)[g], [], B(aim), nc_ok=True)
            self.dma("sp", ldt.ap[ps_, :], I["s5_log_dt"][L].rearrange("(j two) -> two j", two=2)[g:g + 1, :].to_broadcast([64, 32]), [], B(ldt), nc_ok=True)
            self.dma("sp", Bre.ap[ps_, :].rearrange("p (j c) -> p j c", c=16), I["s5_b_re"][L].rearrange("(j two) p c -> two p j c", two=2)[g], [], B(Bre), nc_ok=True)
            self.dma("sp", Bim.ap[ps_, :].rearrange("p (j c) -> p j c", c=16), I["s5_b_im"][L].rearrange("(j two) p c -> two p j c", two=2)[g], [], B(Bim), nc_ok=True)
            for s in range(4):
                p0 = 64 * g + 16 * s
                self.dma("sp", dK.ap[p0:p0 + 16, :], I["s5_d"][L].rearrange("(j g c) -> g c j", g=2, c=16)[g], [], B(dK), nc_ok=True)
        for nm, Ct in (("s5_c_re", Cre), ("s5_c_im", Cim)):
            stg = f(2048)
            self.dma("sp", stg.ap[0:32, :], I[nm][L].rearrange("(j two) c p -> j (two c p)", two=2), [], B(stg))
            bk = self.bank()
            for g in range(2):
                for c in range(16):
                    self.mm(bk.ap[64 * g:64 * g + 64, c * 32:(c + 1) * 32], stg.ap[0:32, (g * 16 + c) * 64:(g * 16 + c + 1) * 64],
                            self.identF.ap[0:32, 0:32], True, True, B(stg, self.identF), B(bk))
            self.cp(Ct.ap.rearrange("p (j c) -> p j c", c=16), bk.ap[:, :].rearrange("p (c j) -> p j c", c=16, j=32), B(bk), B(Ct))
            self.sfree(stg)
        dt = f(32); ardt = f(32); aidt = f(32)
        self.act(dt.ap, ldt.ap, AF.Exp, B(ldt), B(dt))
        self.tt(ardt.ap, are.ap, dt.ap, ALU.mult, B(are, dt), B(ardt))
        self.tt(aidt.ap, aim.ap, dt.ap, ALU.mult, B(aim, dt), B(aidt))
        MAG = f(256); TSC = f(512); R1 = f(512); R2 = f(512); SC = f(512)
        for k in range(8):
            m = k - 3
            self.act(MAG.ap[:, k * 32:(k + 1) * 32], ardt.ap, AF.Exp, B(ardt), B(MAG), scale=float(m))
            self.ts(TSC.ap[:, k * 32:(k + 1) * 32], aidt.ap, float(m) / TWO_PI, None, ALU.mult, None, B(aidt), B(TSC))
        self.ts(TSC.ap[:, 256:512], TSC.ap[:, 0:256], 0.25, None, ALU.add, None, B(TSC), B(TSC))
        self.ts(R1.ap, TSC.ap, MAGIC, None, ALU.add, None, B(TSC), B(R1))
        self.ts(R2.ap, R1.ap, -MAGIC, None, ALU.add, None, B(R1), B(R2))
        self.tt(R1.ap, TSC.ap, R2.ap, ALU.subtract, B(TSC, R2), B(R1))
        self.act(SC.ap, R1.ap, AF.Sin, B(R1), B(SC), scale=TWO_PI)
        PWr = f(256); PWi = f(256)
        self.tt(PWr.ap, MAG.ap, SC.ap[:, 256:512], ALU.mult, B(MAG, SC), B(PWr))
        self.tt(PWi.ap, MAG.ap, SC.ap[:, 0:256], ALU.mult, B(MAG, SC), B(PWi))
        self.sfree(MAG, TSC, R1, R2, SC, dt, ardt, aidt, ldt)
        pw = lambda Tt, k: Tt.ap[:, k * 32:(k + 1) * 32]
        T4 = self.T4
        self.cp(T4.ap[:, L, 0, 0, :], pw(PWr, 7), B(PWr), B(T4))
        self.cp(T4.ap[:, L, 0, 1, :], pw(PWr, 7), B(PWr), B(T4))
        self.cp(T4.ap[:, L, 1, 0, :], pw(PWi, 7), B(PWi), B(T4))
        self.ts(T4.ap[:, L, 1, 1, :], pw(PWi, 7), -1.0, None, ALU.mult, None, B(PWi), B(T4))
        nr = f(32); t1 = f(32); t2 = f(32); den = f(32); Ere = f(32); Eim = f(32)
        self.ts(nr.ap, pw(PWr, 4), -1.0, None, ALU.add, None, B(PWr), B(nr))
        ni = pw(PWi, 4)
        self.tt(den.ap, are.ap, are.ap, ALU.mult, B(are), B(den))
        self.tt(t1.ap, aim.ap, aim.ap, ALU.mult, B(aim), B(t1))
        self.tt(den.ap, den.ap, t1.ap, ALU.add, B(den, t1), B(den))
        self.recip(den.ap, den.ap, B(den), B(den))
        self.tt(t1.ap, nr.ap, are.ap, ALU.mult, B(nr, are), B(t1))
        self.tt(t2.ap, ni, aim.ap, ALU.mult, B(PWi, aim), B(t2))
        self.tt(t1.ap, t1.ap, t2.ap, ALU.add, B(t1, t2), B(t1))
        self.tt(Ere.ap, t1.ap, den.ap, ALU.mult, B(t1, den), B(Ere))
        self.tt(t1.ap, ni, are.ap, ALU.mult, B(PWi, are), B(t1))
        self.tt(t2.ap, nr.ap, aim.ap, ALU.mult, B(nr, aim), B(t2))
        self.tt(t1.ap, t1.ap, t2.ap, ALU.subtract, B(t1, t2), B(t1))
        self.tt(Eim.ap, t1.ap, den.ap, ALU.mult, B(t1, den), B(Eim))
        self.sfree(nr, t2, den, are, aim)

        def v3(Tt):
            return Tt.ap.rearrange("p (j c) -> p j c", c=16)

        def bc(ap32):
            return ap32.rearrange("p (j o) -> p j o", o=1).to_broadcast([128, 32, 16])

        def cmul(outr, outi, ar, ai, br, bi, rb, tmpT, neg_im=False):
            tv = v3(tmpT)
            if outr is not None:
                self.tt(outr, bc(ar), br, ALU.mult, rb, rb)
                self.tt(tv, bc(ai), bi, ALU.mult, rb + B(tmpT), B(tmpT))
                self.tt(outr, outr, tv, ALU.subtract, rb + B(tmpT), rb)
            if outi is not None:
                self.tt(outi, bc(ar), bi, ALU.mult, rb, rb)
                self.tt(tv, bc(ai), br, ALU.mult, rb + B(tmpT), B(tmpT))
                self.tt(outi, outi, tv, ALU.add, rb + B(tmpT), rb)
                if neg_im:
                    self.ts(outi, outi, -1.0, None, ALU.mult, None, rb, rb)

        tmp = f(512)
        bbr = f(512); bbi = f(512)
        allb = B(PWr, PWi, Ere, Eim, Bre, Bim, Cre, Cim, bbr, bbi)
        cmul(v3(bbr), v3(bbi), Ere.ap, Eim.ap, v3(Bre), v3(Bim), allb, tmp)
        self.sfree(Ere, Eim, Bre, Bim, t1)
        Xr = f(2048); XiN = f(2048); Zr = f(2048); Zi = f(2048)
        v4 = lambda Tt: Tt.ap.rearrange("p (j s c) -> p j s c", s=4, c=16)
        rb = allb + B(Xr, XiN, Zr, Zi)
        for s in range(4):
            cmul(v4(Xr)[:, :, s, :], v4(XiN)[:, :, s, :], pw(PWr, 3 - s), pw(PWi, 3 - s), v3(bbr), v3(bbi), rb, tmp, neg_im=True)
            cmul(v4(Zr)[:, :, s, :], v4(Zi)[:, :, s, :], pw(PWr, 3 + s), pw(PWi, 3 + s), v3(Cre), v3(Cim), rb, tmp)
        Mf = f(4096)
        self.memset(Mf.ap, 0.0, B(Mf))
        Mf3 = Mf.ap.rearrange("p (j n) -> p j n", n=128)
        for j4 in range(8):
            bk = self.bank()
            for jj in range(4):
                j = j4 * 4 + jj
                for g in range(2):
                    pr = slice(64 * g, 64 * g + 64)
                    o = bk.ap[pr, jj * 128 + 64 * g: jj * 128 + 64 * g + 64]
                    self.mm(o, Xr.ap[pr, j * 64:(j + 1) * 64], Zr.ap[pr, j * 64:(j + 1) * 64], True, False, B(Xr, Zr), B(bk))
                    self.mm(o, XiN.ap[pr, j * 64:(j + 1) * 64], Zi.ap[pr, j * 64:(j + 1) * 64], False, True, B(XiN, Zi), B(bk))
            for g in range(2):
                pr = slice(64 * g, 64 * g + 64)
                TM = self.TMa if g == 0 else self.TMb
                outv = Mf3[pr, j4 * 4:(j4 + 1) * 4, :].rearrange("p j (t g c) -> p j t g c", t=4, g=2, c=16)[:, :, :, g, :]
                inv = bk.ap[pr, :].rearrange("p (j g t c) -> p j g t c", j=4, g=2, t=4, c=16)[:, :, g, :, :]
                mk = TM.ap[pr, :, :].rearrange("p (o t) c -> p o t c", o=1).to_broadcast([64, 4, 4, 16])
                self.tt(outv, inv, mk, ALU.mult, B(bk, TM), B(Mf))
        for j in range(32):
            self.stt(Mf3[:, j, :], self.permI.ap[:, :], dK.ap[:, j:j + 1], Mf3[:, j, :], ALU.mult, ALU.add, B(self.permI, dK, Mf), B(Mf))
        Mb = self.salloc(4096, BF16)
        self.cp(Mb.ap, Mf.ap, B(Mf), B(Mb), eng="act")
        self.dma("sp", self.s5w[L, 0], Mb.ap, B(Mb), [self.s5w_buf[L][0]])
        self.sfree(Xr, XiN, Zr, Zi, Mf, Mb, dK)
        Wn_r = f(2048); Wn_i = f(2048)
        rb = allb + B(Wn_r, Wn_i)
        for s in range(4):
            cmul(v4(Wn_r)[:, :, s, :], v4(Wn_i)[:, :, s, :], pw(PWr, 6 - s), pw(PWi, 6 - s), v3(bbr), v3(bbi), rb, tmp)
        for plane, Wn in enumerate((Wn_r, Wn_i)):
            VS = f(4096)
            self.memset(VS.ap, 0.0, B(VS))
            VS3 = VS.ap.rearrange("p (j n) -> p j n", n=128)
            Wn3 = Wn.ap.rearrange("p (j n) -> p j n", n=64)
            for g in range(2):
                pr = slice(64 * g, 64 * g + 64)
                self.cp(VS3[pr, :, 64 * g:64 * g + 64], Wn3[pr, :, :], B(Wn), B(VS))
            WSt = self.salloc(4096, BF16)
            for j4 in range(8):
                bk = self.bank()
                for jj in range(4):
                    j = j4 * 4 + jj
                    self.tr(bk.ap[:, jj * 128:(jj + 1) * 128], VS3[:, j, :], self.identF.ap[:, :], B(VS, self.identF), B(bk))
                self.cp(WSt.ap[:, j4 * 512:(j4 + 1) * 512], bk.ap[:, :], B(bk), B(WSt), eng="act")
            self.dma("sp", self.s5w[L, 1 + plane], WSt.ap, B(WSt), [self.s5w_buf[L][1 + plane]])
            self.sfree(VS, WSt)
        rb = allb + B(Wn_r, Wn_i)
        for t in range(4):
            cmul(v4(Wn_r)[:, :, t, :], v4(Wn_i)[:, :, t, :], pw(PWr, 4 + t), pw(PWi, 4 + t), v3(Cre), v3(Cim), rb, tmp, neg_im=True)
        for plane, Wn in enumerate((Wn_r, Wn_i)):
            WYb = self.salloc(4096, BF16)
            self.memset(WYb.ap, 0.0, B(WYb))
            for g in range(2):
                pr = slice(64 * g, 64 * g + 64)
                outv = WYb.ap[pr, :].rearrange("p (j t g c) -> p j t g c", t=4, g=2, c=16)[:, :, :, g, :]
                inv = Wn.ap[pr, :].rearrange("p (j t c) -> p j t c", t=4, c=16)
                self.cp(outv, inv, B(Wn), B(WYb))
            self.dma("sp", self.s5w[L, 3 + plane], WYb.ap, B(WYb), [self.s5w_buf[L][3 + plane]])
            self.sfree(WYb)
        self.sfree(Wn_r, Wn_i, tmp, bbr, bbi, Cre, Cim, PWr, PWi)

    def win(self, L, col0, ncols, KT=16):
        return self.I["w_in"][L, :, col0:col0 + ncols].rearrange("(k p) n -> p k n", p=128)

    def wmat(self, name, L, col0, ncols):
        return self.I[name][L, :, col0:col0 + ncols].rearrange("(k p) n -> p k n", p=128)

    def weight_recipes(self):
        r = []
        for cs in range(4):
            r.append(("w_in", OFF["u"] + 256 * cs, 256, 16))
        for cs in range(4):
            r.append(("w_in", OFF["ca"] + 256 * cs, 256, 16)); r.append(("w_in", OFF["cb"] + 256 * cs, 256, 16))
        for cs in range(4):
            r.append(("w_in", OFF["cz"] + 256 * cs, 256, 16))
        for cs in range(4):
            r.append(("w_in", OFF["z"] + 256 * cs, 256, 16))
        for hs in range(2):
            r.append(("w_glu", 512 * hs, 512, 8))
        r.append(("w_in", OFF["al"], 16, 16))
        for nm in ("q", "k"):
            for cs in range(2):
                r.append(("w_in", OFF[nm] + 256 * cs, 256, 16))
        for nm in ("v", "gz"):
            for cs in range(4):
                r.append(("w_in", OFF[nm] + 256 * cs, 256, 16))
        for jo4 in range(4):
            for bi, wn in enumerate(("p_s5", "p_gla", "p_conv")):
                r.append((wn, 512 * jo4, 512, 8))
                for half in range(2):
                    r.append(("w_in", OFF["g"] + bi * 2048 + jo4 * 512 + half * 256, 256, 16))
        for cs in range(8):
            r.append(("w_o", 256 * cs, 256, 16))
        r.append(("w_pe", 0, 2048, 2))
        for cs in range(8):
            r.append(("w_pg", 256 * cs, 256, 16))
        return r

    def preconvert(self, L, lo=0, hi=None):
        rec = self.weight_recipes()
        hi = len(rec) if hi is None else min(hi, len(rec))
        for (name, col0, ncols, KT) in rec[lo:hi]:
            key = (name, L, col0, ncols)
            if key in self.wblk:
                continue
            src = self.I[name][L, :, col0:col0 + ncols].rearrange("(k p) n -> p k n", p=128)
            bi = self.wblk_n[L]; self.wblk_n[L] += 1
            assert bi < self.NBLK
            bb = Buf()
            self.wblk[key] = (bi, bb)
            dstv = self.wscr[L][bi, :, 0:KT * ncols].rearrange("p (k n) -> p k n", n=ncols)
            self.dma("pool", dstv, src, r=[], w=[bb])

    def wl_in(self, L, col0, ncols):
        return self.wload(self.win(L, col0, ncols), 16, ncols, key=("w_in", L, col0, ncols))

    def wl_mat(self, name, L, col0, ncols, KT):
        return self.wload(self.wmat(name, L, col0, ncols), KT, ncols, key=(name, L, col0, ncols))

    def proj_fm(self, ws, jj, KT, rhs_fn, rhsT, TT):
        bk = self.bank()
        for kt in range(KT):
            self.mm(bk.ap[:, 0:TT], ws.ap[:, kt, jj * 128:(jj + 1) * 128], rhs_fn(kt), kt == 0, kt == KT - 1, B(ws, rhsT), B(bk))
        return bk

    def tile_layer(self, L, tc):
        if tc.kind == "p" and tc.first:
            self.preconvert(L)
        self.load_x(L, tc)
        self.s5_phase(L, tc)
        self.conv_phase(L, tc)
        self.dbg_dump("dbg_conv", self.ycg, L, tc)
        self.s5_back(L, tc)
        self.dbg_dump("dbg_s5", self.ys5g, L, tc)
        self.gla_phase(L, tc)
        self.dbg_dump("dbg_gla", self.ogg, L, tc)
        self.merge_phase(L, tc)

    def dbg_dump(self, name, t, L, tc):
        if not self.debug or L != 0:
            return
        key = name + "_" + tc.kind + str(tc.idx)
        o = self.dram_out(key, [128, 8, 512], BF16)
        self.dma("sp", o[:, :, :], t.ap[:, :, :], B(t), [], store=True)

    def xrhs(self, tc):
        xT = self.xT
        return lambda kt: xT.ap[:, kt, 0:tc.TT]

    def load_x(self, L, tc):
        I = self.I; TT = tc.TT
        ngr = TT // 128
        if L == 0:
            src = I["xp"] if tc.kind == "p" else I["xs"]
            for tg in range(ngr):
                stg = self.salloc(2048, F32)
                r0 = tc.tok0 + tg * 128
                self.dma("pool", stg.ap, src[r0:r0 + 128, :], [], B(stg))
                for k4 in range(4):
                    bk = self.bank()
                    for kk in range(4):
                        kt = k4 * 4 + kk
                        self.tr(bk.ap[:, kk * 128:(kk + 1) * 128], stg.ap[:, kt * 128:(kt + 1) * 128], self.identF.ap[:, :], B(stg, self.identF), B(bk))
                    outv = self.xT.ap[:, k4 * 4:(k4 + 1) * 4, tg * 128:(tg + 1) * 128]
                    self.cp(outv, bk.ap[:, :].rearrange("p (k t) -> p k t", t=128), B(bk), B(self.xT), eng=("act" if k4 % 2 else "dve"))
                self.sfree(stg)
        else:
            par = (L - 1) % 2
            src = self.spill[par, tc.idx].rearrange("p (k t) -> p k t", t=512)[:, :, 0:TT]
            self.dma("pool", self.xT.ap[:, :, 0:TT], src, [self.spill_buf[par][tc.idx]], B(self.xT))
        psrc = I["pp"][L] if tc.kind == "p" else I["pps"][L]
        for tg in range(ngr):
            stg = self.salloc(256, F32)
            r0 = tc.tok0 + tg * 128
            self.dma("pool", stg.ap, psrc[r0:r0 + 128, :], [], B(stg))
            bk = self.bank()
            for kk in range(2):
                self.tr(bk.ap[:, kk * 128:(kk + 1) * 128], stg.ap[:, kk * 128:(kk + 1) * 128], self.identF.ap[:, :], B(stg, self.identF), B(bk))
            self.cp(self.pT.ap[:, :, tg * 128:(tg + 1) * 128], bk.ap[:, 0:256].rearrange("p (k t) -> p k t", t=128), B(bk), B(self.pT), eng="act")
            self.sfree(stg)

    def s5_phase(self, L, tc):
        I = self.I; O = self.O
        TT, NB, NSEG, SEGB = tc.TT, tc.NB, tc.NSEG, tc.SEGB
        xT = self.xT
        Dt = self.salloc(4096, BF16)
        D5 = Dt.ap.rearrange("p (j g s c) -> p j g s c", j=32, g=2, s=4, c=16)
        for cs in range(4):
            ws = self.wl_in(L, OFF["u"] + 256 * cs, 256)
            for s in range(4):
                bk = self.bank()
                for kt in range(16):
                    self.mm(bk.ap[0:NB, 0:256], xT.ap[:, kt, s:TT:4], ws.ap[:, kt, :], kt == 0, kt == 15, B(xT, ws), B(bk))
                outv = D5[0:NB, 8 * cs:8 * cs + 8, :, s, :]
                inv = bk.ap[0:NB, 0:256].rearrange("p (j g c) -> p j g c", j=8, g=2, c=16)
                bv = self.bubc.ap[0:NB, 256 * cs:256 * cs + 256].rearrange("p (j g c) -> p j g c", j=8, g=2, c=16)
                self.tt(outv, inv, bv, ALU.add, B(bk, self.bubc), B(Dt))
        U2 = self.salloc(32 * NB, BF16)
        U23 = U2.ap.rearrange("p (j n) -> p j n", n=NB)
        per = 1024 // NB
        j = 0
        while j < 32:
            bk = self.bank()
            bkb = bk.ap.bitcast(BF16)
            nj = min(per, 32 - j)
            for jj in range(nj):
                self.tr(bkb[:, jj * NB:(jj + 1) * NB], Dt.ap[0:NB, (j + jj) * 128:(j + jj + 1) * 128], self.identB.ap[0:NB, 0:NB], B(Dt, self.identB), B(bk))
            self.cp(U2.ap[:, j * NB:(j + nj) * NB], bkb[:, 0:nj * NB], B(bk), B(U2), eng="act")
            j += nj
        self.sfree(Dt)
        wSr = self.wload_s5(L, 1)
        wSi = self.wload_s5(L, 2)
        HW = NSEG * (SEGB + 1)
        H = self.salloc(2 * 32 * HW, F32)
        H5 = H.ap.rearrange("p (a j q b) -> p a j q b", a=2, j=32, q=NSEG, b=SEGB + 1)
        if tc.kind == "p":
            if tc.first:
                self.memset(self.carry.ap[:], 0.0, B(self.carry))
            self.cp(H5[:, :, :, 0, 0], self.carry.ap[:, :, :], B(self.carry), B(H))
        else:
            for plane, nm in enumerate(("sre", "sim")):
                for r in range(4):
                    stg = self.salloc(128, F32)
                    self.dma("pool", stg.ap, I[nm][L, r * 128:(r + 1) * 128, :], [], B(stg))
                    bk = self.bank()
                    self.tr(bk.ap[:, 0:128], stg.ap, self.identF.ap[:, :], B(stg, self.identF), B(bk))
                    outv = H5[:, plane, :, 4 * r:4 * r + 4, 0]
                    inv = bk.ap[:, 0:128].rearrange("p (q j) -> p j q", q=4, j=32)
                    self.cp(outv, inv, B(bk), B(H))
                    self.sfree(stg)
        pairs_per_bank = 512 // NB
        for plane, wS in enumerate((wSr, wSi)):
            j = 0
            while j < 32:
                bk = self.bank()
                nj = min(pairs_per_bank, 32 - j)
                for jj in range(nj):
                    self.mm(bk.ap[:, jj * NB:(jj + 1) * NB], wS.ap[:, j + jj, :], U23[:, j + jj, :], True, True, B(wS, U2), B(bk))
                outv = H5[:, plane, j:j + nj, :, 1:SEGB + 1]
                inv = bk.ap[:, 0:nj * NB].rearrange("p (j q b) -> p j q b", j=nj, q=NSEG, b=SEGB)
                self.cp(outv, inv, B(bk), B(H), eng="act")
                j += nj
        u = self.salloc(2 * 2 * 32 * NSEG, F32)
        u5 = u.ap.rearrange("p (r a j q) -> p r a j q", r=2, a=2, j=32, q=NSEG)
        T4 = self.T4
        for b in range(SEGB):
            if NSEG == 1:
                hb = H5[:, :, :, 0, b].rearrange("p (o a) j -> p o a j", o=1).to_broadcast([128, 2, 2, 32])
                self.tt(u5[:, :, :, :, 0], hb, T4.ap[:, L, :, :, :], ALU.mult, B(H, T4), B(u), eng=SCAN_ENG)
            else:
                for rpt in range(2):
                    tb = T4.ap[:, L, rpt, :, :].rearrange("p a (j o) -> p a j o", o=1).to_broadcast([128, 2, 32, NSEG])
                    self.tt(u5[:, rpt], H5[:, :, :, :, b], tb, ALU.mult, B(H, T4), B(u), eng=SCAN_ENG)
            self.tt(H5[:, :, :, :, b + 1], H5[:, :, :, :, b + 1], u5[:, 0], ALU.add, B(H, u), B(H), eng=SCAN_ENG)
            self.tt(H5[:, :, :, :, b + 1], H5[:, :, :, :, b + 1], u5[:, 1, ::-1], ALU.add, B(H, u), B(H), eng=SCAN_ENG)
        self.s5_ctx = (U2, U23, H, H5, u)

    def s5_back(self, L, tc):
        I = self.I; O = self.O
        TT, NB, NSEG, SEGB = tc.TT, tc.NB, tc.NSEG, tc.SEGB
        xT = self.xT
        U2, U23, H, H5, u = self.s5_ctx
        self.sfree(u)
        Hbf = self.salloc(2 * 32 * NB, BF16)
        Hb4 = Hbf.ap.rearrange("p (a j n) -> p a j n", a=2, j=32, n=NB)
        for plane in range(2):
            outv = Hbf.ap[:, plane * 32 * NB:(plane + 1) * 32 * NB].rearrange("p (j q b) -> p j q b", j=32, q=NSEG, b=SEGB)
            self.cp(outv, H5[:, plane, :, :, 0:SEGB], B(H), B(Hbf), eng=("act" if plane else "dve"))
        if tc.kind == "p":
            self.cp(self.carry.ap[:, :, :], H5[:, :, :, 0, SEGB], B(H), B(self.carry))
            if tc.last:
                for plane, nm in enumerate(("sre_p", "sim_p")):
                    bk = self.bank()
                    self.tr(bk.ap[0:32, 0:128], self.carry.ap[:, plane, :], self.identF.ap[:, :], B(self.carry, self.identF), B(bk))
                    stg = self.salloc(128, F32)
                    self.cp(stg.ap[0:32, :], bk.ap[0:32, 0:128], B(bk), B(stg))
                    self.dma("sp", O[nm][L, :, :], stg.ap[0:32, :], B(stg), [], store=True)
                    self.sfree(stg)
        else:
            for plane, nm in enumerate(("sre_s", "sim_s")):
                fin = self.salloc(512, F32)
                self.cp(fin.ap.rearrange("p (q j) -> p j q", q=16, j=32), H5[:, plane, :, :, SEGB], B(H), B(fin))
                for r in range(4):
                    bk = self.bank()
                    self.tr(bk.ap[:, 0:128], fin.ap[:, r * 128:(r + 1) * 128], self.identF.ap[:, :], B(fin, self.identF), B(bk))
                    stg = self.salloc(128, F32)
                    self.cp(stg.ap, bk.ap[:, 0:128], B(bk), B(stg), eng="act")
                    self.dma("sp", O[nm][L, r * 128:(r + 1) * 128, :], stg.ap, B(stg), [], store=True)
                    self.sfree(stg)
                self.sfree(fin)
        self.sfree(H)
        wM = self.wload_s5(L, 0)
        wYr = self.wload_s5(L, 3)
        wYi = self.wload_s5(L, 4)
        ys = self.salloc(4096, BF16)
        ys3 = ys.ap.rearrange("p (t c) -> p t c", t=4)
        for j4 in range(8):
            bk = self.bank()
            for jj in range(4):
                j = j4 * 4 + jj
                o = bk.ap[0:NB, jj * 128:(jj + 1) * 128]
                self.mm(o, U23[:, j, :], wM.ap[:, j, :], True, False, B(U2, wM), B(bk))
                self.mm(o, Hb4[:, 0, j, :], wYr.ap[:, j, :], False, False, B(Hbf, wYr), B(bk))
                self.mm(o, Hb4[:, 1, j, :], wYi.ap[:, j, :], False, True, B(Hbf, wYi), B(bk))
            yf = self.salloc(512, F32); tq = self.salloc(512, F32)
            self.cp(yf.ap[0:NB, :], bk.ap[0:NB, :], B(bk), B(yf), eng="act")
            self.act(tq.ap[0:NB, :], bk.ap[0:NB, :], AF.Square, B(bk), B(tq))
            self.ts(tq.ap[0:NB, :], tq.ap[0:NB, :], 0.044715, 1.0, ALU.mult, ALU.add, B(tq), B(tq))
            self.tt(tq.ap[0:NB, :], tq.ap[0:NB, :], yf.ap[0:NB, :], ALU.mult, B(tq, yf), B(tq))
            self.act(tq.ap[0:NB, :], tq.ap[0:NB, :], AF.Sigmoid, B(tq), B(tq), scale=GELU_C)
            outv = ys3[0:NB, :, j4 * 128:(j4 + 1) * 128].rearrange("p t (j c) -> p j t c", j=4, c=32)
            self.tt(outv, yf.ap[0:NB, :].rearrange("p (j t c) -> p j t c", j=4, t=4, c=32),
                    tq.ap[0:NB, :].rearrange("p (j t c) -> p j t c", j=4, t=4, c=32), ALU.mult, B(yf, tq), B(ys))
            self.sfree(yf, tq)
        self.sfree(U2, Hbf)
        ysT = self.salloc(8 * TT, BF16)
        ysT3 = ysT.ap.rearrange("p (k t) -> p k t", t=TT)
        per = 1024 // NB
        items = [(t, ct) for ct in range(8) for t in range(4)]
        i = 0
        while i < len(items):
            bk = self.bank(); bkb = bk.ap.bitcast(BF16)
            grp = items[i:i + per]
            for gi, (t, ct) in enumerate(grp):
                self.tr(bkb[:, gi * NB:(gi + 1) * NB], ys3[0:NB, t, ct * 128:(ct + 1) * 128], self.identB.ap[0:NB, 0:NB], B(ys, self.identB), B(bk))
            for gi, (t, ct) in enumerate(grp):
                self.cp(ysT3[:, ct, t:TT:4], bkb[:, gi * NB:(gi + 1) * NB], B(bk), B(ysT), eng=("act" if gi % 2 else "dve"))
            i += per
        self.sfree(ys)
        zsT = self.salloc(8 * TT, BF16)
        zs3 = zsT.ap.rearrange("p (k t) -> p k t", t=TT)
        for cs in range(4):
            ws = self.wl_in(L, OFF["z"] + 256 * cs, 256)
            for jj in range(2):
                ct = cs * 2 + jj
                bk = self.proj_fm(ws, jj, 16, self.xrhs(tc), xT, TT)
                self.act(zs3[:, ct, :], bk.ap[:, 0:TT], AF.Silu, B(bk, self.bcol), B(zsT), bias=self.bcol.ap[:, self.BC["z"] + ct:self.BC["z"] + ct + 1])
        for hs in range(2):
            ws = self.wl_mat("w_glu", L, 512 * hs, 512, 8)
            for jj in range(4):
                o = hs * 4 + jj
                bk = self.proj_fm(ws, jj, 8, lambda kt: ysT3[:, kt, :], ysT, TT)
                sg = self.salloc(TT, F32)
                self.act(sg.ap, bk.ap[:, 0:TT], AF.Sigmoid, B(bk, self.bglu), B(sg), bias=self.bglu.ap[:, o:o + 1])
                self.tt(sg.ap, sg.ap, ysT3[:, o, :], ALU.mult, B(sg, ysT), B(sg))
                self.tt(self.ys5g.ap[:, o, 0:TT], sg.ap, zs3[:, o, :], ALU.mult, B(sg, zsT), B(self.ys5g))
                self.sfree(sg)
        self.sfree(ysT, zsT)

    def wload_s5(self, L, idx):
        return self.wload(self.s5w[L, idx].rearrange("p (k n) -> p k n", n=128), 32, 128, rb=[self.s5w_buf[L][idx]])

    def gla_phase(self, L, tc):
        I = self.I; O = self.O
        TT, NCH, NE, EL = tc.TT, tc.NCH, tc.NE, tc.EL
        xT = self.xT; xr = self.xrhs(tc)
        BCq, BCk, BCgz = self.BC["q"], self.BC["k"], self.BC["gz"]
        ws = self.wl_in(L, OFF["al"], 16)
        bk = self.bank()
        for kt in range(16):
            self.mm(bk.ap[0:16, 0:TT], ws.ap[:, kt, :], xr(kt), kt == 0, kt == 15, B(ws, xT), B(bk))
        alT = self.salloc(TT, BF16)
        self.act(alT.ap[0:16, :], bk.ap[0:16, 0:TT], AF.Identity, B(bk, self.bal), B(alT), bias=self.bal.ap[:, 0:1])
        la = self.salloc(4 * TT, F32); cs = self.salloc(4 * TT, F32)
        la3 = la.ap.rearrange("p (h t) -> p h t", t=TT); cs3 = cs.ap.rearrange("p (h t) -> p h t", t=TT)
        coef = self.coefP if tc.kind == "p" else self.coefS
        for h in range(4):
            bk = self.bank()
            self.mm(bk.ap[:, 0:TT], self.wa2.ap[0:16, h * 128:(h + 1) * 128], alT.ap[0:16, :], True, True, B(self.wa2, alT), B(bk))
            self.act(la3[:, h, :], bk.ap[:, 0:TT], AF.Exp, B(bk, self.nba), B(la), bias=self.nba.ap[:, h:h + 1], scale=-1.0)
            self.act(la3[:, h, :], la3[:, h, :], AF.Ln, B(la), B(la), bias=1.0)
            self.scan(cs3[:, h, :], coef.ap[:, 0:TT], la3[:, h, :], B(coef, la), B(cs))
        self.sfree(alT, la)
        ecs = self.salloc(4 * TT, F32); encs = self.salloc(4 * TT, F32); el = self.salloc(4 * NE, F32)
        ecs3 = ecs.ap.rearrange("p (h t) -> p h t", t=TT); encs3 = encs.ap.rearrange("p (h t) -> p h t", t=TT)
        el3 = el.ap.rearrange("p (h e) -> p h e", e=NE)
        self.act(ecs.ap, cs.ap, AF.Exp, B(cs), B(ecs), scale=-1.0 / 16.0, bias=float(math.log(128.0 ** -0.5)))
        self.act(encs.ap, cs.ap, AF.Exp, B(cs), B(encs), scale=1.0 / 16.0)
        self.act(el3, cs3[:, :, EL - 1:TT:EL], AF.Exp, B(cs), B(el), scale=-1.0 / 16.0)
        self.sfree(cs)
        qd = self.salloc(4 * TT, BF16); kd = self.salloc(4 * TT, BF16)
        qd3 = qd.ap.rearrange("p (h t) -> p h t", t=TT); kd3 = kd.ap.rearrange("p (h t) -> p h t", t=TT)
        for nm, dst3, dstT, sc3, scT, bc0 in (("q", qd3, qd, ecs3, ecs, BCq), ("k", kd3, kd, encs3, encs, BCk)):
            for cs_ in range(2):
                ws = self.wl_in(L, OFF[nm] + 256 * cs_, 256)
                for jj in range(2):
                    h = cs_ * 2 + jj
                    bk = self.proj_fm(ws, jj, 16, xr, xT, TT)
                    self.stt(dst3[:, h, :], bk.ap[:, 0:TT], self.bcol.ap[:, bc0 + h:bc0 + h + 1], sc3[:, h, :], ALU.add, ALU.mult, B(bk, self.bcol, scT), B(dstT))
        self.sfree(ecs, encs)
        kk = self.salloc(4 * TT, BF16)
        kk3 = kk.ap.rearrange("p (h t) -> p h t", t=TT)
        for h in range(4):
            outv = kk3[:, h, :].rearrange("p (e t) -> p e t", t=EL)
            inv = kd3[:, h, :].rearrange("p (e t) -> p e t", t=EL)
            ev = el3[:, h, :].rearrange("p (e o) -> p e o", o=1).to_broadcast([128, NE, EL])
            self.tt(outv, inv, ev, ALU.mult, B(kd, el), B(kk))
        kkT = self.salloc(NCH * 512, BF16)
        kkT4 = kkT.ap.rearrange("p (c h d) -> p c h d", h=4, d=128)
        for ch2 in range(0, NCH, 2):
            bk = self.bank(); bkb = bk.ap.bitcast(BF16)
            for cc in range(2):
                for h in range(4):
                    c = ch2 + cc
                    self.tr(bkb[0:64, (cc * 4 + h) * 128:(cc * 4 + h + 1) * 128], kk3[:, h, c * 64:(c + 1) * 64], self.identB.ap[:, :], B(kk, self.identB), B(bk))
            self.cp(kkT.ap[0:64, ch2 * 512:(ch2 + 2) * 512], bkb[0:64, 0:1024], B(bk), B(kkT), eng="act")
        self.sfree(kk)
        vt = self.salloc(NCH * 1024, BF16)
        vt3 = vt.ap.rearrange("p (c v) -> p c v", v=1024)
        for cs_ in range(4):
            ws = self.wl_in(L, OFF["v"] + 256 * cs_, 256)
            for c in range(NCH):
                bk = self.bank()
                for kt in range(16):
                    self.mm(bk.ap[0:64, 0:256], xT.ap[:, kt, c * 64:(c + 1) * 64], ws.ap[:, kt, :], kt == 0, kt == 15, B(xT, ws), B(bk))
                self.tt(vt3[0:64, c, cs_ * 256:(cs_ + 1) * 256], bk.ap[0:64, 0:256], self.bvbc.ap[0:64, cs_ * 256:(cs_ + 1) * 256], ALU.add, B(bk, self.bvbc), B(vt))
        gz = self.salloc(8 * TT, BF16)
        gz3 = gz.ap.rearrange("p (k t) -> p k t", t=TT)
        for cs_ in range(4):
            ws = self.wl_in(L, OFF["gz"] + 256 * cs_, 256)
            for jj in range(2):
                ct = cs_ * 2 + jj
                bk = self.proj_fm(ws, jj, 16, xr, xT, TT)
                self.act(gz3[:, ct, :], bk.ap[:, 0:TT], AF.Silu, B(bk, self.bcol), B(gz), bias=self.bcol.ap[:, BCgz + ct:BCgz + ct + 1])
        o = self.salloc(8 * TT, F32)
        o3 = o.ap.rearrange("p (k t) -> p k t", t=TT)
        attT = self.salloc(NCH * 256, BF16)
        attT4 = attT.ap.rearrange("p (c h t) -> p c h t", h=4, t=64)
        Sst, Sbf = self.Sst, self.Sbf
        if tc.kind == "p" and tc.first:
            self.memset(Sst.ap[:], 0.0, B(Sst))
            self.memset(Sbf.ap[:], 0.0, B(Sbf))
        for c in range(NCH):
            csl = slice(c * 64, (c + 1) * 64)
            bkA = self.bank()
            for h in range(4):
                self.mm(bkA.ap[0:64, h * 64:(h + 1) * 64], kd3[:, h, csl], qd3[:, h, csl], True, True, B(kd, qd), B(bkA))
            if tc.kind == "p":
                mk = self.maskP.ap[:, :].rearrange("p (o t) -> p o t", o=1).to_broadcast([64, 4, 64]); mkT = self.maskP
            else:
                mk = self.maskS.ap[:, :, :].rearrange("p a b -> p (a b)").rearrange("p (o t) -> p o t", o=1).to_broadcast([64, 4, 64]); mkT = self.maskS
            self.tt(attT4[0:64, c, :, :], bkA.ap[0:64, 0:256].rearrange("p (h t) -> p h t", t=64), mk, ALU.mult, B(bkA, mkT), B(attT))
            bkO = self.bank()
            if tc.kind == "p":
                for h in range(4):
                    for vh in range(2):
                        oo = bkO.ap[:, (h * 2 + vh) * 64:(h * 2 + vh + 1) * 64]
                        self.mm(oo, vt3[0:64, c, h * 256 + vh * 128:h * 256 + (vh + 1) * 128], attT4[0:64, c, h, :], True, False, B(vt, attT), B(bkO))
                        self.mm(oo, Sbf.ap[:, h, vh * 128:(vh + 1) * 128], qd3[:, h, csl], False, True, B(Sbf, qd), B(bkO))
                self.cp(o3[:, :, csl], bkO.ap[:, :].rearrange("p (k t) -> p k t", t=64), B(bkO), B(o), eng="act")
                bkK = [self.bank(), self.bank()]
                for h in range(4):
                    self.mm(bkK[h // 2].ap[:, (h % 2) * 256:(h % 2 + 1) * 256], kkT4[0:64, c, h, :], vt3[0:64, c, h * 256:(h + 1) * 256], True, True, B(kkT, vt), B(bkK[h // 2]))
                for h in range(4):
                    self.stt(Sst.ap[:, h, :], Sst.ap[:, h, :], el3[:, h, c:c + 1], bkK[h // 2].ap[:, (h % 2) * 256:(h % 2 + 1) * 256], ALU.mult, ALU.add, B(Sst, el, bkK[h // 2]), B(Sst))
                self.cp(Sbf.ap[:], Sst.ap[:], B(Sst), B(Sbf), eng="act")
            else:
                S0f = self.salloc(8 * 1024, F32); S0b = self.salloc(8 * 1024, BF16)
                S0f4 = S0f.ap.rearrange("p (q h v) -> p q h v", q=8, h=4, v=256)
                S0b4 = S0b.ap.rearrange("p (q h v) -> p q h v", q=8, h=4, v=256)
                for q in range(8):
                    seq = c * 8 + q
                    self.dma("pool", S0f4[:, q, :, :], I["sgla"][L, seq].rearrange("h d v -> d h v"), [], B(S0f))
                self.cp(S0b.ap[:, 0:4096], S0f.ap[:, 0:4096], B(S0f), B(S0b), eng="act")
                self.cp(S0b.ap[:, 4096:8192], S0f.ap[:, 4096:8192], B(S0f), B(S0b), eng="dve")
                for h in range(4):
                    for vh in range(2):
                        oo = bkO.ap[:, (h * 2 + vh) * 64:(h * 2 + vh + 1) * 64]
                        self.mm(oo, vt3[0:64, c, h * 256 + vh * 128:h * 256 + (vh + 1) * 128], attT4[0:64, c, h, :], True, False, B(vt, attT), B(bkO))
                        for q in range(8):
                            tsl = slice(c * 64 + q * 8, c * 64 + q * 8 + 8)
                            self.mm(oo[:, q * 8:(q + 1) * 8], S0b4[:, q, h, vh * 128:(vh + 1) * 128], qd3[:, h, tsl], False, q == 7, B(S0b, qd), B(bkO))
                self.cp(o3[:, :, csl], bkO.ap[:, :].rearrange("p (k t) -> p k t", t=64), B(bkO), B(o), eng="act")
                for q in range(8):
                    seq = c * 8 + q
                    kkm = self.salloc(512, BF16)
                    self.ts(kkm.ap[0:64, :], kkT.ap[0:64, c * 512:(c + 1) * 512], self.rowm.ap[:, q:q + 1], None, ALU.mult, None, B(kkT, self.rowm), B(kkm))
                    bkK = [self.bank(), self.bank()]
                    for h in range(4):
                        self.mm(bkK[h // 2].ap[:, (h % 2) * 256:(h % 2 + 1) * 256], kkm.ap[0:64, h * 128:(h + 1) * 128], vt3[0:64, c, h * 256:(h + 1) * 256], True, True, B(kkm, vt), B(bkK[h // 2]))
                    Sn = self.salloc(1024, F32)
                    Sn3 = Sn.ap.rearrange("p (h v) -> p h v", v=256)
                    for h in range(4):
                        self.stt(Sn3[:, h, :], S0f4[:, q, h, :], el3[:, h, seq:seq + 1], bkK[h // 2].ap[:, (h % 2) * 256:(h % 2 + 1) * 256], ALU.mult, ALU.add, B(S0f, el, bkK[h // 2]), B(Sn))
                    self.dma("sp", O["gla_s"][L, seq].rearrange("h d v -> d h v"), Sn3, B(Sn), [], store=True)
                    self.sfree(kkm, Sn)
                self.sfree(S0f, S0b)
        if tc.kind == "p" and tc.last:
            self.dma("sp", O["gla_p"][L].rearrange("h d v -> d h v"), Sst.ap[:, :, :], B(Sst), [], store=True)
        self.sfree(attT, kkT, vt, qd, kd, el)
        for h in range(4):
            sq = self.salloc(2 * TT, F32)
            self.act(sq.ap, o.ap[:, 2 * h * TT:(2 * h + 2) * TT], AF.Square, B(o), B(sq))
            bkM = self.bank(); bkQ = self.bank()
            for vh in range(2):
                self.mm(bkM.ap[:, 0:TT], self.ones[256].ap[:, :], o3[:, 2 * h + vh, :], vh == 0, vh == 1, B(self.ones[256], o), B(bkM))
            for vh in range(2):
                self.mm(bkQ.ap[:, 0:TT], self.ones[256].ap[:, :], sq.ap[:, vh * TT:(vh + 1) * TT], vh == 0, vh == 1, B(self.ones[256], sq), B(bkQ))
            mean, rstd = self.ln_stats(bkM, bkQ, TT)
            self.sfree(sq)
            for vh in range(2):
                k = 2 * h + vh
                tmp = self.salloc(TT, F32)
                self.tt(tmp.ap, o3[:, k, :], mean.ap, ALU.subtract, B(o, mean), B(tmp))
                self.tt(tmp.ap, tmp.ap, rstd.ap, ALU.mult, B(tmp, rstd), B(tmp))
                self.stt(self.ogg.ap[:, k, 0:TT], tmp.ap, self.normg.ap[:, k:k + 1], gz3[:, k, :], ALU.mult, ALU.mult, B(tmp, self.normg, gz), B(self.ogg))
                self.sfree(tmp)
            self.sfree(mean, rstd)
        self.sfree(o, gz)

    def ln_stats(self, bkM, bkQ, TT):
        mean = self.salloc(TT, F32); rstd = self.salloc(TT, F32); m2 = self.salloc(TT, F32)
        self.cp(mean.ap, bkM.ap[:, 0:TT], B(bkM), B(mean), eng="act")
        self.act(m2.ap, bkM.ap[:, 0:TT], AF.Square, B(bkM), B(m2))
        self.tt(rstd.ap, bkQ.ap[:, 0:TT], m2.ap, ALU.subtract, B(bkQ, m2), B(rstd))
        self.ts(rstd.ap, rstd.ap, 0.0, None, ALU.max, None, B(rstd), B(rstd))
        self.act(rstd.ap, rstd.ap, AF.Sqrt, B(rstd), B(rstd), bias=LN_EPS)
        self.recip(rstd.ap, rstd.ap, B(rstd), B(rstd))
        self.sfree(m2)
        return mean, rstd

    def conv_phase(self, L, tc):
        I = self.I; O = self.O
        TT = tc.TT; xT = self.xT; xr = self.xrhs(tc)
        BCa, BCb, BCz = self.BC["ca"], self.BC["cb"], self.BC["cz"]
        if tc.kind == "p":
            GW = 30 + TT
            G = self.salloc(8 * GW, BF16)
            G3 = G.ap.rearrange("p (k t) -> p k t", t=GW)
            if tc.first:
                self.memset(self.halo.ap[:], 0.0, B(self.halo))
            self.cp(G3[:, :, 0:30], self.halo.ap[:, :, :], B(self.halo), B(G))
            gdst = lambda ct: G3[:, ct, 30:30 + TT]
        else:
            GW = 16 * 38
            G = self.salloc(8 * GW, F32)
            G4 = G.ap.rearrange("p (k q t) -> p k q t", q=16, t=38)
            for r in range(4):
                stg = self.salloc(1024, F32)
                self.dma("pool", stg.ap[0:120, :], I["cconv"][L, r * 120:(r + 1) * 120, :], [], B(stg))
                for c4 in range(2):
                    bk = self.bank()
                    for cc in range(4):
                        ct = c4 * 4 + cc
                        self.tr(bk.ap[:, cc * 120:(cc + 1) * 120], stg.ap[0:120, ct * 128:(ct + 1) * 128], self.identF.ap[0:120, 0:120], B(stg, self.identF), B(bk))
                    outv = G4[:, c4 * 4:(c4 + 1) * 4, 4 * r:4 * r + 4, 0:30]
                    inv = bk.ap[:, 0:480].rearrange("p (k q t) -> p k q t", k=4, q=4, t=30)
                    self.cp(outv, inv, B(bk), B(G))
                self.sfree(stg)
            gdst = lambda ct: G4[:, ct, :, 30:38]
        czs = self.salloc(8 * TT, BF16)
        czs3 = czs.ap.rearrange("p (k t) -> p k t", t=TT)
        for cs_ in range(4):
            wa = self.wl_in(L, OFF["ca"] + 256 * cs_, 256)
            wb = self.wl_in(L, OFF["cb"] + 256 * cs_, 256)
            for jj in range(2):
                ct = cs_ * 2 + jj
                bkb_ = self.proj_fm(wb, jj, 16, xr, xT, TT)
                sg = self.salloc(TT, F32)
                self.act(sg.ap, bkb_.ap[:, 0:TT], AF.Sigmoid, B(bkb_, self.bcol), B(sg), bias=self.bcol.ap[:, BCb + ct:BCb + ct + 1])
                bka = self.proj_fm(wa, jj, 16, xr, xT, TT)
                if tc.kind == "p":
                    self.stt(gdst(ct), bka.ap[:, 0:TT], self.bcol.ap[:, BCa + ct:BCa + ct + 1], sg.ap, ALU.add, ALU.mult, B(bka, self.bcol, sg), B(G))
                    self.stt(self.halo.ap[:, ct, :], bka.ap[:, TT - 30:TT], self.bcol.ap[:, BCa + ct:BCa + ct + 1], sg.ap[:, TT - 30:TT], ALU.add, ALU.mult, B(bka, self.bcol, sg), B(self.halo))
                else:
                    self.stt(gdst(ct), bka.ap[:, 0:TT].rearrange("p (q t) -> p q t", t=8), self.bcol.ap[:, BCa + ct:BCa + ct + 1],
                             sg.ap.rearrange("p (q t) -> p q t", t=8), ALU.add, ALU.mult, B(bka, self.bcol, sg), B(G))
                self.sfree(sg)
        for cs_ in range(4):
            ws = self.wl_in(L, OFF["cz"] + 256 * cs_, 256)
            for jj in range(2):
                ct = cs_ * 2 + jj
                bk = self.proj_fm(ws, jj, 16, xr, xT, TT)
                self.act(czs3[:, ct, :], bk.ap[:, 0:TT], AF.Silu, B(bk, self.bcol), B(czs), bias=self.bcol.ap[:, BCz + ct:BCz + ct + 1])
        acc = self.salloc(8 * TT, F32)
        acc3 = acc.ap.rearrange("p (k t) -> p k t", t=TT)
        cw = self.convw
        if tc.kind == "p":
            for ct in range(8):
                dg = self.salloc(31 * 128, BF16)
                dg3 = dg.ap.rearrange("p (k m) -> p k m", m=128)
                idb = self.identB.ap[:, :].rearrange("p (o m) -> p o m", o=1).to_broadcast([128, 31, 128])
                wv = cw.ap[:, ct, :].rearrange("p (k o) -> p k o", o=1).to_broadcast([128, 31, 128])
                self.tt(dg3, idb, wv, ALU.mult, B(self.identB, cw), B(dg))
                bk = self.bank()
                for k in range(31):
                    self.mm(bk.ap[:, 0:TT], dg3[:, k, :], G3[:, ct, k:k + TT], k == 0, k == 30, B(dg, G), B(bk))
                self.act(acc3[:, ct, :], bk.ap[:, 0:TT], AF.Identity, B(bk, self.convb), B(acc), bias=self.convb.ap[:, ct:ct + 1])
                self.sfree(dg)
        else:
            for ct in range(8):
                src = lambda k: G4[:, ct, :, k:k + 8]
                dst = acc3[:, ct, :].rearrange("p (q t) -> p q t", t=8)
                self.ts(dst, src(0), cw.ap[:, ct, 0:1], self.convb.ap[:, ct:ct + 1], ALU.mult, ALU.add, B(G, cw, self.convb), B(acc))
                for k in range(1, 31):
                    self.stt(dst, src(k), cw.ap[:, ct, k:k + 1], dst, ALU.mult, ALU.add, B(G, cw, acc), B(acc))
        if tc.kind == "p":
            if tc.last:
                stg = self.salloc(1024, F32)
                for c4 in range(2):
                    bk = self.bank()
                    for cc in range(4):
                        ct = c4 * 4 + cc
                        self.tr(bk.ap[0:30, cc * 128:(cc + 1) * 128], self.halo.ap[:, ct, :], self.identF.ap[:, :], B(self.halo, self.identF), B(bk))
                    self.cp(stg.ap[0:30, c4 * 512:(c4 + 1) * 512], bk.ap[0:30, :], B(bk), B(stg), eng="act")
                self.dma("sp", O["conv_p"][L, :, :], stg.ap[0:30, :], B(stg), [], store=True)
                self.sfree(stg)
        else:
            cn = self.salloc(8 * 480, F32)
            cn4 = cn.ap.rearrange("p (k q t) -> p k q t", q=16, t=30)
            self.cp(cn4, G4[:, :, :, 8:38], B(G), B(cn), eng="act")
            for r in range(4):
                stg = self.salloc(1024, F32)
                for c4 in range(2):
                    bk = self.bank()
                    for cc in range(4):
                        ct = c4 * 4 + cc
                        self.tr(bk.ap[0:120, cc * 128:(cc + 1) * 128], cn.ap[:, ct * 480 + r * 120:ct * 480 + (r + 1) * 120], self.identF.ap[:, :], B(cn, self.identF), B(bk))
                    self.cp(stg.ap[0:120, c4 * 512:(c4 + 1) * 512], bk.ap[0:120, :], B(bk), B(stg), eng="act")
                self.dma("sp", O["conv_s"][L, r * 120:(r + 1) * 120, :], stg.ap[0:120, :], B(stg), [], store=True)
                self.sfree(stg)
            self.sfree(cn)
        self.sfree(G)
        bkM = self.bank(); bkQ = self.bank()
        for ct in range(8):
            sq = self.salloc(TT, F32)
            self.act(sq.ap, acc3[:, ct, :], AF.Square, B(acc), B(sq))
            self.mm(bkM.ap[:, 0:TT], self.ones[1024].ap[:, :], acc3[:, ct, :], ct == 0, ct == 7, B(self.ones[1024], acc), B(bkM))
            self.mm(bkQ.ap[:, 0:TT], self.ones[1024].ap[:, :], sq.ap, ct == 0, ct == 7, B(self.ones[1024], sq), B(bkQ))
            self.sfree(sq)
        mean, rstd = self.ln_stats(bkM, bkQ, TT)
        for ct in range(8):
            tmp = self.salloc(TT, F32)
            self.tt(tmp.ap, acc3[:, ct, :], mean.ap, ALU.subtract, B(acc, mean), B(tmp))
            self.tt(tmp.ap, tmp.ap, rstd.ap, ALU.mult, B(tmp, rstd), B(tmp))
            self.act(tmp.ap, tmp.ap, AF.Silu, B(tmp, self.clng, self.clnb), B(tmp), bias=self.clnb.ap[:, ct:ct + 1], scale=self.clng.ap[:, ct:ct + 1])
            self.tt(self.ycg.ap[:, ct, 0:TT], tmp.ap, czs3[:, ct, :], ALU.mult, B(tmp, czs), B(self.ycg))
            self.sfree(tmp)
        self.sfree(mean, rstd, acc, czs)

    def merge_phase(self, L, tc):
        I = self.I; O = self.O
        TT = tc.TT; xT = self.xT; xr = self.xrhs(tc)
        NL = self.NL
        BCg = self.BC["g"]
        merged = self.salloc(16 * TT, BF16)
        mg3 = merged.ap.rearrange("p (k t) -> p k t", t=TT)
        branches = [("p_s5", self.ys5g, 0), ("p_gla", self.ogg, 1), ("p_conv", self.ycg, 2)]
        for jo4 in range(4):
            macc = self.salloc(4 * TT, F32)
            macc3 = macc.ap.rearrange("p (k t) -> p k t", t=TT)
            for (wn, br, bi) in branches:
                wp = self.wl_mat(wn, L, 512 * jo4, 512, 8)
                for half in range(2):
                    gcol = OFF["g"] + bi * 2048 + jo4 * 512 + half * 256
                    wg = self.wl_in(L, gcol, 256)
                    for jj in range(2):
                        jl = half * 2 + jj
                        jo = jo4 * 4 + jl
                        bkG = self.proj_fm(wg, jj, 16, xr, xT, TT)
                        sg = self.salloc(TT, F32)
                        bcix = BCg + bi * 16 + jo
                        self.act(sg.ap, bkG.ap[:, 0:TT], AF.Sigmoid, B(bkG, self.bcol), B(sg), bias=self.bcol.ap[:, bcix:bcix + 1])
                        bkA = self.proj_fm(wp, jl, 8, lambda kt, br=br: br.ap[:, kt, 0:TT], br, TT)
                        if bi == 0:
                            self.tt(macc3[:, jl, :], bkA.ap[:, 0:TT], sg.ap, ALU.mult, B(bkA, sg), B(macc))
                        else:
                            self.tt(sg.ap, bkA.ap[:, 0:TT], sg.ap, ALU.mult, B(bkA, sg), B(sg))
                            if bi == 1:
                                self.tt(macc3[:, jl, :], macc3[:, jl, :], sg.ap, ALU.add, B(macc, sg), B(macc))
                            else:
                                self.tt(mg3[:, jo, :], macc3[:, jl, :], sg.ap, ALU.add, B(macc, sg), B(merged))
                        self.sfree(sg)
            self.sfree(macc)
        hT = self.salloc(16 * TT, F32); hbf = self.salloc(16 * TT, BF16)
        h3 = hT.ap.rearrange("p (k t) -> p k t", t=TT); hb3 = hbf.ap.rearrange("p (k t) -> p k t", t=TT)
        for cs_ in range(8):
            ws = self.wl_mat("w_o", L, 256 * cs_, 256, 16)
            for jj in range(2):
                jo = cs_ * 2 + jj
                bk = self.proj_fm(ws, jj, 16, lambda kt: mg3[:, kt, :], merged, TT)
                self.stt(h3[:, jo, :], xT.ap[:, jo, 0:TT], float(DN_ALPHA), bk.ap[:, 0:TT], ALU.mult, ALU.add, B(xT, bk), B(hT))
                self.cp(hb3[:, jo, :], h3[:, jo, :], B(hT), B(hbf), eng="act")
        self.sfree(merged)
        wpe = self.wload(self.I["w_pe"][L].rearrange("(k p) n -> p k n", p=128), 2, 2048, key=("w_pe", L, 0, 2048))
        pe_all = self.salloc(16 * TT, BF16)
        pe3 = pe_all.ap.rearrange("p (k t) -> p k t", t=TT)
        for jo in range(16):
            bk = self.bank()
            for kt in range(2):
                self.mm(bk.ap[:, 0:TT], wpe.ap[:, kt, jo * 128:(jo + 1) * 128], self.pT.ap[:, kt, 0:TT], kt == 0, kt == 1, B(wpe, self.pT), B(bk))
            self.cp(pe3[:, jo, :], bk.ap[:, 0:TT], B(bk), B(pe_all), eng="act")
        for cs_ in range(8):
            ws = self.wl_mat("w_pg", L, 256 * cs_, 256, 16)
            for jj in range(2):
                jo = cs_ * 2 + jj
                bk = self.proj_fm(ws, jj, 16, lambda kt: hb3[:, kt, :], hbf, TT)
                sg = self.salloc(TT, F32)
                self.act(sg.ap, bk.ap[:, 0:TT], AF.Sigmoid, B(bk), B(sg))
                self.tt(sg.ap, sg.ap, pe3[:, jo, :], ALU.mult, B(sg, pe_all), B(sg))
                self.tt(h3[:, jo, :], h3[:, jo, :], sg.ap, ALU.add, B(hT, sg), B(hT))
                self.sfree(sg)
        self.sfree(hbf, pe_all)
        bkM = self.bank(); bkQ = self.bank()
        for jo in range(16):
            sq = self.salloc(TT, F32)
            self.act(sq.ap, h3[:, jo, :], AF.Square, B(hT), B(sq))
            self.mm(bkM.ap[:, 0:TT], self.ones[2048].ap[:, :], h3[:, jo, :], jo == 0, jo == 15, B(self.ones[2048], hT), B(bkM))
            self.mm(bkQ.ap[:, 0:TT], self.ones[2048].ap[:, :], sq.ap, jo == 0, jo == 15, B(self.ones[2048], sq), B(bkQ))
            self.sfree(sq)
        mean, rstd = self.ln_stats(bkM, bkQ, TT)
        last = (L == NL - 1)
        if not last:
            xo = self.salloc(16 * 512, BF16)
            xo3 = xo.ap.rearrange("p (k t) -> p k t", t=512)
        for jo in range(16):
            self.tt(h3[:, jo, :], h3[:, jo, :], mean.ap, ALU.subtract, B(hT, mean), B(hT))
            self.tt(h3[:, jo, :], h3[:, jo, :], rstd.ap, ALU.mult, B(hT, rstd), B(hT))
            if last:
                self.act(h3[:, jo, :], h3[:, jo, :], AF.Identity, B(hT, self.lng, self.lnb), B(hT), bias=self.lnb.ap[:, jo:jo + 1], scale=self.lng.ap[:, jo:jo + 1])
            else:
                self.act(xo3[:, jo, 0:TT], h3[:, jo, :], AF.Identity, B(hT, self.lng, self.lnb), B(xo), bias=self.lnb.ap[:, jo:jo + 1], scale=self.lng.ap[:, jo:jo + 1])
        self.sfree(mean, rstd)
        if not last:
            par = L % 2
            dst = self.spill[par, tc.idx].rearrange("p (k t) -> p k t", t=512)[:, :, 0:TT]
            self.dma("pool", dst, xo3[:, :, 0:TT], B(xo), [self.spill_buf[par][tc.idx]])
            self.sfree(xo)
        else:
            dsto = O["yp"] if tc.kind == "p" else O["ys"]
            for tg in range(TT // 128):
                stg = self.salloc(2048, F32)
                for k4 in range(4):
                    bk = self.bank()
                    for kk in range(4):
                        jo = k4 * 4 + kk
                        self.tr(bk.ap[:, kk * 128:(kk + 1) * 128], h3[:, jo, tg * 128:(tg + 1) * 128], self.identF.ap[:, :], B(hT, self.identF), B(bk))
                    self.cp(stg.ap[:, k4 * 512:(k4 + 1) * 512], bk.ap[:, :], B(bk), B(stg), eng=("act" if k4 % 2 else "dve"))
                r0 = tc.tok0 + tg * 128
                self.dma("sp", dsto[r0:r0 + 128, :], stg.ap, B(stg), [], store=True)
                self.sfree(stg)
        self.sfree(hT)


_CACHE = {}


def _get_prog(NL, NPT):
    key = (NL, NPT)
    if key not in _CACHE:
        _CACHE[key] = Builder(NL, NPT).build()
    return _CACHE[key]


def core_inputs(inputs, c, NL, NPT, prompt_b, seq0):
    f = lambda a: np.ascontiguousarray(np.asarray(a, dtype=np.float32))
    TP = NPT * 512
    m = {}
    m["xp"] = f(inputs["x_prompt"][prompt_b, :TP])
    m["xs"] = f(inputs["x_sample"][seq0:seq0 + 16]).reshape(128, D)
    m["pp"] = f(inputs["p_prompt"][:NL, prompt_b, :TP])
    m["pps"] = f(inputs["p_sample"][:NL, seq0:seq0 + 16]).reshape(NL, 128, 256)
    m["sre"] = f(inputs["state_s5_re"][:NL, seq0:seq0 + 16]).reshape(NL, 512, 128)
    m["sim"] = f(inputs["state_s5_im"][:NL, seq0:seq0 + 16]).reshape(NL, 512, 128)
    m["sgla"] = f(inputs["state_gla"][:NL, seq0:seq0 + 16])
    m["cconv"] = f(inputs["cache_conv"][:NL, seq0:seq0 + 16]).reshape(NL, 480, 1024)
    return m


def kernel(**inputs):
    NL, NPT = 4, 4
    nc = _get_prog(NL, NPT)
    wshared = {n: np.ascontiguousarray(np.asarray(inputs[n], dtype=np.float32)[:NL]) for n in W_NAMES}
    in_maps = []
    for c in range(8):
        m = core_inputs(inputs, c, NL, NPT, c // 2, 16 * c)
        m.update(wshared)
        in_maps.append(m)
    res = run_bass_kernel_spmd(nc, in_maps, core_ids=list(range(8)))
    R = res.results
    y_p = np.stack([R[2 * b]["yp"] for b in range(4)]).reshape(4, 2048, D)
    y_s = np.concatenate([R[c]["ys"].reshape(16, 8, D) for c in range(8)], axis=0)
    s5re_p = np.stack([R[2 * b]["sre_p"].reshape(NL, 64, 64) for b in range(4)], axis=1)
    s5im_p = np.stack([R[2 * b]["sim_p"].reshape(NL, 64, 64) for b in range(4)], axis=1)
    gla_p = np.stack([R[2 * b]["gla_p"] for b in range(4)], axis=1)
    conv_p = np.stack([R[2 * b]["conv_p"] for b in range(4)], axis=1)
    s5re_s = np.concatenate([R[c]["sre_s"].reshape(NL, 16, 64, 64) for c in range(8)], axis=1)
    s5im_s = np.concatenate([R[c]["sim_s"].reshape(NL, 16, 64, 64) for c in range(8)], axis=1)
    gla_s = np.concatenate([R[c]["gla_s"] for c in range(8)], axis=1)
    conv_s = np.concatenate([R[c]["conv_s"].reshape(NL, 16, 30, 1024) for c in range(8)], axis=1)
    outs = (y_p, y_s, s5re_p, s5im_p, gla_p, conv_p, s5re_s, s5im_s, gla_s, conv_s)
    return tuple(np.ascontiguousarray(o, dtype=np.float32) for o in outs)
```

```python
import math
from contextlib import ExitStack

import numpy as np
import concourse.bass as bass
import concourse.mybir as mybir
from concourse.bass_utils import run_bass_kernel_spmd

F32 = mybir.dt.float32
BF16 = mybir.dt.bfloat16
AF = mybir.ActivationFunctionType
ALU = mybir.AluOpType

ENGS = ["pe", "act", "dve", "pool", "sp"]
SAME_ENGINE_SYNC = True


class Buf:
    __slots__ = ("writers", "readers")

    def __init__(self):
        self.writers = {}
        self.readers = {}


class Op:
    __slots__ = ("id", "eng", "fn", "deps", "dma", "inc", "count", "semi", "target", "nd")

    def __init__(self, id, eng, fn, deps, dma, nd):
        self.id = id; self.eng = eng; self.fn = fn; self.deps = deps; self.dma = dma
        self.inc = False; self.count = 0; self.semi = -1; self.target = 0; self.nd = nd


class Prog:
    def __init__(self, n_dma_sems=40):
        self.ops = []
        self.n_dma_sems = n_dma_sems

    def add(self, eng, fn, r=(), w=(), dma=False, nd=1, extra=()):
        deps = set(extra)
        for b in r:
            deps.update(b.writers.values())
        for b in w:
            deps.update(b.writers.values())
            deps.update(b.readers.values())
        oid = len(self.ops)
        self.ops.append(Op(oid, eng, fn, deps, dma, nd))
        key = ("d", oid) if dma else eng
        for b in r:
            b.readers[key] = oid
        for b in w:
            if b.readers:
                b.writers = {key: oid}
                b.readers = {}
            else:
                b.writers[key] = oid
        return oid

    def emit(self, block, sems, dma_sems):
        ops = self.ops

        def skip(dop, op):
            return (not dop.dma) and (not op.dma) and dop.eng == op.eng and (op.eng == "pe" or not SAME_ENGINE_SYNC)

        for op in ops:
            for d in op.deps:
                dop = ops[d]
                if dop.dma or skip(dop, op):
                    continue
                dop.inc = True
        cnt = {e: 0 for e in ENGS}
        dtarget = [0] * self.n_dma_sems
        dlast = [None] * self.n_dma_sems
        half = self.n_dma_sems // 2
        rrs = {"pool": 0, "sp": 0}
        for op in ops:
            if op.dma:
                base = 0 if op.eng == "pool" else half
                rr = base + rrs[op.eng]
                rrs[op.eng] = (rrs[op.eng] + 1) % half
                op.semi = rr
                if dlast[rr] is not None:
                    op.deps.add(dlast[rr])
                dtarget[rr] += 16 * op.nd
                op.target = dtarget[rr]
                dlast[rr] = op.id
            elif op.inc:
                cnt[op.eng] += 1
                op.count = cnt[op.eng]
        per_eng = {e: [] for e in ENGS}
        for op in ops:
            per_eng[op.eng].append(op)
        self.counts = cnt

        def run(ename, eh):
            waited = {}
            for op in per_eng[ename]:
                for d in sorted(op.deps):
                    dop = ops[d]
                    if dop.dma:
                        s = dma_sems[dop.semi]; v = dop.target; k = ("d", dop.semi)
                    else:
                        if skip(dop, op):
                            continue
                        s = sems[dop.eng]; v = dop.count; k = dop.eng
                    if waited.get(k, 0) >= v:
                        continue
                    waited[k] = v
                    eh.wait_ge(s, v)
                ins = op.fn(eh)
                if ins is None:
                    continue
                if op.dma:
                    for i in ins:
                        i.then_inc(dma_sems[op.semi], 16)
                elif op.inc:
                    ins.then_inc(sems[ename], 1)

        @block.tensor
        def _(e):
            run("pe", e)

        @block.scalar
        def _(e):
            run("act", e)

        @block.vector
        def _(e):
            run("dve", e)

        @block.gpsimd
        def _(e):
            run("pool", e)

        @block.sync
        def _(e):
            run("sp", e)


class T:
    __slots__ = ("ap", "bufs", "slabs")

    def __init__(self, ap, bufs, slabs=None):
        self.ap = ap; self.bufs = bufs; self.slabs = slabs


def B(*ts):
    out = []
    for t in ts:
        if t is None:
            continue
        out.extend(t.bufs)
    return out


D = 2048
NIN = 14352
OFF = dict(u=0, z=1024, q=2048, k=2560, v=3072, al=4096, gz=4112, ca=5136, cb=6160, cz=7184, g=8208)
DN_ALPHA = (2 * 4) ** 0.25
LN_EPS = 1e-5
TWO_PI = 2.0 * math.pi
MAGIC = 12582912.0
GELU_C = 1.5957691216057308
SLAB = 2048
NSLAB = 46
WSLOT_ELEMS = 4096
NWSLOT = 4
SCAN_ENG = "pool"
PRECONVERT = False

W_NAMES = ["w_in", "b_in", "s5_a_re", "s5_a_im", "s5_log_dt", "s5_b_re", "s5_b_im", "s5_c_re", "s5_c_im", "s5_d",
           "w_glu", "b_glu", "gla_w_a2", "gla_b_a", "gla_norm_g", "conv_w", "conv_b", "conv_ln_g", "conv_ln_b",
           "p_s5", "p_gla", "p_conv", "w_o", "w_pg", "w_pe", "ln_g", "ln_b"]


class TileCfg:
    def __init__(self, kind, idx, TT, tok0, first, last):
        self.kind = kind; self.idx = idx; self.TT = TT; self.tok0 = tok0; self.first = first; self.last = last
        self.NB = TT // 4
        if kind == "p":
            self.NSEG = 1; self.SEGB = self.NB; self.NCH = TT // 64; self.NE = self.NCH; self.EL = 64
        else:
            self.NSEG = 16; self.SEGB = 2; self.NCH = 2; self.NE = 16; self.EL = 8


class Builder:
    def __init__(self, NL, NPT, debug=None):
        self.NL = NL; self.NPT = NPT; self.TP = NPT * 512
        self.debug = debug or {}
        self.nc = bass.Bass("TRN2", target_bir_lowering=False)
        self.P = Prog()
        self.st = ExitStack()
        self.store_ops = []
        self.bank_i = 0
        self.bank_alloc = [None] * 8
        self.wslot_i = 0
        self.uid = 0

    def sb(self, shape, dt=F32, name=None):
        self.uid += 1
        h = self.st.enter_context(self.nc.sbuf_tensor(name or ("t%d" % self.uid), shape, dt))
        return h

    def pt(self, shape, dt=F32):
        h = self.sb(shape, dt)
        return T(h, [Buf()])

    def dram_in(self, name, shape, dt=F32):
        return self.nc.dram_tensor(name, list(shape), dt, kind="ExternalInput").ap()

    def dram_out(self, name, shape, dt=F32):
        return self.nc.dram_tensor(name, list(shape), dt, kind="ExternalOutput").ap()

    def salloc(self, nelem, dt):
        esz = 4 if dt == F32 else 2
        nb = nelem * esz
        ns = (nb + SLAB - 1) // SLAB
        free = self.slab_free
        for s0 in range(0, NSLAB - ns + 1):
            if all(free[s0:s0 + ns]):
                for i in range(s0, s0 + ns):
                    free[i] = False
                if dt == F32:
                    ap = self.scrF[:, s0 * (SLAB // 4): s0 * (SLAB // 4) + nelem]
                else:
                    ap = self.scrB[:, s0 * (SLAB // 2): s0 * (SLAB // 2) + nelem]
                return T(ap, self.slab_bufs[s0:s0 + ns], (s0, ns))
        raise RuntimeError("scratch exhausted: need %d slabs, free map %s" % (ns, "".join("1" if f else "0" for f in free)))

    def sfree(self, *ts):
        for t in ts:
            s0, ns = t.slabs
            for i in range(s0, s0 + ns):
                assert not self.slab_free[i]
                self.slab_free[i] = True

    def bank(self):
        for k in range(8):
            i = (self.bank_i + k) % 8
            b = self.banks[i].bufs[0]
            a = self.bank_alloc[i]
            free = a is None or (b.writers and max(b.writers.values()) >= a and len(b.readers) > 0)
            if free:
                self.bank_alloc[i] = len(self.P.ops)
                self.bank_i = (i + 1) % 8
                return self.banks[i]
        raise RuntimeError("no free PSUM bank")

    def mm(self, out, lhsT, rhs, start, stop, r, w):
        self.P.add("pe", lambda e: e.matmul(out, lhsT=lhsT, rhs=rhs, start=start, stop=stop), r=r, w=w)

    def tr(self, out, in_, ident, r, w):
        self.P.add("pe", lambda e: e.transpose(out, in_, ident), r=r, w=w)

    def act(self, out, in_, func, r, w, bias=None, scale=None):
        kw = {}
        if bias is not None:
            kw["bias"] = bias
        if scale is not None:
            kw["scale"] = scale
        self.P.add("act", lambda e: e.activation(out=out, in_=in_, func=func, **kw), r=r, w=w)

    def tt(self, out, in0, in1, op, r, w, eng="dve"):
        self.P.add(eng, lambda e: e.tensor_tensor(out=out, in0=in0, in1=in1, op=op), r=r, w=w)

    def ts(self, out, in0, s1, s2, op0, op1, r, w, eng="dve"):
        if s2 is None:
            self.P.add(eng, lambda e: e.tensor_scalar(out=out, in0=in0, scalar1=s1, scalar2=None, op0=op0), r=r, w=w)
        else:
            self.P.add(eng, lambda e: e.tensor_scalar(out=out, in0=in0, scalar1=s1, scalar2=s2, op0=op0, op1=op1), r=r, w=w)

    def stt(self, out, in0, scalar, in1, op0, op1, r, w, eng="dve"):
        self.P.add(eng, lambda e: e.scalar_tensor_tensor(out=out, in0=in0, scalar=scalar, in1=in1, op0=op0, op1=op1), r=r, w=w)

    def cp(self, out, in_, r, w, eng="dve"):
        if eng == "act":
            self.act(out, in_, AF.Identity, r, w)
        else:
            self.P.add(eng, lambda e: e.tensor_copy(out=out, in_=in_), r=r, w=w)

    def memset(self, ap, val, w, eng="dve"):
        self.P.add(eng, lambda e: e.memset(ap, val), w=w)

    def recip(self, out, in_, r, w):
        self.P.add("dve", lambda e: e.reciprocal(out=out, in_=in_), r=r, w=w)

    def scan(self, out, d0, d1, r, w):
        self.P.add("dve", lambda e: e.tensor_tensor_scan(out=out, data0=d0, data1=d1, initial=0.0, op0=ALU.mult, op1=ALU.add), r=r, w=w)

    def asel(self, ap, pattern, op, base, cm, w):
        self.P.add("pool", lambda e: e.affine_select(out=ap, in_=ap, pattern=pattern, compare_op=op, fill=0.0, base=base, channel_multiplier=cm), r=w, w=w)

    def dma(self, eng, out, in_, r, w, nc_ok=False, store=False):
        nc = self.nc
        if nc_ok:
            def fn(e):
                with nc.allow_non_contiguous_dma(reason="small strided parameter / layout load"):
                    return [e.dma_start(out=out, in_=in_)]
        else:
            def fn(e):
                return [e.dma_start(out=out, in_=in_)]
        if store and eng == "sp":
            eng = "pool"
        oid = self.P.add(eng, fn, r=r, w=w, dma=True)
        if store:
            self.store_ops.append(oid)
        return oid

    def wload(self, src, KT, NC, rb=(), key=None):
        slot = self.wslots[self.wslot_i]
        self.wslot_i = (self.wslot_i + 1) % NWSLOT
        view = slot.ap[:, 0:KT * NC].rearrange("p (k n) -> p k n", n=NC)
        if key is None:
            self.dma("sp", view, src, r=list(rb), w=B(slot))
            return T(view, slot.bufs)
        Lk = key[1]
        if key not in self.wblk:
            bi = self.wblk_n[Lk]
            self.wblk_n[Lk] += 1
            assert bi < self.NBLK, "too many weight blocks"
            bb = Buf()
            self.wblk[key] = (bi, bb)
            dstv = self.wscr[Lk][bi, :, 0:KT * NC].rearrange("p (k n) -> p k n", n=NC)
            self.dma("pool", dstv, src, r=[], w=[bb])
        bi, bb = self.wblk[key]
        self.dma("sp", slot.ap[:, 0:KT * NC], self.wscr[Lk][bi, :, 0:KT * NC], r=[bb], w=B(slot))
        return T(view, slot.bufs)

    def build(self):
        nc = self.nc; NL = self.NL; TP = self.TP
        I = {}
        I["xp"] = self.dram_in("xp", [TP, D]); I["xs"] = self.dram_in("xs", [128, D])
        I["pp"] = self.dram_in("pp", [NL, TP, 256]); I["pps"] = self.dram_in("pps", [NL, 128, 256])
        I["sre"] = self.dram_in("sre", [NL, 512, 128]); I["sim"] = self.dram_in("sim", [NL, 512, 128])
        I["sgla"] = self.dram_in("sgla", [NL, 16, 4, 128, 256]); I["cconv"] = self.dram_in("cconv", [NL, 480, 1024])
        shapes = dict(w_in=[NL, D, NIN], b_in=[NL, NIN], s5_a_re=[NL, 64, 64], s5_a_im=[NL, 64, 64], s5_log_dt=[NL, 64],
                      s5_b_re=[NL, 64, 64, 16], s5_b_im=[NL, 64, 64, 16], s5_c_re=[NL, 64, 16, 64], s5_c_im=[NL, 64, 16, 64],
                      s5_d=[NL, 1024], w_glu=[NL, 1024, 1024], b_glu=[NL, 1024], gla_w_a2=[NL, 16, 512], gla_b_a=[NL, 512],
                      gla_norm_g=[NL, 1024], conv_w=[NL, 31, 1024], conv_b=[NL, 1024], conv_ln_g=[NL, 1024], conv_ln_b=[NL, 1024],
                      p_s5=[NL, 1024, D], p_gla=[NL, 1024, D], p_conv=[NL, 1024, D], w_o=[NL, D, D], w_pg=[NL, D, D],
                      w_pe=[NL, 256, D], ln_g=[NL, D], ln_b=[NL, D])
        for n in W_NAMES:
            I[n] = self.dram_in(n, shapes[n])
        O = {}
        O["yp"] = self.dram_out("yp", [TP, D]); O["ys"] = self.dram_out("ys", [128, D])
        O["sre_p"] = self.dram_out("sre_p", [NL, 32, 128]); O["sim_p"] = self.dram_out("sim_p", [NL, 32, 128])
        O["gla_p"] = self.dram_out("gla_p", [NL, 4, 128, 256]); O["conv_p"] = self.dram_out("conv_p", [NL, 30, 1024])
        O["sre_s"] = self.dram_out("sre_s", [NL, 512, 128]); O["sim_s"] = self.dram_out("sim_s", [NL, 512, 128])
        O["gla_s"] = self.dram_out("gla_s", [NL, 16, 4, 128, 256]); O["conv_s"] = self.dram_out("conv_s", [NL, 480, 1024])
        self.I = I; self.O = O
        ntiles = self.NPT + 1
        self.s5w = nc.dram_tensor("s5w_scr", [NL, 5, 128, 4096], BF16, kind="Internal").ap()
        self.spill = nc.dram_tensor("x_spill", [2, ntiles, 128, 16 * 512], BF16, kind="Internal").ap()
        self.s5w_buf = [[Buf() for _ in range(5)] for _ in range(NL)]
        self.NBLK = 90
        self.wscr = [nc.dram_tensor("w_bf16_scr%d" % l, [self.NBLK, 128, WSLOT_ELEMS], BF16, kind="Internal").ap() for l in range(NL)]
        self.wblk = {}
        self.wblk_n = [0] * NL
        self.spill_buf = [[Buf() for _ in range(ntiles)] for _ in range(2)]

        st = self.st
        with st:
            self.sems = {e: st.enter_context(nc.semaphore("s_" + e)) for e in ENGS}
            self.dsems = [st.enter_context(nc.semaphore("d%d" % i)) for i in range(self.P.n_dma_sems)]
            self.banks = []
            for i in range(8):
                h = st.enter_context(nc.psum_tensor("bank%d" % i, [128, 512], F32))
                t = T(h, [Buf()])
                self.banks.append(t)
            scr = self.sb([128, NSLAB * SLAB // 2], BF16, "scratch")
            self.scrB = scr
            self.scrF = scr.bitcast(F32)
            self.slab_bufs = [Buf() for _ in range(NSLAB)]
            self.slab_free = [True] * NSLAB
            self.wslots = [self.pt([128, WSLOT_ELEMS], BF16) for _ in range(NWSLOT)]
            self.xT = self.pt([128, 16, 512], BF16)
            self.pT = self.pt([128, 2, 512], BF16)
            self.ys5g = self.pt([128, 8, 512], BF16)
            self.ogg = self.pt([128, 8, 512], BF16)
            self.ycg = self.pt([128, 8, 512], BF16)
            self.Sst = self.pt([128, 4, 256], F32)
            self.Sbf = self.pt([128, 4, 256], BF16)
            self.halo = self.pt([128, 8, 30], F32)
            self.carry = self.pt([128, 2, 32], F32)
            self.T4 = self.pt([128, NL, 2, 2, 32], F32)
            self.consts()
            self.params_alloc()
            for L in range(NL):
                self.s5_prep(L)
            tiles = [TileCfg("p", i, 512, 512 * i, i == 0, i == self.NPT - 1) for i in range(self.NPT)]
            tiles.append(TileCfg("s", self.NPT, 128, 0, True, True))
            for L in range(NL):
                self.params_load(L)
                for tc in tiles:
                    self.tile_layer(L, tc)
            self.P.add("sp", lambda e: None, extra=list(self.store_ops))
            assert all(self.slab_free), "scratch leak"
            with nc.Block() as block:
                self.P.emit(block, self.sems, self.dsems)
        return nc

    def consts(self):
        self.identF = self.pt([128, 128], F32)
        self.identB = self.pt([128, 128], BF16)
        for t in (self.identF, self.identB):
            self.memset(t.ap[:], 1.0, B(t), eng="pool")
            self.asel(t.ap[:], [[-1, 128]], ALU.is_equal, 0, 1, B(t))
        self.permI = self.pt([128, 128], F32)
        self.cp(self.permI.ap[:].rearrange("p (t g c) -> p t g c", t=4, g=2, c=16),
                self.identF.ap[:].rearrange("p (g t c) -> p t g c", t=4, g=2, c=16), B(self.identF), B(self.permI))
        self.ones = {}
        for n in (256, 1024, 2048):
            t = self.pt([128, 128], F32)
            self.memset(t.ap[:], 1.0 / n, B(t), eng="pool")
            self.ones[n] = t
        self.maskP = self.pt([64, 64], F32)
        self.memset(self.maskP.ap[:], 1.0, B(self.maskP), eng="pool")
        self.asel(self.maskP.ap[:], [[1, 64]], ALU.is_ge, 0, -1, B(self.maskP))
        self.maskS = self.pt([64, 8, 8], F32)
        self.memset(self.maskS.ap[:], 1.0, B(self.maskS), eng="pool")
        self.asel(self.maskS.ap[:], [[8, 8], [1, 8]], ALU.is_ge, 0, -1, B(self.maskS))
        self.asel(self.maskS.ap[:], [[-8, 8], [0, 8]], ALU.is_ge, 0, 1, B(self.maskS))
        self.rowm = self.pt([64, 8], F32)
        self.memset(self.rowm.ap[:], 1.0, B(self.rowm), eng="pool")
        self.asel(self.rowm.ap[:], [[-8, 8]], ALU.is_ge, 0, 1, B(self.rowm))
        self.asel(self.rowm.ap[:], [[8, 8]], ALU.is_ge, 7, -1, B(self.rowm))
        self.TMa = self.pt([128, 4, 16], F32)
        self.TMb = self.pt([128, 4, 16], F32)
        self.memset(self.TMa.ap[:], 1.0, B(self.TMa), eng="pool")
        self.asel(self.TMa.ap[:], [[16, 4], [0, 16]], ALU.is_ge, 15, -1, B(self.TMa))
        self.memset(self.TMb.ap[:], 1.0, B(self.TMb), eng="pool")
        self.asel(self.TMb.ap[:], [[16, 4], [0, 16]], ALU.is_ge, 79, -1, B(self.TMb))
        self.coefP = self.pt([128, 512], F32)
        self.memset(self.coefP.ap[:], 1.0, B(self.coefP), eng="pool")
        self.memset(self.coefP.ap[:, 0:512:64], 0.0, B(self.coefP), eng="pool")
        self.coefS = self.pt([128, 128], F32)
        self.memset(self.coefS.ap[:], 1.0, B(self.coefS), eng="pool")
        self.memset(self.coefS.ap[:, 0:128:8], 0.0, B(self.coefS), eng="pool")

    def params_alloc(self):
        self.bcol = self.pt([128, 96], F32)
        self.bal = self.pt([16, 1], F32)
        self.wa2 = self.pt([16, 512], BF16)
        self.bubc = self.pt([128, 1024], F32)
        self.bvbc = self.pt([128, 1024], F32)
        self.bglu = self.pt([128, 8], F32)
        self.nba = self.pt([128, 4], F32)
        self.normg = self.pt([128, 8], F32)
        self.convw = self.pt([128, 8, 31], F32)
        self.convb = self.pt([128, 8], F32)
        self.clng = self.pt([128, 8], F32)
        self.clnb = self.pt([128, 8], F32)
        self.lng = self.pt([128, 16], F32)
        self.lnb = self.pt([128, 16], F32)
        self.BC = dict(z=0, q=8, k=12, gz=16, ca=24, cb=32, cz=40, g=48)

    def params_load(self, L):
        I = self.I
        segs = [("z", 8), ("q", 4), ("k", 4), ("gz", 8), ("ca", 8), ("cb", 8), ("cz", 8), ("g", 48)]
        for nm, n in segs:
            c0 = self.BC[nm]
            src = I["b_in"][L, OFF[nm]:OFF[nm] + 128 * n].rearrange("(n p) -> p n", p=128)
            self.dma("sp", self.bcol.ap[:, c0:c0 + n], src, [], B(self.bcol), nc_ok=True)
        self.dma("sp", self.bal.ap[:, :], I["b_in"][L, OFF["al"]:OFF["al"] + 16].rearrange("(p o) -> p o", o=1), [], B(self.bal), nc_ok=True)
        self.dma("pool", self.wa2.ap[:, :], I["gla_w_a2"][L, :, :], [], B(self.wa2))
        self.dma("sp", self.bubc.ap[:, :], I["b_in"][L:L + 1, 0:1024].to_broadcast([128, 1024]), [], B(self.bubc))
        self.dma("sp", self.bvbc.ap[:, :], I["b_in"][L:L + 1, OFF["v"]:OFF["v"] + 1024].to_broadcast([128, 1024]), [], B(self.bvbc))

        def col(dst, src1d, n):
            self.dma("sp", dst.ap[:, 0:n], src1d.rearrange("(n p) -> p n", p=128), [], B(dst), nc_ok=True)
        col(self.bglu, I["b_glu"][L, :], 8)
        col(self.nba, I["gla_b_a"][L, :], 4)
        self.ts(self.nba.ap[:, :], self.nba.ap[:, :], -1.0, None, ALU.mult, None, B(self.nba), B(self.nba))
        col(self.normg, I["gla_norm_g"][L, :], 8)
        col(self.convb, I["conv_b"][L, :], 8)
        col(self.clng, I["conv_ln_g"][L, :], 8)
        col(self.clnb, I["conv_ln_b"][L, :], 8)
        col(self.lng, I["ln_g"][L, :], 16)
        col(self.lnb, I["ln_b"][L, :], 16)
        for ct in range(8):
            self.dma("sp", self.convw.ap[:, ct, :], I["conv_w"][L, :, ct * 128:(ct + 1) * 128].rearrange("k p -> p k"), [], B(self.convw), nc_ok=True)

    def s5_prep(self, L):
        I = self.I
        f = lambda n: self.salloc(n, F32)
        are = f(32); aim = f(32); ldt = f(32); dK = f(32)
        Bre = f(512); Bim = f(512); Cre = f(512); Cim = f(512)
        for g in range(2):
            ps_ = slice(64 * g, 64 * g + 64)
            self.dma("sp", are.ap[ps_, :], I["s5_a_re"][L].rearrange("(j two) p -> two p j", two=2)[g], [], B(are), nc_ok=True)
            self.dma("sp", aim.ap[ps_, :], I["s5_a_im"][L].rearrange("(j two) p -> two p j", two=2)[g], [], B(aim), nc_ok=True)
            self.dma("sp", ldt.ap[ps_, :], I["s5_log_dt"][L].rearrange("(j two) -> two j", two=2)[g:g + 1, :].to_broadcast([64, 32]), [], B(ldt), nc_ok=True)
            self.dma("sp", Bre.ap[ps_, :].rearrange("p (j c) -> p j c", c=16), I["s5_b_re"][L].rearrange("(j two) p c -> two p j c", two=2)[g], [], B(Bre), nc_ok=True)
            self.dma("sp", Bim.ap[ps_, :].rearrange("p (j c) -> p j c", c=16), I["s5_b_im"][L].rearrange("(j two) p c -> two p j c", two=2)[g], [], B(Bim), nc_ok=True)
            for s in range(4):
                p0 = 64 * g + 16 * s
                self.dma("sp", dK.ap[p0:p0 + 16, :], I["s5_d"][L].rearrange("(j g c) -> g c j", g=2, c=16)[g], [], B(dK), nc_ok=True)
        for nm, Ct in (("s5_c_re", Cre), ("s5_c_im", Cim)):
            stg = f(2048)
            self.dma("sp", stg.ap[0:32, :], I[nm][L].rearrange("(j two) c p -> j (two c p)", two=2), [], B(stg))
            bk = self.bank()
            for g in range(2):
                for c in range(16):
                    self.mm(bk.ap[64 * g:64 * g + 64, c * 32:(c + 1) * 32], stg.ap[0:32, (g * 16 + c) * 64:(g * 16 + c + 1) * 64],
                            self.identF.ap[0:32, 0:32], True, True, B(stg, self.identF), B(bk))
            self.cp(Ct.ap.rearrange("p (j c) -> p j c", c=16), bk.ap[:, :].rearrange("p (c j) -> p j c", c=16, j=32), B(bk), B(Ct))
            self.sfree(stg)
        dt = f(32); ardt = f(32); aidt = f(32)
        self.act(dt.ap, ldt.ap, AF.Exp, B(ldt), B(dt))
        self.tt(ardt.ap, are.ap, dt.ap, ALU.mult, B(are, dt), B(ardt))
        self.tt(aidt.ap, aim.ap, dt.ap, ALU.mult, B(aim, dt), B(aidt))
        MAG = f(256); TSC = f(512); R1 = f(512); R2 = f(512); SC = f(512)
        for k in range(8):
            m = k - 3
            self.act(MAG.ap[:, k * 32:(k + 1) * 32], ardt.ap, AF.Exp, B(ardt), B(MAG), scale=float(m))
            self.ts(TSC.ap[:, k * 32:(k + 1) * 32], aidt.ap, float(m) / TWO_PI, None, ALU.mult, None, B(aidt), B(TSC))
        self.ts(TSC.ap[:, 256:512], TSC.ap[:, 0:256], 0.25, None, ALU.add, None, B(TSC), B(TSC))
        self.ts(R1.ap, TSC.ap, MAGIC, None, ALU.add, None, B(TSC), B(R1))
        self.ts(R2.ap, R1.ap, -MAGIC, None, ALU.add, None, B(R1), B(R2))
        self.tt(R1.ap, TSC.ap, R2.ap, ALU.subtract, B(TSC, R2), B(R1))
        self.act(SC.ap, R1.ap, AF.Sin, B(R1), B(SC), scale=TWO_PI)
        PWr = f(256); PWi = f(256)
        self.tt(PWr.ap, MAG.ap, SC.ap[:, 256:512], ALU.mult, B(MAG, SC), B(PWr))
        self.tt(PWi.ap, MAG.ap, SC.ap[:, 0:256], ALU.mult, B(MAG, SC), B(PWi))
        self.sfree(MAG, TSC, R1, R2, SC, dt, ardt, aidt, ldt)
        pw = lambda Tt, k: Tt.ap[:, k * 32:(k + 1) * 32]
        T4 = self.T4
        self.cp(T4.ap[:, L, 0, 0, :], pw(PWr, 7), B(PWr), B(T4))
        self.cp(T4.ap[:, L, 0, 1, :], pw(PWr, 7), B(PWr), B(T4))
        self.cp(T4.ap[:, L, 1, 0, :], pw(PWi, 7), B(PWi), B(T4))
        self.ts(T4.ap[:, L, 1, 1, :], pw(PWi, 7), -1.0, None, ALU.mult, None, B(PWi), B(T4))
        nr = f(32); t1 = f(32); t2 = f(32); den = f(32); Ere = f(32); Eim = f(32)
        self.ts(nr.ap, pw(PWr, 4), -1.0, None, ALU.add, None, B(PWr), B(nr))
        ni = pw(PWi, 4)
        self.tt(den.ap, are.ap, are.ap, ALU.mult, B(are), B(den))
        self.tt(t1.ap, aim.ap, aim.ap, ALU.mult, B(aim), B(t1))
        self.tt(den.ap, den.ap, t1.ap, ALU.add, B(den, t1), B(den))
        self.recip(den.ap, den.ap, B(den), B(den))
        self.tt(t1.ap, nr.ap, are.ap, ALU.mult, B(nr, are), B(t1))
        self.tt(t2.ap, ni, aim.ap, ALU.mult, B(PWi, aim), B(t2))
        self.tt(t1.ap, t1.ap, t2.ap, ALU.add, B(t1, t2), B(t1))
        self.tt(Ere.ap, t1.ap, den.ap, ALU.mult, B(t1, den), B(Ere))
        self.tt(t1.ap, ni, are.ap, ALU.mult, B(PWi, are), B(t1))
        self.tt(t2.ap, nr.ap, aim.ap, ALU.mult, B(nr, aim), B(t2))
        self.tt(t1.ap, t1.ap, t2.ap, ALU.subtract, B(t1, t2), B(t1))
        self.tt(Eim.ap, t1.ap, den.ap, ALU.mult, B(t1, den), B(Eim))
        self.sfree(nr, t2, den, are, aim)

        def v3(Tt):
            return Tt.ap.rearrange("p (j c) -> p j c", c=16)

        def bc(ap32):
            return ap32.rearrange("p (j o) -> p j o", o=1).to_broadcast([128, 32, 16])

        def cmul(outr, outi, ar, ai, br, bi, rb, tmpT, neg_im=False):
            tv = v3(tmpT)
            if outr is not None:
                self.tt(outr, bc(ar), br, ALU.mult, rb, rb)
                self.tt(tv, bc(ai), bi, ALU.mult, rb + B(tmpT), B(tmpT))
                self.tt(outr, outr, tv, ALU.subtract, rb + B(tmpT), rb)
            if outi is not None:
                self.tt(outi, bc(ar), bi, ALU.mult, rb, rb)
                self.tt(tv, bc(ai), br, ALU.mult, rb + B(tmpT), B(tmpT))
                self.tt(outi, outi, tv, ALU.add, rb + B(tmpT), rb)
                if neg_im:
                    self.ts(outi, outi, -1.0, None, ALU.mult, None, rb, rb)

        tmp = f(512)
        bbr = f(512); bbi = f(512)
        allb = B(PWr, PWi, Ere, Eim, Bre, Bim, Cre, Cim, bbr, bbi)
        cmul(v3(bbr), v3(bbi), Ere.ap, Eim.ap, v3(Bre), v3(Bim), allb, tmp)
        self.sfree(Ere, Eim, Bre, Bim, t1)
        Xr = f(2048); XiN = f(2048); Zr = f(2048); Zi = f(2048)
        v4 = lambda Tt: Tt.ap.rearrange("p (j s c) -> p j s c", s=4, c=16)
        rb = allb + B(Xr, XiN, Zr, Zi)
        for s in range(4):
            cmul(v4(Xr)[:, :, s, :], v4(XiN)[:, :, s, :], pw(PWr, 3 - s), pw(PWi, 3 - s), v3(bbr), v3(bbi), rb, tmp, neg_im=True)
            cmul(v4(Zr)[:, :, s, :], v4(Zi)[:, :, s, :], pw(PWr, 3 + s), pw(PWi, 3 + s), v3(Cre), v3(Cim), rb, tmp)
        Mf = f(4096)
        self.memset(Mf.ap, 0.0, B(Mf))
        Mf3 = Mf.ap.rearrange("p (j n) -> p j n", n=128)
        for j4 in range(8):
            bk = self.bank()
            for jj in range(4):
                j = j4 * 4 + jj
                for g in range(2):
                    pr = slice(64 * g, 64 * g + 64)
                    o = bk.ap[pr, jj * 128 + 64 * g: jj * 128 + 64 * g + 64]
                    self.mm(o, Xr.ap[pr, j * 64:(j + 1) * 64], Zr.ap[pr, j * 64:(j + 1) * 64], True, False, B(Xr, Zr), B(bk))
                    self.mm(o, XiN.ap[pr, j * 64:(j + 1) * 64], Zi.ap[pr, j * 64:(j + 1) * 64], False, True, B(XiN, Zi), B(bk))
            for g in range(2):
                pr = slice(64 * g, 64 * g + 64)
                TM = self.TMa if g == 0 else self.TMb
                outv = Mf3[pr, j4 * 4:(j4 + 1) * 4, :].rearrange("p j (t g c) -> p j t g c", t=4, g=2, c=16)[:, :, :, g, :]
                inv = bk.ap[pr, :].rearrange("p (j g t c) -> p j g t c", j=4, g=2, t=4, c=16)[:, :, g, :, :]
                mk = TM.ap[pr, :, :].rearrange("p (o t) c -> p o t c", o=1).to_broadcast([64, 4, 4, 16])
                self.tt(outv, inv, mk, ALU.mult, B(bk, TM), B(Mf))
        for j in range(32):
            self.stt(Mf3[:, j, :], self.permI.ap[:, :], dK.ap[:, j:j + 1], Mf3[:, j, :], ALU.mult, ALU.add, B(self.permI, dK, Mf), B(Mf))
        Mb = self.salloc(4096, BF16)
        self.cp(Mb.ap, Mf.ap, B(Mf), B(Mb), eng="act")
        self.dma("sp", self.s5w[L, 0], Mb.ap, B(Mb), [self.s5w_buf[L][0]])
        self.sfree(Xr, XiN, Zr, Zi, Mf, Mb, dK)
        Wn_r = f(2048); Wn_i = f(2048)
        rb = allb + B(Wn_r, Wn_i)
        for s in range(4):
            cmul(v4(Wn_r)[:, :, s, :], v4(Wn_i)[:, :, s, :], pw(PWr, 6 - s), pw(PWi, 6 - s), v3(bbr), v3(bbi), rb, tmp)
        for plane, Wn in enumerate((Wn_r, Wn_i)):
            VS = f(4096)
            self.memset(VS.ap, 0.0, B(VS))
            VS3 = VS.ap.rearrange("p (j n) -> p j n", n=128)
            Wn3 = Wn.ap.rearrange("p (j n) -> p j n", n=64)
            for g in range(2):
                pr = slice(64 * g, 64 * g + 64)
                self.cp(VS3[pr, :, 64 * g:64 * g + 64], Wn3[pr, :, :], B(Wn), B(VS))
            WSt = self.salloc(4096, BF16)
            for j4 in range(8):
                bk = self.bank()
                for jj in range(4):
                    j = j4 * 4 + jj
                    self.tr(bk.ap[:, jj * 128:(jj + 1) * 128], VS3[:, j, :], self.identF.ap[:, :], B(VS, self.identF), B(bk))
                self.cp(WSt.ap[:, j4 * 512:(j4 + 1) * 512], bk.ap[:, :], B(bk), B(WSt), eng="act")
            self.dma("sp", self.s5w[L, 1 + plane], WSt.ap, B(WSt), [self.s5w_buf[L][1 + plane]])
            self.sfree(VS, WSt)
        rb = allb + B(Wn_r, Wn_i)
        for t in range(4):
            cmul(v4(Wn_r)[:, :, t, :], v4(Wn_i)[:, :, t, :], pw(PWr, 4 + t), pw(PWi, 4 + t), v3(Cre), v3(Cim), rb, tmp, neg_im=True)
        for plane, Wn in enumerate((Wn_r, Wn_i)):
            WYb = self.salloc(4096, BF16)
            self.memset(WYb.ap, 0.0, B(WYb))
            for g in range(2):
                pr = slice(64 * g, 64 * g + 64)
                outv = WYb.ap[pr, :].rearrange("p (j t g c) -> p j t g c", t=4, g=2, c=16)[:, :, :, g, :]
                inv = Wn.ap[pr, :].rearrange("p (j t c) -> p j t c", t=4, c=16)
                self.cp(outv, inv, B(Wn), B(WYb))
            self.dma("sp", self.s5w[L, 3 + plane], WYb.ap, B(WYb), [self.s5w_buf[L][3 + plane]])
            self.sfree(WYb)
        self.sfree(Wn_r, Wn_i, tmp, bbr, bbi, Cre, Cim, PWr, PWi)

    def win(self, L, col0, ncols, KT=16):
        return self.I["w_in"][L, :, col0:col0 + ncols].rearrange("(k p) n -> p k n", p=128)

    def wmat(self, name, L, col0, ncols):
        return self.I[name][L, :, col0:col0 + ncols].rearrange("(k p) n -> p k n", p=128)

    def weight_recipes(self):
        r = []
        for cs in range(4):
            r.append(("w_in", OFF["u"] + 256 * cs, 256, 16))
        for cs in range(4):
            r.append(("w_in", OFF["ca"] + 256 * cs, 256, 16)); r.append(("w_in", OFF["cb"] + 256 * cs, 256, 16))
        for cs in range(4):
            r.append(("w_in", OFF["cz"] + 256 * cs, 256, 16))
        for cs in range(4):
            r.append(("w_in", OFF["z"] + 256 * cs, 256, 16))
        for hs in range(2):
            r.append(("w_glu", 512 * hs, 512, 8))
        r.append(("w_in", OFF["al"], 16, 16))
        for nm in ("q", "k"):
            for cs in range(2):
                r.append(("w_in", OFF[nm] + 256 * cs, 256, 16))
        for nm in ("v", "gz"):
            for cs in range(4):
                r.append(("w_in", OFF[nm] + 256 * cs, 256, 16))
        for jo4 in range(4):
            for bi, wn in enumerate(("p_s5", "p_gla", "p_conv")):
                r.append((wn, 512 * jo4, 512, 8))
                for half in range(2):
                    r.append(("w_in", OFF["g"] + bi * 2048 + jo4 * 512 + half * 256, 256, 16))
        for cs in range(8):
            r.append(("w_o", 256 * cs, 256, 16))
        r.append(("w_pe", 0, 2048, 2))
        for cs in range(8):
            r.append(("w_pg", 256 * cs, 256, 16))
        return r

    def preconvert(self, L, lo=0, hi=None):
        rec = self.weight_recipes()
        hi = len(rec) if hi is None else min(hi, len(rec))
        for (name, col0, ncols, KT) in rec[lo:hi]:
            key = (name, L, col0, ncols)
            if key in self.wblk:
                continue
            src = self.I[name][L, :, col0:col0 + ncols].rearrange("(k p) n -> p k n", p=128)
            bi = self.wblk_n[L]; self.wblk_n[L] += 1
            assert bi < self.NBLK
            bb = Buf()
            self.wblk[key] = (bi, bb)
            dstv = self.wscr[L][bi, :, 0:KT * ncols].rearrange("p (k n) -> p k n", n=ncols)
            self.dma("pool", dstv, src, r=[], w=[bb])

    def wl_in(self, L, col0, ncols):
        return self.wload(self.win(L, col0, ncols), 16, ncols, key=("w_in", L, col0, ncols))

    def wl_mat(self, name, L, col0, ncols, KT):
        return self.wload(self.wmat(name, L, col0, ncols), KT, ncols, key=(name, L, col0, ncols))

    def proj_fm(self, ws, jj, KT, rhs_fn, rhsT, TT):
        bk = self.bank()
        for kt in range(KT):
            self.mm(bk.ap[:, 0:TT], ws.ap[:, kt, jj * 128:(jj + 1) * 128], rhs_fn(kt), kt == 0, kt == KT - 1, B(ws, rhsT), B(bk))
        return bk

    def tile_layer(self, L, tc):
        if PRECONVERT and tc.kind == "p" and tc.first:
            self.preconvert(L)
        self.load_x(L, tc)
        self.s5_phase(L, tc)
        self.conv_phase(L, tc)
        self.dbg_dump("dbg_conv", self.ycg, L, tc)
        self.s5_back(L, tc)
        self.dbg_dump("dbg_s5", self.ys5g, L, tc)
        self.gla_phase(L, tc)
        self.dbg_dump("dbg_gla", self.ogg, L, tc)
        self.merge_phase(L, tc)

    def dbg_dump(self, name, t, L, tc):
        if not self.debug or L != 0:
            return
        key = name + "_" + tc.kind + str(tc.idx)
        o = self.dram_out(key, [128, 8, 512], BF16)
        self.dma("sp", o[:, :, :], t.ap[:, :, :], B(t), [], store=True)

    def xrhs(self, tc):
        xT = self.xT
        return lambda kt: xT.ap[:, kt, 0:tc.TT]

    def load_x(self, L, tc):
        I = self.I; TT = tc.TT
        ngr = TT // 128
        if L == 0:
            src = I["xp"] if tc.kind == "p" else I["xs"]
            for tg in range(ngr):
                stg = self.salloc(2048, F32)
                r0 = tc.tok0 + tg * 128
                self.dma("pool", stg.ap, src[r0:r0 + 128, :], [], B(stg))
                for k4 in range(4):
                    bk = self.bank()
                    for kk in range(4):
                        kt = k4 * 4 + kk
                        self.tr(bk.ap[:, kk * 128:(kk + 1) * 128], stg.ap[:, kt * 128:(kt + 1) * 128], self.identF.ap[:, :], B(stg, self.identF), B(bk))
                    outv = self.xT.ap[:, k4 * 4:(k4 + 1) * 4, tg * 128:(tg + 1) * 128]
                    self.cp(outv, bk.ap[:, :].rearrange("p (k t) -> p k t", t=128), B(bk), B(self.xT), eng=("act" if k4 % 2 else "dve"))
                self.sfree(stg)
        else:
            par = (L - 1) % 2
            src = self.spill[par, tc.idx].rearrange("p (k t) -> p k t", t=512)[:, :, 0:TT]
            self.dma("pool", self.xT.ap[:, :, 0:TT], src, [self.spill_buf[par][tc.idx]], B(self.xT))
        psrc = I["pp"][L] if tc.kind == "p" else I["pps"][L]
        for tg in range(ngr):
            stg = self.salloc(256, F32)
            r0 = tc.tok0 + tg * 128
            self.dma("pool", stg.ap, psrc[r0:r0 + 128, :], [], B(stg))
            bk = self.bank()
            for kk in range(2):
                self.tr(bk.ap[:, kk * 128:(kk + 1) * 128], stg.ap[:, kk * 128:(kk + 1) * 128], self.identF.ap[:, :], B(stg, self.identF), B(bk))
            self.cp(self.pT.ap[:, :, tg * 128:(tg + 1) * 128], bk.ap[:, 0:256].rearrange("p (k t) -> p k t", t=128), B(bk), B(self.pT), eng="act")
            self.sfree(stg)

    def s5_phase(self, L, tc):
        I = self.I; O = self.O
        TT, NB, NSEG, SEGB = tc.TT, tc.NB, tc.NSEG, tc.SEGB
        xT = self.xT
        Dt = self.salloc(4096, BF16)
        D5 = Dt.ap.rearrange("p (j g s c) -> p j g s c", j=32, g=2, s=4, c=16)
        for cs in range(4):
            ws = self.wl_in(L, OFF["u"] + 256 * cs, 256)
            for s in range(4):
                bk = self.bank()
                for kt in range(16):
                    self.mm(bk.ap[0:NB, 0:256], xT.ap[:, kt, s:TT:4], ws.ap[:, kt, :], kt == 0, kt == 15, B(xT, ws), B(bk))
                outv = D5[0:NB, 8 * cs:8 * cs + 8, :, s, :]
                inv = bk.ap[0:NB, 0:256].rearrange("p (j g c) -> p j g c", j=8, g=2, c=16)
                bv = self.bubc.ap[0:NB, 256 * cs:256 * cs + 256].rearrange("p (j g c) -> p j g c", j=8, g=2, c=16)
                self.tt(outv, inv, bv, ALU.add, B(bk, self.bubc), B(Dt))
        U2 = self.salloc(32 * NB, BF16)
        U23 = U2.ap.rearrange("p (j n) -> p j n", n=NB)
        per = 1024 // NB
        j = 0
        while j < 32:
            bk = self.bank()
            bkb = bk.ap.bitcast(BF16)
            nj = min(per, 32 - j)
            for jj in range(nj):
                self.tr(bkb[:, jj * NB:(jj + 1) * NB], Dt.ap[0:NB, (j + jj) * 128:(j + jj + 1) * 128], self.identB.ap[0:NB, 0:NB], B(Dt, self.identB), B(bk))
            self.cp(U2.ap[:, j * NB:(j + nj) * NB], bkb[:, 0:nj * NB], B(bk), B(U2), eng="act")
            j += nj
        self.sfree(Dt)
        wSr = self.wload_s5(L, 1)
        wSi = self.wload_s5(L, 2)
        HW = NSEG * (SEGB + 1)
        H = self.salloc(2 * 32 * HW, F32)
        H5 = H.ap.rearrange("p (a j q b) -> p a j q b", a=2, j=32, q=NSEG, b=SEGB + 1)
        if tc.kind == "p":
            if tc.first:
                self.memset(self.carry.ap[:], 0.0, B(self.carry))
            self.cp(H5[:, :, :, 0, 0], self.carry.ap[:, :, :], B(self.carry), B(H))
        else:
            for plane, nm in enumerate(("sre", "sim")):
                for r in range(4):
                    stg = self.salloc(128, F32)
                    self.dma("pool", stg.ap, I[nm][L, r * 128:(r + 1) * 128, :], [], B(stg))
                    bk = self.bank()
                    self.tr(bk.ap[:, 0:128], stg.ap, self.identF.ap[:, :], B(stg, self.identF), B(bk))
                    outv = H5[:, plane, :, 4 * r:4 * r + 4, 0]
                    inv = bk.ap[:, 0:128].rearrange("p (q j) -> p j q", q=4, j=32)
                    self.cp(outv, inv, B(bk), B(H))
                    self.sfree(stg)
        pairs_per_bank = 512 // NB
        for plane, wS in enumerate((wSr, wSi)):
            j = 0
            while j < 32:
                bk = self.bank()
                nj = min(pairs_per_bank, 32 - j)
                for jj in range(nj):
                    self.mm(bk.ap[:, jj * NB:(jj + 1) * NB], wS.ap[:, j + jj, :], U23[:, j + jj, :], True, True, B(wS, U2), B(bk))
                outv = H5[:, plane, j:j + nj, :, 1:SEGB + 1]
                inv = bk.ap[:, 0:nj * NB].rearrange("p (j q b) -> p j q b", j=nj, q=NSEG, b=SEGB)
                self.cp(outv, inv, B(bk), B(H), eng="act")
                j += nj
        u = self.salloc(2 * 2 * 32 * NSEG, F32)
        u5 = u.ap.rearrange("p (r a j q) -> p r a j q", r=2, a=2, j=32, q=NSEG)
        T4 = self.T4
        for b in range(SEGB):
            if NSEG == 1:
                hb = H5[:, :, :, 0, b].rearrange("p (o a) j -> p o a j", o=1).to_broadcast([128, 2, 2, 32])
                self.tt(u5[:, :, :, :, 0], hb, T4.ap[:, L, :, :, :], ALU.mult, B(H, T4), B(u), eng=SCAN_ENG)
            else:
                for rpt in range(2):
                    tb = T4.ap[:, L, rpt, :, :].rearrange("p a (j o) -> p a j o", o=1).to_broadcast([128, 2, 32, NSEG])
                    self.tt(u5[:, rpt], H5[:, :, :, :, b], tb, ALU.mult, B(H, T4), B(u), eng=SCAN_ENG)
            self.tt(H5[:, :, :, :, b + 1], H5[:, :, :, :, b + 1], u5[:, 0], ALU.add, B(H, u), B(H), eng=SCAN_ENG)
            self.tt(H5[:, :, :, :, b + 1], H5[:, :, :, :, b + 1], u5[:, 1, ::-1], ALU.add, B(H, u), B(H), eng=SCAN_ENG)
        self.s5_ctx = (U2, U23, H, H5, u)

    def s5_back(self, L, tc):
        I = self.I; O = self.O
        TT, NB, NSEG, SEGB = tc.TT, tc.NB, tc.NSEG, tc.SEGB
        xT = self.xT
        U2, U23, H, H5, u = self.s5_ctx
        zsT = self.salloc(8 * TT, BF16)
        zs3 = zsT.ap.rearrange("p (k t) -> p k t", t=TT)
        for cs in range(4):
            ws = self.wl_in(L, OFF["z"] + 256 * cs, 256)
            for jj in range(2):
                ct = cs * 2 + jj
                bk = self.proj_fm(ws, jj, 16, self.xrhs(tc), xT, TT)
                self.act(zs3[:, ct, :], bk.ap[:, 0:TT], AF.Silu, B(bk, self.bcol), B(zsT), bias=self.bcol.ap[:, self.BC["z"] + ct:self.BC["z"] + ct + 1])
        self.sfree(u)
        Hbf = self.salloc(2 * 32 * NB, BF16)
        Hb4 = Hbf.ap.rearrange("p (a j n) -> p a j n", a=2, j=32, n=NB)
        for plane in range(2):
            outv = Hbf.ap[:, plane * 32 * NB:(plane + 1) * 32 * NB].rearrange("p (j q b) -> p j q b", j=32, q=NSEG, b=SEGB)
            self.cp(outv, H5[:, plane, :, :, 0:SEGB], B(H), B(Hbf), eng=("act" if plane else "dve"))
        if tc.kind == "p":
            self.cp(self.carry.ap[:, :, :], H5[:, :, :, 0, SEGB], B(H), B(self.carry))
            if tc.last:
                for plane, nm in enumerate(("sre_p", "sim_p")):
                    bk = self.bank()
                    self.tr(bk.ap[0:32, 0:128], self.carry.ap[:, plane, :], self.identF.ap[:, :], B(self.carry, self.identF), B(bk))
                    stg = self.salloc(128, F32)
                    self.cp(stg.ap[0:32, :], bk.ap[0:32, 0:128], B(bk), B(stg))
                    self.dma("sp", O[nm][L, :, :], stg.ap[0:32, :], B(stg), [], store=True)
                    self.sfree(stg)
        else:
            for plane, nm in enumerate(("sre_s", "sim_s")):
                fin = self.salloc(512, F32)
                self.cp(fin.ap.rearrange("p (q j) -> p j q", q=16, j=32), H5[:, plane, :, :, SEGB], B(H), B(fin))
                for r in range(4):
                    bk = self.bank()
                    self.tr(bk.ap[:, 0:128], fin.ap[:, r * 128:(r + 1) * 128], self.identF.ap[:, :], B(fin, self.identF), B(bk))
                    stg = self.salloc(128, F32)
                    self.cp(stg.ap, bk.ap[:, 0:128], B(bk), B(stg), eng="act")
                    self.dma("sp", O[nm][L, r * 128:(r + 1) * 128, :], stg.ap, B(stg), [], store=True)
                    self.sfree(stg)
                self.sfree(fin)
        self.sfree(H)
        wM = self.wload_s5(L, 0)
        wYr = self.wload_s5(L, 3)
        wYi = self.wload_s5(L, 4)
        ys = self.salloc(4096, BF16)
        ys3 = ys.ap.rearrange("p (t c) -> p t c", t=4)
        for j4 in range(8):
            bk = self.bank()
            for jj in range(4):
                j = j4 * 4 + jj
                o = bk.ap[0:NB, jj * 128:(jj + 1) * 128]
                self.mm(o, U23[:, j, :], wM.ap[:, j, :], True, False, B(U2, wM), B(bk))
                self.mm(o, Hb4[:, 0, j, :], wYr.ap[:, j, :], False, False, B(Hbf, wYr), B(bk))
                self.mm(o, Hb4[:, 1, j, :], wYi.ap[:, j, :], False, True, B(Hbf, wYi), B(bk))
            yf = self.salloc(512, F32); tq = self.salloc(512, F32)
            self.cp(yf.ap[0:NB, :], bk.ap[0:NB, :], B(bk), B(yf), eng="act")
            self.act(tq.ap[0:NB, :], bk.ap[0:NB, :], AF.Square, B(bk), B(tq))
            self.ts(tq.ap[0:NB, :], tq.ap[0:NB, :], 0.044715, 1.0, ALU.mult, ALU.add, B(tq), B(tq))
            self.tt(tq.ap[0:NB, :], tq.ap[0:NB, :], yf.ap[0:NB, :], ALU.mult, B(tq, yf), B(tq))
            self.act(tq.ap[0:NB, :], tq.ap[0:NB, :], AF.Sigmoid, B(tq), B(tq), scale=GELU_C)
            outv = ys3[0:NB, :, j4 * 128:(j4 + 1) * 128].rearrange("p t (j c) -> p j t c", j=4, c=32)
            self.tt(outv, yf.ap[0:NB, :].rearrange("p (j t c) -> p j t c", j=4, t=4, c=32),
                    tq.ap[0:NB, :].rearrange("p (j t c) -> p j t c", j=4, t=4, c=32), ALU.mult, B(yf, tq), B(ys))
            self.sfree(yf, tq)
        self.sfree(U2, Hbf)
        ysT = self.salloc(8 * TT, BF16)
        ysT3 = ysT.ap.rearrange("p (k t) -> p k t", t=TT)
        per = 1024 // NB
        items = [(t, ct) for ct in range(8) for t in range(4)]
        i = 0
        while i < len(items):
            bk = self.bank(); bkb = bk.ap.bitcast(BF16)
            grp = items[i:i + per]
            for gi, (t, ct) in enumerate(grp):
                self.tr(bkb[:, gi * NB:(gi + 1) * NB], ys3[0:NB, t, ct * 128:(ct + 1) * 128], self.identB.ap[0:NB, 0:NB], B(ys, self.identB), B(bk))
            for gi, (t, ct) in enumerate(grp):
                self.cp(ysT3[:, ct, t:TT:4], bkb[:, gi * NB:(gi + 1) * NB], B(bk), B(ysT), eng=("act" if gi % 2 else "dve"))
            i += per
        self.sfree(ys)
        for hs in range(2):
            ws = self.wl_mat("w_glu", L, 512 * hs, 512, 8)
            for jj in range(4):
                o = hs * 4 + jj
                bk = self.proj_fm(ws, jj, 8, lambda kt: ysT3[:, kt, :], ysT, TT)
                sg = self.salloc(TT, F32)
                self.act(sg.ap, bk.ap[:, 0:TT], AF.Sigmoid, B(bk, self.bglu), B(sg), bias=self.bglu.ap[:, o:o + 1])
                self.tt(sg.ap, sg.ap, ysT3[:, o, :], ALU.mult, B(sg, ysT), B(sg))
                self.tt(self.ys5g.ap[:, o, 0:TT], sg.ap, zs3[:, o, :], ALU.mult, B(sg, zsT), B(self.ys5g))
                self.sfree(sg)
        self.sfree(ysT, zsT)

    def wload_s5(self, L, idx):
        return self.wload(self.s5w[L, idx].rearrange("p (k n) -> p k n", n=128), 32, 128, rb=[self.s5w_buf[L][idx]])

    def gla_phase(self, L, tc):
        I = self.I; O = self.O
        TT, NCH, NE, EL = tc.TT, tc.NCH, tc.NE, tc.EL
        xT = self.xT; xr = self.xrhs(tc)
        BCq, BCk, BCgz = self.BC["q"], self.BC["k"], self.BC["gz"]
        ws = self.wl_in(L, OFF["al"], 16)
        bk = self.bank()
        for kt in range(16):
            self.mm(bk.ap[0:16, 0:TT], ws.ap[:, kt, :], xr(kt), kt == 0, kt == 15, B(ws, xT), B(bk))
        alT = self.salloc(TT, BF16)
        self.act(alT.ap[0:16, :], bk.ap[0:16, 0:TT], AF.Identity, B(bk, self.bal), B(alT), bias=self.bal.ap[:, 0:1])
        la = self.salloc(4 * TT, F32); cs = self.salloc(4 * TT, F32)
        la3 = la.ap.rearrange("p (h t) -> p h t", t=TT); cs3 = cs.ap.rearrange("p (h t) -> p h t", t=TT)
        coef = self.coefP if tc.kind == "p" else self.coefS
        for h in range(4):
            bk = self.bank()
            self.mm(bk.ap[:, 0:TT], self.wa2.ap[0:16, h * 128:(h + 1) * 128], alT.ap[0:16, :], True, True, B(self.wa2, alT), B(bk))
            self.act(la3[:, h, :], bk.ap[:, 0:TT], AF.Exp, B(bk, self.nba), B(la), bias=self.nba.ap[:, h:h + 1], scale=-1.0)
            self.act(la3[:, h, :], la3[:, h, :], AF.Ln, B(la), B(la), bias=1.0)
            self.scan(cs3[:, h, :], coef.ap[:, 0:TT], la3[:, h, :], B(coef, la), B(cs))
        self.sfree(alT, la)
        ecs = self.salloc(4 * TT, F32); encs = self.salloc(4 * TT, F32); el = self.salloc(4 * NE, F32)
        ecs3 = ecs.ap.rearrange("p (h t) -> p h t", t=TT); encs3 = encs.ap.rearrange("p (h t) -> p h t", t=TT)
        el3 = el.ap.rearrange("p (h e) -> p h e", e=NE)
        self.act(ecs.ap, cs.ap, AF.Exp, B(cs), B(ecs), scale=-1.0 / 16.0, bias=float(math.log(128.0 ** -0.5)))
        self.act(encs.ap, cs.ap, AF.Exp, B(cs), B(encs), scale=1.0 / 16.0)
        self.act(el3, cs3[:, :, EL - 1:TT:EL], AF.Exp, B(cs), B(el), scale=-1.0 / 16.0)
        self.sfree(cs)
        qd = self.salloc(4 * TT, BF16); kd = self.salloc(4 * TT, BF16)
        qd3 = qd.ap.rearrange("p (h t) -> p h t", t=TT); kd3 = kd.ap.rearrange("p (h t) -> p h t", t=TT)
        for nm, dst3, dstT, sc3, scT, bc0 in (("q", qd3, qd, ecs3, ecs, BCq), ("k", kd3, kd, encs3, encs, BCk)):
            for cs_ in range(2):
                ws = self.wl_in(L, OFF[nm] + 256 * cs_, 256)
                for jj in range(2):
                    h = cs_ * 2 + jj
                    bk = self.proj_fm(ws, jj, 16, xr, xT, TT)
                    self.stt(dst3[:, h, :], bk.ap[:, 0:TT], self.bcol.ap[:, bc0 + h:bc0 + h + 1], sc3[:, h, :], ALU.add, ALU.mult, B(bk, self.bcol, scT), B(dstT))
        self.sfree(ecs, encs)
        kk = self.salloc(4 * TT, BF16)
        kk3 = kk.ap.rearrange("p (h t) -> p h t", t=TT)
        for h in range(4):
            outv = kk3[:, h, :].rearrange("p (e t) -> p e t", t=EL)
            inv = kd3[:, h, :].rearrange("p (e t) -> p e t", t=EL)
            ev = el3[:, h, :].rearrange("p (e o) -> p e o", o=1).to_broadcast([128, NE, EL])
            self.tt(outv, inv, ev, ALU.mult, B(kd, el), B(kk))
        kkT = self.salloc(NCH * 512, BF16)
        kkT4 = kkT.ap.rearrange("p (c h d) -> p c h d", h=4, d=128)
        for ch2 in range(0, NCH, 2):
            bk = self.bank(); bkb = bk.ap.bitcast(BF16)
            for cc in range(2):
                for h in range(4):
                    c = ch2 + cc
                    self.tr(bkb[0:64, (cc * 4 + h) * 128:(cc * 4 + h + 1) * 128], kk3[:, h, c * 64:(c + 1) * 64], self.identB.ap[:, :], B(kk, self.identB), B(bk))
            self.cp(kkT.ap[0:64, ch2 * 512:(ch2 + 2) * 512], bkb[0:64, 0:1024], B(bk), B(kkT), eng="act")
        self.sfree(kk)
        vt = self.salloc(NCH * 1024, BF16)
        vt3 = vt.ap.rearrange("p (c v) -> p c v", v=1024)
        for cs_ in range(4):
            ws = self.wl_in(L, OFF["v"] + 256 * cs_, 256)
            for c in range(NCH):
                bk = self.bank()
                for kt in range(16):
                    self.mm(bk.ap[0:64, 0:256], xT.ap[:, kt, c * 64:(c + 1) * 64], ws.ap[:, kt, :], kt == 0, kt == 15, B(xT, ws), B(bk))
                self.tt(vt3[0:64, c, cs_ * 256:(cs_ + 1) * 256], bk.ap[0:64, 0:256], self.bvbc.ap[0:64, cs_ * 256:(cs_ + 1) * 256], ALU.add, B(bk, self.bvbc), B(vt))
        gz = self.salloc(8 * TT, BF16)
        gz3 = gz.ap.rearrange("p (k t) -> p k t", t=TT)
        for cs_ in range(4):
            ws = self.wl_in(L, OFF["gz"] + 256 * cs_, 256)
            for jj in range(2):
                ct = cs_ * 2 + jj
                bk = self.proj_fm(ws, jj, 16, xr, xT, TT)
                self.act(gz3[:, ct, :], bk.ap[:, 0:TT], AF.Silu, B(bk, self.bcol), B(gz), bias=self.bcol.ap[:, BCgz + ct:BCgz + ct + 1])
        o = self.salloc(8 * TT, F32)
        o3 = o.ap.rearrange("p (k t) -> p k t", t=TT)
        attT = self.salloc(NCH * 256, BF16)
        attT4 = attT.ap.rearrange("p (c h t) -> p c h t", h=4, t=64)
        Sst, Sbf = self.Sst, self.Sbf
        if tc.kind == "p" and tc.first:
            self.memset(Sst.ap[:], 0.0, B(Sst))
            self.memset(Sbf.ap[:], 0.0, B(Sbf))
        for c in range(NCH):
            csl = slice(c * 64, (c + 1) * 64)
            bkA = self.bank()
            for h in range(4):
                self.mm(bkA.ap[0:64, h * 64:(h + 1) * 64], kd3[:, h, csl], qd3[:, h, csl], True, True, B(kd, qd), B(bkA))
            if tc.kind == "p":
                mk = self.maskP.ap[:, :].rearrange("p (o t) -> p o t", o=1).to_broadcast([64, 4, 64]); mkT = self.maskP
            else:
                mk = self.maskS.ap[:, :, :].rearrange("p a b -> p (a b)").rearrange("p (o t) -> p o t", o=1).to_broadcast([64, 4, 64]); mkT = self.maskS
            self.tt(attT4[0:64, c, :, :], bkA.ap[0:64, 0:256].rearrange("p (h t) -> p h t", t=64), mk, ALU.mult, B(bkA, mkT), B(attT))
            bkO = self.bank()
            if tc.kind == "p":
                for h in range(4):
                    for vh in range(2):
                        oo = bkO.ap[:, (h * 2 + vh) * 64:(h * 2 + vh + 1) * 64]
                        self.mm(oo, vt3[0:64, c, h * 256 + vh * 128:h * 256 + (vh + 1) * 128], attT4[0:64, c, h, :], True, False, B(vt, attT), B(bkO))
                        self.mm(oo, Sbf.ap[:, h, vh * 128:(vh + 1) * 128], qd3[:, h, csl], False, True, B(Sbf, qd), B(bkO))
                self.cp(o3[:, :, csl], bkO.ap[:, :].rearrange("p (k t) -> p k t", t=64), B(bkO), B(o), eng="act")
                bkK = [self.bank(), self.bank()]
                for h in range(4):
                    self.mm(bkK[h // 2].ap[:, (h % 2) * 256:(h % 2 + 1) * 256], kkT4[0:64, c, h, :], vt3[0:64, c, h * 256:(h + 1) * 256], True, True, B(kkT, vt), B(bkK[h // 2]))
                for h in range(4):
                    self.stt(Sst.ap[:, h, :], Sst.ap[:, h, :], el3[:, h, c:c + 1], bkK[h // 2].ap[:, (h % 2) * 256:(h % 2 + 1) * 256], ALU.mult, ALU.add, B(Sst, el, bkK[h // 2]), B(Sst))
                self.cp(Sbf.ap[:], Sst.ap[:], B(Sst), B(Sbf), eng="act")
            else:
                S0f = self.salloc(8 * 1024, F32); S0b = self.salloc(8 * 1024, BF16)
                S0f4 = S0f.ap.rearrange("p (q h v) -> p q h v", q=8, h=4, v=256)
                S0b4 = S0b.ap.rearrange("p (q h v) -> p q h v", q=8, h=4, v=256)
                for q in range(8):
                    seq = c * 8 + q
                    self.dma("pool", S0f4[:, q, :, :], I["sgla"][L, seq].rearrange("h d v -> d h v"), [], B(S0f))
                self.cp(S0b.ap[:, 0:4096], S0f.ap[:, 0:4096], B(S0f), B(S0b), eng="act")
                self.cp(S0b.ap[:, 4096:8192], S0f.ap[:, 4096:8192], B(S0f), B(S0b), eng="dve")
                for h in range(4):
                    for vh in range(2):
                        oo = bkO.ap[:, (h * 2 + vh) * 64:(h * 2 + vh + 1) * 64]
                        self.mm(oo, vt3[0:64, c, h * 256 + vh * 128:h * 256 + (vh + 1) * 128], attT4[0:64, c, h, :], True, False, B(vt, attT), B(bkO))
                        for q in range(8):
                            tsl = slice(c * 64 + q * 8, c * 64 + q * 8 + 8)
                            self.mm(oo[:, q * 8:(q + 1) * 8], S0b4[:, q, h, vh * 128:(vh + 1) * 128], qd3[:, h, tsl], False, q == 7, B(S0b, qd), B(bkO))
                self.cp(o3[:, :, csl], bkO.ap[:, :].rearrange("p (k t) -> p k t", t=64), B(bkO), B(o), eng="act")
                for q in range(8):
                    seq = c * 8 + q
                    kkm = self.salloc(512, BF16)
                    self.ts(kkm.ap[0:64, :], kkT.ap[0:64, c * 512:(c + 1) * 512], self.rowm.ap[:, q:q + 1], None, ALU.mult, None, B(kkT, self.rowm), B(kkm))
                    bkK = [self.bank(), self.bank()]
                    for h in range(4):
                        self.mm(bkK[h // 2].ap[:, (h % 2) * 256:(h % 2 + 1) * 256], kkm.ap[0:64, h * 128:(h + 1) * 128], vt3[0:64, c, h * 256:(h + 1) * 256], True, True, B(kkm, vt), B(bkK[h // 2]))
                    Sn = self.salloc(1024, F32)
                    Sn3 = Sn.ap.rearrange("p (h v) -> p h v", v=256)
                    for h in range(4):
                        self.stt(Sn3[:, h, :], S0f4[:, q, h, :], el3[:, h, seq:seq + 1], bkK[h // 2].ap[:, (h % 2) * 256:(h % 2 + 1) * 256], ALU.mult, ALU.add, B(S0f, el, bkK[h // 2]), B(Sn))
                    self.dma("sp", O["gla_s"][L, seq].rearrange("h d v -> d h v"), Sn3, B(Sn), [], store=True)
                    self.sfree(kkm, Sn)
                self.sfree(S0f, S0b)
        if tc.kind == "p" and tc.last:
            self.dma("sp", O["gla_p"][L].rearrange("h d v -> d h v"), Sst.ap[:, :, :], B(Sst), [], store=True)
        self.sfree(attT, kkT, vt, qd, kd, el)
        for h in range(4):
            sq = self.salloc(2 * TT, F32)
            self.act(sq.ap, o.ap[:, 2 * h * TT:(2 * h + 2) * TT], AF.Square, B(o), B(sq))
            bkM = self.bank(); bkQ = self.bank()
            for vh in range(2):
                self.mm(bkM.ap[:, 0:TT], self.ones[256].ap[:, :], o3[:, 2 * h + vh, :], vh == 0, vh == 1, B(self.ones[256], o), B(bkM))
            for vh in range(2):
                self.mm(bkQ.ap[:, 0:TT], self.ones[256].ap[:, :], sq.ap[:, vh * TT:(vh + 1) * TT], vh == 0, vh == 1, B(self.ones[256], sq), B(bkQ))
            mean, rstd = self.ln_stats(bkM, bkQ, TT)
            self.sfree(sq)
            for vh in range(2):
                k = 2 * h + vh
                tmp = self.salloc(TT, F32)
                self.tt(tmp.ap, o3[:, k, :], mean.ap, ALU.subtract, B(o, mean), B(tmp))
                self.tt(tmp.ap, tmp.ap, rstd.ap, ALU.mult, B(tmp, rstd), B(tmp))
                self.stt(self.ogg.ap[:, k, 0:TT], tmp.ap, self.normg.ap[:, k:k + 1], gz3[:, k, :], ALU.mult, ALU.mult, B(tmp, self.normg, gz), B(self.ogg))
                self.sfree(tmp)
            self.sfree(mean, rstd)
        self.sfree(o, gz)

    def ln_stats(self, bkM, bkQ, TT):
        mean = self.salloc(TT, F32); rstd = self.salloc(TT, F32); m2 = self.salloc(TT, F32)
        self.cp(mean.ap, bkM.ap[:, 0:TT], B(bkM), B(mean), eng="act")
        self.act(m2.ap, bkM.ap[:, 0:TT], AF.Square, B(bkM), B(m2))
        self.tt(rstd.ap, bkQ.ap[:, 0:TT], m2.ap, ALU.subtract, B(bkQ, m2), B(rstd))
        self.ts(rstd.ap, rstd.ap, 0.0, None, ALU.max, None, B(rstd), B(rstd))
        self.act(rstd.ap, rstd.ap, AF.Sqrt, B(rstd), B(rstd), bias=LN_EPS)
        self.recip(rstd.ap, rstd.ap, B(rstd), B(rstd))
        self.sfree(m2)
        return mean, rstd

    def conv_phase(self, L, tc):
        I = self.I; O = self.O
        TT = tc.TT; xT = self.xT; xr = self.xrhs(tc)
        BCa, BCb, BCz = self.BC["ca"], self.BC["cb"], self.BC["cz"]
        if tc.kind == "p":
            GW = 30 + TT
            G = self.salloc(8 * GW, BF16)
            G3 = G.ap.rearrange("p (k t) -> p k t", t=GW)
            if tc.first:
                self.memset(self.halo.ap[:], 0.0, B(self.halo))
            self.cp(G3[:, :, 0:30], self.halo.ap[:, :, :], B(self.halo), B(G))
            gdst = lambda ct: G3[:, ct, 30:30 + TT]
        else:
            GW = 16 * 38
            G = self.salloc(8 * GW, F32)
            G4 = G.ap.rearrange("p (k q t) -> p k q t", q=16, t=38)
            for r in range(4):
                stg = self.salloc(1024, F32)
                self.dma("pool", stg.ap[0:120, :], I["cconv"][L, r * 120:(r + 1) * 120, :], [], B(stg))
                for c4 in range(2):
                    bk = self.bank()
                    for cc in range(4):
                        ct = c4 * 4 + cc
                        self.tr(bk.ap[:, cc * 120:(cc + 1) * 120], stg.ap[0:120, ct * 128:(ct + 1) * 128], self.identF.ap[0:120, 0:120], B(stg, self.identF), B(bk))
                    outv = G4[:, c4 * 4:(c4 + 1) * 4, 4 * r:4 * r + 4, 0:30]
                    inv = bk.ap[:, 0:480].rearrange("p (k q t) -> p k q t", k=4, q=4, t=30)
                    self.cp(outv, inv, B(bk), B(G))
                self.sfree(stg)
            gdst = lambda ct: G4[:, ct, :, 30:38]
        czs = self.salloc(8 * TT, BF16)
        czs3 = czs.ap.rearrange("p (k t) -> p k t", t=TT)
        for cs_ in range(4):
            wa = self.wl_in(L, OFF["ca"] + 256 * cs_, 256)
            wb = self.wl_in(L, OFF["cb"] + 256 * cs_, 256)
            for jj in range(2):
                ct = cs_ * 2 + jj
                bkb_ = self.proj_fm(wb, jj, 16, xr, xT, TT)
                sg = self.salloc(TT, F32)
                self.act(sg.ap, bkb_.ap[:, 0:TT], AF.Sigmoid, B(bkb_, self.bcol), B(sg), bias=self.bcol.ap[:, BCb + ct:BCb + ct + 1])
                bka = self.proj_fm(wa, jj, 16, xr, xT, TT)
                if tc.kind == "p":
                    self.stt(gdst(ct), bka.ap[:, 0:TT], self.bcol.ap[:, BCa + ct:BCa + ct + 1], sg.ap, ALU.add, ALU.mult, B(bka, self.bcol, sg), B(G))
                    self.stt(self.halo.ap[:, ct, :], bka.ap[:, TT - 30:TT], self.bcol.ap[:, BCa + ct:BCa + ct + 1], sg.ap[:, TT - 30:TT], ALU.add, ALU.mult, B(bka, self.bcol, sg), B(self.halo))
                else:
                    self.stt(gdst(ct), bka.ap[:, 0:TT].rearrange("p (q t) -> p q t", t=8), self.bcol.ap[:, BCa + ct:BCa + ct + 1],
                             sg.ap.rearrange("p (q t) -> p q t", t=8), ALU.add, ALU.mult, B(bka, self.bcol, sg), B(G))
                self.sfree(sg)
        for cs_ in range(4):
            ws = self.wl_in(L, OFF["cz"] + 256 * cs_, 256)
            for jj in range(2):
                ct = cs_ * 2 + jj
                bk = self.proj_fm(ws, jj, 16, xr, xT, TT)
                self.act(czs3[:, ct, :], bk.ap[:, 0:TT], AF.Silu, B(bk, self.bcol), B(czs), bias=self.bcol.ap[:, BCz + ct:BCz + ct + 1])
        acc = self.salloc(8 * TT, F32)
        acc3 = acc.ap.rearrange("p (k t) -> p k t", t=TT)
        cw = self.convw
        if tc.kind == "p":
            for ct in range(8):
                dg = self.salloc(31 * 128, BF16)
                dg3 = dg.ap.rearrange("p (k m) -> p k m", m=128)
                idb = self.identB.ap[:, :].rearrange("p (o m) -> p o m", o=1).to_broadcast([128, 31, 128])
                wv = cw.ap[:, ct, :].rearrange("p (k o) -> p k o", o=1).to_broadcast([128, 31, 128])
                self.tt(dg3, idb, wv, ALU.mult, B(self.identB, cw), B(dg))
                bk = self.bank()
                for k in range(31):
                    self.mm(bk.ap[:, 0:TT], dg3[:, k, :], G3[:, ct, k:k + TT], k == 0, k == 30, B(dg, G), B(bk))
                self.act(acc3[:, ct, :], bk.ap[:, 0:TT], AF.Identity, B(bk, self.convb), B(acc), bias=self.convb.ap[:, ct:ct + 1])
                self.sfree(dg)
        else:
            for ct in range(8):
                src = lambda k: G4[:, ct, :, k:k + 8]
                dst = acc3[:, ct, :].rearrange("p (q t) -> p q t", t=8)
                self.ts(dst, src(0), cw.ap[:, ct, 0:1], self.convb.ap[:, ct:ct + 1], ALU.mult, ALU.add, B(G, cw, self.convb), B(acc))
                for k in range(1, 31):
                    self.stt(dst, src(k), cw.ap[:, ct, k:k + 1], dst, ALU.mult, ALU.add, B(G, cw, acc), B(acc))
        if tc.kind == "p":
            if tc.last:
                stg = self.salloc(1024, F32)
                for c4 in range(2):
                    bk = self.bank()
                    for cc in range(4):
                        ct = c4 * 4 + cc
                        self.tr(bk.ap[0:30, cc * 128:(cc + 1) * 128], self.halo.ap[:, ct, :], self.identF.ap[:, :], B(self.halo, self.identF), B(bk))
                    self.cp(stg.ap[0:30, c4 * 512:(c4 + 1) * 512], bk.ap[0:30, :], B(bk), B(stg), eng="act")
                self.dma("sp", O["conv_p"][L, :, :], stg.ap[0:30, :], B(stg), [], store=True)
                self.sfree(stg)
        else:
            cn = self.salloc(8 * 480, F32)
            cn4 = cn.ap.rearrange("p (k q t) -> p k q t", q=16, t=30)
            self.cp(cn4, G4[:, :, :, 8:38], B(G), B(cn), eng="act")
            for r in range(4):
                stg = self.salloc(1024, F32)
                for c4 in range(2):
                    bk = self.bank()
                    for cc in range(4):
                        ct = c4 * 4 + cc
                        self.tr(bk.ap[0:120, cc * 128:(cc + 1) * 128], cn.ap[:, ct * 480 + r * 120:ct * 480 + (r + 1) * 120], self.identF.ap[:, :], B(cn, self.identF), B(bk))
                    self.cp(stg.ap[0:120, c4 * 512:(c4 + 1) * 512], bk.ap[0:120, :], B(bk), B(stg), eng="act")
                self.dma("sp", O["conv_s"][L, r * 120:(r + 1) * 120, :], stg.ap[0:120, :], B(stg), [], store=True)
                self.sfree(stg)
            self.sfree(cn)
        self.sfree(G)
        bkM = self.bank(); bkQ = self.bank()
        for ct in range(8):
            sq = self.salloc(TT, F32)
            self.act(sq.ap, acc3[:, ct, :], AF.Square, B(acc), B(sq))
            self.mm(bkM.ap[:, 0:TT], self.ones[1024].ap[:, :], acc3[:, ct, :], ct == 0, ct == 7, B(self.ones[1024], acc), B(bkM))
            self.mm(bkQ.ap[:, 0:TT], self.ones[1024].ap[:, :], sq.ap, ct == 0, ct == 7, B(self.ones[1024], sq), B(bkQ))
            self.sfree(sq)
        mean, rstd = self.ln_stats(bkM, bkQ, TT)
        for ct in range(8):
            tmp = self.salloc(TT, F32)
            self.tt(tmp.ap, acc3[:, ct, :], mean.ap, ALU.subtract, B(acc, mean), B(tmp))
            self.tt(tmp.ap, tmp.ap, rstd.ap, ALU.mult, B(tmp, rstd), B(tmp))
            self.act(tmp.ap, tmp.ap, AF.Silu, B(tmp, self.clng, self.clnb), B(tmp), bias=self.clnb.ap[:, ct:ct + 1], scale=self.clng.ap[:, ct:ct + 1])
            self.tt(self.ycg.ap[:, ct, 0:TT], tmp.ap, czs3[:, ct, :], ALU.mult, B(tmp, czs), B(self.ycg))
            self.sfree(tmp)
        self.sfree(mean, rstd, acc, czs)

    def merge_phase(self, L, tc):
        I = self.I; O = self.O
        TT = tc.TT; xT = self.xT; xr = self.xrhs(tc)
        NL = self.NL
        BCg = self.BC["g"]
        merged = self.salloc(16 * TT, BF16)
        mg3 = merged.ap.rearrange("p (k t) -> p k t", t=TT)
        branches = [("p_s5", self.ys5g, 0), ("p_gla", self.ogg, 1), ("p_conv", self.ycg, 2)]
        for jo4 in range(4):
            macc = self.salloc(4 * TT, F32)
            macc3 = macc.ap.rearrange("p (k t) -> p k t", t=TT)
            for (wn, br, bi) in branches:
                wp = self.wl_mat(wn, L, 512 * jo4, 512, 8)
                for half in range(2):
                    gcol = OFF["g"] + bi * 2048 + jo4 * 512 + half * 256
                    wg = self.wl_in(L, gcol, 256)
                    for jj in range(2):
                        jl = half * 2 + jj
                        jo = jo4 * 4 + jl
                        bkG = self.proj_fm(wg, jj, 16, xr, xT, TT)
                        sg = self.salloc(TT, F32)
                        bcix = BCg + bi * 16 + jo
                        self.act(sg.ap, bkG.ap[:, 0:TT], AF.Sigmoid, B(bkG, self.bcol), B(sg), bias=self.bcol.ap[:, bcix:bcix + 1])
                        bkA = self.proj_fm(wp, jl, 8, lambda kt, br=br: br.ap[:, kt, 0:TT], br, TT)
                        if bi == 0:
                            self.tt(macc3[:, jl, :], bkA.ap[:, 0:TT], sg.ap, ALU.mult, B(bkA, sg), B(macc))
                        else:
                            self.tt(sg.ap, bkA.ap[:, 0:TT], sg.ap, ALU.mult, B(bkA, sg), B(sg))
                            if bi == 1:
                                self.tt(macc3[:, jl, :], macc3[:, jl, :], sg.ap, ALU.add, B(macc, sg), B(macc))
                            else:
                                self.tt(mg3[:, jo, :], macc3[:, jl, :], sg.ap, ALU.add, B(macc, sg), B(merged))
                        self.sfree(sg)
            self.sfree(macc)
        hT = self.salloc(16 * TT, F32); hbf = self.salloc(16 * TT, BF16)
        h3 = hT.ap.rearrange("p (k t) -> p k t", t=TT); hb3 = hbf.ap.rearrange("p (k t) -> p k t", t=TT)
        for cs_ in range(8):
            ws = self.wl_mat("w_o", L, 256 * cs_, 256, 16)
            for jj in range(2):
                jo = cs_ * 2 + jj
                bk = self.proj_fm(ws, jj, 16, lambda kt: mg3[:, kt, :], merged, TT)
                self.stt(h3[:, jo, :], xT.ap[:, jo, 0:TT], float(DN_ALPHA), bk.ap[:, 0:TT], ALU.mult, ALU.add, B(xT, bk), B(hT))
                self.cp(hb3[:, jo, :], h3[:, jo, :], B(hT), B(hbf), eng="act")
        self.sfree(merged)
        wpe = self.wload(self.I["w_pe"][L].rearrange("(k p) n -> p k n", p=128), 2, 2048, key=("w_pe", L, 0, 2048))
        pe_all = self.salloc(16 * TT, BF16)
        pe3 = pe_all.ap.rearrange("p (k t) -> p k t", t=TT)
        for jo in range(16):
            bk = self.bank()
            for kt in range(2):
                self.mm(bk.ap[:, 0:TT], wpe.ap[:, kt, jo * 128:(jo + 1) * 128], self.pT.ap[:, kt, 0:TT], kt == 0, kt == 1, B(wpe, self.pT), B(bk))
            self.cp(pe3[:, jo, :], bk.ap[:, 0:TT], B(bk), B(pe_all), eng="act")
        for cs_ in range(8):
            ws = self.wl_mat("w_pg", L, 256 * cs_, 256, 16)
            for jj in range(2):
                jo = cs_ * 2 + jj
                bk = self.proj_fm(ws, jj, 16, lambda kt: hb3[:, kt, :], hbf, TT)
                sg = self.salloc(TT, F32)
                self.act(sg.ap, bk.ap[:, 0:TT], AF.Sigmoid, B(bk), B(sg))
                self.tt(sg.ap, sg.ap, pe3[:, jo, :], ALU.mult, B(sg, pe_all), B(sg))
                self.tt(h3[:, jo, :], h3[:, jo, :], sg.ap, ALU.add, B(hT, sg), B(hT))
                self.sfree(sg)
        self.sfree(hbf, pe_all)
        bkM = self.bank(); bkQ = self.bank()
        for jo in range(16):
            sq = self.salloc(TT, F32)
            self.act(sq.ap, h3[:, jo, :], AF.Square, B(hT), B(sq))
            self.mm(bkM.ap[:, 0:TT], self.ones[2048].ap[:, :], h3[:, jo, :], jo == 0, jo == 15, B(self.ones[2048], hT), B(bkM))
            self.mm(bkQ.ap[:, 0:TT], self.ones[2048].ap[:, :], sq.ap, jo == 0, jo == 15, B(self.ones[2048], sq), B(bkQ))
            self.sfree(sq)
        mean, rstd = self.ln_stats(bkM, bkQ, TT)
        last = (L == NL - 1)
        if not last:
            xo = self.salloc(16 * 512, BF16)
            xo3 = xo.ap.rearrange("p (k t) -> p k t", t=512)
        for jo in range(16):
            self.tt(h3[:, jo, :], h3[:, jo, :], mean.ap, ALU.subtract, B(hT, mean), B(hT))
            self.tt(h3[:, jo, :], h3[:, jo, :], rstd.ap, ALU.mult, B(hT, rstd), B(hT))
            if last:
                self.act(h3[:, jo, :], h3[:, jo, :], AF.Identity, B(hT, self.lng, self.lnb), B(hT), bias=self.lnb.ap[:, jo:jo + 1], scale=self.lng.ap[:, jo:jo + 1])
            else:
                self.act(xo3[:, jo, 0:TT], h3[:, jo, :], AF.Identity, B(hT, self.lng, self.lnb), B(xo), bias=self.lnb.ap[:, jo:jo + 1], scale=self.lng.ap[:, jo:jo + 1])
        self.sfree(mean, rstd)
        if not last:
            par = L % 2
            dst = self.spill[par, tc.idx].rearrange("p (k t) -> p k t", t=512)[:, :, 0:TT]
            self.dma("pool", dst, xo3[:, :, 0:TT], B(xo), [self.spill_buf[par][tc.idx]])
            self.sfree(xo)
        else:
            dsto = O["yp"] if tc.kind == "p" else O["ys"]
            for tg in range(TT // 128):
                stg = self.salloc(2048, F32)
                for k4 in range(4):
                    bk = self.bank()
                    for kk in range(4):
                        jo = k4 * 4 + kk
                        self.tr(bk.ap[:, kk * 128:(kk + 1) * 128], h3[:, jo, tg * 128:(tg + 1) * 128], self.identF.ap[:, :], B(hT, self.identF), B(bk))
                    self.cp(stg.ap[:, k4 * 512:(k4 + 1) * 512], bk.ap[:, :], B(bk), B(stg), eng=("act" if k4 % 2 else "dve"))
                r0 = tc.tok0 + tg * 128
                self.dma("sp", dsto[r0:r0 + 128, :], stg.ap, B(stg), [], store=True)
                self.sfree(stg)
        self.sfree(hT)


_CACHE = {}


def _get_prog(NL, NPT):
    key = (NL, NPT)
    if key not in _CACHE:
        _CACHE[key] = Builder(NL, NPT).build()
    return _CACHE[key]


def core_inputs(inputs, c, NL, NPT, prompt_b, seq0):
    f = lambda a: np.ascontiguousarray(np.asarray(a, dtype=np.float32))
    TP = NPT * 512
    m = {}
    m["xp"] = f(inputs["x_prompt"][prompt_b, :TP])
    m["xs"] = f(inputs["x_sample"][seq0:seq0 + 16]).reshape(128, D)
    m["pp"] = f(inputs["p_prompt"][:NL, prompt_b, :TP])
    m["pps"] = f(inputs["p_sample"][:NL, seq0:seq0 + 16]).reshape(NL, 128, 256)
    m["sre"] = f(inputs["state_s5_re"][:NL, seq0:seq0 + 16]).reshape(NL, 512, 128)
    m["sim"] = f(inputs["state_s5_im"][:NL, seq0:seq0 + 16]).reshape(NL, 512, 128)
    m["sgla"] = f(inputs["state_gla"][:NL, seq0:seq0 + 16])
    m["cconv"] = f(inputs["cache_conv"][:NL, seq0:seq0 + 16]).reshape(NL, 480, 1024)
    return m


def kernel(**inputs):
    NL, NPT = 4, 4
    nc = _get_prog(NL, NPT)
    wshared = {n: np.ascontiguousarray(np.asarray(inputs[n], dtype=np.float32)[:NL]) for n in W_NAMES}
    in_maps = []
    for c in range(8):
        m = core_inputs(inputs, c, NL, NPT, c // 2, 16 * c)
        m.update(wshared)
        in_maps.append(m)
    res = run_bass_kernel_spmd(nc, in_maps, core_ids=list(range(8)))
    R = res.results
    y_p = np.stack([R[2 * b]["yp"] for b in range(4)]).reshape(4, 2048, D)
    y_s = np.concatenate([R[c]["ys"].reshape(16, 8, D) for c in range(8)], axis=0)
    s5re_p = np.stack([R[2 * b]["sre_p"].reshape(NL, 64, 64) for b in range(4)], axis=1)
    s5im_p = np.stack([R[2 * b]["sim_p"].reshape(NL, 64, 64) for b in range(4)], axis=1)
    gla_p = np.stack([R[2 * b]["gla_p"] for b in range(4)], axis=1)
    conv_p = np.stack([R[2 * b]["conv_p"] for b in range(4)], axis=1)
    s5re_s = np.concatenate([R[c]["sre_s"].reshape(NL, 16, 64, 64) for c in range(8)], axis=1)
    s5im_s = np.concatenate([R[c]["sim_s"].reshape(NL, 16, 64, 64) for c in range(8)], axis=1)
    gla_s = np.concatenate([R[c]["gla_s"] for c in range(8)], axis=1)
    conv_s = np.concatenate([R[c]["conv_s"].reshape(NL, 16, 30, 1024) for c in range(8)], axis=1)
    outs = (y_p, y_s, s5re_p, s5im_p, gla_p, conv_p, s5re_s, s5im_s, gla_s, conv_s)
    return tuple(np.ascontiguousarray(o, dtype=np.float32) for o in outs)
```

```python
import math
from contextlib import ExitStack

import numpy as np
import concourse.bass as bass
import concourse.mybir as mybir
from concourse.bass_utils import run_bass_kernel_spmd

F32 = mybir.dt.float32
BF16 = mybir.dt.bfloat16
AF = mybir.ActivationFunctionType
ALU = mybir.AluOpType

ENGS = ["pe", "act", "dve", "pool", "sp"]
SAME_ENGINE_SYNC = True


class Buf:
    __slots__ = ("writers", "readers")

    def __init__(self):
        self.writers = {}
        self.readers = {}


class Op:
    __slots__ = ("id", "eng", "fn", "deps", "dma", "inc", "count", "semi", "target", "nd")

    def __init__(self, id, eng, fn, deps, dma, nd):
        self.id = id; self.eng = eng; self.fn = fn; self.deps = deps; self.dma = dma
        self.inc = False; self.count = 0; self.semi = -1; self.target = 0; self.nd = nd


class Prog:
    def __init__(self, n_dma_sems=40):
        self.ops = []
        self.n_dma_sems = n_dma_sems

    def add(self, eng, fn, r=(), w=(), dma=False, nd=1, extra=()):
        deps = set(extra)
        for b in r:
            deps.update(b.writers.values())
        for b in w:
            deps.update(b.writers.values())
            deps.update(b.readers.values())
        oid = len(self.ops)
        self.ops.append(Op(oid, eng, fn, deps, dma, nd))
        key = ("d", oid) if dma else eng
        for b in r:
            b.readers[key] = oid
        for b in w:
            if b.readers:
                b.writers = {key: oid}
                b.readers = {}
            else:
                b.writers[key] = oid
        return oid

    def emit(self, block, sems, dma_sems):
        ops = self.ops

        def skip(dop, op):
            return (not dop.dma) and (not op.dma) and dop.eng == op.eng and (op.eng == "pe" or not SAME_ENGINE_SYNC)

        for op in ops:
            for d in op.deps:
                dop = ops[d]
                if dop.dma or skip(dop, op):
                    continue
                dop.inc = True
        cnt = {e: 0 for e in ENGS}
        dtarget = [0] * self.n_dma_sems
        dlast = [None] * self.n_dma_sems
        half = self.n_dma_sems // 2
        rrs = {"pool": 0, "sp": 0}
        for op in ops:
            if op.dma:
                base = 0 if op.eng == "pool" else half
                rr = base + rrs[op.eng]
                rrs[op.eng] = (rrs[op.eng] + 1) % half
                op.semi = rr
                if dlast[rr] is not None:
                    op.deps.add(dlast[rr])
                dtarget[rr] += 16 * op.nd
                op.target = dtarget[rr]
                dlast[rr] = op.id
            elif op.inc:
                cnt[op.eng] += 1
                op.count = cnt[op.eng]
        per_eng = {e: [] for e in ENGS}
        for op in ops:
            per_eng[op.eng].append(op)
        self.counts = cnt

        def run(ename, eh):
            waited = {}
            for op in per_eng[ename]:
                for d in sorted(op.deps):
                    dop = ops[d]
                    if dop.dma:
                        s = dma_sems[dop.semi]; v = dop.target; k = ("d", dop.semi)
                    else:
                        if skip(dop, op):
                            continue
                        s = sems[dop.eng]; v = dop.count; k = dop.eng
                    if waited.get(k, 0) >= v:
                        continue
                    waited[k] = v
                    eh.wait_ge(s, v)
                ins = op.fn(eh)
                if ins is None:
                    continue
                if op.dma:
                    for i in ins:
                        i.then_inc(dma_sems[op.semi], 16)
                elif op.inc:
                    ins.then_inc(sems[ename], 1)

        @block.tensor
        def _(e):
            run("pe", e)

        @block.scalar
        def _(e):
            run("act", e)

        @block.vector
        def _(e):
            run("dve", e)

        @block.gpsimd
        def _(e):
            run("pool", e)

        @block.sync
        def _(e):
            run("sp", e)


class T:
    __slots__ = ("ap", "bufs", "slabs")

    def __init__(self, ap, bufs, slabs=None):
        self.ap = ap; self.bufs = bufs; self.slabs = slabs


def B(*ts):
    out = []
    for t in ts:
        if t is None:
            continue
        out.extend(t.bufs)
    return out


D = 2048
NIN = 14352
OFF = dict(u=0, z=1024, q=2048, k=2560, v=3072, al=4096, gz=4112, ca=5136, cb=6160, cz=7184, g=8208)
DN_ALPHA = (2 * 4) ** 0.25
LN_EPS = 1e-5
TWO_PI = 2.0 * math.pi
MAGIC = 12582912.0
GELU_C = 1.5957691216057308
SLAB = 2048
NSLAB = 46
WSLOT_ELEMS = 4096
NWSLOT = 4
SCAN_ENG = "pool"
PRECONVERT = False

W_NAMES = ["w_in", "b_in", "s5_a_re", "s5_a_im", "s5_log_dt", "s5_b_re", "s5_b_im", "s5_c_re", "s5_c_im", "s5_d",
           "w_glu", "b_glu", "gla_w_a2", "gla_b_a", "gla_norm_g", "conv_w", "conv_b", "conv_ln_g", "conv_ln_b",
           "p_s5", "p_gla", "p_conv", "w_o", "w_pg", "w_pe", "ln_g", "ln_b"]


class TileCfg:
    def __init__(self, kind, idx, TT, tok0, first, last):
        self.kind = kind; self.idx = idx; self.TT = TT; self.tok0 = tok0; self.first = first; self.last = last
        self.NB = TT // 4
        if kind == "p":
            self.NSEG = 1; self.SEGB = self.NB; self.NCH = TT // 64; self.NE = self.NCH; self.EL = 64
        else:
            self.NSEG = 16; self.SEGB = 2; self.NCH = 2; self.NE = 16; self.EL = 8


class Builder:
    def __init__(self, NL, NPT, debug=None):
        self.NL = NL; self.NPT = NPT; self.TP = NPT * 512
        self.debug = debug or {}
        self.nc = bass.Bass("TRN2", target_bir_lowering=False)
        self.P = Prog()
        self.st = ExitStack()
        self.store_ops = []
        self.bank_i = 0
        self.bank_alloc = [None] * 8
        self.wslot_i = 0
        self.uid = 0

    def sb(self, shape, dt=F32, name=None):
        self.uid += 1
        h = self.st.enter_context(self.nc.sbuf_tensor(name or ("t%d" % self.uid), shape, dt))
        return h

    def pt(self, shape, dt=F32):
        h = self.sb(shape, dt)
        return T(h, [Buf()])

    def dram_in(self, name, shape, dt=F32):
        return self.nc.dram_tensor(name, list(shape), dt, kind="ExternalInput").ap()

    def dram_out(self, name, shape, dt=F32):
        return self.nc.dram_tensor(name, list(shape), dt, kind="ExternalOutput").ap()

    def salloc(self, nelem, dt):
        esz = 4 if dt == F32 else 2
        nb = nelem * esz
        ns = (nb + SLAB - 1) // SLAB
        free = self.slab_free
        for s0 in range(0, NSLAB - ns + 1):
            if all(free[s0:s0 + ns]):
                for i in range(s0, s0 + ns):
                    free[i] = False
                if dt == F32:
                    ap = self.scrF[:, s0 * (SLAB // 4): s0 * (SLAB // 4) + nelem]
                else:
                    ap = self.scrB[:, s0 * (SLAB // 2): s0 * (SLAB // 2) + nelem]
                return T(ap, self.slab_bufs[s0:s0 + ns], (s0, ns))
        raise RuntimeError("scratch exhausted: need %d slabs, free map %s" % (ns, "".join("1" if f else "0" for f in free)))

    def sfree(self, *ts):
        for t in ts:
            s0, ns = t.slabs
            for i in range(s0, s0 + ns):
                assert not self.slab_free[i]
                self.slab_free[i] = True

    def bank(self):
        for k in range(8):
            i = (self.bank_i + k) % 8
            b = self.banks[i].bufs[0]
            a = self.bank_alloc[i]
            free = a is None or (b.writers and max(b.writers.values()) >= a and len(b.readers) > 0)
            if free:
                self.bank_alloc[i] = len(self.P.ops)
                self.bank_i = (i + 1) % 8
                return self.banks[i]
        raise RuntimeError("no free PSUM bank")

    def mm(self, out, lhsT, rhs, start, stop, r, w):
        self.P.add("pe", lambda e: e.matmul(out, lhsT=lhsT, rhs=rhs, start=start, stop=stop), r=r, w=w)

    def tr(self, out, in_, ident, r, w):
        self.P.add("pe", lambda e: e.transpose(out, in_, ident), r=r, w=w)

    def act(self, out, in_, func, r, w, bias=None, scale=None):
        kw = {}
        if bias is not None:
            kw["bias"] = bias
        if scale is not None:
            kw["scale"] = scale
        self.P.add("act", lambda e: e.activation(out=out, in_=in_, func=func, **kw), r=r, w=w)

    def tt(self, out, in0, in1, op, r, w, eng="dve"):
        self.P.add(eng, lambda e: e.tensor_tensor(out=out, in0=in0, in1=in1, op=op), r=r, w=w)

    def ts(self, out, in0, s1, s2, op0, op1, r, w, eng="dve"):
        if s2 is None:
            self.P.add(eng, lambda e: e.tensor_scalar(out=out, in0=in0, scalar1=s1, scalar2=None, op0=op0), r=r, w=w)
        else:
            self.P.add(eng, lambda e: e.tensor_scalar(out=out, in0=in0, scalar1=s1, scalar2=s2, op0=op0, op1=op1), r=r, w=w)

    def stt(self, out, in0, scalar, in1, op0, op1, r, w, eng="dve"):
        self.P.add(eng, lambda e: e.scalar_tensor_tensor(out=out, in0=in0, scalar=scalar, in1=in1, op0=op0, op1=op1), r=r, w=w)

    def cp(self, out, in_, r, w, eng="dve"):
        if eng == "act":
            self.act(out, in_, AF.Identity, r, w)
        else:
            self.P.add(eng, lambda e: e.tensor_copy(out=out, in_=in_), r=r, w=w)

    def memset(self, ap, val, w, eng="dve"):
        self.P.add(eng, lambda e: e.memset(ap, val), w=w)

    def recip(self, out, in_, r, w):
        self.P.add("dve", lambda e: e.reciprocal(out=out, in_=in_), r=r, w=w)

    def scan(self, out, d0, d1, r, w):
        self.P.add("dve", lambda e: e.tensor_tensor_scan(out=out, data0=d0, data1=d1, initial=0.0, op0=ALU.mult, op1=ALU.add), r=r, w=w)

    def asel(self, ap, pattern, op, base, cm, w):
        self.P.add("pool", lambda e: e.affine_select(out=ap, in_=ap, pattern=pattern, compare_op=op, fill=0.0, base=base, channel_multiplier=cm), r=w, w=w)

    def dma(self, eng, out, in_, r, w, nc_ok=False, store=False):
        nc = self.nc
        if nc_ok:
            def fn(e):
                with nc.allow_non_contiguous_dma(reason="small strided parameter / layout load"):
                    return [e.dma_start(out=out, in_=in_)]
        else:
            def fn(e):
                return [e.dma_start(out=out, in_=in_)]
        if store and eng == "sp":
            eng = "pool"
        oid = self.P.add(eng, fn, r=r, w=w, dma=True)
        if store:
            self.store_ops.append(oid)
        return oid

    def wload(self, src, KT, NC, rb=(), key=None):
        slot = self.wslots[self.wslot_i]
        self.wslot_i = (self.wslot_i + 1) % NWSLOT
        view = slot.ap[:, 0:KT * NC].rearrange("p (k n) -> p k n", n=NC)
        if key is None:
            self.dma("sp", view, src, r=list(rb), w=B(slot))
            return T(view, slot.bufs)
        Lk = key[1]
        if key not in self.wblk:
            bi = self.wblk_n[Lk]
            self.wblk_n[Lk] += 1
            assert bi < self.NBLK, "too many weight blocks"
            bb = Buf()
            self.wblk[key] = (bi, bb)
            dstv = self.wscr[Lk][bi, :, 0:KT * NC].rearrange("p (k n) -> p k n", n=NC)
            self.dma("pool", dstv, src, r=[], w=[bb])
        bi, bb = self.wblk[key]
        self.dma("sp", slot.ap[:, 0:KT * NC], self.wscr[Lk][bi, :, 0:KT * NC], r=[bb], w=B(slot))
        return T(view, slot.bufs)

    def build(self):
        nc = self.nc; NL = self.NL; TP = self.TP
        I = {}
        I["xp"] = self.dram_in("xp", [TP, D]); I["xs"] = self.dram_in("xs", [128, D])
        I["pp"] = self.dram_in("pp", [NL, TP, 256]); I["pps"] = self.dram_in("pps", [NL, 128, 256])
        I["sre"] = self.dram_in("sre", [NL, 512, 128]); I["sim"] = self.dram_in("sim", [NL, 512, 128])
        I["sgla"] = self.dram_in("sgla", [NL, 16, 4, 128, 256]); I["cconv"] = self.dram_in("cconv", [NL, 480, 1024])
        shapes = dict(w_in=[NL, D, NIN], b_in=[NL, NIN], s5_a_re=[NL, 64, 64], s5_a_im=[NL, 64, 64], s5_log_dt=[NL, 64],
                      s5_b_re=[NL, 64, 64, 16], s5_b_im=[NL, 64, 64, 16], s5_c_re=[NL, 64, 16, 64], s5_c_im=[NL, 64, 16, 64],
                      s5_d=[NL, 1024], w_glu=[NL, 1024, 1024], b_glu=[NL, 1024], gla_w_a2=[NL, 16, 512], gla_b_a=[NL, 512],
                      gla_norm_g=[NL, 1024], conv_w=[NL, 31, 1024], conv_b=[NL, 1024], conv_ln_g=[NL, 1024], conv_ln_b=[NL, 1024],
                      p_s5=[NL, 1024, D], p_gla=[NL, 1024, D], p_conv=[NL, 1024, D], w_o=[NL, D, D], w_pg=[NL, D, D],
                      w_pe=[NL, 256, D], ln_g=[NL, D], ln_b=[NL, D])
        for n in W_NAMES:
            I[n] = self.dram_in(n, shapes[n])
        O = {}
        O["yp"] = self.dram_out("yp", [TP, D]); O["ys"] = self.dram_out("ys", [128, D])
        O["sre_p"] = self.dram_out("sre_p", [NL, 32, 128]); O["sim_p"] = self.dram_out("sim_p", [NL, 32, 128])
        O["gla_p"] = self.dram_out("gla_p", [NL, 4, 128, 256]); O["conv_p"] = self.dram_out("conv_p", [NL, 30, 1024])
        O["sre_s"] = self.dram_out("sre_s", [NL, 512, 128]); O["sim_s"] = self.dram_out("sim_s", [NL, 512, 128])
        O["gla_s"] = self.dram_out("gla_s", [NL, 16, 4, 128, 256]); O["conv_s"] = self.dram_out("conv_s", [NL, 480, 1024])
        self.I = I; self.O = O
        ntiles = self.NPT + 1
        self.s5w = nc.dram_tensor("s5w_scr", [NL, 5, 128, 4096], BF16, kind="Internal").ap()
        self.spill = nc.dram_tensor("x_spill", [2, ntiles, 128, 16 * 512], BF16, kind="Internal").ap()
        self.s5w_buf = [[Buf() for _ in range(5)] for _ in range(NL)]
        self.NBLK = 90
        self.wscr = [nc.dram_tensor("w_bf16_scr%d" % l, [self.NBLK, 128, WSLOT_ELEMS], BF16, kind="Internal").ap() for l in range(NL)]
        self.wblk = {}
        self.wblk_n = [0] * NL
        self.spill_buf = [[Buf() for _ in range(ntiles)] for _ in range(2)]

        st = self.st
        with st:
            self.sems = {e: st.enter_context(nc.semaphore("s_" + e)) for e in ENGS}
            self.dsems = [st.enter_context(nc.semaphore("d%d" % i)) for i in range(self.P.n_dma_sems)]
            self.banks = []
            for i in range(8):
                h = st.enter_context(nc.psum_tensor("bank%d" % i, [128, 512], F32))
                t = T(h, [Buf()])
                self.banks.append(t)
            scr = self.sb([128, NSLAB * SLAB // 2], BF16, "scratch")
            self.scrB = scr
            self.scrF = scr.bitcast(F32)
            self.slab_bufs = [Buf() for _ in range(NSLAB)]
            self.slab_free = [True] * NSLAB
            self.wslots = [self.pt([128, WSLOT_ELEMS], BF16) for _ in range(NWSLOT)]
            self.xT = self.pt([128, 16, 512], BF16)
            self.pT = self.pt([128, 2, 512], BF16)
            self.ys5g = self.pt([128, 8, 512], BF16)
            self.ogg = self.pt([128, 8, 512], BF16)
            self.ycg = self.pt([128, 8, 512], BF16)
            self.Sst = self.pt([128, 4, 256], F32)
            self.Sbf = self.pt([128, 4, 256], BF16)
            self.halo = self.pt([128, 8, 30], F32)
            self.carry = self.pt([128, 2, 32], F32)
            self.T4 = self.pt([128, NL, 2, 2, 32], F32)
            self.consts()
            self.params_alloc()
            for L in range(NL):
                self.s5_prep(L)
            tiles = [TileCfg("p", i, 512, 512 * i, i == 0, i == self.NPT - 1) for i in range(self.NPT)]
            tiles.append(TileCfg("s", self.NPT, 128, 0, True, True))
            for L in range(NL):
                self.params_load(L)
                for tc in tiles:
                    self.tile_layer(L, tc)
            self.P.add("sp", lambda e: None, extra=list(self.store_ops))
            assert all(self.slab_free), "scratch leak"
            with nc.Block() as block:
                self.P.emit(block, self.sems, self.dsems)
        return nc

    def consts(self):
        self.identF = self.pt([128, 128], F32)
        self.identB = self.pt([128, 128], BF16)
        for t in (self.identF, self.identB):
            self.memset(t.ap[:], 1.0, B(t), eng="pool")
            self.asel(t.ap[:], [[-1, 128]], ALU.is_equal, 0, 1, B(t))
        self.permI = self.pt([128, 128], F32)
        self.cp(self.permI.ap[:].rearrange("p (t g c) -> p t g c", t=4, g=2, c=16),
                self.identF.ap[:].rearrange("p (g t c) -> p t g c", t=4, g=2, c=16), B(self.identF), B(self.permI))
        self.ones = {}
        for n in (256, 1024, 2048):
            t = self.pt([128, 128], F32)
            self.memset(t.ap[:], 1.0 / n, B(t), eng="pool")
            self.ones[n] = t
        self.maskP = self.pt([64, 64], F32)
        self.memset(self.maskP.ap[:], 1.0, B(self.maskP), eng="pool")
        self.asel(self.maskP.ap[:], [[1, 64]], ALU.is_ge, 0, -1, B(self.maskP))
        self.maskS = self.pt([64, 8, 8], F32)
        self.memset(self.maskS.ap[:], 1.0, B(self.maskS), eng="pool")
        self.asel(self.maskS.ap[:], [[8, 8], [1, 8]], ALU.is_ge, 0, -1, B(self.maskS))
        self.asel(self.maskS.ap[:], [[-8, 8], [0, 8]], ALU.is_ge, 0, 1, B(self.maskS))
        self.rowm = self.pt([64, 8], F32)
        self.memset(self.rowm.ap[:], 1.0, B(self.rowm), eng="pool")
        self.asel(self.rowm.ap[:], [[-8, 8]], ALU.is_ge, 0, 1, B(self.rowm))
        self.asel(self.rowm.ap[:], [[8, 8]], ALU.is_ge, 7, -1, B(self.rowm))
        self.TMa = self.pt([128, 4, 16], F32)
        self.TMb = self.pt([128, 4, 16], F32)
        self.memset(self.TMa.ap[:], 1.0, B(self.TMa), eng="pool")
        self.asel(self.TMa.ap[:], [[16, 4], [0, 16]], ALU.is_ge, 15, -1, B(self.TMa))
        self.memset(self.TMb.ap[:], 1.0, B(self.TMb), eng="pool")
        self.asel(self.TMb.ap[:], [[16, 4], [0, 16]], ALU.is_ge, 79, -1, B(self.TMb))
        self.coefP = self.pt([128, 512], F32)
        self.memset(self.coefP.ap[:], 1.0, B(self.coefP), eng="pool")
        self.memset(self.coefP.ap[:, 0:512:64], 0.0, B(self.coefP), eng="pool")
        self.coefS = self.pt([128, 128], F32)
        self.memset(self.coefS.ap[:], 1.0, B(self.coefS), eng="pool")
        self.memset(self.coefS.ap[:, 0:128:8], 0.0, B(self.coefS), eng="pool")

    def params_alloc(self):
        self.bcol = self.pt([128, 96], F32)
        self.bal = self.pt([16, 1], F32)
        self.wa2 = self.pt([16, 512], BF16)
        self.bubc = self.pt([128, 1024], F32)
        self.bvbc = self.pt([128, 1024], F32)
        self.bglu = self.pt([128, 8], F32)
        self.nba = self.pt([128, 4], F32)
        self.normg = self.pt([128, 8], F32)
        self.convw = self.pt([128, 8, 31], F32)
        self.convb = self.pt([128, 8], F32)
        self.clng = self.pt([128, 8], F32)
        self.clnb = self.pt([128, 8], F32)
        self.lng = self.pt([128, 16], F32)
        self.lnb = self.pt([128, 16], F32)
        self.BC = dict(z=0, q=8, k=12, gz=16, ca=24, cb=32, cz=40, g=48)

    def params_load(self, L):
        I = self.I
        segs = [("z", 8), ("q", 4), ("k", 4), ("gz", 8), ("ca", 8), ("cb", 8), ("cz", 8), ("g", 48)]
        for nm, n in segs:
            c0 = self.BC[nm]
            src = I["b_in"][L, OFF[nm]:OFF[nm] + 128 * n].rearrange("(n p) -> p n", p=128)
            self.dma("sp", self.bcol.ap[:, c0:c0 + n], src, [], B(self.bcol), nc_ok=True)
        self.dma("sp", self.bal.ap[:, :], I["b_in"][L, OFF["al"]:OFF["al"] + 16].rearrange("(p o) -> p o", o=1), [], B(self.bal), nc_ok=True)
        self.dma("pool", self.wa2.ap[:, :], I["gla_w_a2"][L, :, :], [], B(self.wa2))
        self.dma("sp", self.bubc.ap[:, :], I["b_in"][L:L + 1, 0:1024].to_broadcast([128, 1024]), [], B(self.bubc))
        self.dma("sp", self.bvbc.ap[:, :], I["b_in"][L:L + 1, OFF["v"]:OFF["v"] + 1024].to_broadcast([128, 1024]), [], B(self.bvbc))

        def col(dst, src1d, n):
            self.dma("sp", dst.ap[:, 0:n], src1d.rearrange("(n p) -> p n", p=128), [], B(dst), nc_ok=True)
        col(self.bglu, I["b_glu"][L, :], 8)
        col(self.nba, I["gla_b_a"][L, :], 4)
        self.ts(self.nba.ap[:, :], self.nba.ap[:, :], -1.0, None, ALU.mult, None, B(self.nba), B(self.nba))
        col(self.normg, I["gla_norm_g"][L, :], 8)
        col(self.convb, I["conv_b"][L, :], 8)
        col(self.clng, I["conv_ln_g"][L, :], 8)
        col(self.clnb, I["conv_ln_b"][L, :], 8)
        col(self.lng, I["ln_g"][L, :], 16)
        col(self.lnb, I["ln_b"][L, :], 16)
        for ct in range(8):
            self.dma("sp", self.convw.ap[:, ct, :], I["conv_w"][L, :, ct * 128:(ct + 1) * 128].rearrange("k p -> p k"), [], B(self.convw), nc_ok=True)

    def s5_prep(self, L):
        I = self.I
        f = lambda n: self.salloc(n, F32)
        are = f(32); aim = f(32); ldt = f(32); dK = f(32)
        Bre = f(512); Bim = f(512); Cre = f(512); Cim = f(512)
        for g in range(2):
            ps_ = slice(64 * g, 64 * g + 64)
            self.dma("sp", are.ap[ps_, :], I["s5_a_re"][L].rearrange("(j two) p -> two p j", two=2)[g], [], B(are), nc_ok=True)
            self.dma("sp", aim.ap[ps_, :], I["s5_a_im"][L].rearrange("(j two) p -> two p j", two=2)[g], [], B(aim), nc_ok=True)
            self.dma("sp", ldt.ap[ps_, :], I["s5_log_dt"][L].rearrange("(j two) -> two j", two=2)[g:g + 1, :].to_broadcast([64, 32]), [], B(ldt), nc_ok=True)
            self.dma("sp", Bre.ap[ps_, :].rearrange("p (j c) -> p j c", c=16), I["s5_b_re"][L].rearrange("(j two) p c -> two p j c", two=2)[g], [], B(Bre), nc_ok=True)
            self.dma("sp", Bim.ap[ps_, :].rearrange("p (j c) -> p j c", c=16), I["s5_b_im"][L].rearrange("(j two) p c -> two p j c", two=2)[g], [], B(Bim), nc_ok=True)
            for s in range(4):
                p0 = 64 * g + 16 * s
                self.dma("sp", dK.ap[p0:p0 + 16, :], I["s5_d"][L].rearrange("(j g c) -> g c j", g=2, c=16)[g], [], B(dK), nc_ok=True)
        for nm, Ct in (("s5_c_re", Cre), ("s5_c_im", Cim)):
            stg = f(2048)
            self.dma("sp", stg.ap[0:32, :], I[nm][L].rearrange("(j two) c p -> j (two c p)", two=2), [], B(stg))
            bk = self.bank()
            for g in range(2):
                for c in range(16):
                    self.mm(bk.ap[64 * g:64 * g + 64, c * 32:(c + 1) * 32], stg.ap[0:32, (g * 16 + c) * 64:(g * 16 + c + 1) * 64],
                            self.identF.ap[0:32, 0:32], True, True, B(stg, self.identF), B(bk))
            self.cp(Ct.ap.rearrange("p (j c) -> p j c", c=16), bk.ap[:, :].rearrange("p (c j) -> p j c", c=16, j=32), B(bk), B(Ct))
            self.sfree(stg)
        dt = f(32); ardt = f(32); aidt = f(32)
        self.act(dt.ap, ldt.ap, AF.Exp, B(ldt), B(dt))
        self.tt(ardt.ap, are.ap, dt.ap, ALU.mult, B(are, dt), B(ardt))
        self.tt(aidt.ap, aim.ap, dt.ap, ALU.mult, B(aim, dt), B(aidt))
        MAG = f(256); TSC = f(512); R1 = f(512); R2 = f(512); SC = f(512)
        for k in range(8):
            m = k - 3
            self.act(MAG.ap[:, k * 32:(k + 1) * 32], ardt.ap, AF.Exp, B(ardt), B(MAG), scale=float(m))
            self.ts(TSC.ap[:, k * 32:(k + 1) * 32], aidt.ap, float(m) / TWO_PI, None, ALU.mult, None, B(aidt), B(TSC))
        self.ts(TSC.ap[:, 256:512], TSC.ap[:, 0:256], 0.25, None, ALU.add, None, B(TSC), B(TSC))
        self.ts(R1.ap, TSC.ap, MAGIC, None, ALU.add, None, B(TSC), B(R1))
        self.ts(R2.ap, R1.ap, -MAGIC, None, ALU.add, None, B(R1), B(R2))
        self.tt(R1.ap, TSC.ap, R2.ap, ALU.subtract, B(TSC, R2), B(R1))
        self.act(SC.ap, R1.ap, AF.Sin, B(R1), B(SC), scale=TWO_PI)
        PWr = f(256); PWi = f(256)
        self.tt(PWr.ap, MAG.ap, SC.ap[:, 256:512], ALU.mult, B(MAG, SC), B(PWr))
        self.tt(PWi.ap, MAG.ap, SC.ap[:, 0:256], ALU.mult, B(MAG, SC), B(PWi))
        self.sfree(MAG, TSC, R1, R2, SC, dt, ardt, aidt, ldt)
        pw = lambda Tt, k: Tt.ap[:, k * 32:(k + 1) * 32]
        T4 = self.T4
        self.cp(T4.ap[:, L, 0, 0, :], pw(PWr, 7), B(PWr), B(T4))
        self.cp(T4.ap[:, L, 0, 1, :], pw(PWr, 7), B(PWr), B(T4))
        self.cp(T4.ap[:, L, 1, 0, :], pw(PWi, 7), B(PWi), B(T4))
        self.ts(T4.ap[:, L, 1, 1, :], pw(PWi, 7), -1.0, None, ALU.mult, None, B(PWi), B(T4))
        nr = f(32); t1 = f(32); t2 = f(32); den = f(32); Ere = f(32); Eim = f(32)
        self.ts(nr.ap, pw(PWr, 4), -1.0, None, ALU.add, None, B(PWr), B(nr))
        ni = pw(PWi, 4)
        self.tt(den.ap, are.ap, are.ap, ALU.mult, B(are), B(den))
        self.tt(t1.ap, aim.ap, aim.ap, ALU.mult, B(aim), B(t1))
        self.tt(den.ap, den.ap, t1.ap, ALU.add, B(den, t1), B(den))
        self.recip(den.ap, den.ap, B(den), B(den))
        self.tt(t1.ap, nr.ap, are.ap, ALU.mult, B(nr, are), B(t1))
        self.tt(t2.ap, ni, aim.ap, ALU.mult, B(PWi, aim), B(t2))
        self.tt(t1.ap, t1.ap, t2.ap, ALU.add, B(t1, t2), B(t1))
        self.tt(Ere.ap, t1.ap, den.ap, ALU.mult, B(t1, den), B(Ere))
        self.tt(t1.ap, ni, are.ap, ALU.mult, B(PWi, are), B(t1))
        self.tt(t2.ap, nr.ap, aim.ap, ALU.mult, B(nr, aim), B(t2))
        self.tt(t1.ap, t1.ap, t2.ap, ALU.subtract, B(t1, t2), B(t1))
        self.tt(Eim.ap, t1.ap, den.ap, ALU.mult, B(t1, den), B(Eim))
        self.sfree(nr, t2, den, are, aim)

        def v3(Tt):
            return Tt.ap.rearrange("p (j c) -> p j c", c=16)

        def bc(ap32):
            return ap32.rearrange("p (j o) -> p j o", o=1).to_broadcast([128, 32, 16])

        def cmul(outr, outi, ar, ai, br, bi, rb, tmpT, neg_im=False):
            tv = v3(tmpT)
            if outr is not None:
                self.tt(outr, bc(ar), br, ALU.mult, rb, rb)
                self.tt(tv, bc(ai), bi, ALU.mult, rb + B(tmpT), B(tmpT))
                self.tt(outr, outr, tv, ALU.subtract, rb + B(tmpT), rb)
            if outi is not None:
                self.tt(outi, bc(ar), bi, ALU.mult, rb, rb)
                self.tt(tv, bc(ai), br, ALU.mult, rb + B(tmpT), B(tmpT))
                self.tt(outi, outi, tv, ALU.add, rb + B(tmpT), rb)
                if neg_im:
                    self.ts(outi, outi, -1.0, None, ALU.mult, None, rb, rb)

        tmp = f(512)
        bbr = f(512); bbi = f(512)
        allb = B(PWr, PWi, Ere, Eim, Bre, Bim, Cre, Cim, bbr, bbi)
        cmul(v3(bbr), v3(bbi), Ere.ap, Eim.ap, v3(Bre), v3(Bim), allb, tmp)
        self.sfree(Ere, Eim, Bre, Bim, t1)
        Xr = f(2048); XiN = f(2048); Zr = f(2048); Zi = f(2048)
        v4 = lambda Tt: Tt.ap.rearrange("p (j s c) -> p j s c", s=4, c=16)
        rb = allb + B(Xr, XiN, Zr, Zi)
        for s in range(4):
            cmul(v4(Xr)[:, :, s, :], v4(XiN)[:, :, s, :], pw(PWr, 3 - s), pw(PWi, 3 - s), v3(bbr), v3(bbi), rb, tmp, neg_im=True)
            cmul(v4(Zr)[:, :, s, :], v4(Zi)[:, :, s, :], pw(PWr, 3 + s), pw(PWi, 3 + s), v3(Cre), v3(Cim), rb, tmp)
        Mf = f(4096)
        self.memset(Mf.ap, 0.0, B(Mf))
        Mf3 = Mf.ap.rearrange("p (j n) -> p j n", n=128)
        for j4 in range(8):
            bk = self.bank()
            for jj in range(4):
                j = j4 * 4 + jj
                for g in range(2):
                    pr = slice(64 * g, 64 * g + 64)
                    o = bk.ap[pr, jj * 128 + 64 * g: jj * 128 + 64 * g + 64]
                    self.mm(o, Xr.ap[pr, j * 64:(j + 1) * 64], Zr.ap[pr, j * 64:(j + 1) * 64], True, False, B(Xr, Zr), B(bk))
                    self.mm(o, XiN.ap[pr, j * 64:(j + 1) * 64], Zi.ap[pr, j * 64:(j + 1) * 64], False, True, B(XiN, Zi), B(bk))
            for g in range(2):
                pr = slice(64 * g, 64 * g + 64)
                TM = self.TMa if g == 0 else self.TMb
                outv = Mf3[pr, j4 * 4:(j4 + 1) * 4, :].rearrange("p j (t g c) -> p j t g c", t=4, g=2, c=16)[:, :, :, g, :]
                inv = bk.ap[pr, :].rearrange("p (j g t c) -> p j g t c", j=4, g=2, t=4, c=16)[:, :, g, :, :]
                mk = TM.ap[pr, :, :].rearrange("p (o t) c -> p o t c", o=1).to_broadcast([64, 4, 4, 16])
                self.tt(outv, inv, mk, ALU.mult, B(bk, TM), B(Mf))
        for j in range(32):
            self.stt(Mf3[:, j, :], self.permI.ap[:, :], dK.ap[:, j:j + 1], Mf3[:, j, :], ALU.mult, ALU.add, B(self.permI, dK, Mf), B(Mf))
        Mb = self.salloc(4096, BF16)
        self.cp(Mb.ap, Mf.ap, B(Mf), B(Mb), eng="act")
        self.dma("sp", self.s5w[L, 0], Mb.ap, B(Mb), [self.s5w_buf[L][0]])
        self.sfree(Xr, XiN, Zr, Zi, Mf, Mb, dK)
        Wn_r = f(2048); Wn_i = f(2048)
        rb = allb + B(Wn_r, Wn_i)
        for s in range(4):
            cmul(v4(Wn_r)[:, :, s, :], v4(Wn_i)[:, :, s, :], pw(PWr, 6 - s), pw(PWi, 6 - s), v3(bbr), v3(bbi), rb, tmp)
        for plane, Wn in enumerate((Wn_r, Wn_i)):
            VS = f(4096)
            self.memset(VS.ap, 0.0, B(VS))
            VS3 = VS.ap.rearrange("p (j n) -> p j n", n=128)
            Wn3 = Wn.ap.rearrange("p (j n) -> p j n", n=64)
            for g in range(2):
                pr = slice(64 * g, 64 * g + 64)
                self.cp(VS3[pr, :, 64 * g:64 * g + 64], Wn3[pr, :, :], B(Wn), B(VS))
            WSt = self.salloc(4096, BF16)
            for j4 in range(8):
                bk = self.bank()
                for jj in range(4):
                    j = j4 * 4 + jj
                    self.tr(bk.ap[:, jj * 128:(jj + 1) * 128], VS3[:, j, :], self.identF.ap[:, :], B(VS, self.identF), B(bk))
                self.cp(WSt.ap[:, j4 * 512:(j4 + 1) * 512], bk.ap[:, :], B(bk), B(WSt), eng="act")
            self.dma("sp", self.s5w[L, 1 + plane], WSt.ap, B(WSt), [self.s5w_buf[L][1 + plane]])
            self.sfree(VS, WSt)
        rb = allb + B(Wn_r, Wn_i)
        for t in range(4):
            cmul(v4(Wn_r)[:, :, t, :], v4(Wn_i)[:, :, t, :], pw(PWr, 4 + t), pw(PWi, 4 + t), v3(Cre), v3(Cim), rb, tmp, neg_im=True)
        for plane, Wn in enumerate((Wn_r, Wn_i)):
            WYb = self.salloc(4096, BF16)
            self.memset(WYb.ap, 0.0, B(WYb))
            for g in range(2):
                pr = slice(64 * g, 64 * g + 64)
                outv = WYb.ap[pr, :].rearrange("p (j t g c) -> p j t g c", t=4, g=2, c=16)[:, :, :, g, :]
                inv = Wn.ap[pr, :].rearrange("p (j t c) -> p j t c", t=4, c=16)
                self.cp(outv, inv, B(Wn), B(WYb))
            self.dma("sp", self.s5w[L, 3 + plane], WYb.ap, B(WYb), [self.s5w_buf[L][3 + plane]])
            self.sfree(WYb)
        self.sfree(Wn_r, Wn_i, tmp, bbr, bbi, Cre, Cim, PWr, PWi)

    def win(self, L, col0, ncols, KT=16):
        return self.I["w_in"][L, :, col0:col0 + ncols].rearrange("(k p) n -> p k n", p=128)

    def wmat(self, name, L, col0, ncols):
        return self.I[name][L, :, col0:col0 + ncols].rearrange("(k p) n -> p k n", p=128)

    def weight_recipes(self):
        r = []
        for cs in range(4):
            r.append(("w_in", OFF["u"] + 256 * cs, 256, 16))
        for cs in range(4):
            r.append(("w_in", OFF["ca"] + 256 * cs, 256, 16)); r.append(("w_in", OFF["cb"] + 256 * cs, 256, 16))
        for cs in range(4):
            r.append(("w_in", OFF["cz"] + 256 * cs, 256, 16))
        for cs in range(4):
            r.append(("w_in", OFF["z"] + 256 * cs, 256, 16))
        for hs in range(2):
            r.append(("w_glu", 512 * hs, 512, 8))
        r.append(("w_in", OFF["al"], 16, 16))
        for nm in ("q", "k"):
            for cs in range(2):
                r.append(("w_in", OFF[nm] + 256 * cs, 256, 16))
        for nm in ("v", "gz"):
            for cs in range(4):
                r.append(("w_in", OFF[nm] + 256 * cs, 256, 16))
        for jo4 in range(4):
            for bi, wn in enumerate(("p_s5", "p_gla", "p_conv")):
                r.append((wn, 512 * jo4, 512, 8))
                for half in range(2):
                    r.append(("w_in", OFF["g"] + bi * 2048 + jo4 * 512 + half * 256, 256, 16))
        for cs in range(8):
            r.append(("w_o", 256 * cs, 256, 16))
        r.append(("w_pe", 0, 2048, 2))
        for cs in range(8):
            r.append(("w_pg", 256 * cs, 256, 16))
        return r

    def preconvert(self, L, lo=0, hi=None):
        rec = self.weight_recipes()
        hi = len(rec) if hi is None else min(hi, len(rec))
        for (name, col0, ncols, KT) in rec[lo:hi]:
            key = (name, L, col0, ncols)
            if key in self.wblk:
                continue
            src = self.I[name][L, :, col0:col0 + ncols].rearrange("(k p) n -> p k n", p=128)
            bi = self.wblk_n[L]; self.wblk_n[L] += 1
            assert bi < self.NBLK
            bb = Buf()
            self.wblk[key] = (bi, bb)
            dstv = self.wscr[L][bi, :, 0:KT * ncols].rearrange("p (k n) -> p k n", n=ncols)
            self.dma("pool", dstv, src, r=[], w=[bb])

    def wl_in(self, L, col0, ncols):
        return self.wload(self.win(L, col0, ncols), 16, ncols, key=("w_in", L, col0, ncols))

    def wl_mat(self, name, L, col0, ncols, KT):
        return self.wload(self.wmat(name, L, col0, ncols), KT, ncols, key=(name, L, col0, ncols))

    def proj_fm(self, ws, jj, KT, rhs_fn, rhsT, TT):
        bk = self.bank()
        for kt in range(KT):
            self.mm(bk.ap[:, 0:TT], ws.ap[:, kt, jj * 128:(jj + 1) * 128], rhs_fn(kt), kt == 0, kt == KT - 1, B(ws, rhsT), B(bk))
        return bk

    def tile_layer(self, L, tc):
        if PRECONVERT and tc.kind == "p" and tc.first:
            self.preconvert(L)
        self.load_x(L, tc)
        self.s5_phase(L, tc)
        self.conv_phase(L, tc)
        self.dbg_dump("dbg_conv", self.ycg, L, tc)
        self.s5_back(L, tc)
        self.dbg_dump("dbg_s5", self.ys5g, L, tc)
        self.gla_phase(L, tc)
        self.dbg_dump("dbg_gla", self.ogg, L, tc)
        self.merge_phase(L, tc)

    def dbg_dump(self, name, t, L, tc):
        if not self.debug or L != 0:
            return
        key = name + "_" + tc.kind + str(tc.idx)
        o = self.dram_out(key, [128, 8, 512], BF16)
        self.dma("sp", o[:, :, :], t.ap[:, :, :], B(t), [], store=True)

    def xrhs(self, tc):
        xT = self.xT
        return lambda kt: xT.ap[:, kt, 0:tc.TT]

    def load_x(self, L, tc):
        I = self.I; TT = tc.TT
        ngr = TT // 128
        if L == 0:
            src = I["xp"] if tc.kind == "p" else I["xs"]
            for tg in range(ngr):
                stg = self.salloc(2048, F32)
                r0 = tc.tok0 + tg * 128
                self.dma("pool", stg.ap, src[r0:r0 + 128, :], [], B(stg))
                for k4 in range(4):
                    bk = self.bank()
                    for kk in range(4):
                        kt = k4 * 4 + kk
                        self.tr(bk.ap[:, kk * 128:(kk + 1) * 128], stg.ap[:, kt * 128:(kt + 1) * 128], self.identF.ap[:, :], B(stg, self.identF), B(bk))
                    outv = self.xT.ap[:, k4 * 4:(k4 + 1) * 4, tg * 128:(tg + 1) * 128]
                    self.cp(outv, bk.ap[:, :].rearrange("p (k t) -> p k t", t=128), B(bk), B(self.xT), eng=("act" if k4 % 2 else "dve"))
                self.sfree(stg)
        else:
            par = (L - 1) % 2
            src = self.spill[par, tc.idx].rearrange("p (k t) -> p k t", t=512)[:, :, 0:TT]
            self.dma("pool", self.xT.ap[:, :, 0:TT], src, [self.spill_buf[par][tc.idx]], B(self.xT))
        psrc = I["pp"][L] if tc.kind == "p" else I["pps"][L]
        for tg in range(ngr):
            stg = self.salloc(256, F32)
            r0 = tc.tok0 + tg * 128
            self.dma("pool", stg.ap, psrc[r0:r0 + 128, :], [], B(stg))
            bk = self.bank()
            for kk in range(2):
                self.tr(bk.ap[:, kk * 128:(kk + 1) * 128], stg.ap[:, kk * 128:(kk + 1) * 128], self.identF.ap[:, :], B(stg, self.identF), B(bk))
            self.cp(self.pT.ap[:, :, tg * 128:(tg + 1) * 128], bk.ap[:, 0:256].rearrange("p (k t) -> p k t", t=128), B(bk), B(self.pT), eng="act")
            self.sfree(stg)

    def s5_phase(self, L, tc):
        I = self.I; O = self.O
        TT, NB, NSEG, SEGB = tc.TT, tc.NB, tc.NSEG, tc.SEGB
        xT = self.xT
        Dt = self.salloc(4096, BF16)
        D5 = Dt.ap.rearrange("p (j g s c) -> p j g s c", j=32, g=2, s=4, c=16)
        for cs in range(4):
            ws = self.wl_in(L, OFF["u"] + 256 * cs, 256)
            for s in range(4):
                bk = self.bank()
                for kt in range(16):
                    self.mm(bk.ap[0:NB, 0:256], xT.ap[:, kt, s:TT:4], ws.ap[:, kt, :], kt == 0, kt == 15, B(xT, ws), B(bk))
                outv = D5[0:NB, 8 * cs:8 * cs + 8, :, s, :]
                inv = bk.ap[0:NB, 0:256].rearrange("p (j g c) -> p j g c", j=8, g=2, c=16)
                bv = self.bubc.ap[0:NB, 256 * cs:256 * cs + 256].rearrange("p (j g c) -> p j g c", j=8, g=2, c=16)
                self.tt(outv, inv, bv, ALU.add, B(bk, self.bubc), B(Dt))
        U2 = self.salloc(32 * NB, BF16)
        U23 = U2.ap.rearrange("p (j n) -> p j n", n=NB)
        per = 1024 // NB
        j = 0
        while j < 32:
            bk = self.bank()
            bkb = bk.ap.bitcast(BF16)
            nj = min(per, 32 - j)
            for jj in range(nj):
                self.tr(bkb[:, jj * NB:(jj + 1) * NB], Dt.ap[0:NB, (j + jj) * 128:(j + jj + 1) * 128], self.identB.ap[0:NB, 0:NB], B(Dt, self.identB), B(bk))
            self.cp(U2.ap[:, j * NB:(j + nj) * NB], bkb[:, 0:nj * NB], B(bk), B(U2), eng="act")
            j += nj
        self.sfree(Dt)
        wSr = self.wload_s5(L, 1)
        wSi = self.wload_s5(L, 2)
        HW = NSEG * (SEGB + 1)
        H = self.salloc(2 * 32 * HW, F32)
        H5 = H.ap.rearrange("p (a j q b) -> p a j q b", a=2, j=32, q=NSEG, b=SEGB + 1)
        if tc.kind == "p":
            if tc.first:
                self.memset(self.carry.ap[:], 0.0, B(self.carry))
            self.cp(H5[:, :, :, 0, 0], self.carry.ap[:, :, :], B(self.carry), B(H))
        else:
            for plane, nm in enumerate(("sre", "sim")):
                for r in range(4):
                    stg = self.salloc(128, F32)
                    self.dma("pool", stg.ap, I[nm][L, r * 128:(r + 1) * 128, :], [], B(stg))
                    bk = self.bank()
                    self.tr(bk.ap[:, 0:128], stg.ap, self.identF.ap[:, :], B(stg, self.identF), B(bk))
                    outv = H5[:, plane, :, 4 * r:4 * r + 4, 0]
                    inv = bk.ap[:, 0:128].rearrange("p (q j) -> p j q", q=4, j=32)
                    self.cp(outv, inv, B(bk), B(H))
                    self.sfree(stg)
        pairs_per_bank = 512 // NB
        for plane, wS in enumerate((wSr, wSi)):
            j = 0
            while j < 32:
                bk = self.bank()
                nj = min(pairs_per_bank, 32 - j)
                for jj in range(nj):
                    self.mm(bk.ap[:, jj * NB:(jj + 1) * NB], wS.ap[:, j + jj, :], U23[:, j + jj, :], True, True, B(wS, U2), B(bk))
                outv = H5[:, plane, j:j + nj, :, 1:SEGB + 1]
                inv = bk.ap[:, 0:nj * NB].rearrange("p (j q b) -> p j q b", j=nj, q=NSEG, b=SEGB)
                self.cp(outv, inv, B(bk), B(H), eng="act")
                j += nj
        if tc.kind == "p" and tc.first:
            self.preconvert(L, 4, 24)
        u = self.salloc(2 * 2 * 32 * NSEG, F32)
        u5 = u.ap.rearrange("p (r a j q) -> p r a j q", r=2, a=2, j=32, q=NSEG)
        T4 = self.T4
        for b in range(SEGB):
            if NSEG == 1:
                hb = H5[:, :, :, 0, b].rearrange("p (o a) j -> p o a j", o=1).to_broadcast([128, 2, 2, 32])
                self.tt(u5[:, :, :, :, 0], hb, T4.ap[:, L, :, :, :], ALU.mult, B(H, T4), B(u), eng=SCAN_ENG)
            else:
                for rpt in range(2):
                    tb = T4.ap[:, L, rpt, :, :].rearrange("p a (j o) -> p a j o", o=1).to_broadcast([128, 2, 32, NSEG])
                    self.tt(u5[:, rpt], H5[:, :, :, :, b], tb, ALU.mult, B(H, T4), B(u), eng=SCAN_ENG)
            self.tt(H5[:, :, :, :, b + 1], H5[:, :, :, :, b + 1], u5[:, 0], ALU.add, B(H, u), B(H), eng=SCAN_ENG)
            self.tt(H5[:, :, :, :, b + 1], H5[:, :, :, :, b + 1], u5[:, 1, ::-1], ALU.add, B(H, u), B(H), eng=SCAN_ENG)
        self.s5_ctx = (U2, U23, H, H5, u)

    def s5_back(self, L, tc):
        I = self.I; O = self.O
        TT, NB, NSEG, SEGB = tc.TT, tc.NB, tc.NSEG, tc.SEGB
        xT = self.xT
        U2, U23, H, H5, u = self.s5_ctx
        zsT = self.salloc(8 * TT, BF16)
        zs3 = zsT.ap.rearrange("p (k t) -> p k t", t=TT)
        for cs in range(4):
            ws = self.wl_in(L, OFF["z"] + 256 * cs, 256)
            for jj in range(2):
                ct = cs * 2 + jj
                bk = self.proj_fm(ws, jj, 16, self.xrhs(tc), xT, TT)
                self.act(zs3[:, ct, :], bk.ap[:, 0:TT], AF.Silu, B(bk, self.bcol), B(zsT), bias=self.bcol.ap[:, self.BC["z"] + ct:self.BC["z"] + ct + 1])
        self.sfree(u)
        Hbf = self.salloc(2 * 32 * NB, BF16)
        Hb4 = Hbf.ap.rearrange("p (a j n) -> p a j n", a=2, j=32, n=NB)
        for plane in range(2):
            outv = Hbf.ap[:, plane * 32 * NB:(plane + 1) * 32 * NB].rearrange("p (j q b) -> p j q b", j=32, q=NSEG, b=SEGB)
            self.cp(outv, H5[:, plane, :, :, 0:SEGB], B(H), B(Hbf), eng=("act" if plane else "dve"))
        if tc.kind == "p":
            self.cp(self.carry.ap[:, :, :], H5[:, :, :, 0, SEGB], B(H), B(self.carry))
            if tc.last:
                for plane, nm in enumerate(("sre_p", "sim_p")):
                    bk = self.bank()
                    self.tr(bk.ap[0:32, 0:128], self.carry.ap[:, plane, :], self.identF.ap[:, :], B(self.carry, self.identF), B(bk))
                    stg = self.salloc(128, F32)
                    self.cp(stg.ap[0:32, :], bk.ap[0:32, 0:128], B(bk), B(stg))
                    self.dma("sp", O[nm][L, :, :], stg.ap[0:32, :], B(stg), [], store=True)
                    self.sfree(stg)
        else:
            for plane, nm in enumerate(("sre_s", "sim_s")):
                fin = self.salloc(512, F32)
                self.cp(fin.ap.rearrange("p (q j) -> p j q", q=16, j=32), H5[:, plane, :, :, SEGB], B(H), B(fin))
                for r in range(4):
                    bk = self.bank()
                    self.tr(bk.ap[:, 0:128], fin.ap[:, r * 128:(r + 1) * 128], self.identF.ap[:, :], B(fin, self.identF), B(bk))
                    stg = self.salloc(128, F32)
                    self.cp(stg.ap, bk.ap[:, 0:128], B(bk), B(stg), eng="act")
                    self.dma("sp", O[nm][L, r * 128:(r + 1) * 128, :], stg.ap, B(stg), [], store=True)
                    self.sfree(stg)
                self.sfree(fin)
        self.sfree(H)
        wM = self.wload_s5(L, 0)
        wYr = self.wload_s5(L, 3)
        wYi = self.wload_s5(L, 4)
        ys = self.salloc(4096, BF16)
        ys3 = ys.ap.rearrange("p (t c) -> p t c", t=4)
        for j4 in range(8):
            bk = self.bank()
            for jj in range(4):
                j = j4 * 4 + jj
                o = bk.ap[0:NB, jj * 128:(jj + 1) * 128]
                self.mm(o, U23[:, j, :], wM.ap[:, j, :], True, False, B(U2, wM), B(bk))
                self.mm(o, Hb4[:, 0, j, :], wYr.ap[:, j, :], False, False, B(Hbf, wYr), B(bk))
                self.mm(o, Hb4[:, 1, j, :], wYi.ap[:, j, :], False, True, B(Hbf, wYi), B(bk))
            yf = self.salloc(512, F32); tq = self.salloc(512, F32)
            self.cp(yf.ap[0:NB, :], bk.ap[0:NB, :], B(bk), B(yf), eng="act")
            self.act(tq.ap[0:NB, :], bk.ap[0:NB, :], AF.Square, B(bk), B(tq))
            self.ts(tq.ap[0:NB, :], tq.ap[0:NB, :], 0.044715, 1.0, ALU.mult, ALU.add, B(tq), B(tq))
            self.tt(tq.ap[0:NB, :], tq.ap[0:NB, :], yf.ap[0:NB, :], ALU.mult, B(tq, yf), B(tq))
            self.act(tq.ap[0:NB, :], tq.ap[0:NB, :], AF.Sigmoid, B(tq), B(tq), scale=GELU_C)
            outv = ys3[0:NB, :, j4 * 128:(j4 + 1) * 128].rearrange("p t (j c) -> p j t c", j=4, c=32)
            self.tt(outv, yf.ap[0:NB, :].rearrange("p (j t c) -> p j t c", j=4, t=4, c=32),
                    tq.ap[0:NB, :].rearrange("p (j t c) -> p j t c", j=4, t=4, c=32), ALU.mult, B(yf, tq), B(ys))
            self.sfree(yf, tq)
        self.sfree(U2, Hbf)
        ysT = self.salloc(8 * TT, BF16)
        ysT3 = ysT.ap.rearrange("p (k t) -> p k t", t=TT)
        per = 1024 // NB
        items = [(t, ct) for ct in range(8) for t in range(4)]
        i = 0
        while i < len(items):
            bk = self.bank(); bkb = bk.ap.bitcast(BF16)
            grp = items[i:i + per]
            for gi, (t, ct) in enumerate(grp):
                self.tr(bkb[:, gi * NB:(gi + 1) * NB], ys3[0:NB, t, ct * 128:(ct + 1) * 128], self.identB.ap[0:NB, 0:NB], B(ys, self.identB), B(bk))
            for gi, (t, ct) in enumerate(grp):
                self.cp(ysT3[:, ct, t:TT:4], bkb[:, gi * NB:(gi + 1) * NB], B(bk), B(ysT), eng=("act" if gi % 2 else "dve"))
            i += per
        self.sfree(ys)
        for hs in range(2):
            ws = self.wl_mat("w_glu", L, 512 * hs, 512, 8)
            for jj in range(4):
                o = hs * 4 + jj
                bk = self.proj_fm(ws, jj, 8, lambda kt: ysT3[:, kt, :], ysT, TT)
                sg = self.salloc(TT, F32)
                self.act(sg.ap, bk.ap[:, 0:TT], AF.Sigmoid, B(bk, self.bglu), B(sg), bias=self.bglu.ap[:, o:o + 1])
                self.tt(sg.ap, sg.ap, ysT3[:, o, :], ALU.mult, B(sg, ysT), B(sg))
                self.tt(self.ys5g.ap[:, o, 0:TT], sg.ap, zs3[:, o, :], ALU.mult, B(sg, zsT), B(self.ys5g))
                self.sfree(sg)
        self.sfree(ysT, zsT)

    def wload_s5(self, L, idx):
        return self.wload(self.s5w[L, idx].rearrange("p (k n) -> p k n", n=128), 32, 128, rb=[self.s5w_buf[L][idx]])

    def gla_phase(self, L, tc):
        I = self.I; O = self.O
        TT, NCH, NE, EL = tc.TT, tc.NCH, tc.NE, tc.EL
        xT = self.xT; xr = self.xrhs(tc)
        BCq, BCk, BCgz = self.BC["q"], self.BC["k"], self.BC["gz"]
        ws = self.wl_in(L, OFF["al"], 16)
        bk = self.bank()
        for kt in range(16):
            self.mm(bk.ap[0:16, 0:TT], ws.ap[:, kt, :], xr(kt), kt == 0, kt == 15, B(ws, xT), B(bk))
        alT = self.salloc(TT, BF16)
        self.act(alT.ap[0:16, :], bk.ap[0:16, 0:TT], AF.Identity, B(bk, self.bal), B(alT), bias=self.bal.ap[:, 0:1])
        la = self.salloc(4 * TT, F32); cs = self.salloc(4 * TT, F32)
        la3 = la.ap.rearrange("p (h t) -> p h t", t=TT); cs3 = cs.ap.rearrange("p (h t) -> p h t", t=TT)
        coef = self.coefP if tc.kind == "p" else self.coefS
        for h in range(4):
            bk = self.bank()
            self.mm(bk.ap[:, 0:TT], self.wa2.ap[0:16, h * 128:(h + 1) * 128], alT.ap[0:16, :], True, True, B(self.wa2, alT), B(bk))
            self.act(la3[:, h, :], bk.ap[:, 0:TT], AF.Exp, B(bk, self.nba), B(la), bias=self.nba.ap[:, h:h + 1], scale=-1.0)
            self.act(la3[:, h, :], la3[:, h, :], AF.Ln, B(la), B(la), bias=1.0)
            self.scan(cs3[:, h, :], coef.ap[:, 0:TT], la3[:, h, :], B(coef, la), B(cs))
        self.sfree(alT, la)
        ecs = self.salloc(4 * TT, F32); encs = self.salloc(4 * TT, F32); el = self.salloc(4 * NE, F32)
        ecs3 = ecs.ap.rearrange("p (h t) -> p h t", t=TT); encs3 = encs.ap.rearrange("p (h t) -> p h t", t=TT)
        el3 = el.ap.rearrange("p (h e) -> p h e", e=NE)
        self.act(ecs.ap, cs.ap, AF.Exp, B(cs), B(ecs), scale=-1.0 / 16.0, bias=float(math.log(128.0 ** -0.5)))
        self.act(encs.ap, cs.ap, AF.Exp, B(cs), B(encs), scale=1.0 / 16.0)
        self.act(el3, cs3[:, :, EL - 1:TT:EL], AF.Exp, B(cs), B(el), scale=-1.0 / 16.0)
        self.sfree(cs)
        qd = self.salloc(4 * TT, BF16); kd = self.salloc(4 * TT, BF16)
        qd3 = qd.ap.rearrange("p (h t) -> p h t", t=TT); kd3 = kd.ap.rearrange("p (h t) -> p h t", t=TT)
        for nm, dst3, dstT, sc3, scT, bc0 in (("q", qd3, qd, ecs3, ecs, BCq), ("k", kd3, kd, encs3, encs, BCk)):
            for cs_ in range(2):
                ws = self.wl_in(L, OFF[nm] + 256 * cs_, 256)
                for jj in range(2):
                    h = cs_ * 2 + jj
                    bk = self.proj_fm(ws, jj, 16, xr, xT, TT)
                    self.stt(dst3[:, h, :], bk.ap[:, 0:TT], self.bcol.ap[:, bc0 + h:bc0 + h + 1], sc3[:, h, :], ALU.add, ALU.mult, B(bk, self.bcol, scT), B(dstT))
        self.sfree(ecs, encs)
        kk = self.salloc(4 * TT, BF16)
        kk3 = kk.ap.rearrange("p (h t) -> p h t", t=TT)
        for h in range(4):
            outv = kk3[:, h, :].rearrange("p (e t) -> p e t", t=EL)
            inv = kd3[:, h, :].rearrange("p (e t) -> p e t", t=EL)
            ev = el3[:, h, :].rearrange("p (e o) -> p e o", o=1).to_broadcast([128, NE, EL])
            self.tt(outv, inv, ev, ALU.mult, B(kd, el), B(kk))
        kkT = self.salloc(NCH * 512, BF16)
        kkT4 = kkT.ap.rearrange("p (c h d) -> p c h d", h=4, d=128)
        for ch2 in range(0, NCH, 2):
            bk = self.bank(); bkb = bk.ap.bitcast(BF16)
            for cc in range(2):
                for h in range(4):
                    c = ch2 + cc
                    self.tr(bkb[0:64, (cc * 4 + h) * 128:(cc * 4 + h + 1) * 128], kk3[:, h, c * 64:(c + 1) * 64], self.identB.ap[:, :], B(kk, self.identB), B(bk))
            self.cp(kkT.ap[0:64, ch2 * 512:(ch2 + 2) * 512], bkb[0:64, 0:1024], B(bk), B(kkT), eng="act")
        self.sfree(kk)
        vt = self.salloc(NCH * 1024, BF16)
        vt3 = vt.ap.rearrange("p (c v) -> p c v", v=1024)
        for cs_ in range(4):
            ws = self.wl_in(L, OFF["v"] + 256 * cs_, 256)
            for c in range(NCH):
                bk = self.bank()
                for kt in range(16):
                    self.mm(bk.ap[0:64, 0:256], xT.ap[:, kt, c * 64:(c + 1) * 64], ws.ap[:, kt, :], kt == 0, kt == 15, B(xT, ws), B(bk))
                self.tt(vt3[0:64, c, cs_ * 256:(cs_ + 1) * 256], bk.ap[0:64, 0:256], self.bvbc.ap[0:64, cs_ * 256:(cs_ + 1) * 256], ALU.add, B(bk, self.bvbc), B(vt))
        gz = self.salloc(8 * TT, BF16)
        gz3 = gz.ap.rearrange("p (k t) -> p k t", t=TT)
        for cs_ in range(4):
            ws = self.wl_in(L, OFF["gz"] + 256 * cs_, 256)
            for jj in range(2):
                ct = cs_ * 2 + jj
                bk = self.proj_fm(ws, jj, 16, xr, xT, TT)
                self.act(gz3[:, ct, :], bk.ap[:, 0:TT], AF.Silu, B(bk, self.bcol), B(gz), bias=self.bcol.ap[:, BCgz + ct:BCgz + ct + 1])
        o = self.salloc(8 * TT, F32)
        o3 = o.ap.rearrange("p (k t) -> p k t", t=TT)
        attT = self.salloc(NCH * 256, BF16)
        attT4 = attT.ap.rearrange("p (c h t) -> p c h t", h=4, t=64)
        Sst, Sbf = self.Sst, self.Sbf
        if tc.kind == "p" and tc.first:
            self.memset(Sst.ap[:], 0.0, B(Sst))
            self.memset(Sbf.ap[:], 0.0, B(Sbf))
        for c in range(NCH):
            csl = slice(c * 64, (c + 1) * 64)
            bkA = self.bank()
            for h in range(4):
                self.mm(bkA.ap[0:64, h * 64:(h + 1) * 64], kd3[:, h, csl], qd3[:, h, csl], True, True, B(kd, qd), B(bkA))
            if tc.kind == "p":
                mk = self.maskP.ap[:, :].rearrange("p (o t) -> p o t", o=1).to_broadcast([64, 4, 64]); mkT = self.maskP
            else:
                mk = self.maskS.ap[:, :, :].rearrange("p a b -> p (a b)").rearrange("p (o t) -> p o t", o=1).to_broadcast([64, 4, 64]); mkT = self.maskS
            self.tt(attT4[0:64, c, :, :], bkA.ap[0:64, 0:256].rearrange("p (h t) -> p h t", t=64), mk, ALU.mult, B(bkA, mkT), B(attT))
            bkO = self.bank()
            if tc.kind == "p":
                for h in range(4):
                    for vh in range(2):
                        oo = bkO.ap[:, (h * 2 + vh) * 64:(h * 2 + vh + 1) * 64]
                        self.mm(oo, vt3[0:64, c, h * 256 + vh * 128:h * 256 + (vh + 1) * 128], attT4[0:64, c, h, :], True, False, B(vt, attT), B(bkO))
                        self.mm(oo, Sbf.ap[:, h, vh * 128:(vh + 1) * 128], qd3[:, h, csl], False, True, B(Sbf, qd), B(bkO))
                self.cp(o3[:, :, csl], bkO.ap[:, :].rearrange("p (k t) -> p k t", t=64), B(bkO), B(o), eng="act")
                bkK = [self.bank(), self.bank()]
                for h in range(4):
                    self.mm(bkK[h // 2].ap[:, (h % 2) * 256:(h % 2 + 1) * 256], kkT4[0:64, c, h, :], vt3[0:64, c, h * 256:(h + 1) * 256], True, True, B(kkT, vt), B(bkK[h // 2]))
                for h in range(4):
                    self.stt(Sst.ap[:, h, :], Sst.ap[:, h, :], el3[:, h, c:c + 1], bkK[h // 2].ap[:, (h % 2) * 256:(h % 2 + 1) * 256], ALU.mult, ALU.add, B(Sst, el, bkK[h // 2]), B(Sst))
                self.cp(Sbf.ap[:], Sst.ap[:], B(Sst), B(Sbf), eng="act")
            else:
                S0f = self.salloc(8 * 1024, F32); S0b = self.salloc(8 * 1024, BF16)
                S0f4 = S0f.ap.rearrange("p (q h v) -> p q h v", q=8, h=4, v=256)
                S0b4 = S0b.ap.rearrange("p (q h v) -> p q h v", q=8, h=4, v=256)
                for q in range(8):
                    seq = c * 8 + q
                    self.dma("pool", S0f4[:, q, :, :], I["sgla"][L, seq].rearrange("h d v -> d h v"), [], B(S0f))
                self.cp(S0b.ap[:, 0:4096], S0f.ap[:, 0:4096], B(S0f), B(S0b), eng="act")
                self.cp(S0b.ap[:, 4096:8192], S0f.ap[:, 4096:8192], B(S0f), B(S0b), eng="dve")
                for h in range(4):
                    for vh in range(2):
                        oo = bkO.ap[:, (h * 2 + vh) * 64:(h * 2 + vh + 1) * 64]
                        self.mm(oo, vt3[0:64, c, h * 256 + vh * 128:h * 256 + (vh + 1) * 128], attT4[0:64, c, h, :], True, False, B(vt, attT), B(bkO))
                        for q in range(8):
                            tsl = slice(c * 64 + q * 8, c * 64 + q * 8 + 8)
                            self.mm(oo[:, q * 8:(q + 1) * 8], S0b4[:, q, h, vh * 128:(vh + 1) * 128], qd3[:, h, tsl], False, q == 7, B(S0b, qd), B(bkO))
                self.cp(o3[:, :, csl], bkO.ap[:, :].rearrange("p (k t) -> p k t", t=64), B(bkO), B(o), eng="act")
                for q in range(8):
                    seq = c * 8 + q
                    kkm = self.salloc(512, BF16)
                    self.ts(kkm.ap[0:64, :], kkT.ap[0:64, c * 512:(c + 1) * 512], self.rowm.ap[:, q:q + 1], None, ALU.mult, None, B(kkT, self.rowm), B(kkm))
                    bkK = [self.bank(), self.bank()]
                    for h in range(4):
                        self.mm(bkK[h // 2].ap[:, (h % 2) * 256:(h % 2 + 1) * 256], kkm.ap[0:64, h * 128:(h + 1) * 128], vt3[0:64, c, h * 256:(h + 1) * 256], True, True, B(kkm, vt), B(bkK[h // 2]))
                    Sn = self.salloc(1024, F32)
                    Sn3 = Sn.ap.rearrange("p (h v) -> p h v", v=256)
                    for h in range(4):
                        self.stt(Sn3[:, h, :], S0f4[:, q, h, :], el3[:, h, seq:seq + 1], bkK[h // 2].ap[:, (h % 2) * 256:(h % 2 + 1) * 256], ALU.mult, ALU.add, B(S0f, el, bkK[h // 2]), B(Sn))
                    self.dma("sp", O["gla_s"][L, seq].rearrange("h d v -> d h v"), Sn3, B(Sn), [], store=True)
                    self.sfree(kkm, Sn)
                self.sfree(S0f, S0b)
        if tc.kind == "p" and tc.last:
            self.dma("sp", O["gla_p"][L].rearrange("h d v -> d h v"), Sst.ap[:, :, :], B(Sst), [], store=True)
        self.sfree(attT, kkT, vt, qd, kd, el)
        for h in range(4):
            sq = self.salloc(2 * TT, F32)
            self.act(sq.ap, o.ap[:, 2 * h * TT:(2 * h + 2) * TT], AF.Square, B(o), B(sq))
            bkM = self.bank(); bkQ = self.bank()
            for vh in range(2):
                self.mm(bkM.ap[:, 0:TT], self.ones[256].ap[:, :], o3[:, 2 * h + vh, :], vh == 0, vh == 1, B(self.ones[256], o), B(bkM))
            for vh in range(2):
                self.mm(bkQ.ap[:, 0:TT], self.ones[256].ap[:, :], sq.ap[:, vh * TT:(vh + 1) * TT], vh == 0, vh == 1, B(self.ones[256], sq), B(bkQ))
            mean, rstd = self.ln_stats(bkM, bkQ, TT)
            self.sfree(sq)
            for vh in range(2):
                k = 2 * h + vh
                tmp = self.salloc(TT, F32)
                self.tt(tmp.ap, o3[:, k, :], mean.ap, ALU.subtract, B(o, mean), B(tmp))
                self.tt(tmp.ap, tmp.ap, rstd.ap, ALU.mult, B(tmp, rstd), B(tmp))
                self.stt(self.ogg.ap[:, k, 0:TT], tmp.ap, self.normg.ap[:, k:k + 1], gz3[:, k, :], ALU.mult, ALU.mult, B(tmp, self.normg, gz), B(self.ogg))
                self.sfree(tmp)
            self.sfree(mean, rstd)
        self.sfree(o, gz)

    def ln_stats(self, bkM, bkQ, TT):
        mean = self.salloc(TT, F32); rstd = self.salloc(TT, F32); m2 = self.salloc(TT, F32)
        self.cp(mean.ap, bkM.ap[:, 0:TT], B(bkM), B(mean), eng="act")
        self.act(m2.ap, bkM.ap[:, 0:TT], AF.Square, B(bkM), B(m2))
        self.tt(rstd.ap, bkQ.ap[:, 0:TT], m2.ap, ALU.subtract, B(bkQ, m2), B(rstd))
        self.ts(rstd.ap, rstd.ap, 0.0, None, ALU.max, None, B(rstd), B(rstd))
        self.act(rstd.ap, rstd.ap, AF.Sqrt, B(rstd), B(rstd), bias=LN_EPS)
        self.recip(rstd.ap, rstd.ap, B(rstd), B(rstd))
        self.sfree(m2)
        return mean, rstd

    def conv_phase(self, L, tc):
        I = self.I; O = self.O
        TT = tc.TT; xT = self.xT; xr = self.xrhs(tc)
        BCa, BCb, BCz = self.BC["ca"], self.BC["cb"], self.BC["cz"]
        if tc.kind == "p":
            GW = 30 + TT
            G = self.salloc(8 * GW, BF16)
            G3 = G.ap.rearrange("p (k t) -> p k t", t=GW)
            if tc.first:
                self.memset(self.halo.ap[:], 0.0, B(self.halo))
            self.cp(G3[:, :, 0:30], self.halo.ap[:, :, :], B(self.halo), B(G))
            gdst = lambda ct: G3[:, ct, 30:30 + TT]
        else:
            GW = 16 * 38
            G = self.salloc(8 * GW, F32)
            G4 = G.ap.rearrange("p (k q t) -> p k q t", q=16, t=38)
            for r in range(4):
                stg = self.salloc(1024, F32)
                self.dma("pool", stg.ap[0:120, :], I["cconv"][L, r * 120:(r + 1) * 120, :], [], B(stg))
                for c4 in range(2):
                    bk = self.bank()
                    for cc in range(4):
                        ct = c4 * 4 + cc
                        self.tr(bk.ap[:, cc * 120:(cc + 1) * 120], stg.ap[0:120, ct * 128:(ct + 1) * 128], self.identF.ap[0:120, 0:120], B(stg, self.identF), B(bk))
                    outv = G4[:, c4 * 4:(c4 + 1) * 4, 4 * r:4 * r + 4, 0:30]
                    inv = bk.ap[:, 0:480].rearrange("p (k q t) -> p k q t", k=4, q=4, t=30)
                    self.cp(outv, inv, B(bk), B(G))
                self.sfree(stg)
            gdst = lambda ct: G4[:, ct, :, 30:38]
        czs = self.salloc(8 * TT, BF16)
        czs3 = czs.ap.rearrange("p (k t) -> p k t", t=TT)
        for cs_ in range(4):
            wa = self.wl_in(L, OFF["ca"] + 256 * cs_, 256)
            wb = self.wl_in(L, OFF["cb"] + 256 * cs_, 256)
            for jj in range(2):
                ct = cs_ * 2 + jj
                bkb_ = self.proj_fm(wb, jj, 16, xr, xT, TT)
                sg = self.salloc(TT, F32)
                self.act(sg.ap, bkb_.ap[:, 0:TT], AF.Sigmoid, B(bkb_, self.bcol), B(sg), bias=self.bcol.ap[:, BCb + ct:BCb + ct + 1])
                bka = self.proj_fm(wa, jj, 16, xr, xT, TT)
                if tc.kind == "p":
                    self.stt(gdst(ct), bka.ap[:, 0:TT], self.bcol.ap[:, BCa + ct:BCa + ct + 1], sg.ap, ALU.add, ALU.mult, B(bka, self.bcol, sg), B(G))
                    self.stt(self.halo.ap[:, ct, :], bka.ap[:, TT - 30:TT], self.bcol.ap[:, BCa + ct:BCa + ct + 1], sg.ap[:, TT - 30:TT], ALU.add, ALU.mult, B(bka, self.bcol, sg), B(self.halo))
                else:
                    self.stt(gdst(ct), bka.ap[:, 0:TT].rearrange("p (q t) -> p q t", t=8), self.bcol.ap[:, BCa + ct:BCa + ct + 1],
                             sg.ap.rearrange("p (q t) -> p q t", t=8), ALU.add, ALU.mult, B(bka, self.bcol, sg), B(G))
                self.sfree(sg)
        for cs_ in range(4):
            ws = self.wl_in(L, OFF["cz"] + 256 * cs_, 256)
            for jj in range(2):
                ct = cs_ * 2 + jj
                bk = self.proj_fm(ws, jj, 16, xr, xT, TT)
                self.act(czs3[:, ct, :], bk.ap[:, 0:TT], AF.Silu, B(bk, self.bcol), B(czs), bias=self.bcol.ap[:, BCz + ct:BCz + ct + 1])
        acc = self.salloc(8 * TT, F32)
        acc3 = acc.ap.rearrange("p (k t) -> p k t", t=TT)
        cw = self.convw
        if tc.kind == "p":
            for ct in range(8):
                dg = self.salloc(31 * 128, BF16)
                dg3 = dg.ap.rearrange("p (k m) -> p k m", m=128)
                idb = self.identB.ap[:, :].rearrange("p (o m) -> p o m", o=1).to_broadcast([128, 31, 128])
                wv = cw.ap[:, ct, :].rearrange("p (k o) -> p k o", o=1).to_broadcast([128, 31, 128])
                self.tt(dg3, idb, wv, ALU.mult, B(self.identB, cw), B(dg))
                bk = self.bank()
                for k in range(31):
                    self.mm(bk.ap[:, 0:TT], dg3[:, k, :], G3[:, ct, k:k + TT], k == 0, k == 30, B(dg, G), B(bk))
                self.act(acc3[:, ct, :], bk.ap[:, 0:TT], AF.Identity, B(bk, self.convb), B(acc), bias=self.convb.ap[:, ct:ct + 1])
                self.sfree(dg)
        else:
            for ct in range(8):
                src = lambda k: G4[:, ct, :, k:k + 8]
                dst = acc3[:, ct, :].rearrange("p (q t) -> p q t", t=8)
                self.ts(dst, src(0), cw.ap[:, ct, 0:1], self.convb.ap[:, ct:ct + 1], ALU.mult, ALU.add, B(G, cw, self.convb), B(acc))
                for k in range(1, 31):
                    self.stt(dst, src(k), cw.ap[:, ct, k:k + 1], dst, ALU.mult, ALU.add, B(G, cw, acc), B(acc))
        if tc.kind == "p":
            if tc.last:
                stg = self.salloc(1024, F32)
                for c4 in range(2):
                    bk = self.bank()
                    for cc in range(4):
                        ct = c4 * 4 + cc
                        self.tr(bk.ap[0:30, cc * 128:(cc + 1) * 128], self.halo.ap[:, ct, :], self.identF.ap[:, :], B(self.halo, self.identF), B(bk))
                    self.cp(stg.ap[0:30, c4 * 512:(c4 + 1) * 512], bk.ap[0:30, :], B(bk), B(stg), eng="act")
                self.dma("sp", O["conv_p"][L, :, :], stg.ap[0:30, :], B(stg), [], store=True)
                self.sfree(stg)
        else:
            cn = self.salloc(8 * 480, F32)
            cn4 = cn.ap.rearrange("p (k q t) -> p k q t", q=16, t=30)
            self.cp(cn4, G4[:, :, :, 8:38], B(G), B(cn), eng="act")
            for r in range(4):
                stg = self.salloc(1024, F32)
                for c4 in range(2):
                    bk = self.bank()
                    for cc in range(4):
                        ct = c4 * 4 + cc
                        self.tr(bk.ap[0:120, cc * 128:(cc + 1) * 128], cn.ap[:, ct * 480 + r * 120:ct * 480 + (r + 1) * 120], self.identF.ap[:, :], B(cn, self.identF), B(bk))
                    self.cp(stg.ap[0:120, c4 * 512:(c4 + 1) * 512], bk.ap[0:120, :], B(bk), B(stg), eng="act")
                self.dma("sp", O["conv_s"][L, r * 120:(r + 1) * 120, :], stg.ap[0:120, :], B(stg), [], store=True)
                self.sfree(stg)
            self.sfree(cn)
        self.sfree(G)
        bkM = self.bank(); bkQ = self.bank()
        for ct in range(8):
            sq = self.salloc(TT, F32)
            self.act(sq.ap, acc3[:, ct, :], AF.Square, B(acc), B(sq))
            self.mm(bkM.ap[:, 0:TT], self.ones[1024].ap[:, :], acc3[:, ct, :], ct == 0, ct == 7, B(self.ones[1024], acc), B(bkM))
            self.mm(bkQ.ap[:, 0:TT], self.ones[1024].ap[:, :], sq.ap, ct == 0, ct == 7, B(self.ones[1024], sq), B(bkQ))
            self.sfree(sq)
        mean, rstd = self.ln_stats(bkM, bkQ, TT)
        for ct in range(8):
            tmp = self.salloc(TT, F32)
            self.tt(tmp.ap, acc3[:, ct, :], mean.ap, ALU.subtract, B(acc, mean), B(tmp))
            self.tt(tmp.ap, tmp.ap, rstd.ap, ALU.mult, B(tmp, rstd), B(tmp))
            self.act(tmp.ap, tmp.ap, AF.Silu, B(tmp, self.clng, self.clnb), B(tmp), bias=self.clnb.ap[:, ct:ct + 1], scale=self.clng.ap[:, ct:ct + 1])
            self.tt(self.ycg.ap[:, ct, 0:TT], tmp.ap, czs3[:, ct, :], ALU.mult, B(tmp, czs), B(self.ycg))
            self.sfree(tmp)
        self.sfree(mean, rstd, acc, czs)

    def merge_phase(self, L, tc):
        I = self.I; O = self.O
        TT = tc.TT; xT = self.xT; xr = self.xrhs(tc)
        NL = self.NL
        BCg = self.BC["g"]
        merged = self.salloc(16 * TT, BF16)
        mg3 = merged.ap.rearrange("p (k t) -> p k t", t=TT)
        branches = [("p_s5", self.ys5g, 0), ("p_gla", self.ogg, 1), ("p_conv", self.ycg, 2)]
        for jo4 in range(4):
            macc = self.salloc(4 * TT, F32)
            macc3 = macc.ap.rearrange("p (k t) -> p k t", t=TT)
            for (wn, br, bi) in branches:
                wp = self.wl_mat(wn, L, 512 * jo4, 512, 8)
                for half in range(2):
                    gcol = OFF["g"] + bi * 2048 + jo4 * 512 + half * 256
                    wg = self.wl_in(L, gcol, 256)
                    for jj in range(2):
                        jl = half * 2 + jj
                        jo = jo4 * 4 + jl
                        bkG = self.proj_fm(wg, jj, 16, xr, xT, TT)
                        sg = self.salloc(TT, F32)
                        bcix = BCg + bi * 16 + jo
                        self.act(sg.ap, bkG.ap[:, 0:TT], AF.Sigmoid, B(bkG, self.bcol), B(sg), bias=self.bcol.ap[:, bcix:bcix + 1])
                        bkA = self.proj_fm(wp, jl, 8, lambda kt, br=br: br.ap[:, kt, 0:TT], br, TT)
                        if bi == 0:
                            self.tt(macc3[:, jl, :], bkA.ap[:, 0:TT], sg.ap, ALU.mult, B(bkA, sg), B(macc))
                        else:
                            self.tt(sg.ap, bkA.ap[:, 0:TT], sg.ap, ALU.mult, B(bkA, sg), B(sg))
                            if bi == 1:
                                self.tt(macc3[:, jl, :], macc3[:, jl, :], sg.ap, ALU.add, B(macc, sg), B(macc))
                            else:
                                self.tt(mg3[:, jo, :], macc3[:, jl, :], sg.ap, ALU.add, B(macc, sg), B(merged))
                        self.sfree(sg)
            self.sfree(macc)
        hT = self.salloc(16 * TT, F32); hbf = self.salloc(16 * TT, BF16)
        h3 = hT.ap.rearrange("p (k t) -> p k t", t=TT); hb3 = hbf.ap.rearrange("p (k t) -> p k t", t=TT)
        for cs_ in range(8):
            ws = self.wl_mat("w_o", L, 256 * cs_, 256, 16)
            for jj in range(2):
                jo = cs_ * 2 + jj
                bk = self.proj_fm(ws, jj, 16, lambda kt: mg3[:, kt, :], merged, TT)
                self.stt(h3[:, jo, :], xT.ap[:, jo, 0:TT], float(DN_ALPHA), bk.ap[:, 0:TT], ALU.mult, ALU.add, B(xT, bk), B(hT))
                self.cp(hb3[:, jo, :], h3[:, jo, :], B(hT), B(hbf), eng="act")
        self.sfree(merged)
        wpe = self.wload(self.I["w_pe"][L].rearrange("(k p) n -> p k n", p=128), 2, 2048, key=("w_pe", L, 0, 2048))
        pe_all = self.salloc(16 * TT, BF16)
        pe3 = pe_all.ap.rearrange("p (k t) -> p k t", t=TT)
        for jo in range(16):
            bk = self.bank()
            for kt in range(2):
                self.mm(bk.ap[:, 0:TT], wpe.ap[:, kt, jo * 128:(jo + 1) * 128], self.pT.ap[:, kt, 0:TT], kt == 0, kt == 1, B(wpe, self.pT), B(bk))
            self.cp(pe3[:, jo, :], bk.ap[:, 0:TT], B(bk), B(pe_all), eng="act")
        for cs_ in range(8):
            ws = self.wl_mat("w_pg", L, 256 * cs_, 256, 16)
            for jj in range(2):
                jo = cs_ * 2 + jj
                bk = self.proj_fm(ws, jj, 16, lambda kt: hb3[:, kt, :], hbf, TT)
                sg = self.salloc(TT, F32)
                self.act(sg.ap, bk.ap[:, 0:TT], AF.Sigmoid, B(bk), B(sg))
                self.tt(sg.ap, sg.ap, pe3[:, jo, :], ALU.mult, B(sg, pe_all), B(sg))
                self.tt(h3[:, jo, :], h3[:, jo, :], sg.ap, ALU.add, B(hT, sg), B(hT))
                self.sfree(sg)
        self.sfree(hbf, pe_all)
        bkM = self.bank(); bkQ = self.bank()
        for jo in range(16):
            sq = self.salloc(TT, F32)
            self.act(sq.ap, h3[:, jo, :], AF.Square, B(hT), B(sq))
            self.mm(bkM.ap[:, 0:TT], self.ones[2048].ap[:, :], h3[:, jo, :], jo == 0, jo == 15, B(self.ones[2048], hT), B(bkM))
            self.mm(bkQ.ap[:, 0:TT], self.ones[2048].ap[:, :], sq.ap, jo == 0, jo == 15, B(self.ones[2048], sq), B(bkQ))
            self.sfree(sq)
        mean, rstd = self.ln_stats(bkM, bkQ, TT)
        last = (L == NL - 1)
        if not last:
            xo = self.salloc(16 * 512, BF16)
            xo3 = xo.ap.rearrange("p (k t) -> p k t", t=512)
        for jo in range(16):
            self.tt(h3[:, jo, :], h3[:, jo, :], mean.ap, ALU.subtract, B(hT, mean), B(hT))
            self.tt(h3[:, jo, :], h3[:, jo, :], rstd.ap, ALU.mult, B(hT, rstd), B(hT))
            if last:
                self.act(h3[:, jo, :], h3[:, jo, :], AF.Identity, B(hT, self.lng, self.lnb), B(hT), bias=self.lnb.ap[:, jo:jo + 1], scale=self.lng.ap[:, jo:jo + 1])
            else:
                self.act(xo3[:, jo, 0:TT], h3[:, jo, :], AF.Identity, B(hT, self.lng, self.lnb), B(xo), bias=self.lnb.ap[:, jo:jo + 1], scale=self.lng.ap[:, jo:jo + 1])
        self.sfree(mean, rstd)
        if not last:
            par = L % 2
            dst = self.spill[par, tc.idx].rearrange("p (k t) -> p k t", t=512)[:, :, 0:TT]
            self.dma("pool", dst, xo3[:, :, 0:TT], B(xo), [self.spill_buf[par][tc.idx]])
            self.sfree(xo)
        else:
            dsto = O["yp"] if tc.kind == "p" else O["ys"]
            for tg in range(TT // 128):
                stg = self.salloc(2048, F32)
                for k4 in range(4):
                    bk = self.bank()
                    for kk in range(4):
                        jo = k4 * 4 + kk
                        self.tr(bk.ap[:, kk * 128:(kk + 1) * 128], h3[:, jo, tg * 128:(tg + 1) * 128], self.identF.ap[:, :], B(hT, self.identF), B(bk))
                    self.cp(stg.ap[:, k4 * 512:(k4 + 1) * 512], bk.ap[:, :], B(bk), B(stg), eng=("act" if k4 % 2 else "dve"))
                r0 = tc.tok0 + tg * 128
                self.dma("sp", dsto[r0:r0 + 128, :], stg.ap, B(stg), [], store=True)
                self.sfree(stg)
        self.sfree(hT)


_CACHE = {}


def _get_prog(NL, NPT):
    key = (NL, NPT)
    if key not in _CACHE:
        _CACHE[key] = Builder(NL, NPT).build()
    return _CACHE[key]


def core_inputs(inputs, c, NL, NPT, prompt_b, seq0):
    f = lambda a: np.ascontiguousarray(np.asarray(a, dtype=np.float32))
    TP = NPT * 512
    m = {}
    m["xp"] = f(inputs["x_prompt"][prompt_b, :TP])
    m["xs"] = f(inputs["x_sample"][seq0:seq0 + 16]).reshape(128, D)
    m["pp"] = f(inputs["p_prompt"][:NL, prompt_b, :TP])
    m["pps"] = f(inputs["p_sample"][:NL, seq0:seq0 + 16]).reshape(NL, 128, 256)
    m["sre"] = f(inputs["state_s5_re"][:NL, seq0:seq0 + 16]).reshape(NL, 512, 128)
    m["sim"] = f(inputs["state_s5_im"][:NL, seq0:seq0 + 16]).reshape(NL, 512, 128)
    m["sgla"] = f(inputs["state_gla"][:NL, seq0:seq0 + 16])
    m["cconv"] = f(inputs["cache_conv"][:NL, seq0:seq0 + 16]).reshape(NL, 480, 1024)
    return m


def kernel(**inputs):
    NL, NPT = 4, 4
    nc = _get_prog(NL, NPT)
    wshared = {n: np.ascontiguousarray(np.asarray(inputs[n], dtype=np.float32)[:NL]) for n in W_NAMES}
    in_maps = []
    for c in range(8):
        m = core_inputs(inputs, c, NL, NPT, c // 2, 16 * c)
        m.update(wshared)
        in_maps.append(m)
    res = run_bass_kernel_spmd(nc, in_maps, core_ids=list(range(8)))
    R = res.results
    y_p = np.stack([R[2 * b]["yp"] for b in range(4)]).reshape(4, 2048, D)
    y_s = np.concatenate([R[c]["ys"].reshape(16, 8, D) for c in range(8)], axis=0)
    s5re_p = np.stack([R[2 * b]["sre_p"].reshape(NL, 64, 64) for b in range(4)], axis=1)
    s5im_p = np.stack([R[2 * b]["sim_p"].reshape(NL, 64, 64) for b in range(4)], axis=1)
    gla_p = np.stack([R[2 * b]["gla_p"] for b in range(4)], axis=1)
    conv_p = np.stack([R[2 * b]["conv_p"] for b in range(4)], axis=1)
    s5re_s = np.concatenate([R[c]["sre_s"].reshape(NL, 16, 64, 64) for c in range(8)], axis=1)
    s5im_s = np.concatenate([R[c]["sim_s"].reshape(NL, 16, 64, 64) for c in range(8)], axis=1)
    gla_s = np.concatenate([R[c]["gla_s"] for c in range(8)], axis=1)
    conv_s = np.concatenate([R[c]["conv_s"].reshape(NL, 16, 30, 1024) for c in range(8)], axis=1)
    outs = (y_p, y_s, s5re_p, s5im_p, gla_p, conv_p, s5re_s, s5im_s, gla_s, conv_s)
    return tuple(np.ascontiguousarray(o, dtype=np.float32) for o in outs)
```

```python
import math
from contextlib import ExitStack

import numpy as np
import concourse.bass as bass
import concourse.mybir as mybir
from concourse.bass_utils import run_bass_kernel_spmd

F32 = mybir.dt.float32
BF16 = mybir.dt.bfloat16
AF = mybir.ActivationFunctionType
ALU = mybir.AluOpType

ENGS = ["pe", "act", "dve", "pool", "sp"]
SAME_ENGINE_SYNC = True


class Buf:
    __slots__ = ("writers", "readers")

    def __init__(self):
        self.writers = {}
        self.readers = {}


class Op:
    __slots__ = ("id", "eng", "fn", "deps", "dma", "inc", "count", "semi", "target", "nd")

    def __init__(self, id, eng, fn, deps, dma, nd):
        self.id = id; self.eng = eng; self.fn = fn; self.deps = deps; self.dma = dma
        self.inc = False; self.count = 0; self.semi = -1; self.target = 0; self.nd = nd


class Prog:
    def __init__(self, n_dma_sems=40):
        self.ops = []
        self.n_dma_sems = n_dma_sems

    def add(self, eng, fn, r=(), w=(), dma=False, nd=1, extra=()):
        deps = set(extra)
        for b in r:
            deps.update(b.writers.values())
        for b in w:
            deps.update(b.writers.values())
            deps.update(b.readers.values())
        oid = len(self.ops)
        self.ops.append(Op(oid, eng, fn, deps, dma, nd))
        key = ("d", oid) if dma else eng
        for b in r:
            b.readers[key] = oid
        for b in w:
            if b.readers:
                b.writers = {key: oid}
                b.readers = {}
            else:
                b.writers[key] = oid
        return oid

    def emit(self, block, sems, dma_sems):
        ops = self.ops

        def skip(dop, op):
            return (not dop.dma) and (not op.dma) and dop.eng == op.eng and (op.eng == "pe" or not SAME_ENGINE_SYNC)

        for op in ops:
            for d in op.deps:
                dop = ops[d]
                if dop.dma or skip(dop, op):
                    continue
                dop.inc = True
        cnt = {e: 0 for e in ENGS}
        dtarget = [0] * self.n_dma_sems
        dlast = [None] * self.n_dma_sems
        half = self.n_dma_sems // 2
        rrs = {"pool": 0, "sp": 0}
        for op in ops:
            if op.dma:
                base = 0 if op.eng == "pool" else half
                rr = base + rrs[op.eng]
                rrs[op.eng] = (rrs[op.eng] + 1) % half
                op.semi = rr
                if dlast[rr] is not None:
                    op.deps.add(dlast[rr])
                dtarget[rr] += 16 * op.nd
                op.target = dtarget[rr]
                dlast[rr] = op.id
            elif op.inc:
                cnt[op.eng] += 1
                op.count = cnt[op.eng]
        per_eng = {e: [] for e in ENGS}
        for op in ops:
            per_eng[op.eng].append(op)
        self.counts = cnt

        def run(ename, eh):
            waited = {}
            for op in per_eng[ename]:
                for d in sorted(op.deps):
                    dop = ops[d]
                    if dop.dma:
                        s = dma_sems[dop.semi]; v = dop.target; k = ("d", dop.semi)
                    else:
                        if skip(dop, op):
                            continue
                        s = sems[dop.eng]; v = dop.count; k = dop.eng
                    if waited.get(k, 0) >= v:
                        continue
                    waited[k] = v
                    eh.wait_ge(s, v)
                ins = op.fn(eh)
                if ins is None:
                    continue
                if op.dma:
                    for i in ins:
                        i.then_inc(dma_sems[op.semi], 16)
                elif op.inc:
                    ins.then_inc(sems[ename], 1)

        @block.tensor
        def _(e):
            run("pe", e)

        @block.scalar
        def _(e):
            run("act", e)

        @block.vector
        def _(e):
            run("dve", e)

        @block.gpsimd
        def _(e):
            run("pool", e)

        @block.sync
        def _(e):
            run("sp", e)


class T:
    __slots__ = ("ap", "bufs", "slabs")

    def __init__(self, ap, bufs, slabs=None):
        self.ap = ap; self.bufs = bufs; self.slabs = slabs


def B(*ts):
    out = []
    for t in ts:
        if t is None:
            continue
        out.extend(t.bufs)
    return out


D = 2048
NIN = 14352
OFF = dict(u=0, z=1024, q=2048, k=2560, v=3072, al=4096, gz=4112, ca=5136, cb=6160, cz=7184, g=8208)
DN_ALPHA = (2 * 4) ** 0.25
LN_EPS = 1e-5
TWO_PI = 2.0 * math.pi
MAGIC = 12582912.0
GELU_C = 1.5957691216057308
SLAB = 2048
NSLAB = 46
WSLOT_ELEMS = 4096
NWSLOT = 4
SCAN_ENG = "pool"
PRECONVERT = False

W_NAMES = ["w_in", "b_in", "s5_a_re", "s5_a_im", "s5_log_dt", "s5_b_re", "s5_b_im", "s5_c_re", "s5_c_im", "s5_d",
           "w_glu", "b_glu", "gla_w_a2", "gla_b_a", "gla_norm_g", "conv_w", "conv_b", "conv_ln_g", "conv_ln_b",
           "p_s5", "p_gla", "p_conv", "w_o", "w_pg", "w_pe", "ln_g", "ln_b"]


class TileCfg:
    def __init__(self, kind, idx, TT, tok0, first, last):
        self.kind = kind; self.idx = idx; self.TT = TT; self.tok0 = tok0; self.first = first; self.last = last
        self.NB = TT // 4
        if kind == "p":
            self.NSEG = 1; self.SEGB = self.NB; self.NCH = TT // 64; self.NE = self.NCH; self.EL = 64
        else:
            self.NSEG = 16; self.SEGB = 2; self.NCH = 2; self.NE = 16; self.EL = 8


class Builder:
    def __init__(self, NL, NPT, debug=None):
        self.NL = NL; self.NPT = NPT; self.TP = NPT * 512
        self.debug = debug or {}
        self.nc = bass.Bass("TRN2", target_bir_lowering=False)
        self.P = Prog()
        self.st = ExitStack()
        self.store_ops = []
        self.bank_i = 0
        self.bank_alloc = [None] * 8
        self.wslot_i = 0
        self.uid = 0

    def sb(self, shape, dt=F32, name=None):
        self.uid += 1
        h = self.st.enter_context(self.nc.sbuf_tensor(name or ("t%d" % self.uid), shape, dt))
        return h

    def pt(self, shape, dt=F32):
        h = self.sb(shape, dt)
        return T(h, [Buf()])

    def dram_in(self, name, shape, dt=F32):
        return self.nc.dram_tensor(name, list(shape), dt, kind="ExternalInput").ap()

    def dram_out(self, name, shape, dt=F32):
        return self.nc.dram_tensor(name, list(shape), dt, kind="ExternalOutput").ap()

    def salloc(self, nelem, dt):
        esz = 4 if dt == F32 else 2
        nb = nelem * esz
        ns = (nb + SLAB - 1) // SLAB
        free = self.slab_free
        for s0 in range(0, NSLAB - ns + 1):
            if all(free[s0:s0 + ns]):
                for i in range(s0, s0 + ns):
                    free[i] = False
                if dt == F32:
                    ap = self.scrF[:, s0 * (SLAB // 4): s0 * (SLAB // 4) + nelem]
                else:
                    ap = self.scrB[:, s0 * (SLAB // 2): s0 * (SLAB // 2) + nelem]
                return T(ap, self.slab_bufs[s0:s0 + ns], (s0, ns))
        raise RuntimeError("scratch exhausted: need %d slabs, free map %s" % (ns, "".join("1" if f else "0" for f in free)))

    def sfree(self, *ts):
        for t in ts:
            s0, ns = t.slabs
            for i in range(s0, s0 + ns):
                assert not self.slab_free[i]
                self.slab_free[i] = True

    def bank(self):
        for k in range(8):
            i = (self.bank_i + k) % 8
            b = self.banks[i].bufs[0]
            a = self.bank_alloc[i]
            free = a is None or (b.writers and max(b.writers.values()) >= a and len(b.readers) > 0)
            if free:
                self.bank_alloc[i] = len(self.P.ops)
                self.bank_i = (i + 1) % 8
                return self.banks[i]
        raise RuntimeError("no free PSUM bank")

    def mm(self, out, lhsT, rhs, start, stop, r, w):
        self.P.add("pe", lambda e: e.matmul(out, lhsT=lhsT, rhs=rhs, start=start, stop=stop), r=r, w=w)

    def tr(self, out, in_, ident, r, w):
        self.P.add("pe", lambda e: e.transpose(out, in_, ident), r=r, w=w)

    def act(self, out, in_, func, r, w, bias=None, scale=None):
        kw = {}
        if bias is not None:
            kw["bias"] = bias
        if scale is not None:
            kw["scale"] = scale
        self.P.add("act", lambda e: e.activation(out=out, in_=in_, func=func, **kw), r=r, w=w)

    def tt(self, out, in0, in1, op, r, w, eng="dve"):
        self.P.add(eng, lambda e: e.tensor_tensor(out=out, in0=in0, in1=in1, op=op), r=r, w=w)

    def ts(self, out, in0, s1, s2, op0, op1, r, w, eng="dve"):
        if s2 is None:
            self.P.add(eng, lambda e: e.tensor_scalar(out=out, in0=in0, scalar1=s1, scalar2=None, op0=op0), r=r, w=w)
        else:
            self.P.add(eng, lambda e: e.tensor_scalar(out=out, in0=in0, scalar1=s1, scalar2=s2, op0=op0, op1=op1), r=r, w=w)

    def stt(self, out, in0, scalar, in1, op0, op1, r, w, eng="dve"):
        self.P.add(eng, lambda e: e.scalar_tensor_tensor(out=out, in0=in0, scalar=scalar, in1=in1, op0=op0, op1=op1), r=r, w=w)

    def cp(self, out, in_, r, w, eng="dve"):
        if eng == "act":
            self.act(out, in_, AF.Identity, r, w)
        else:
            self.P.add(eng, lambda e: e.tensor_copy(out=out, in_=in_), r=r, w=w)

    def memset(self, ap, val, w, eng="dve"):
        self.P.add(eng, lambda e: e.memset(ap, val), w=w)

    def recip(self, out, in_, r, w):
        self.P.add("dve", lambda e: e.reciprocal(out=out, in_=in_), r=r, w=w)

    def scan(self, out, d0, d1, r, w):
        self.P.add("dve", lambda e: e.tensor_tensor_scan(out=out, data0=d0, data1=d1, initial=0.0, op0=ALU.mult, op1=ALU.add), r=r, w=w)

    def asel(self, ap, pattern, op, base, cm, w):
        self.P.add("pool", lambda e: e.affine_select(out=ap, in_=ap, pattern=pattern, compare_op=op, fill=0.0, base=base, channel_multiplier=cm), r=w, w=w)

    def dma(self, eng, out, in_, r, w, nc_ok=False, store=False):
        nc = self.nc
        if nc_ok:
            def fn(e):
                with nc.allow_non_contiguous_dma(reason="small strided parameter / layout load"):
                    return [e.dma_start(out=out, in_=in_)]
        else:
            def fn(e):
                return [e.dma_start(out=out, in_=in_)]
        if store and eng == "sp":
            eng = "pool"
        oid = self.P.add(eng, fn, r=r, w=w, dma=True)
        if store:
            self.store_ops.append(oid)
        return oid

    def wload(self, src, KT, NC, rb=(), key=None):
        slot = self.wslots[self.wslot_i]
        self.wslot_i = (self.wslot_i + 1) % NWSLOT
        view = slot.ap[:, 0:KT * NC].rearrange("p (k n) -> p k n", n=NC)
        if key is None:
            self.dma("sp", view, src, r=list(rb), w=B(slot))
            return T(view, slot.bufs)
        Lk = key[1]
        if key not in self.wblk:
            bi = self.wblk_n[Lk]
            self.wblk_n[Lk] += 1
            assert bi < self.NBLK, "too many weight blocks"
            bb = Buf()
            self.wblk[key] = (bi, bb)
            dstv = self.wscr[Lk][bi, :, 0:KT * NC].rearrange("p (k n) -> p k n", n=NC)
            self.dma("pool", dstv, src, r=[], w=[bb])
        bi, bb = self.wblk[key]
        self.dma("sp", slot.ap[:, 0:KT * NC], self.wscr[Lk][bi, :, 0:KT * NC], r=[bb], w=B(slot))
        return T(view, slot.bufs)

    def build(self):
        nc = self.nc; NL = self.NL; TP = self.TP
        I = {}
        I["xp"] = self.dram_in("xp", [TP, D]); I["xs"] = self.dram_in("xs", [128, D])
        I["pp"] = self.dram_in("pp", [NL, TP, 256]); I["pps"] = self.dram_in("pps", [NL, 128, 256])
        I["sre"] = self.dram_in("sre", [NL, 512, 128]); I["sim"] = self.dram_in("sim", [NL, 512, 128])
        I["sgla"] = self.dram_in("sgla", [NL, 16, 4, 128, 256]); I["cconv"] = self.dram_in("cconv", [NL, 480, 1024])
        shapes = dict(w_in=[NL, D, NIN], b_in=[NL, NIN], s5_a_re=[NL, 64, 64], s5_a_im=[NL, 64, 64], s5_log_dt=[NL, 64],
                      s5_b_re=[NL, 64, 64, 16], s5_b_im=[NL, 64, 64, 16], s5_c_re=[NL, 64, 16, 64], s5_c_im=[NL, 64, 16, 64],
                      s5_d=[NL, 1024], w_glu=[NL, 1024, 1024], b_glu=[NL, 1024], gla_w_a2=[NL, 16, 512], gla_b_a=[NL, 512],
                      gla_norm_g=[NL, 1024], conv_w=[NL, 31, 1024], conv_b=[NL, 1024], conv_ln_g=[NL, 1024], conv_ln_b=[NL, 1024],
                      p_s5=[NL, 1024, D], p_gla=[NL, 1024, D], p_conv=[NL, 1024, D], w_o=[NL, D, D], w_pg=[NL, D, D],
                      w_pe=[NL, 256, D], ln_g=[NL, D], ln_b=[NL, D])
        for n in W_NAMES:
            I[n] = self.dram_in(n, shapes[n])
        O = {}
        O["yp"] = self.dram_out("yp", [TP, D]); O["ys"] = self.dram_out("ys", [128, D])
        O["sre_p"] = self.dram_out("sre_p", [NL, 32, 128]); O["sim_p"] = self.dram_out("sim_p", [NL, 32, 128])
        O["gla_p"] = self.dram_out("gla_p", [NL, 4, 128, 256]); O["conv_p"] = self.dram_out("conv_p", [NL, 30, 1024])
        O["sre_s"] = self.dram_out("sre_s", [NL, 512, 128]); O["sim_s"] = self.dram_out("sim_s", [NL, 512, 128])
        O["gla_s"] = self.dram_out("gla_s", [NL, 16, 4, 128, 256]); O["conv_s"] = self.dram_out("conv_s", [NL, 480, 1024])
        self.I = I; self.O = O
        ntiles = self.NPT + 1
        self.s5w = nc.dram_tensor("s5w_scr", [NL, 5, 128, 4096], BF16, kind="Internal").ap()
        self.spill = nc.dram_tensor("x_spill", [2, ntiles, 128, 16 * 512], BF16, kind="Internal").ap()
        self.s5w_buf = [[Buf() for _ in range(5)] for _ in range(NL)]
        self.NBLK = 90
        self.wscr = [nc.dram_tensor("w_bf16_scr%d" % l, [self.NBLK, 128, WSLOT_ELEMS], BF16, kind="Internal").ap() for l in range(NL)]
        self.wblk = {}
        self.wblk_n = [0] * NL
        self.spill_buf = [[Buf() for _ in range(ntiles)] for _ in range(2)]

        st = self.st
        with st:
            self.sems = {e: st.enter_context(nc.semaphore("s_" + e)) for e in ENGS}
            self.dsems = [st.enter_context(nc.semaphore("d%d" % i)) for i in range(self.P.n_dma_sems)]
            self.banks = []
            for i in range(8):
                h = st.enter_context(nc.psum_tensor("bank%d" % i, [128, 512], F32))
                t = T(h, [Buf()])
                self.banks.append(t)
            scr = self.sb([128, NSLAB * SLAB // 2], BF16, "scratch")
            self.scrB = scr
            self.scrF = scr.bitcast(F32)
            self.slab_bufs = [Buf() for _ in range(NSLAB)]
            self.slab_free = [True] * NSLAB
            self.wslots = [self.pt([128, WSLOT_ELEMS], BF16) for _ in range(NWSLOT)]
            self.xT = self.pt([128, 16, 512], BF16)
            self.pT = self.pt([128, 2, 512], BF16)
            self.ys5g = self.pt([128, 8, 512], BF16)
            self.ogg = self.pt([128, 8, 512], BF16)
            self.ycg = self.pt([128, 8, 512], BF16)
            self.Sst = self.pt([128, 4, 256], F32)
            self.Sbf = self.pt([128, 4, 256], BF16)
            self.halo = self.pt([128, 8, 30], F32)
            self.carry = self.pt([128, 2, 32], F32)
            self.T4 = self.pt([128, NL, 2, 2, 32], F32)
            self.consts()
            self.params_alloc()
            for L in range(NL):
                self.s5_prep(L)
            tiles = [TileCfg("p", i, 512, 512 * i, i == 0, i == self.NPT - 1) for i in range(self.NPT)]
            tiles.append(TileCfg("s", self.NPT, 128, 0, True, True))
            for L in range(NL):
                self.params_load(L)
                for tc in tiles:
                    self.tile_layer(L, tc)
            self.P.add("sp", lambda e: None, extra=list(self.store_ops))
            assert all(self.slab_free), "scratch leak"
            with nc.Block() as block:
                self.P.emit(block, self.sems, self.dsems)
        return nc

    def consts(self):
        self.identF = self.pt([128, 128], F32)
        self.identB = self.pt([128, 128], BF16)
        for t in (self.identF, self.identB):
            self.memset(t.ap[:], 1.0, B(t), eng="pool")
            self.asel(t.ap[:], [[-1, 128]], ALU.is_equal, 0, 1, B(t))
        self.permI = self.pt([128, 128], F32)
        self.cp(self.permI.ap[:].rearrange("p (t g c) -> p t g c", t=4, g=2, c=16),
                self.identF.ap[:].rearrange("p (g t c) -> p t g c", t=4, g=2, c=16), B(self.identF), B(self.permI))
        self.ones = {}
        for n in (256, 1024, 2048):
            t = self.pt([128, 128], BF16)
            self.memset(t.ap[:], 1.0 / n, B(t), eng="pool")
            self.ones[n] = t
        self.maskP = self.pt([64, 64], F32)
        self.memset(self.maskP.ap[:], 1.0, B(self.maskP), eng="pool")
        self.asel(self.maskP.ap[:], [[1, 64]], ALU.is_ge, 0, -1, B(self.maskP))
        self.maskS = self.pt([64, 8, 8], F32)
        self.memset(self.maskS.ap[:], 1.0, B(self.maskS), eng="pool")
        self.asel(self.maskS.ap[:], [[8, 8], [1, 8]], ALU.is_ge, 0, -1, B(self.maskS))
        self.asel(self.maskS.ap[:], [[-8, 8], [0, 8]], ALU.is_ge, 0, 1, B(self.maskS))
        self.rowm = self.pt([64, 8], F32)
        self.memset(self.rowm.ap[:], 1.0, B(self.rowm), eng="pool")
        self.asel(self.rowm.ap[:], [[-8, 8]], ALU.is_ge, 0, 1, B(self.rowm))
        self.asel(self.rowm.ap[:], [[8, 8]], ALU.is_ge, 7, -1, B(self.rowm))
        self.TMa = self.pt([128, 4, 16], F32)
        self.TMb = self.pt([128, 4, 16], F32)
        self.memset(self.TMa.ap[:], 1.0, B(self.TMa), eng="pool")
        self.asel(self.TMa.ap[:], [[16, 4], [0, 16]], ALU.is_ge, 15, -1, B(self.TMa))
        self.memset(self.TMb.ap[:], 1.0, B(self.TMb), eng="pool")
        self.asel(self.TMb.ap[:], [[16, 4], [0, 16]], ALU.is_ge, 79, -1, B(self.TMb))
        self.coefP = self.pt([128, 512], F32)
        self.memset(self.coefP.ap[:], 1.0, B(self.coefP), eng="pool")
        self.memset(self.coefP.ap[:, 0:512:64], 0.0, B(self.coefP), eng="pool")
        self.coefS = self.pt([128, 128], F32)
        self.memset(self.coefS.ap[:], 1.0, B(self.coefS), eng="pool")
        self.memset(self.coefS.ap[:, 0:128:8], 0.0, B(self.coefS), eng="pool")

    def params_alloc(self):
        self.bcol = self.pt([128, 96], F32)
        self.bal = self.pt([16, 1], F32)
        self.wa2 = self.pt([16, 512], BF16)
        self.bubc = self.pt([128, 1024], F32)
        self.bvbc = self.pt([128, 1024], F32)
        self.bglu = self.pt([128, 8], F32)
        self.nba = self.pt([128, 4], F32)
        self.normg = self.pt([128, 8], F32)
        self.convw = self.pt([128, 8, 31], F32)
        self.convb = self.pt([128, 8], F32)
        self.clng = self.pt([128, 8], F32)
        self.clnb = self.pt([128, 8], F32)
        self.lng = self.pt([128, 16], F32)
        self.lnb = self.pt([128, 16], F32)
        self.BC = dict(z=0, q=8, k=12, gz=16, ca=24, cb=32, cz=40, g=48)

    def params_load(self, L):
        I = self.I
        segs = [("z", 8), ("q", 4), ("k", 4), ("gz", 8), ("ca", 8), ("cb", 8), ("cz", 8), ("g", 48)]
        for nm, n in segs:
            c0 = self.BC[nm]
            src = I["b_in"][L, OFF[nm]:OFF[nm] + 128 * n].rearrange("(n p) -> p n", p=128)
            self.dma("sp", self.bcol.ap[:, c0:c0 + n], src, [], B(self.bcol), nc_ok=True)
        self.dma("sp", self.bal.ap[:, :], I["b_in"][L, OFF["al"]:OFF["al"] + 16].rearrange("(p o) -> p o", o=1), [], B(self.bal), nc_ok=True)
        self.dma("pool", self.wa2.ap[:, :], I["gla_w_a2"][L, :, :], [], B(self.wa2))
        self.dma("sp", self.bubc.ap[:, :], I["b_in"][L:L + 1, 0:1024].to_broadcast([128, 1024]), [], B(self.bubc))
        self.dma("sp", self.bvbc.ap[:, :], I["b_in"][L:L + 1, OFF["v"]:OFF["v"] + 1024].to_broadcast([128, 1024]), [], B(self.bvbc))

        def col(dst, src1d, n):
            self.dma("sp", dst.ap[:, 0:n], src1d.rearrange("(n p) -> p n", p=128), [], B(dst), nc_ok=True)
        col(self.bglu, I["b_glu"][L, :], 8)
        col(self.nba, I["gla_b_a"][L, :], 4)
        self.ts(self.nba.ap[:, :], self.nba.ap[:, :], -1.0, None, ALU.mult, None, B(self.nba), B(self.nba))
        col(self.normg, I["gla_norm_g"][L, :], 8)
        col(self.convb, I["conv_b"][L, :], 8)
        col(self.clng, I["conv_ln_g"][L, :], 8)
        col(self.clnb, I["conv_ln_b"][L, :], 8)
        col(self.lng, I["ln_g"][L, :], 16)
        col(self.lnb, I["ln_b"][L, :], 16)
        for ct in range(8):
            self.dma("sp", self.convw.ap[:, ct, :], I["conv_w"][L, :, ct * 128:(ct + 1) * 128].rearrange("k p -> p k"), [], B(self.convw), nc_ok=True)

    def s5_prep(self, L):
        I = self.I
        f = lambda n: self.salloc(n, F32)
        are = f(32); aim = f(32); ldt = f(32); dK = f(32)
        Bre = f(512); Bim = f(512); Cre = f(512); Cim = f(512)
        for g in range(2):
            ps_ = slice(64 * g, 64 * g + 64)
            self.dma("sp", are.ap[ps_, :], I["s5_a_re"][L].rearrange("(j two) p -> two p j", two=2)[g], [], B(are), nc_ok=True)
            self.dma("sp", aim.ap[ps_, :], I["s5_a_im"][L].rearrange("(j two) p -> two p j", two=2)[g], [], B(aim), nc_ok=True)
            self.dma("sp", ldt.ap[ps_, :], I["s5_log_dt"][L].rearrange("(j two) -> two j", two=2)[g:g + 1, :].to_broadcast([64, 32]), [], B(ldt), nc_ok=True)
            self.dma("sp", Bre.ap[ps_, :].rearrange("p (j c) -> p j c", c=16), I["s5_b_re"][L].rearrange("(j two) p c -> two p j c", two=2)[g], [], B(Bre), nc_ok=True)
            self.dma("sp", Bim.ap[ps_, :].rearrange("p (j c) -> p j c", c=16), I["s5_b_im"][L].rearrange("(j two) p c -> two p j c", two=2)[g], [], B(Bim), nc_ok=True)
            for s in range(4):
                p0 = 64 * g + 16 * s
                self.dma("sp", dK.ap[p0:p0 + 16, :], I["s5_d"][L].rearrange("(j g c) -> g c j", g=2, c=16)[g], [], B(dK), nc_ok=True)
        for nm, Ct in (("s5_c_re", Cre), ("s5_c_im", Cim)):
            stg = f(2048)
            self.dma("sp", stg.ap[0:32, :], I[nm][L].rearrange("(j two) c p -> j (two c p)", two=2), [], B(stg))
            bk = self.bank()
            for g in range(2):
                for c in range(16):
                    self.mm(bk.ap[64 * g:64 * g + 64, c * 32:(c + 1) * 32], stg.ap[0:32, (g * 16 + c) * 64:(g * 16 + c + 1) * 64],
                            self.identF.ap[0:32, 0:32], True, True, B(stg, self.identF), B(bk))
            self.cp(Ct.ap.rearrange("p (j c) -> p j c", c=16), bk.ap[:, :].rearrange("p (c j) -> p j c", c=16, j=32), B(bk), B(Ct))
            self.sfree(stg)
        dt = f(32); ardt = f(32); aidt = f(32)
        self.act(dt.ap, ldt.ap, AF.Exp, B(ldt), B(dt))
        self.tt(ardt.ap, are.ap, dt.ap, ALU.mult, B(are, dt), B(ardt))
        self.tt(aidt.ap, aim.ap, dt.ap, ALU.mult, B(aim, dt), B(aidt))
        MAG = f(256); TSC = f(512); R1 = f(512); R2 = f(512); SC = f(512)
        for k in range(8):
            m = k - 3
            self.act(MAG.ap[:, k * 32:(k + 1) * 32], ardt.ap, AF.Exp, B(ardt), B(MAG), scale=float(m))
            self.ts(TSC.ap[:, k * 32:(k + 1) * 32], aidt.ap, float(m) / TWO_PI, None, ALU.mult, None, B(aidt), B(TSC))
        self.ts(TSC.ap[:, 256:512], TSC.ap[:, 0:256], 0.25, None, ALU.add, None, B(TSC), B(TSC))
        self.ts(R1.ap, TSC.ap, MAGIC, None, ALU.add, None, B(TSC), B(R1))
        self.ts(R2.ap, R1.ap, -MAGIC, None, ALU.add, None, B(R1), B(R2))
        self.tt(R1.ap, TSC.ap, R2.ap, ALU.subtract, B(TSC, R2), B(R1))
        self.act(SC.ap, R1.ap, AF.Sin, B(R1), B(SC), scale=TWO_PI)
        PWr = f(256); PWi = f(256)
        self.tt(PWr.ap, MAG.ap, SC.ap[:, 256:512], ALU.mult, B(MAG, SC), B(PWr))
        self.tt(PWi.ap, MAG.ap, SC.ap[:, 0:256], ALU.mult, B(MAG, SC), B(PWi))
        self.sfree(MAG, TSC, R1, R2, SC, dt, ardt, aidt, ldt)
        pw = lambda Tt, k: Tt.ap[:, k * 32:(k + 1) * 32]
        T4 = self.T4
        self.cp(T4.ap[:, L, 0, 0, :], pw(PWr, 7), B(PWr), B(T4))
        self.cp(T4.ap[:, L, 0, 1, :], pw(PWr, 7), B(PWr), B(T4))
        self.cp(T4.ap[:, L, 1, 0, :], pw(PWi, 7), B(PWi), B(T4))
        self.ts(T4.ap[:, L, 1, 1, :], pw(PWi, 7), -1.0, None, ALU.mult, None, B(PWi), B(T4))
        nr = f(32); t1 = f(32); t2 = f(32); den = f(32); Ere = f(32); Eim = f(32)
        self.ts(nr.ap, pw(PWr, 4), -1.0, None, ALU.add, None, B(PWr), B(nr))
        ni = pw(PWi, 4)
        self.tt(den.ap, are.ap, are.ap, ALU.mult, B(are), B(den))
        self.tt(t1.ap, aim.ap, aim.ap, ALU.mult, B(aim), B(t1))
        self.tt(den.ap, den.ap, t1.ap, ALU.add, B(den, t1), B(den))
        self.recip(den.ap, den.ap, B(den), B(den))
        self.tt(t1.ap, nr.ap, are.ap, ALU.mult, B(nr, are), B(t1))
        self.tt(t2.ap, ni, aim.ap, ALU.mult, B(PWi, aim), B(t2))
        self.tt(t1.ap, t1.ap, t2.ap, ALU.add, B(t1, t2), B(t1))
        self.tt(Ere.ap, t1.ap, den.ap, ALU.mult, B(t1, den), B(Ere))
        self.tt(t1.ap, ni, are.ap, ALU.mult, B(PWi, are), B(t1))
        self.tt(t2.ap, nr.ap, aim.ap, ALU.mult, B(nr, aim), B(t2))
        self.tt(t1.ap, t1.ap, t2.ap, ALU.subtract, B(t1, t2), B(t1))
        self.tt(Eim.ap, t1.ap, den.ap, ALU.mult, B(t1, den), B(Eim))
        self.sfree(nr, t2, den, are, aim)

        def v3(Tt):
            return Tt.ap.rearrange("p (j c) -> p j c", c=16)

        def bc(ap32):
            return ap32.rearrange("p (j o) -> p j o", o=1).to_broadcast([128, 32, 16])

        def cmul(outr, outi, ar, ai, br, bi, rb, tmpT, neg_im=False):
            tv = v3(tmpT)
            if outr is not None:
                self.tt(outr, bc(ar), br, ALU.mult, rb, rb)
                self.tt(tv, bc(ai), bi, ALU.mult, rb + B(tmpT), B(tmpT))
                self.tt(outr, outr, tv, ALU.subtract, rb + B(tmpT), rb)
            if outi is not None:
                self.tt(outi, bc(ar), bi, ALU.mult, rb, rb)
                self.tt(tv, bc(ai), br, ALU.mult, rb + B(tmpT), B(tmpT))
                self.tt(outi, outi, tv, ALU.add, rb + B(tmpT), rb)
                if neg_im:
                    self.ts(outi, outi, -1.0, None, ALU.mult, None, rb, rb)

        tmp = f(512)
        bbr = f(512); bbi = f(512)
        allb = B(PWr, PWi, Ere, Eim, Bre, Bim, Cre, Cim, bbr, bbi)
        cmul(v3(bbr), v3(bbi), Ere.ap, Eim.ap, v3(Bre), v3(Bim), allb, tmp)
        self.sfree(Ere, Eim, Bre, Bim, t1)
        Xr = f(2048); XiN = f(2048); Zr = f(2048); Zi = f(2048)
        v4 = lambda Tt: Tt.ap.rearrange("p (j s c) -> p j s c", s=4, c=16)
        rb = allb + B(Xr, XiN, Zr, Zi)
        for s in range(4):
            cmul(v4(Xr)[:, :, s, :], v4(XiN)[:, :, s, :], pw(PWr, 3 - s), pw(PWi, 3 - s), v3(bbr), v3(bbi), rb, tmp, neg_im=True)
            cmul(v4(Zr)[:, :, s, :], v4(Zi)[:, :, s, :], pw(PWr, 3 + s), pw(PWi, 3 + s), v3(Cre), v3(Cim), rb, tmp)
        Mf = f(4096)
        self.memset(Mf.ap, 0.0, B(Mf))
        Mf3 = Mf.ap.rearrange("p (j n) -> p j n", n=128)
        for j4 in range(8):
            bk = self.bank()
            for jj in range(4):
                j = j4 * 4 + jj
                for g in range(2):
                    pr = slice(64 * g, 64 * g + 64)
                    o = bk.ap[pr, jj * 128 + 64 * g: jj * 128 + 64 * g + 64]
                    self.mm(o, Xr.ap[pr, j * 64:(j + 1) * 64], Zr.ap[pr, j * 64:(j + 1) * 64], True, False, B(Xr, Zr), B(bk))
                    self.mm(o, XiN.ap[pr, j * 64:(j + 1) * 64], Zi.ap[pr, j * 64:(j + 1) * 64], False, True, B(XiN, Zi), B(bk))
            for g in range(2):
                pr = slice(64 * g, 64 * g + 64)
                TM = self.TMa if g == 0 else self.TMb
                outv = Mf3[pr, j4 * 4:(j4 + 1) * 4, :].rearrange("p j (t g c) -> p j t g c", t=4, g=2, c=16)[:, :, :, g, :]
                inv = bk.ap[pr, :].rearrange("p (j g t c) -> p j g t c", j=4, g=2, t=4, c=16)[:, :, g, :, :]
                mk = TM.ap[pr, :, :].rearrange("p (o t) c -> p o t c", o=1).to_broadcast([64, 4, 4, 16])
                self.tt(outv, inv, mk, ALU.mult, B(bk, TM), B(Mf))
        for j in range(32):
            self.stt(Mf3[:, j, :], self.permI.ap[:, :], dK.ap[:, j:j + 1], Mf3[:, j, :], ALU.mult, ALU.add, B(self.permI, dK, Mf), B(Mf))
        Mb = self.salloc(4096, BF16)
        self.cp(Mb.ap, Mf.ap, B(Mf), B(Mb), eng="act")
        self.dma("sp", self.s5w[L, 0], Mb.ap, B(Mb), [self.s5w_buf[L][0]])
        self.sfree(Xr, XiN, Zr, Zi, Mf, Mb, dK)
        Wn_r = f(2048); Wn_i = f(2048)
        rb = allb + B(Wn_r, Wn_i)
        for s in range(4):
            cmul(v4(Wn_r)[:, :, s, :], v4(Wn_i)[:, :, s, :], pw(PWr, 6 - s), pw(PWi, 6 - s), v3(bbr), v3(bbi), rb, tmp)
        for plane, Wn in enumerate((Wn_r, Wn_i)):
            VS = f(4096)
            self.memset(VS.ap, 0.0, B(VS))
            VS3 = VS.ap.rearrange("p (j n) -> p j n", n=128)
            Wn3 = Wn.ap.rearrange("p (j n) -> p j n", n=64)
            for g in range(2):
                pr = slice(64 * g, 64 * g + 64)
                self.cp(VS3[pr, :, 64 * g:64 * g + 64], Wn3[pr, :, :], B(Wn), B(VS))
            WSt = self.salloc(4096, BF16)
            for j4 in range(8):
                bk = self.bank()
                for jj in range(4):
                    j = j4 * 4 + jj
                    self.tr(bk.ap[:, jj * 128:(jj + 1) * 128], VS3[:, j, :], self.identF.ap[:, :], B(VS, self.identF), B(bk))
                self.cp(WSt.ap[:, j4 * 512:(j4 + 1) * 512], bk.ap[:, :], B(bk), B(WSt), eng="act")
            self.dma("sp", self.s5w[L, 1 + plane], WSt.ap, B(WSt), [self.s5w_buf[L][1 + plane]])
            self.sfree(VS, WSt)
        rb = allb + B(Wn_r, Wn_i)
        for t in range(4):
            cmul(v4(Wn_r)[:, :, t, :], v4(Wn_i)[:, :, t, :], pw(PWr, 4 + t), pw(PWi, 4 + t), v3(Cre), v3(Cim), rb, tmp, neg_im=True)
        for plane, Wn in enumerate((Wn_r, Wn_i)):
            WYb = self.salloc(4096, BF16)
            self.memset(WYb.ap, 0.0, B(WYb))
            for g in range(2):
                pr = slice(64 * g, 64 * g + 64)
                outv = WYb.ap[pr, :].rearrange("p (j t g c) -> p j t g c", t=4, g=2, c=16)[:, :, :, g, :]
                inv = Wn.ap[pr, :].rearrange("p (j t c) -> p j t c", t=4, c=16)
                self.cp(outv, inv, B(Wn), B(WYb))
            self.dma("sp", self.s5w[L, 3 + plane], WYb.ap, B(WYb), [self.s5w_buf[L][3 + plane]])
            self.sfree(WYb)
        self.sfree(Wn_r, Wn_i, tmp, bbr, bbi, Cre, Cim, PWr, PWi)

    def win(self, L, col0, ncols, KT=16):
        return self.I["w_in"][L, :, col0:col0 + ncols].rearrange("(k p) n -> p k n", p=128)

    def wmat(self, name, L, col0, ncols):
        return self.I[name][L, :, col0:col0 + ncols].rearrange("(k p) n -> p k n", p=128)

    def weight_recipes(self):
        r = []
        for cs in range(4):
            r.append(("w_in", OFF["u"] + 256 * cs, 256, 16))
        for cs in range(4):
            r.append(("w_in", OFF["ca"] + 256 * cs, 256, 16)); r.append(("w_in", OFF["cb"] + 256 * cs, 256, 16))
        for cs in range(4):
            r.append(("w_in", OFF["cz"] + 256 * cs, 256, 16))
        for cs in range(4):
            r.append(("w_in", OFF["z"] + 256 * cs, 256, 16))
        for hs in range(2):
            r.append(("w_glu", 512 * hs, 512, 8))
        r.append(("w_in", OFF["al"], 16, 16))
        for nm in ("q", "k"):
            for cs in range(2):
                r.append(("w_in", OFF[nm] + 256 * cs, 256, 16))
        for nm in ("v", "gz"):
            for cs in range(4):
                r.append(("w_in", OFF[nm] + 256 * cs, 256, 16))
        for jo4 in range(4):
            for bi, wn in enumerate(("p_s5", "p_gla", "p_conv")):
                r.append((wn, 512 * jo4, 512, 8))
                for half in range(2):
                    r.append(("w_in", OFF["g"] + bi * 2048 + jo4 * 512 + half * 256, 256, 16))
        for cs in range(8):
            r.append(("w_o", 256 * cs, 256, 16))
        r.append(("w_pe", 0, 2048, 2))
        for cs in range(8):
            r.append(("w_pg", 256 * cs, 256, 16))
        return r

    def preconvert(self, L, lo=0, hi=None):
        rec = self.weight_recipes()
        hi = len(rec) if hi is None else min(hi, len(rec))
        for (name, col0, ncols, KT) in rec[lo:hi]:
            key = (name, L, col0, ncols)
            if key in self.wblk:
                continue
            src = self.I[name][L, :, col0:col0 + ncols].rearrange("(k p) n -> p k n", p=128)
            bi = self.wblk_n[L]; self.wblk_n[L] += 1
            assert bi < self.NBLK
            bb = Buf()
            self.wblk[key] = (bi, bb)
            dstv = self.wscr[L][bi, :, 0:KT * ncols].rearrange("p (k n) -> p k n", n=ncols)
            self.dma("pool", dstv, src, r=[], w=[bb])

    def wl_in(self, L, col0, ncols):
        return self.wload(self.win(L, col0, ncols), 16, ncols, key=("w_in", L, col0, ncols))

    def wl_mat(self, name, L, col0, ncols, KT):
        return self.wload(self.wmat(name, L, col0, ncols), KT, ncols, key=(name, L, col0, ncols))

    def proj_fm(self, ws, jj, KT, rhs_fn, rhsT, TT):
        bk = self.bank()
        for kt in range(KT):
            self.mm(bk.ap[:, 0:TT], ws.ap[:, kt, jj * 128:(jj + 1) * 128], rhs_fn(kt), kt == 0, kt == KT - 1, B(ws, rhsT), B(bk))
        return bk

    def tile_layer(self, L, tc):
        if PRECONVERT and tc.kind == "p" and tc.first:
            self.preconvert(L)
        self.load_x(L, tc)
        self.s5_phase(L, tc)
        self.conv_phase(L, tc)
        self.dbg_dump("dbg_conv", self.ycg, L, tc)
        self.s5_back(L, tc)
        self.dbg_dump("dbg_s5", self.ys5g, L, tc)
        self.gla_phase(L, tc)
        self.dbg_dump("dbg_gla", self.ogg, L, tc)
        self.merge_phase(L, tc)

    def dbg_dump(self, name, t, L, tc):
        if not self.debug or L != 0:
            return
        key = name + "_" + tc.kind + str(tc.idx)
        o = self.dram_out(key, [128, 8, 512], BF16)
        self.dma("sp", o[:, :, :], t.ap[:, :, :], B(t), [], store=True)

    def xrhs(self, tc):
        xT = self.xT
        return lambda kt: xT.ap[:, kt, 0:tc.TT]

    def load_x(self, L, tc):
        I = self.I; TT = tc.TT
        ngr = TT // 128
        if L == 0:
            src = I["xp"] if tc.kind == "p" else I["xs"]
            for tg in range(ngr):
                stg = self.salloc(2048, F32)
                r0 = tc.tok0 + tg * 128
                self.dma("pool", stg.ap, src[r0:r0 + 128, :], [], B(stg))
                for k4 in range(4):
                    bk = self.bank()
                    for kk in range(4):
                        kt = k4 * 4 + kk
                        self.tr(bk.ap[:, kk * 128:(kk + 1) * 128], stg.ap[:, kt * 128:(kt + 1) * 128], self.identF.ap[:, :], B(stg, self.identF), B(bk))
                    outv = self.xT.ap[:, k4 * 4:(k4 + 1) * 4, tg * 128:(tg + 1) * 128]
                    self.cp(outv, bk.ap[:, :].rearrange("p (k t) -> p k t", t=128), B(bk), B(self.xT), eng=("act" if k4 % 2 else "dve"))
                self.sfree(stg)
        else:
            par = (L - 1) % 2
            src = self.spill[par, tc.idx].rearrange("p (k t) -> p k t", t=512)[:, :, 0:TT]
            self.dma("pool", self.xT.ap[:, :, 0:TT], src, [self.spill_buf[par][tc.idx]], B(self.xT))
        psrc = I["pp"][L] if tc.kind == "p" else I["pps"][L]
        for tg in range(ngr):
            stg = self.salloc(256, F32)
            r0 = tc.tok0 + tg * 128
            self.dma("pool", stg.ap, psrc[r0:r0 + 128, :], [], B(stg))
            bk = self.bank()
            for kk in range(2):
                self.tr(bk.ap[:, kk * 128:(kk + 1) * 128], stg.ap[:, kk * 128:(kk + 1) * 128], self.identF.ap[:, :], B(stg, self.identF), B(bk))
            self.cp(self.pT.ap[:, :, tg * 128:(tg + 1) * 128], bk.ap[:, 0:256].rearrange("p (k t) -> p k t", t=128), B(bk), B(self.pT), eng="act")
            self.sfree(stg)

    def s5_phase(self, L, tc):
        I = self.I; O = self.O
        TT, NB, NSEG, SEGB = tc.TT, tc.NB, tc.NSEG, tc.SEGB
        xT = self.xT
        Dt = self.salloc(4096, BF16)
        D5 = Dt.ap.rearrange("p (j g s c) -> p j g s c", j=32, g=2, s=4, c=16)
        for cs in range(4):
            ws = self.wl_in(L, OFF["u"] + 256 * cs, 256)
            for s in range(4):
                bk = self.bank()
                for kt in range(16):
                    self.mm(bk.ap[0:NB, 0:256], xT.ap[:, kt, s:TT:4], ws.ap[:, kt, :], kt == 0, kt == 15, B(xT, ws), B(bk))
                outv = D5[0:NB, 8 * cs:8 * cs + 8, :, s, :]
                inv = bk.ap[0:NB, 0:256].rearrange("p (j g c) -> p j g c", j=8, g=2, c=16)
                bv = self.bubc.ap[0:NB, 256 * cs:256 * cs + 256].rearrange("p (j g c) -> p j g c", j=8, g=2, c=16)
                self.tt(outv, inv, bv, ALU.add, B(bk, self.bubc), B(Dt))
        U2 = self.salloc(32 * NB, BF16)
        U23 = U2.ap.rearrange("p (j n) -> p j n", n=NB)
        per = 1024 // NB
        j = 0
        while j < 32:
            bk = self.bank()
            bkb = bk.ap.bitcast(BF16)
            nj = min(per, 32 - j)
            for jj in range(nj):
                self.tr(bkb[:, jj * NB:(jj + 1) * NB], Dt.ap[0:NB, (j + jj) * 128:(j + jj + 1) * 128], self.identB.ap[0:NB, 0:NB], B(Dt, self.identB), B(bk))
            self.cp(U2.ap[:, j * NB:(j + nj) * NB], bkb[:, 0:nj * NB], B(bk), B(U2), eng="act")
            j += nj
        self.sfree(Dt)
        wSr = self.wload_s5(L, 1)
        wSi = self.wload_s5(L, 2)
        HW = NSEG * (SEGB + 1)
        H = self.salloc(2 * 32 * HW, F32)
        H5 = H.ap.rearrange("p (a j q b) -> p a j q b", a=2, j=32, q=NSEG, b=SEGB + 1)
        if tc.kind == "p":
            if tc.first:
                self.memset(self.carry.ap[:], 0.0, B(self.carry))
            self.cp(H5[:, :, :, 0, 0], self.carry.ap[:, :, :], B(self.carry), B(H))
        else:
            for plane, nm in enumerate(("sre", "sim")):
                for r in range(4):
                    stg = self.salloc(128, F32)
                    self.dma("pool", stg.ap, I[nm][L, r * 128:(r + 1) * 128, :], [], B(stg))
                    bk = self.bank()
                    self.tr(bk.ap[:, 0:128], stg.ap, self.identF.ap[:, :], B(stg, self.identF), B(bk))
                    outv = H5[:, plane, :, 4 * r:4 * r + 4, 0]
                    inv = bk.ap[:, 0:128].rearrange("p (q j) -> p j q", q=4, j=32)
                    self.cp(outv, inv, B(bk), B(H))
                    self.sfree(stg)
        pairs_per_bank = 512 // NB
        for plane, wS in enumerate((wSr, wSi)):
            j = 0
            while j < 32:
                bk = self.bank()
                nj = min(pairs_per_bank, 32 - j)
                for jj in range(nj):
                    self.mm(bk.ap[:, jj * NB:(jj + 1) * NB], wS.ap[:, j + jj, :], U23[:, j + jj, :], True, True, B(wS, U2), B(bk))
                outv = H5[:, plane, j:j + nj, :, 1:SEGB + 1]
                inv = bk.ap[:, 0:nj * NB].rearrange("p (j q b) -> p j q b", j=nj, q=NSEG, b=SEGB)
                self.cp(outv, inv, B(bk), B(H), eng="act")
                j += nj
        if tc.kind == "p" and tc.first:
            self.preconvert(L, 4, 24)
        u = self.salloc(2 * 2 * 32 * NSEG, F32)
        u5 = u.ap.rearrange("p (r a j q) -> p r a j q", r=2, a=2, j=32, q=NSEG)
        T4 = self.T4
        for b in range(SEGB):
            if NSEG == 1:
                hb = H5[:, :, :, 0, b].rearrange("p (o a) j -> p o a j", o=1).to_broadcast([128, 2, 2, 32])
                self.tt(u5[:, :, :, :, 0], hb, T4.ap[:, L, :, :, :], ALU.mult, B(H, T4), B(u), eng=SCAN_ENG)
            else:
                for rpt in range(2):
                    tb = T4.ap[:, L, rpt, :, :].rearrange("p a (j o) -> p a j o", o=1).to_broadcast([128, 2, 32, NSEG])
                    self.tt(u5[:, rpt], H5[:, :, :, :, b], tb, ALU.mult, B(H, T4), B(u), eng=SCAN_ENG)
            self.tt(H5[:, :, :, :, b + 1], H5[:, :, :, :, b + 1], u5[:, 0], ALU.add, B(H, u), B(H), eng=SCAN_ENG)
            self.tt(H5[:, :, :, :, b + 1], H5[:, :, :, :, b + 1], u5[:, 1, ::-1], ALU.add, B(H, u), B(H), eng=SCAN_ENG)
        self.s5_ctx = (U2, U23, H, H5, u)

    def s5_back(self, L, tc):
        I = self.I; O = self.O
        TT, NB, NSEG, SEGB = tc.TT, tc.NB, tc.NSEG, tc.SEGB
        xT = self.xT
        U2, U23, H, H5, u = self.s5_ctx
        zsT = self.salloc(8 * TT, BF16)
        zs3 = zsT.ap.rearrange("p (k t) -> p k t", t=TT)
        for cs in range(4):
            ws = self.wl_in(L, OFF["z"] + 256 * cs, 256)
            for jj in range(2):
                ct = cs * 2 + jj
                bk = self.proj_fm(ws, jj, 16, self.xrhs(tc), xT, TT)
                self.act(zs3[:, ct, :], bk.ap[:, 0:TT], AF.Silu, B(bk, self.bcol), B(zsT), bias=self.bcol.ap[:, self.BC["z"] + ct:self.BC["z"] + ct + 1])
        self.sfree(u)
        Hbf = self.salloc(2 * 32 * NB, BF16)
        Hb4 = Hbf.ap.rearrange("p (a j n) -> p a j n", a=2, j=32, n=NB)
        for plane in range(2):
            outv = Hbf.ap[:, plane * 32 * NB:(plane + 1) * 32 * NB].rearrange("p (j q b) -> p j q b", j=32, q=NSEG, b=SEGB)
            self.cp(outv, H5[:, plane, :, :, 0:SEGB], B(H), B(Hbf), eng=("act" if plane else "dve"))
        if tc.kind == "p":
            self.cp(self.carry.ap[:, :, :], H5[:, :, :, 0, SEGB], B(H), B(self.carry))
            if tc.last:
                for plane, nm in enumerate(("sre_p", "sim_p")):
                    bk = self.bank()
                    self.tr(bk.ap[0:32, 0:128], self.carry.ap[:, plane, :], self.identF.ap[:, :], B(self.carry, self.identF), B(bk))
                    stg = self.salloc(128, F32)
                    self.cp(stg.ap[0:32, :], bk.ap[0:32, 0:128], B(bk), B(stg))
                    self.dma("sp", O[nm][L, :, :], stg.ap[0:32, :], B(stg), [], store=True)
                    self.sfree(stg)
        else:
            for plane, nm in enumerate(("sre_s", "sim_s")):
                fin = self.salloc(512, F32)
                self.cp(fin.ap.rearrange("p (q j) -> p j q", q=16, j=32), H5[:, plane, :, :, SEGB], B(H), B(fin))
                for r in range(4):
                    bk = self.bank()
                    self.tr(bk.ap[:, 0:128], fin.ap[:, r * 128:(r + 1) * 128], self.identF.ap[:, :], B(fin, self.identF), B(bk))
                    stg = self.salloc(128, F32)
                    self.cp(stg.ap, bk.ap[:, 0:128], B(bk), B(stg), eng="act")
                    self.dma("sp", O[nm][L, r * 128:(r + 1) * 128, :], stg.ap, B(stg), [], store=True)
                    self.sfree(stg)
                self.sfree(fin)
        self.sfree(H)
        wM = self.wload_s5(L, 0)
        wYr = self.wload_s5(L, 3)
        wYi = self.wload_s5(L, 4)
        ys = self.salloc(4096, BF16)
        ys3 = ys.ap.rearrange("p (t c) -> p t c", t=4)
        for j4 in range(8):
            bk = self.bank()
            for jj in range(4):
                j = j4 * 4 + jj
                o = bk.ap[0:NB, jj * 128:(jj + 1) * 128]
                self.mm(o, U23[:, j, :], wM.ap[:, j, :], True, False, B(U2, wM), B(bk))
                self.mm(o, Hb4[:, 0, j, :], wYr.ap[:, j, :], False, False, B(Hbf, wYr), B(bk))
                self.mm(o, Hb4[:, 1, j, :], wYi.ap[:, j, :], False, True, B(Hbf, wYi), B(bk))
            yf = self.salloc(512, F32); tq = self.salloc(512, F32)
            self.cp(yf.ap[0:NB, :], bk.ap[0:NB, :], B(bk), B(yf), eng="act")
            self.act(tq.ap[0:NB, :], bk.ap[0:NB, :], AF.Square, B(bk), B(tq))
            self.ts(tq.ap[0:NB, :], tq.ap[0:NB, :], 0.044715, 1.0, ALU.mult, ALU.add, B(tq), B(tq))
            self.tt(tq.ap[0:NB, :], tq.ap[0:NB, :], yf.ap[0:NB, :], ALU.mult, B(tq, yf), B(tq))
            self.act(tq.ap[0:NB, :], tq.ap[0:NB, :], AF.Sigmoid, B(tq), B(tq), scale=GELU_C)
            outv = ys3[0:NB, :, j4 * 128:(j4 + 1) * 128].rearrange("p t (j c) -> p j t c", j=4, c=32)
            self.tt(outv, yf.ap[0:NB, :].rearrange("p (j t c) -> p j t c", j=4, t=4, c=32),
                    tq.ap[0:NB, :].rearrange("p (j t c) -> p j t c", j=4, t=4, c=32), ALU.mult, B(yf, tq), B(ys))
            self.sfree(yf, tq)
        self.sfree(U2, Hbf)
        ysT = self.salloc(8 * TT, BF16)
        ysT3 = ysT.ap.rearrange("p (k t) -> p k t", t=TT)
        per = 1024 // NB
        items = [(t, ct) for ct in range(8) for t in range(4)]
        i = 0
        while i < len(items):
            bk = self.bank(); bkb = bk.ap.bitcast(BF16)
            grp = items[i:i + per]
            for gi, (t, ct) in enumerate(grp):
                self.tr(bkb[:, gi * NB:(gi + 1) * NB], ys3[0:NB, t, ct * 128:(ct + 1) * 128], self.identB.ap[0:NB, 0:NB], B(ys, self.identB), B(bk))
            for gi, (t, ct) in enumerate(grp):
                self.cp(ysT3[:, ct, t:TT:4], bkb[:, gi * NB:(gi + 1) * NB], B(bk), B(ysT), eng=("act" if gi % 2 else "dve"))
            i += per
        self.sfree(ys)
        for hs in range(2):
            ws = self.wl_mat("w_glu", L, 512 * hs, 512, 8)
            for jj in range(4):
                o = hs * 4 + jj
                bk = self.proj_fm(ws, jj, 8, lambda kt: ysT3[:, kt, :], ysT, TT)
                sg = self.salloc(TT, F32)
                self.act(sg.ap, bk.ap[:, 0:TT], AF.Sigmoid, B(bk, self.bglu), B(sg), bias=self.bglu.ap[:, o:o + 1])
                self.tt(sg.ap, sg.ap, ysT3[:, o, :], ALU.mult, B(sg, ysT), B(sg))
                self.tt(self.ys5g.ap[:, o, 0:TT], sg.ap, zs3[:, o, :], ALU.mult, B(sg, zsT), B(self.ys5g))
                self.sfree(sg)
        self.sfree(ysT, zsT)

    def wload_s5(self, L, idx):
        return self.wload(self.s5w[L, idx].rearrange("p (k n) -> p k n", n=128), 32, 128, rb=[self.s5w_buf[L][idx]])

    def gla_phase(self, L, tc):
        I = self.I; O = self.O
        TT, NCH, NE, EL = tc.TT, tc.NCH, tc.NE, tc.EL
        xT = self.xT; xr = self.xrhs(tc)
        BCq, BCk, BCgz = self.BC["q"], self.BC["k"], self.BC["gz"]
        ws = self.wl_in(L, OFF["al"], 16)
        bk = self.bank()
        for kt in range(16):
            self.mm(bk.ap[0:16, 0:TT], ws.ap[:, kt, :], xr(kt), kt == 0, kt == 15, B(ws, xT), B(bk))
        alT = self.salloc(TT, BF16)
        self.act(alT.ap[0:16, :], bk.ap[0:16, 0:TT], AF.Identity, B(bk, self.bal), B(alT), bias=self.bal.ap[:, 0:1])
        la = self.salloc(4 * TT, F32); cs = self.salloc(4 * TT, F32)
        la3 = la.ap.rearrange("p (h t) -> p h t", t=TT); cs3 = cs.ap.rearrange("p (h t) -> p h t", t=TT)
        coef = self.coefP if tc.kind == "p" else self.coefS
        for h in range(4):
            bk = self.bank()
            self.mm(bk.ap[:, 0:TT], self.wa2.ap[0:16, h * 128:(h + 1) * 128], alT.ap[0:16, :], True, True, B(self.wa2, alT), B(bk))
            self.act(la3[:, h, :], bk.ap[:, 0:TT], AF.Exp, B(bk, self.nba), B(la), bias=self.nba.ap[:, h:h + 1], scale=-1.0)
            self.act(la3[:, h, :], la3[:, h, :], AF.Ln, B(la), B(la), bias=1.0)
            self.scan(cs3[:, h, :], coef.ap[:, 0:TT], la3[:, h, :], B(coef, la), B(cs))
        self.sfree(alT, la)
        ecs = self.salloc(4 * TT, F32); encs = self.salloc(4 * TT, F32); el = self.salloc(4 * NE, F32)
        ecs3 = ecs.ap.rearrange("p (h t) -> p h t", t=TT); encs3 = encs.ap.rearrange("p (h t) -> p h t", t=TT)
        el3 = el.ap.rearrange("p (h e) -> p h e", e=NE)
        self.act(ecs.ap, cs.ap, AF.Exp, B(cs), B(ecs), scale=-1.0 / 16.0, bias=float(math.log(128.0 ** -0.5)))
        self.act(encs.ap, cs.ap, AF.Exp, B(cs), B(encs), scale=1.0 / 16.0)
        self.act(el3, cs3[:, :, EL - 1:TT:EL], AF.Exp, B(cs), B(el), scale=-1.0 / 16.0)
        self.sfree(cs)
        qd = self.salloc(4 * TT, BF16); kd = self.salloc(4 * TT, BF16)
        qd3 = qd.ap.rearrange("p (h t) -> p h t", t=TT); kd3 = kd.ap.rearrange("p (h t) -> p h t", t=TT)
        for nm, dst3, dstT, sc3, scT, bc0 in (("q", qd3, qd, ecs3, ecs, BCq), ("k", kd3, kd, encs3, encs, BCk)):
            for cs_ in range(2):
                ws = self.wl_in(L, OFF[nm] + 256 * cs_, 256)
                for jj in range(2):
                    h = cs_ * 2 + jj
                    bk = self.proj_fm(ws, jj, 16, xr, xT, TT)
                    self.stt(dst3[:, h, :], bk.ap[:, 0:TT], self.bcol.ap[:, bc0 + h:bc0 + h + 1], sc3[:, h, :], ALU.add, ALU.mult, B(bk, self.bcol, scT), B(dstT))
        self.sfree(ecs, encs)
        kk = self.salloc(4 * TT, BF16)
        kk3 = kk.ap.rearrange("p (h t) -> p h t", t=TT)
        for h in range(4):
            outv = kk3[:, h, :].rearrange("p (e t) -> p e t", t=EL)
            inv = kd3[:, h, :].rearrange("p (e t) -> p e t", t=EL)
            ev = el3[:, h, :].rearrange("p (e o) -> p e o", o=1).to_broadcast([128, NE, EL])
            self.tt(outv, inv, ev, ALU.mult, B(kd, el), B(kk))
        kkT = self.salloc(NCH * 512, BF16)
        kkT4 = kkT.ap.rearrange("p (c h d) -> p c h d", h=4, d=128)
        for ch2 in range(0, NCH, 2):
            bk = self.bank(); bkb = bk.ap.bitcast(BF16)
            for cc in range(2):
                for h in range(4):
                    c = ch2 + cc
                    self.tr(bkb[0:64, (cc * 4 + h) * 128:(cc * 4 + h + 1) * 128], kk3[:, h, c * 64:(c + 1) * 64], self.identB.ap[:, :], B(kk, self.identB), B(bk))
            self.cp(kkT.ap[0:64, ch2 * 512:(ch2 + 2) * 512], bkb[0:64, 0:1024], B(bk), B(kkT), eng="act")
        self.sfree(kk)
        vt = self.salloc(NCH * 1024, BF16)
        vt3 = vt.ap.rearrange("p (c v) -> p c v", v=1024)
        for cs_ in range(4):
            ws = self.wl_in(L, OFF["v"] + 256 * cs_, 256)
            for c in range(NCH):
                bk = self.bank()
                for kt in range(16):
                    self.mm(bk.ap[0:64, 0:256], xT.ap[:, kt, c * 64:(c + 1) * 64], ws.ap[:, kt, :], kt == 0, kt == 15, B(xT, ws), B(bk))
                self.tt(vt3[0:64, c, cs_ * 256:(cs_ + 1) * 256], bk.ap[0:64, 0:256], self.bvbc.ap[0:64, cs_ * 256:(cs_ + 1) * 256], ALU.add, B(bk, self.bvbc), B(vt))
        gz = self.salloc(8 * TT, BF16)
        gz3 = gz.ap.rearrange("p (k t) -> p k t", t=TT)
        for cs_ in range(4):
            ws = self.wl_in(L, OFF["gz"] + 256 * cs_, 256)
            for jj in range(2):
                ct = cs_ * 2 + jj
                bk = self.proj_fm(ws, jj, 16, xr, xT, TT)
                self.act(gz3[:, ct, :], bk.ap[:, 0:TT], AF.Silu, B(bk, self.bcol), B(gz), bias=self.bcol.ap[:, BCgz + ct:BCgz + ct + 1])
        o = self.salloc(8 * TT, F32)
        o3 = o.ap.rearrange("p (k t) -> p k t", t=TT)
        attT = self.salloc(NCH * 256, BF16)
        attT4 = attT.ap.rearrange("p (c h t) -> p c h t", h=4, t=64)
        Sst, Sbf = self.Sst, self.Sbf
        if tc.kind == "p" and tc.first:
            self.memset(Sst.ap[:], 0.0, B(Sst))
            self.memset(Sbf.ap[:], 0.0, B(Sbf))
        for c in range(NCH):
            csl = slice(c * 64, (c + 1) * 64)
            bkA = self.bank()
            for h in range(4):
                self.mm(bkA.ap[0:64, h * 64:(h + 1) * 64], kd3[:, h, csl], qd3[:, h, csl], True, True, B(kd, qd), B(bkA))
            if tc.kind == "p":
                mk = self.maskP.ap[:, :].rearrange("p (o t) -> p o t", o=1).to_broadcast([64, 4, 64]); mkT = self.maskP
            else:
                mk = self.maskS.ap[:, :, :].rearrange("p a b -> p (a b)").rearrange("p (o t) -> p o t", o=1).to_broadcast([64, 4, 64]); mkT = self.maskS
            self.tt(attT4[0:64, c, :, :], bkA.ap[0:64, 0:256].rearrange("p (h t) -> p h t", t=64), mk, ALU.mult, B(bkA, mkT), B(attT))
            bkO = self.bank()
            if tc.kind == "p":
                for h in range(4):
                    for vh in range(2):
                        oo = bkO.ap[:, (h * 2 + vh) * 64:(h * 2 + vh + 1) * 64]
                        self.mm(oo, vt3[0:64, c, h * 256 + vh * 128:h * 256 + (vh + 1) * 128], attT4[0:64, c, h, :], True, False, B(vt, attT), B(bkO))
                        self.mm(oo, Sbf.ap[:, h, vh * 128:(vh + 1) * 128], qd3[:, h, csl], False, True, B(Sbf, qd), B(bkO))
                self.cp(o3[:, :, csl], bkO.ap[:, :].rearrange("p (k t) -> p k t", t=64), B(bkO), B(o), eng="act")
                bkK = [self.bank(), self.bank()]
                for h in range(4):
                    self.mm(bkK[h // 2].ap[:, (h % 2) * 256:(h % 2 + 1) * 256], kkT4[0:64, c, h, :], vt3[0:64, c, h * 256:(h + 1) * 256], True, True, B(kkT, vt), B(bkK[h // 2]))
                for h in range(4):
                    self.stt(Sst.ap[:, h, :], Sst.ap[:, h, :], el3[:, h, c:c + 1], bkK[h // 2].ap[:, (h % 2) * 256:(h % 2 + 1) * 256], ALU.mult, ALU.add, B(Sst, el, bkK[h // 2]), B(Sst))
                self.cp(Sbf.ap[:], Sst.ap[:], B(Sst), B(Sbf), eng="act")
            else:
                S0f = self.salloc(8 * 1024, F32); S0b = self.salloc(8 * 1024, BF16)
                S0f4 = S0f.ap.rearrange("p (q h v) -> p q h v", q=8, h=4, v=256)
                S0b4 = S0b.ap.rearrange("p (q h v) -> p q h v", q=8, h=4, v=256)
                for q in range(8):
                    seq = c * 8 + q
                    self.dma("pool", S0f4[:, q, :, :], I["sgla"][L, seq].rearrange("h d v -> d h v"), [], B(S0f))
                self.cp(S0b.ap[:, 0:4096], S0f.ap[:, 0:4096], B(S0f), B(S0b), eng="act")
                self.cp(S0b.ap[:, 4096:8192], S0f.ap[:, 4096:8192], B(S0f), B(S0b), eng="dve")
                for h in range(4):
                    for vh in range(2):
                        oo = bkO.ap[:, (h * 2 + vh) * 64:(h * 2 + vh + 1) * 64]
                        self.mm(oo, vt3[0:64, c, h * 256 + vh * 128:h * 256 + (vh + 1) * 128], attT4[0:64, c, h, :], True, False, B(vt, attT), B(bkO))
                        for q in range(8):
                            tsl = slice(c * 64 + q * 8, c * 64 + q * 8 + 8)
                            self.mm(oo[:, q * 8:(q + 1) * 8], S0b4[:, q, h, vh * 128:(vh + 1) * 128], qd3[:, h, tsl], False, q == 7, B(S0b, qd), B(bkO))
                self.cp(o3[:, :, csl], bkO.ap[:, :].rearrange("p (k t) -> p k t", t=64), B(bkO), B(o), eng="act")
                for q in range(8):
                    seq = c * 8 + q
                    kkm = self.salloc(512, BF16)
                    self.ts(kkm.ap[0:64, :], kkT.ap[0:64, c * 512:(c + 1) * 512], self.rowm.ap[:, q:q + 1], None, ALU.mult, None, B(kkT, self.rowm), B(kkm))
                    bkK = [self.bank(), self.bank()]
                    for h in range(4):
                        self.mm(bkK[h // 2].ap[:, (h % 2) * 256:(h % 2 + 1) * 256], kkm.ap[0:64, h * 128:(h + 1) * 128], vt3[0:64, c, h * 256:(h + 1) * 256], True, True, B(kkm, vt), B(bkK[h // 2]))
                    Sn = self.salloc(1024, F32)
                    Sn3 = Sn.ap.rearrange("p (h v) -> p h v", v=256)
                    for h in range(4):
                        self.stt(Sn3[:, h, :], S0f4[:, q, h, :], el3[:, h, seq:seq + 1], bkK[h // 2].ap[:, (h % 2) * 256:(h % 2 + 1) * 256], ALU.mult, ALU.add, B(S0f, el, bkK[h // 2]), B(Sn))
                    self.dma("sp", O["gla_s"][L, seq].rearrange("h d v -> d h v"), Sn3, B(Sn), [], store=True)
                    self.sfree(kkm, Sn)
                self.sfree(S0f, S0b)
        if tc.kind == "p" and tc.last:
            self.dma("sp", O["gla_p"][L].rearrange("h d v -> d h v"), Sst.ap[:, :, :], B(Sst), [], store=True)
        self.sfree(attT, kkT, vt, qd, kd, el)
        for h in range(4):
            sq = self.salloc(2 * TT, BF16); ob = self.salloc(2 * TT, BF16)
            self.act(sq.ap, o.ap[:, 2 * h * TT:(2 * h + 2) * TT], AF.Square, B(o), B(sq))
            self.cp(ob.ap, o.ap[:, 2 * h * TT:(2 * h + 2) * TT], B(o), B(ob), eng="act")
            bkM = self.bank(); bkQ = self.bank()
            for vh in range(2):
                self.mm(bkM.ap[:, 0:TT], self.ones[256].ap[:, :], ob.ap[:, vh * TT:(vh + 1) * TT], vh == 0, vh == 1, B(self.ones[256], ob), B(bkM))
            for vh in range(2):
                self.mm(bkQ.ap[:, 0:TT], self.ones[256].ap[:, :], sq.ap[:, vh * TT:(vh + 1) * TT], vh == 0, vh == 1, B(self.ones[256], sq), B(bkQ))
            mean, rstd = self.ln_stats(bkM, bkQ, TT)
            self.sfree(sq, ob)
            for vh in range(2):
                k = 2 * h + vh
                tmp = self.salloc(TT, F32)
                self.tt(tmp.ap, o3[:, k, :], mean.ap, ALU.subtract, B(o, mean), B(tmp))
                self.tt(tmp.ap, tmp.ap, rstd.ap, ALU.mult, B(tmp, rstd), B(tmp))
                self.stt(self.ogg.ap[:, k, 0:TT], tmp.ap, self.normg.ap[:, k:k + 1], gz3[:, k, :], ALU.mult, ALU.mult, B(tmp, self.normg, gz), B(self.ogg))
                self.sfree(tmp)
            self.sfree(mean, rstd)
        self.sfree(o, gz)

    def ln_stats(self, bkM, bkQ, TT):
        mean = self.salloc(TT, F32); rstd = self.salloc(TT, F32); m2 = self.salloc(TT, F32)
        self.cp(mean.ap, bkM.ap[:, 0:TT], B(bkM), B(mean), eng="act")
        self.act(m2.ap, bkM.ap[:, 0:TT], AF.Square, B(bkM), B(m2))
        self.tt(rstd.ap, bkQ.ap[:, 0:TT], m2.ap, ALU.subtract, B(bkQ, m2), B(rstd))
        self.ts(rstd.ap, rstd.ap, 0.0, None, ALU.max, None, B(rstd), B(rstd))
        self.act(rstd.ap, rstd.ap, AF.Sqrt, B(rstd), B(rstd), bias=LN_EPS)
        self.recip(rstd.ap, rstd.ap, B(rstd), B(rstd))
        self.sfree(m2)
        return mean, rstd

    def conv_phase(self, L, tc):
        I = self.I; O = self.O
        TT = tc.TT; xT = self.xT; xr = self.xrhs(tc)
        BCa, BCb, BCz = self.BC["ca"], self.BC["cb"], self.BC["cz"]
        if tc.kind == "p":
            GW = 30 + TT
            G = self.salloc(8 * GW, BF16)
            G3 = G.ap.rearrange("p (k t) -> p k t", t=GW)
            if tc.first:
                self.memset(self.halo.ap[:], 0.0, B(self.halo))
            self.cp(G3[:, :, 0:30], self.halo.ap[:, :, :], B(self.halo), B(G))
            gdst = lambda ct: G3[:, ct, 30:30 + TT]
        else:
            GW = 16 * 38
            G = self.salloc(8 * GW, F32)
            G4 = G.ap.rearrange("p (k q t) -> p k q t", q=16, t=38)
            for r in range(4):
                stg = self.salloc(1024, F32)
                self.dma("pool", stg.ap[0:120, :], I["cconv"][L, r * 120:(r + 1) * 120, :], [], B(stg))
                for c4 in range(2):
                    bk = self.bank()
                    for cc in range(4):
                        ct = c4 * 4 + cc
                        self.tr(bk.ap[:, cc * 120:(cc + 1) * 120], stg.ap[0:120, ct * 128:(ct + 1) * 128], self.identF.ap[0:120, 0:120], B(stg, self.identF), B(bk))
                    outv = G4[:, c4 * 4:(c4 + 1) * 4, 4 * r:4 * r + 4, 0:30]
                    inv = bk.ap[:, 0:480].rearrange("p (k q t) -> p k q t", k=4, q=4, t=30)
                    self.cp(outv, inv, B(bk), B(G))
                self.sfree(stg)
            gdst = lambda ct: G4[:, ct, :, 30:38]
        czs = self.salloc(8 * TT, BF16)
        czs3 = czs.ap.rearrange("p (k t) -> p k t", t=TT)
        for cs_ in range(4):
            wa = self.wl_in(L, OFF["ca"] + 256 * cs_, 256)
            wb = self.wl_in(L, OFF["cb"] + 256 * cs_, 256)
            for jj in range(2):
                ct = cs_ * 2 + jj
                bkb_ = self.proj_fm(wb, jj, 16, xr, xT, TT)
                sg = self.salloc(TT, F32)
                self.act(sg.ap, bkb_.ap[:, 0:TT], AF.Sigmoid, B(bkb_, self.bcol), B(sg), bias=self.bcol.ap[:, BCb + ct:BCb + ct + 1])
                bka = self.proj_fm(wa, jj, 16, xr, xT, TT)
                if tc.kind == "p":
                    self.stt(gdst(ct), bka.ap[:, 0:TT], self.bcol.ap[:, BCa + ct:BCa + ct + 1], sg.ap, ALU.add, ALU.mult, B(bka, self.bcol, sg), B(G))
                    self.stt(self.halo.ap[:, ct, :], bka.ap[:, TT - 30:TT], self.bcol.ap[:, BCa + ct:BCa + ct + 1], sg.ap[:, TT - 30:TT], ALU.add, ALU.mult, B(bka, self.bcol, sg), B(self.halo))
                else:
                    self.stt(gdst(ct), bka.ap[:, 0:TT].rearrange("p (q t) -> p q t", t=8), self.bcol.ap[:, BCa + ct:BCa + ct + 1],
                             sg.ap.rearrange("p (q t) -> p q t", t=8), ALU.add, ALU.mult, B(bka, self.bcol, sg), B(G))
                self.sfree(sg)
        for cs_ in range(4):
            ws = self.wl_in(L, OFF["cz"] + 256 * cs_, 256)
            for jj in range(2):
                ct = cs_ * 2 + jj
                bk = self.proj_fm(ws, jj, 16, xr, xT, TT)
                self.act(czs3[:, ct, :], bk.ap[:, 0:TT], AF.Silu, B(bk, self.bcol), B(czs), bias=self.bcol.ap[:, BCz + ct:BCz + ct + 1])
        acc = self.salloc(8 * TT, F32)
        acc3 = acc.ap.rearrange("p (k t) -> p k t", t=TT)
        cw = self.convw
        if tc.kind == "p":
            for ct in range(8):
                dg = self.salloc(31 * 128, BF16)
                dg3 = dg.ap.rearrange("p (k m) -> p k m", m=128)
                idb = self.identB.ap[:, :].rearrange("p (o m) -> p o m", o=1).to_broadcast([128, 31, 128])
                wv = cw.ap[:, ct, :].rearrange("p (k o) -> p k o", o=1).to_broadcast([128, 31, 128])
                self.tt(dg3, idb, wv, ALU.mult, B(self.identB, cw), B(dg))
                bk = self.bank()
                for k in range(31):
                    self.mm(bk.ap[:, 0:TT], dg3[:, k, :], G3[:, ct, k:k + TT], k == 0, k == 30, B(dg, G), B(bk))
                self.act(acc3[:, ct, :], bk.ap[:, 0:TT], AF.Identity, B(bk, self.convb), B(acc), bias=self.convb.ap[:, ct:ct + 1])
                self.sfree(dg)
        else:
            for ct in range(8):
                src = lambda k: G4[:, ct, :, k:k + 8]
                dst = acc3[:, ct, :].rearrange("p (q t) -> p q t", t=8)
                self.ts(dst, src(0), cw.ap[:, ct, 0:1], self.convb.ap[:, ct:ct + 1], ALU.mult, ALU.add, B(G, cw, self.convb), B(acc))
                for k in range(1, 31):
                    self.stt(dst, src(k), cw.ap[:, ct, k:k + 1], dst, ALU.mult, ALU.add, B(G, cw, acc), B(acc))
        if tc.kind == "p":
            if tc.last:
                stg = self.salloc(1024, F32)
                for c4 in range(2):
                    bk = self.bank()
                    for cc in range(4):
                        ct = c4 * 4 + cc
                        self.tr(bk.ap[0:30, cc * 128:(cc + 1) * 128], self.halo.ap[:, ct, :], self.identF.ap[:, :], B(self.halo, self.identF), B(bk))
                    self.cp(stg.ap[0:30, c4 * 512:(c4 + 1) * 512], bk.ap[0:30, :], B(bk), B(stg), eng="act")
                self.dma("sp", O["conv_p"][L, :, :], stg.ap[0:30, :], B(stg), [], store=True)
                self.sfree(stg)
        else:
            cn = self.salloc(8 * 480, F32)
            cn4 = cn.ap.rearrange("p (k q t) -> p k q t", q=16, t=30)
            self.cp(cn4, G4[:, :, :, 8:38], B(G), B(cn), eng="act")
            for r in range(4):
                stg = self.salloc(1024, F32)
                for c4 in range(2):
                    bk = self.bank()
                    for cc in range(4):
                        ct = c4 * 4 + cc
                        self.tr(bk.ap[0:120, cc * 128:(cc + 1) * 128], cn.ap[:, ct * 480 + r * 120:ct * 480 + (r + 1) * 120], self.identF.ap[:, :], B(cn, self.identF), B(bk))
                    self.cp(stg.ap[0:120, c4 * 512:(c4 + 1) * 512], bk.ap[0:120, :], B(bk), B(stg), eng="act")
                self.dma("sp", O["conv_s"][L, r * 120:(r + 1) * 120, :], stg.ap[0:120, :], B(stg), [], store=True)
                self.sfree(stg)
            self.sfree(cn)
        self.sfree(G)
        bkM = self.bank(); bkQ = self.bank()
        for ct in range(8):
            sq = self.salloc(TT, BF16); ab = self.salloc(TT, BF16)
            self.act(sq.ap, acc3[:, ct, :], AF.Square, B(acc), B(sq))
            self.cp(ab.ap, acc3[:, ct, :], B(acc), B(ab), eng="act")
            self.mm(bkM.ap[:, 0:TT], self.ones[1024].ap[:, :], ab.ap, ct == 0, ct == 7, B(self.ones[1024], ab), B(bkM))
            self.mm(bkQ.ap[:, 0:TT], self.ones[1024].ap[:, :], sq.ap, ct == 0, ct == 7, B(self.ones[1024], sq), B(bkQ))
            self.sfree(sq, ab)
        mean, rstd = self.ln_stats(bkM, bkQ, TT)
        for ct in range(8):
            tmp = self.salloc(TT, F32)
            self.tt(tmp.ap, acc3[:, ct, :], mean.ap, ALU.subtract, B(acc, mean), B(tmp))
            self.tt(tmp.ap, tmp.ap, rstd.ap, ALU.mult, B(tmp, rstd), B(tmp))
            self.act(tmp.ap, tmp.ap, AF.Silu, B(tmp, self.clng, self.clnb), B(tmp), bias=self.clnb.ap[:, ct:ct + 1], scale=self.clng.ap[:, ct:ct + 1])
            self.tt(self.ycg.ap[:, ct, 0:TT], tmp.ap, czs3[:, ct, :], ALU.mult, B(tmp, czs), B(self.ycg))
            self.sfree(tmp)
        self.sfree(mean, rstd, acc, czs)

    def merge_phase(self, L, tc):
        I = self.I; O = self.O
        TT = tc.TT; xT = self.xT; xr = self.xrhs(tc)
        NL = self.NL
        BCg = self.BC["g"]
        merged = self.salloc(16 * TT, BF16)
        mg3 = merged.ap.rearrange("p (k t) -> p k t", t=TT)
        branches = [("p_s5", self.ys5g, 0), ("p_gla", self.ogg, 1), ("p_conv", self.ycg, 2)]
        for jo4 in range(4):
            macc = self.salloc(4 * TT, F32)
            macc3 = macc.ap.rearrange("p (k t) -> p k t", t=TT)
            for (wn, br, bi) in branches:
                wp = self.wl_mat(wn, L, 512 * jo4, 512, 8)
                for half in range(2):
                    gcol = OFF["g"] + bi * 2048 + jo4 * 512 + half * 256
                    wg = self.wl_in(L, gcol, 256)
                    for jj in range(2):
                        jl = half * 2 + jj
                        jo = jo4 * 4 + jl
                        bkG = self.proj_fm(wg, jj, 16, xr, xT, TT)
                        sg = self.salloc(TT, F32)
                        bcix = BCg + bi * 16 + jo
                        self.act(sg.ap, bkG.ap[:, 0:TT], AF.Sigmoid, B(bkG, self.bcol), B(sg), bias=self.bcol.ap[:, bcix:bcix + 1])
                        bkA = self.proj_fm(wp, jl, 8, lambda kt, br=br: br.ap[:, kt, 0:TT], br, TT)
                        if bi == 0:
                            self.tt(macc3[:, jl, :], bkA.ap[:, 0:TT], sg.ap, ALU.mult, B(bkA, sg), B(macc))
                        else:
                            self.tt(sg.ap, bkA.ap[:, 0:TT], sg.ap, ALU.mult, B(bkA, sg), B(sg))
                            if bi == 1:
                                self.tt(macc3[:, jl, :], macc3[:, jl, :], sg.ap, ALU.add, B(macc, sg), B(macc))
                            else:
                                self.tt(mg3[:, jo, :], macc3[:, jl, :], sg.ap, ALU.add, B(macc, sg), B(merged))
                        self.sfree(sg)
            self.sfree(macc)
        hT = self.salloc(16 * TT, F32); hbf = self.salloc(16 * TT, BF16)
        h3 = hT.ap.rearrange("p (k t) -> p k t", t=TT); hb3 = hbf.ap.rearrange("p (k t) -> p k t", t=TT)
        for cs_ in range(8):
            ws = self.wl_mat("w_o", L, 256 * cs_, 256, 16)
            for jj in range(2):
                jo = cs_ * 2 + jj
                bk = self.proj_fm(ws, jj, 16, lambda kt: mg3[:, kt, :], merged, TT)
                self.stt(h3[:, jo, :], xT.ap[:, jo, 0:TT], float(DN_ALPHA), bk.ap[:, 0:TT], ALU.mult, ALU.add, B(xT, bk), B(hT))
                self.cp(hb3[:, jo, :], h3[:, jo, :], B(hT), B(hbf), eng="act")
        self.sfree(merged)
        wpe = self.wload(self.I["w_pe"][L].rearrange("(k p) n -> p k n", p=128), 2, 2048, key=("w_pe", L, 0, 2048))
        pe_all = self.salloc(16 * TT, BF16)
        pe3 = pe_all.ap.rearrange("p (k t) -> p k t", t=TT)
        for jo in range(16):
            bk = self.bank()
            for kt in range(2):
                self.mm(bk.ap[:, 0:TT], wpe.ap[:, kt, jo * 128:(jo + 1) * 128], self.pT.ap[:, kt, 0:TT], kt == 0, kt == 1, B(wpe, self.pT), B(bk))
            self.cp(pe3[:, jo, :], bk.ap[:, 0:TT], B(bk), B(pe_all), eng="act")
        for cs_ in range(8):
            ws = self.wl_mat("w_pg", L, 256 * cs_, 256, 16)
            for jj in range(2):
                jo = cs_ * 2 + jj
                bk = self.proj_fm(ws, jj, 16, lambda kt: hb3[:, kt, :], hbf, TT)
                sg = self.salloc(TT, F32)
                self.act(sg.ap, bk.ap[:, 0:TT], AF.Sigmoid, B(bk), B(sg))
                self.tt(sg.ap, sg.ap, pe3[:, jo, :], ALU.mult, B(sg, pe_all), B(sg))
                self.tt(h3[:, jo, :], h3[:, jo, :], sg.ap, ALU.add, B(hT, sg), B(hT))
                self.sfree(sg)
        self.sfree(hbf, pe_all)
        bkM = self.bank(); bkQ = self.bank()
        for jo in range(16):
            sq = self.salloc(TT, BF16); hb2 = self.salloc(TT, BF16)
            self.act(sq.ap, h3[:, jo, :], AF.Square, B(hT), B(sq))
            self.cp(hb2.ap, h3[:, jo, :], B(hT), B(hb2), eng="act")
            self.mm(bkM.ap[:, 0:TT], self.ones[2048].ap[:, :], hb2.ap, jo == 0, jo == 15, B(self.ones[2048], hb2), B(bkM))
            self.mm(bkQ.ap[:, 0:TT], self.ones[2048].ap[:, :], sq.ap, jo == 0, jo == 15, B(self.ones[2048], sq), B(bkQ))
            self.sfree(sq, hb2)
        mean, rstd = self.ln_stats(bkM, bkQ, TT)
        last = (L == NL - 1)
        if not last:
            xo = self.salloc(16 * 512, BF16)
            xo3 = xo.ap.rearrange("p (k t) -> p k t", t=512)
        for jo in range(16):
            self.tt(h3[:, jo, :], h3[:, jo, :], mean.ap, ALU.subtract, B(hT, mean), B(hT))
            self.tt(h3[:, jo, :], h3[:, jo, :], rstd.ap, ALU.mult, B(hT, rstd), B(hT))
            if last:
                self.act(h3[:, jo, :], h3[:, jo, :], AF.Identity, B(hT, self.lng, self.lnb), B(hT), bias=self.lnb.ap[:, jo:jo + 1], scale=self.lng.ap[:, jo:jo + 1])
            else:
                self.act(xo3[:, jo, 0:TT], h3[:, jo, :], AF.Identity, B(hT, self.lng, self.lnb), B(xo), bias=self.lnb.ap[:, jo:jo + 1], scale=self.lng.ap[:, jo:jo + 1])
        self.sfree(mean, rstd)
        if not last:
            par = L % 2
            dst = self.spill[par, tc.idx].rearrange("p (k t) -> p k t", t=512)[:, :, 0:TT]
            self.dma("pool", dst, xo3[:, :, 0:TT], B(xo), [self.spill_buf[par][tc.idx]])
            self.sfree(xo)
        else:
            dsto = O["yp"] if tc.kind == "p" else O["ys"]
            for tg in range(TT // 128):
                stg = self.salloc(2048, F32)
                for k4 in range(4):
                    bk = self.bank()
                    for kk in range(4):
                        jo = k4 * 4 + kk
                        self.tr(bk.ap[:, kk * 128:(kk + 1) * 128], h3[:, jo, tg * 128:(tg + 1) * 128], self.identF.ap[:, :], B(hT, self.identF), B(bk))
                    self.cp(stg.ap[:, k4 * 512:(k4 + 1) * 512], bk.ap[:, :], B(bk), B(stg), eng=("act" if k4 % 2 else "dve"))
                r0 = tc.tok0 + tg * 128
                self.dma("sp", dsto[r0:r0 + 128, :], stg.ap, B(stg), [], store=True)
                self.sfree(stg)
        self.sfree(hT)


_CACHE = {}


def _get_prog(NL, NPT):
    key = (NL, NPT)
    if key not in _CACHE:
        _CACHE[key] = Builder(NL, NPT).build()
    return _CACHE[key]


def core_inputs(inputs, c, NL, NPT, prompt_b, seq0):
    f = lambda a: np.ascontiguousarray(np.asarray(a, dtype=np.float32))
    TP = NPT * 512
    m = {}
    m["xp"] = f(inputs["x_prompt"][prompt_b, :TP])
    m["xs"] = f(inputs["x_sample"][seq0:seq0 + 16]).reshape(128, D)
    m["pp"] = f(inputs["p_prompt"][:NL, prompt_b, :TP])
    m["pps"] = f(inputs["p_sample"][:NL, seq0:seq0 + 16]).reshape(NL, 128, 256)
    m["sre"] = f(inputs["state_s5_re"][:NL, seq0:seq0 + 16]).reshape(NL, 512, 128)
    m["sim"] = f(inputs["state_s5_im"][:NL, seq0:seq0 + 16]).reshape(NL, 512, 128)
    m["sgla"] = f(inputs["state_gla"][:NL, seq0:seq0 + 16])
    m["cconv"] = f(inputs["cache_conv"][:NL, seq0:seq0 + 16]).reshape(NL, 480, 1024)
    return m


def kernel(**inputs):
    NL, NPT = 4, 4
    nc = _get_prog(NL, NPT)
    wshared = {n: np.ascontiguousarray(np.asarray(inputs[n], dtype=np.float32)[:NL]) for n in W_NAMES}
    in_maps = []
    for c in range(8):
        m = core_inputs(inputs, c, NL, NPT, c // 2, 16 * c)
        m.update(wshared)
        in_maps.append(m)
    res = run_bass_kernel_spmd(nc, in_maps, core_ids=list(range(8)))
    R = res.results
    y_p = np.stack([R[2 * b]["yp"] for b in range(4)]).reshape(4, 2048, D)
    y_s = np.concatenate([R[c]["ys"].reshape(16, 8, D) for c in range(8)], axis=0)
    s5re_p = np.stack([R[2 * b]["sre_p"].reshape(NL, 64, 64) for b in range(4)], axis=1)
    s5im_p = np.stack([R[2 * b]["sim_p"].reshape(NL, 64, 64) for b in range(4)], axis=1)
    gla_p = np.stack([R[2 * b]["gla_p"] for b in range(4)], axis=1)
    conv_p = np.stack([R[2 * b]["conv_p"] for b in range(4)], axis=1)
    s5re_s = np.concatenate([R[c]["sre_s"].reshape(NL, 16, 64, 64) for c in range(8)], axis=1)
    s5im_s = np.concatenate([R[c]["sim_s"].reshape(NL, 16, 64, 64) for c in range(8)], axis=1)
    gla_s = np.concatenate([R[c]["gla_s"] for c in range(8)], axis=1)
    conv_s = np.concatenate([R[c]["conv_s"].reshape(NL, 16, 30, 1024) for c in range(8)], axis=1)
    outs = (y_p, y_s, s5re_p, s5im_p, gla_p, conv_p, s5re_s, s5im_s, gla_s, conv_s)
    return tuple(np.ascontiguousarray(o, dtype=np.float32) for o in outs)
```
